# Optimizing a Trainium2 kernel written in Bass

```python
import math
import jax, jax.numpy as jnp
from jax import lax
import numpy as np

D_MODEL = 2048
BATCH = 4
SEQ = 8192
DEPTH = 4

N_MIXERS = 3
DEEPNORM_ALPHA = (2.0 * DEPTH) ** 0.25
DEEPNORM_BETA = (8.0 * DEPTH) ** -0.25
LN_EPS = 1e-5
RMS_EPS = 1e-6

FFN_DIM = 5632

GDN_QK_HEADS = 16
GDN_V_HEADS = 32
GDN_HEAD_DIM = 128
GDN_CONV = 4
GDN_CHUNK = 64
GDN_QK_DIM = GDN_QK_HEADS * GDN_HEAD_DIM
GDN_V_DIM = GDN_V_HEADS * GDN_HEAD_DIM
GDN_CONV_DIM = 2 * GDN_QK_DIM + GDN_V_DIM
GDN_IN_DIM = GDN_CONV_DIM + GDN_V_DIM + 2 * GDN_V_HEADS

GLA_HEADS = 4
GLA_K_DIM = D_MODEL // 2
GLA_V_DIM = D_MODEL
GLA_HEAD_K = GLA_K_DIM // GLA_HEADS
GLA_HEAD_V = GLA_V_DIM // GLA_HEADS
GLA_GATE_RANK = 16
GLA_GATE_TAU = 16.0
GLA_CHUNK = 64
GLA_IN_DIM = 2 * GLA_K_DIM + 2 * GLA_V_DIM + GLA_GATE_RANK

S5_GROUP = 16
S5_GROUPS = D_MODEL // S5_GROUP
S5_STATE = 64
S5_CHUNK = 512

N_GDN = (DEPTH + 2) // 3
N_GLA = (DEPTH + 1) // 3
N_S5 = DEPTH // 3

kernel_name = "hybrid_gdn_gla_s5_macaron_deepnorm"

F32 = jnp.float32


def layer_norm(x, g, b):
    xf = x.astype(F32)
    mu = jnp.mean(xf, -1, keepdims=True)
    xc = xf - mu
    var = jnp.mean(xc * xc, -1, keepdims=True)
    return (xc * lax.rsqrt(var + LN_EPS) * g.astype(F32) + b.astype(F32)).astype(x.dtype)


def rms_norm(x, w):
    xf = x.astype(F32)
    return xf * lax.rsqrt(jnp.mean(xf * xf, -1, keepdims=True) + RMS_EPS) * w.astype(F32)


def l2_normalize(x):
    xf = x.astype(F32)
    return xf * lax.rsqrt(jnp.sum(xf * xf, -1, keepdims=True) + RMS_EPS)


def swiglu_ffn(x, w_up, w_down):
    gate, up = jnp.split(x @ w_up, 2, axis=-1)
    return (jax.nn.silu(gate) * up) @ w_down


def causal_depthwise_conv(x, w):
    k = w.shape[0]
    return lax.conv_general_dilated(
        x, w[:, None, :].astype(x.dtype), window_strides=(1,), padding=[(k - 1, 0)],
        dimension_numbers=("NWC", "WIO", "NWC"), feature_group_count=x.shape[-1])


def to_chunks(t, c):
    b, s = t.shape[:2]
    t = t.reshape((b, s // c, c) + t.shape[2:])
    return t.transpose((1, 0, 3, 2) + tuple(range(4, t.ndim)))


def from_chunks(t):
    n, b, h, c = t.shape[:4]
    t = t.transpose((1, 0, 3, 2) + tuple(range(4, t.ndim)))
    return t.reshape((b, n * c, h) + t.shape[4:])


def chunk_gated_delta_rule(q, k, v, g, beta):
    bsz, _, hk, dk = q.shape
    h, dv = v.shape[2], v.shape[3]
    rep = h // hk
    c = GDN_CHUNK
    causal = jnp.tril(jnp.ones((c, c), bool))
    strict = jnp.tril(jnp.ones((c, c), bool), -1)

    def step(state, inp):
        qc, kc, vc, gc, bc = inp
        qc = jnp.repeat(qc, rep, axis=1)
        kc = jnp.repeat(kc, rep, axis=1)
        gcum = jnp.cumsum(gc, axis=-1)
        decay = jnp.exp(jnp.where(causal, gcum[..., :, None] - gcum[..., None, :], -jnp.inf))
        kb = kc * bc[..., None]
        lmat = jnp.where(strict, jnp.einsum("bhid,bhjd->bhij", kb, kc) * decay, 0.0)
        rhs = jnp.concatenate([vc * bc[..., None], kb * jnp.exp(gcum)[..., None]], axis=-1)
        sol = lax.linalg.triangular_solve(lmat, rhs, left_side=True, lower=True, unit_diagonal=True)
        u, w = sol[..., :dv], sol[..., dv:]
        v_new = u - jnp.einsum("bhck,bhkv->bhcv", w, state)
        attn = jnp.einsum("bhid,bhjd->bhij", qc, kc) * decay
        o = (jnp.einsum("bhck,bhkv->bhcv", qc * jnp.exp(gcum)[..., None], state)
             + jnp.einsum("bhij,bhjv->bhiv", attn, v_new))
        glast = gcum[..., -1]
        state = (state * jnp.exp(glast)[..., None, None]
                 + jnp.einsum("bhck,bhcv->bhkv", kc * jnp.exp(glast[..., None] - gcum)[..., None], v_new))
        return state, o

    s0 = jnp.zeros((bsz, h, dk, dv), F32)
    _, o = lax.scan(step, s0, (to_chunks(q, c), to_chunks(k, c), to_chunks(v, c),
                               to_chunks(g, c), to_chunks(beta, c)))
    return from_chunks(o)


def gdn_mixer(x, w_in, conv_w, a_log, dt_bias, norm_w, w_out):
    bsz, s, _ = x.shape
    proj = x @ w_in
    qkv, z, b_in, a_in = jnp.split(
        proj, [GDN_CONV_DIM, GDN_CONV_DIM + GDN_V_DIM, GDN_CONV_DIM + GDN_V_DIM + GDN_V_HEADS], axis=-1)
    qkv = jax.nn.silu(causal_depthwise_conv(qkv, conv_w))
    q, k, v = jnp.split(qkv, [GDN_QK_DIM, 2 * GDN_QK_DIM], axis=-1)
    q = l2_normalize(q.reshape(bsz, s, GDN_QK_HEADS, GDN_HEAD_DIM)) * (GDN_HEAD_DIM ** -0.5)
    k = l2_normalize(k.reshape(bsz, s, GDN_QK_HEADS, GDN_HEAD_DIM))
    v = v.reshape(bsz, s, GDN_V_HEADS, GDN_HEAD_DIM).astype(F32)
    beta = jax.nn.sigmoid(b_in.astype(F32))
    g = -jnp.exp(a_log.astype(F32)) * jax.nn.softplus(a_in.astype(F32) + dt_bias.astype(F32))
    o = chunk_gated_delta_rule(q, k, v, g, beta)
    o = rms_norm(o, norm_w) * jax.nn.silu(z.astype(F32).reshape(bsz, s, GDN_V_HEADS, GDN_HEAD_DIM))
    return o.reshape(bsz, s, GDN_V_DIM).astype(x.dtype) @ w_out


def chunk_gla(q, k, v, log_f):
    bsz, _, h, dk = q.shape
    dv = v.shape[-1]
    c = GLA_CHUNK
    causal = jnp.tril(jnp.ones((c, c), bool))[..., None]

    def step(state, inp):
        qc, kc, vc, fc = inp
        bcum = jnp.cumsum(fc, axis=2)
        decay = jnp.exp(jnp.where(causal, bcum[:, :, :, None, :] - bcum[:, :, None, :, :], -jnp.inf))
        attn = jnp.einsum("bhik,bhjk,bhijk->bhij", qc, kc, decay)
        o = (jnp.einsum("bhck,bhkv->bhcv", qc * jnp.exp(bcum), state)
             + jnp.einsum("bhij,bhjv->bhiv", attn, vc))
        blast = bcum[:, :, -1]
        state = (state * jnp.exp(blast)[..., None]
                 + jnp.einsum("bhck,bhcv->bhkv", kc * jnp.exp(blast[:, :, None] - bcum), vc))
        return state, o

    s0 = jnp.zeros((bsz, h, dk, dv), F32)
    _, o = lax.scan(step, s0, (to_chunks(q, c), to_chunks(k, c), to_chunks(v, c), to_chunks(log_f, c)))
    return from_chunks(o)


def gla_mixer(x, w_in, w_gate, gate_bias, norm_w, w_out):
    bsz, s, _ = x.shape
    proj = x @ w_in
    q, k, v, r, gl = jnp.split(
        proj, [GLA_K_DIM, 2 * GLA_K_DIM, 2 * GLA_K_DIM + GLA_V_DIM, 2 * GLA_K_DIM + 2 * GLA_V_DIM], axis=-1)
    log_f = jax.nn.log_sigmoid((gl @ w_gate).astype(F32) + gate_bias.astype(F32)) / GLA_GATE_TAU
    q = q.reshape(bsz, s, GLA_HEADS, GLA_HEAD_K).astype(F32) * (GLA_HEAD_K ** -0.5)
    k = k.reshape(bsz, s, GLA_HEADS, GLA_HEAD_K).astype(F32)
    v = v.reshape(bsz, s, GLA_HEADS, GLA_HEAD_V).astype(F32)
    o = chunk_gla(q, k, v, log_f.reshape(bsz, s, GLA_HEADS, GLA_HEAD_K))
    o = rms_norm(o, norm_w) * jax.nn.silu(r.astype(F32).reshape(bsz, s, GLA_HEADS, GLA_HEAD_V))
    return o.reshape(bsz, s, GLA_V_DIM).astype(x.dtype) @ w_out


def _linear_recurrence_op(e1, e2):
    a1, b1 = e1
    a2, b2 = e2
    return a1 * a2, a2 * b1 + b2


def s5_mixer(x, a_re, a_im, b_re, b_im, c_re, c_im, d_skip, log_dt, w_glu):
    bsz, s, dm = x.shape
    lc = math.gcd(s, S5_CHUNK)
    u = x.astype(F32).reshape(bsz, s // lc, lc, S5_GROUPS, S5_GROUP).transpose(1, 2, 0, 3, 4)
    lam = lax.complex(a_re.astype(F32), a_im.astype(F32))
    dt = jnp.exp(log_dt.astype(F32))[:, None]
    a_bar = jnp.exp(lam * dt)
    b_bar = ((a_bar - 1.0) / lam)[..., None] * lax.complex(b_re.astype(F32), b_im.astype(F32))
    c_mat = lax.complex(c_re.astype(F32), c_im.astype(F32))
    d = d_skip.astype(F32).reshape(S5_GROUPS, S5_GROUP)

    def step(h, uc):
        bu = jnp.einsum("tbgh,gph->tbgp", uc.astype(jnp.complex64), b_bar)
        a = jnp.broadcast_to(a_bar, bu.shape)
        a_cum, b_cum = lax.associative_scan(_linear_recurrence_op, (a, bu), axis=0)
        states = b_cum + a_cum * h[None]
        y = jnp.real(jnp.einsum("tbgp,ghp->tbgh", states, c_mat)) + d * uc
        return states[-1], y

    h0 = jnp.zeros((bsz, S5_GROUPS, S5_STATE), jnp.complex64)
    _, y = lax.scan(step, h0, u)
    y = jax.nn.gelu(y.transpose(2, 0, 1, 3, 4).reshape(bsz, s, dm)).astype(x.dtype)
    val, gate = jnp.split(y @ w_glu, 2, axis=-1)
    return val * jax.nn.sigmoid(gate)


def setup_inputs(seed: int = 0) -> dict:
    key = jax.random.key(seed)
    ks = jax.random.split(key, 24)

    def nrm(k, shape, scale):
        return jax.random.normal(k, shape, F32) * scale

    beta = DEEPNORM_BETA
    x = nrm(ks[0], (BATCH, SEQ, D_MODEL), 1.0)
    ln_g = 1.0 + nrm(ks[1], (DEPTH, 3, D_MODEL), 0.02)
    ln_b = nrm(ks[2], (DEPTH, 3, D_MODEL), 0.02)
    ffn_w_up = nrm(ks[3], (DEPTH, 2, D_MODEL, 2 * FFN_DIM), D_MODEL ** -0.5)
    ffn_w_down = nrm(ks[4], (DEPTH, 2, FFN_DIM, D_MODEL), beta * FFN_DIM ** -0.5)

    gdn_w_in = nrm(ks[5], (N_GDN, D_MODEL, GDN_IN_DIM), D_MODEL ** -0.5)
    gdn_conv_w = nrm(ks[6], (N_GDN, GDN_CONV, GDN_CONV_DIM), GDN_CONV ** -0.5)
    gdn_a_log = jnp.log(jax.random.uniform(ks[7], (N_GDN, GDN_V_HEADS), F32, 1.0, 16.0))
    dt = jnp.exp(jax.random.uniform(ks[8], (N_GDN, GDN_V_HEADS), F32, math.log(1e-3), math.log(1e-1)))
    gdn_dt_bias = dt + jnp.log(-jnp.expm1(-dt))
    gdn_norm_w = 1.0 + nrm(ks[9], (N_GDN, GDN_HEAD_DIM), 0.02)
    gdn_w_out = nrm(ks[10], (N_GDN, GDN_V_DIM, D_MODEL), beta * GDN_V_DIM ** -0.5)

    gla_w_in = nrm(ks[11], (N_GLA, D_MODEL, GLA_IN_DIM), D_MODEL ** -0.5)
    gla_w_gate = nrm(ks[12], (N_GLA, GLA_GATE_RANK, GLA_K_DIM), GLA_GATE_RANK ** -0.5)
    gla_gate_bias = nrm(ks[13], (N_GLA, GLA_K_DIM), 0.1)
    gla_norm_w = 1.0 + nrm(ks[14], (N_GLA, GLA_HEAD_V), 0.02)
    gla_w_out = nrm(ks[15], (N_GLA, GLA_V_DIM, D_MODEL), beta * GLA_V_DIM ** -0.5)

    s5_a_re = -0.5 + nrm(ks[16], (N_S5, S5_GROUPS, S5_STATE), 0.01)
    s5_a_im = jnp.broadcast_to(jnp.pi * jnp.arange(S5_STATE, dtype=F32), (N_S5, S5_GROUPS, S5_STATE))
    s5_b_re = nrm(ks[17], (N_S5, S5_GROUPS, S5_STATE, S5_GROUP), (2 * S5_GROUP) ** -0.5)
    s5_b_im = nrm(ks[18], (N_S5, S5_GROUPS, S5_STATE, S5_GROUP), (2 * S5_GROUP) ** -0.5)
    s5_c_re = nrm(ks[19], (N_S5, S5_GROUPS, S5_GROUP, S5_STATE), (2 * S5_STATE) ** -0.5)
    s5_c_im = nrm(ks[20], (N_S5, S5_GROUPS, S5_GROUP, S5_STATE), (2 * S5_STATE) ** -0.5)
    s5_d = nrm(ks[21], (N_S5, D_MODEL), 1.0)
    s5_log_dt = jax.random.uniform(ks[22], (N_S5, S5_GROUPS), F32, math.log(1e-3), math.log(1e-1))
    glu = nrm(ks[23], (N_S5, D_MODEL, 2 * D_MODEL), D_MODEL ** -0.5)
    s5_w_glu = jnp.concatenate([glu[..., :D_MODEL] * beta, glu[..., D_MODEL:]], axis=-1)

    return {"x": x, "ln_g": ln_g, "ln_b": ln_b, "ffn_w_up": ffn_w_up, "ffn_w_down": ffn_w_down,
            "gdn_w_in": gdn_w_in, "gdn_conv_w": gdn_conv_w, "gdn_a_log": gdn_a_log,
            "gdn_dt_bias": gdn_dt_bias, "gdn_norm_w": gdn_norm_w, "gdn_w_out": gdn_w_out,
            "gla_w_in": gla_w_in, "gla_w_gate": gla_w_gate, "gla_gate_bias": gla_gate_bias,
            "gla_norm_w": gla_norm_w, "gla_w_out": gla_w_out,
            "s5_a_re": s5_a_re, "s5_a_im": s5_a_im, "s5_b_re": s5_b_re, "s5_b_im": s5_b_im,
            "s5_c_re": s5_c_re, "s5_c_im": s5_c_im, "s5_d": s5_d, "s5_log_dt": s5_log_dt,
            "s5_w_glu": s5_w_glu}


def reference(x, ln_g, ln_b, ffn_w_up, ffn_w_down,
              gdn_w_in, gdn_conv_w, gdn_a_log, gdn_dt_bias, gdn_norm_w, gdn_w_out,
              gla_w_in, gla_w_gate, gla_gate_bias, gla_norm_w, gla_w_out,
              s5_a_re, s5_a_im, s5_b_re, s5_b_im, s5_c_re, s5_c_im, s5_d, s5_log_dt, s5_w_glu):
    for i in range(DEPTH):
        x = layer_norm(DEEPNORM_ALPHA * x + 0.5 * swiglu_ffn(x, ffn_w_up[i, 0], ffn_w_down[i, 0]),
                       ln_g[i, 0], ln_b[i, 0])
        kind, j = i % N_MIXERS, i // N_MIXERS
        if kind == 0:
            m = gdn_mixer(x, gdn_w_in[j], gdn_conv_w[j], gdn_a_log[j], gdn_dt_bias[j], gdn_norm_w[j], gdn_w_out[j])
        elif kind == 1:
            m = gla_mixer(x, gla_w_in[j], gla_w_gate[j], gla_gate_bias[j], gla_norm_w[j], gla_w_out[j])
        else:
            m = s5_mixer(x, s5_a_re[j], s5_a_im[j], s5_b_re[j], s5_b_im[j], s5_c_re[j], s5_c_im[j],
                         s5_d[j], s5_log_dt[j], s5_w_glu[j])
        x = layer_norm(DEEPNORM_ALPHA * x + m, ln_g[i, 1], ln_b[i, 1])
        x = layer_norm(DEEPNORM_ALPHA * x + 0.5 * swiglu_ffn(x, ffn_w_up[i, 1], ffn_w_down[i, 1]),
                       ln_g[i, 2], ln_b[i, 2])
    return x
```

```python
import contextlib
import numpy as np
import concourse.bass as bass
import concourse.mybir as mybir

F32 = mybir.dt.float32
BF16 = mybir.dt.bfloat16
AF = mybir.ActivationFunctionType
ALU = mybir.AluOpType
AX = mybir.AxisListType


class Buf:
    __slots__ = ("h", "name", "w", "r", "dsem")

    def __init__(self, h, name):
        self.h = h
        self.name = name
        self.w = {}
        self.r = {}
        self.dsem = None


class Prog:
    ENG = ("pe", "act", "dve", "pool", "sp")

    def __init__(self, nc, n_dma_sems=80, same_engine_sync=True):
        self.nc = nc
        self.es = contextlib.ExitStack()
        self.eng = {"pe": nc.tensor, "act": nc.scalar, "dve": nc.vector, "pool": nc.gpsimd, "sp": nc.sync}
        self.sem = {}
        self.cnt = {}
        for e in self.ENG:
            self.sem[e] = self.es.enter_context(nc.semaphore("prog_" + e))
            self.cnt[e] = 0
        self.free_dsems = []
        for i in range(n_dma_sems):
            k = "d%d" % i
            self.sem[k] = self.es.enter_context(nc.semaphore("dma_" + k))
            self.cnt[k] = 0
            self.free_dsems.append(k)
        self.seen = {e: {} for e in self.ENG}
        self.same_engine_sync = same_engine_sync
        self.n_wait = 0
        self.n_inst = 0
        self.stage_bufs = []
        self.sem["cc"] = self.es.enter_context(nc.semaphore("cc_sem"))
        self.cnt["cc"] = 0

    def uniq(self, base):
        self._u = getattr(self, '_u', 0) + 1
        return '%s_%d' % (base, self._u)

    def sb(self, stack, name, shape, dt):
        b = Buf(stack.enter_context(self.nc.sbuf_tensor(name, list(shape), dt)), name)
        self.stage_bufs.append(b)
        return b

    def view(self, h, name):
        b = Buf(h, name)
        self.stage_bufs.append(b)
        return b

    def ps(self, stack, name, shape, dt=F32):
        return Buf(stack.enter_context(self.nc.psum_tensor(name, list(shape), dt)), name)

    def dram(self, name, shape, dt, kind="Internal"):
        return Buf(self.nc.dram_tensor(name, list(shape), dt, kind=kind).ap(), name)

    def release(self, bufs):
        for b in bufs:
            if b.dsem is not None:
                self.free_dsems.append(b.dsem)
                b.dsem = None

    def _wait(self, e, key, idx):
        if key == e and (e == "pe" or not self.same_engine_sync):
            return
        if self.seen[e].get(key, 0) >= idx:
            return
        self.seen[e][key] = idx
        self.eng[e].wait_ge(self.sem[key], idx)
        self.n_wait += 1

    def _deps(self, e, w, r):
        for b in r:
            for k, i in b.w.items():
                self._wait(e, k, i)
        for b in w:
            for k, i in b.w.items():
                self._wait(e, k, i)
            for k, i in b.r.items():
                self._wait(e, k, i)

    def op(self, e, fn, w=(), r=()):
        self._deps(e, w, r)
        inst = fn(self.eng[e])
        self.cnt[e] += 1
        idx = self.cnt[e]
        inst.then_inc(self.sem[e], 1)
        for b in w:
            b.w[e] = idx
        for b in r:
            b.r[e] = idx
        self.n_inst += 1
        return inst

    def dma(self, q, out_ap, in_ap, w, r, owner=None):
        self._deps(q, [w], [r])
        if owner is None:
            owner = w
        if owner.dsem is None:
            owner.dsem = self.free_dsems.pop()
        k = owner.dsem
        inst = self.eng[q].dma_start(out=out_ap, in_=in_ap)
        self.cnt[k] += 16
        inst.then_inc(self.sem[k], 16)
        w.w[k] = self.cnt[k]
        r.r[k] = self.cnt[k]
        self.n_inst += 1
        return inst

    def barrier(self):
        keys = [k for k in self.cnt if self.cnt[k] > 0]
        for e in self.ENG:
            for k in keys:
                if self.seen[e].get(k, 0) >= self.cnt[k]:
                    continue
                self.seen[e][k] = self.cnt[k]
                self.eng[e].wait_ge(self.sem[k], self.cnt[k])
                self.n_wait += 1

    def end_stage(self):
        self.barrier()
        self.release(self.stage_bufs)
        self.stage_bufs = []

    def collective(self, kind, src, dst, rg, src_ap=None, dst_ap=None):
        self._deps("pool", [dst], [src])
        src_ap = src.h if src_ap is None else src_ap
        dst_ap = dst.h if dst_ap is None else dst_ap
        inst = self.nc.gpsimd.collective_compute(kind, ALU.bypass, ins=[src_ap.opt()], outs=[dst_ap.opt()],
                                                 replica_groups=rg)
        self.cnt["cc"] += 1
        inst.then_inc(self.sem["cc"], 1)
        dst.w["cc"] = self.cnt["cc"]
        src.r["cc"] = self.cnt["cc"]
        self.n_inst += 1

    def gather_pairs(self, src, dstP, rows, PR):
        for p in range(rows // PR):
            self.collective("AllGather", src, dstP, [[0, 1], [2, 3], [4, 5], [6, 7]],
                            src_ap=src.h[p * PR:(p + 1) * PR, :], dst_ap=dstP.h[p * 2 * PR:(p + 1) * 2 * PR, :])

    def finish(self):
        self.barrier()
        self.es.close()


def bc(ap, shape):
    return ap.unsqueeze(len(ap.shape)).to_broadcast(list(shape))


class H:
    def __init__(self, P):
        self.P = P

    def mm(self, out, lhsT, rhs, w, r, start=True, stop=True):
        return self.P.op("pe", lambda e: e.matmul(out, lhsT, rhs, start=start, stop=stop), w=w, r=r)

    def tr(self, out, in_, ident, w, r):
        return self.P.op("pe", lambda e: e.transpose(out=out, in_=in_, identity=ident), w=w, r=r)

    def act(self, out, in_, func, w, r, scale=1.0, bias=None):
        if bias is None:
            return self.P.op("act", lambda e: e.activation(out=out, in_=in_, func=func, scale=scale), w=w, r=r)
        return self.P.op("act", lambda e: e.activation(out=out, in_=in_, func=func, scale=scale, bias=bias), w=w, r=r)

    def tt(self, eng, out, in0, in1, op, w, r):
        return self.P.op(eng, lambda e: e.tensor_tensor(out=out, in0=in0, in1=in1, op=op), w=w, r=r)

    def ts(self, eng, out, in0, s1, op0, w, r, s2=None, op1=None):
        if op1 is None:
            return self.P.op(eng, lambda e: e.tensor_scalar(out=out, in0=in0, scalar1=s1, scalar2=None, op0=op0), w=w, r=r)
        return self.P.op(eng, lambda e: e.tensor_scalar(out=out, in0=in0, scalar1=s1, scalar2=s2, op0=op0, op1=op1), w=w, r=r)

    def stt(self, out, in0, scalar, in1, op0, op1, w, r):
        return self.P.op("dve", lambda e: e.scalar_tensor_tensor(out=out, in0=in0, scalar=scalar, in1=in1, op0=op0, op1=op1), w=w, r=r)

    def cp(self, eng, out, in_, w, r):
        if eng == "act":
            return self.P.op("act", lambda e: e.activation(out=out, in_=in_, func=AF.Copy), w=w, r=r)
        return self.P.op(eng, lambda e: e.tensor_copy(out=out, in_=in_), w=w, r=r)

    def memset(self, eng, out, val, w):
        return self.P.op(eng, lambda e: e.memset(out, val), w=w)

import math
D = 2048
RMS_EPS = 1e-6

FF = 5632
LN_EPS = 1e-5


def load_consts(P, st, ident_d):
    idt = P.sb(st, P.uniq("identsb"), [128, 128], F32)
    P.dma("sp", idt.h[:], ident_d.h[:, :], w=idt, r=ident_d)
    epsT = P.sb(st, P.uniq("epsT"), [128, 1], F32)
    P.op("dve", lambda e: e.memset(epsT.h[:], LN_EPS), w=[epsT])
    return idt, epsT


def layer_norm_tile(P, z, Gt, Bt, epsT, tmp):
    stats, mv, rstd, nmr = tmp["stats"], tmp["mv"], tmp["rstd"], tmp["nmr"]
    for c in range(4):
        P.op("dve", lambda e: e.bn_stats(out=stats.h[:, c * 6:(c + 1) * 6], in_=z.h[:, c * 512:(c + 1) * 512]),
             w=[stats], r=[z])
    P.op("dve", lambda e: e.bn_aggr(out=mv.h[:], in_=stats.h[:]), w=[mv], r=[stats])
    P.op("act", lambda e: e.activation(out=rstd.h[:], in_=mv.h[:, 1:2], func=AF.Sqrt, bias=epsT.h[:, 0:1], scale=1.0),
         w=[rstd], r=[mv, epsT])
    P.op("dve", lambda e: e.reciprocal(out=rstd.h[:], in_=rstd.h[:]), w=[rstd], r=[rstd])
    P.op("dve", lambda e: e.tensor_scalar(out=nmr.h[:], in0=mv.h[:, 0:1], scalar1=rstd.h[:, 0:1], scalar2=-1.0,
                                          op0=ALU.mult, op1=ALU.mult), w=[nmr], r=[mv, rstd])
    P.op("act", lambda e: e.activation(out=z.h[:], in_=z.h[:], func=AF.Identity, scale=rstd.h[:, 0:1],
                                       bias=nmr.h[:, 0:1]), w=[z], r=[z, rstd, nmr])
    P.op("dve", lambda e: e.tensor_tensor(out=z.h[:], in0=z.h[:], in1=Gt.h[:], op=ALU.mult), w=[z], r=[z, Gt])
    P.op("dve", lambda e: e.tensor_tensor(out=z.h[:], in0=z.h[:], in1=Bt.h[:], op=ALU.add), w=[z], r=[z, Bt])


def ln_tmp(P, st, tag):
    return {"stats": P.sb(st, tag + "stats", [128, 24], F32), "mv": P.sb(st, tag + "mv", [128, 2], F32),
            "rstd": P.sb(st, tag + "rstd", [128, 1], F32), "nmr": P.sb(st, tag + "nmr", [128, 1], F32)}


def ffn_stage(P, x_in, x_out, w_up, w_down, ln_g, ln_b, ident_d, T, alpha, tag="f"):
    NB = T // 512
    with contextlib.ExitStack() as st:
        idt, epsT = load_consts(P, st, ident_d)
        Gt = P.sb(st, tag + "G", [128, D], F32)
        Bt = P.sb(st, tag + "B", [128, D], F32)
        P.dma("sp", Gt.h[:], ln_g.h.partition_broadcast(128), w=Gt, r=ln_g)
        P.dma("sp", Bt.h[:], ln_b.h.partition_broadcast(128), w=Bt, r=ln_b)
        xs = [P.sb(st, tag + "xs%d" % i, [128, D], F32) for i in range(2)]
        xT = P.sb(st, tag + "xT", [128, 16, 512], BF16)
        hT = P.sb(st, tag + "hT", [128, 44, 512], BF16)
        wg = [P.sb(st, tag + "wg%d" % i, [128, 16, 256], BF16) for i in range(2)]
        wu = [P.sb(st, tag + "wu%d" % i, [128, 16, 256], BF16) for i in range(2)]
        wd = [P.sb(st, tag + "wd%d" % i, [128, 11, 512], BF16) for i in range(2)]
        sg = [P.sb(st, tag + "sg%d" % i, [128, 512], F32) for i in range(2)]
        z = [P.sb(st, tag + "z%d" % i, [128, D], F32) for i in range(4)]
        tmps = [ln_tmp(P, st, tag + "t%d" % i) for i in range(2)]
        bank = [P.ps(st, tag + "bank%d" % i, [128, 512]) for i in range(8)]
        allb = [idt, epsT, Gt, Bt, xT, hT] + xs + wg + wu + wd + sg + z
        for t in tmps:
            allb += list(t.values())

        nwl = 0
        nwd = 0
        for blk in range(NB):
            t0 = blk * 512
            for s in range(4):
                xb = xs[s % 2]
                P.dma("sp", xb.h[:], x_in.h[t0 + s * 128:t0 + (s + 1) * 128, :], w=xb, r=x_in)
                for q in range(4):
                    bk = bank[4 + (s * 4 + q) % 4]
                    for i in range(4):
                        dk = q * 4 + i
                        P.op("pe", lambda e: e.transpose(out=bk.h[:, i * 128:(i + 1) * 128],
                                                         in_=xb.h[:, dk * 128:(dk + 1) * 128], identity=idt.h[:]),
                             w=[bk], r=[xb, idt])
                    eng = "act" if q % 2 == 0 else "dve"
                    src = bk.h[:].rearrange("p (a b) -> p a b", a=4)
                    dst = xT.h[:, q * 4:(q + 1) * 4, s * 128:(s + 1) * 128]
                    if eng == "act":
                        P.op("act", lambda e: e.activation(out=dst, in_=src, func=AF.Copy), w=[xT], r=[bk])
                    else:
                        P.op("dve", lambda e: e.tensor_copy(out=dst, in_=src), w=[xT], r=[bk])
            for g in range(22):
                sl = nwl % 2
                nwl += 1
                for k0 in range(0, 16, 4):
                    P.dma("pool", wg[sl].h[:, k0:k0 + 4, :],
                          w_up.h[k0 * 128:(k0 + 4) * 128, g * 256:(g + 1) * 256].rearrange("(k p) f -> p k f", p=128),
                          w=wg[sl], r=w_up)
                    P.dma("pool", wu[sl].h[:, k0:k0 + 4, :],
                          w_up.h[k0 * 128:(k0 + 4) * 128, FF + g * 256:FF + (g + 1) * 256].rearrange(
                              "(k p) f -> p k f", p=128),
                          w=wu[sl], r=w_up)
                for c in range(2):
                    j = g * 2 + c
                    pg = bank[(j % 2) * 2]
                    pu = bank[(j % 2) * 2 + 1]
                    for dk in range(16):
                        P.op("pe", lambda e: e.matmul(pg.h[:], wg[sl].h[:, dk, c * 128:(c + 1) * 128], xT.h[:, dk, :],
                                                      start=(dk == 0), stop=(dk == 15)), w=[pg], r=[wg[sl], xT])
                    for dk in range(16):
                        P.op("pe", lambda e: e.matmul(pu.h[:], wu[sl].h[:, dk, c * 128:(c + 1) * 128], xT.h[:, dk, :],
                                                      start=(dk == 0), stop=(dk == 15)), w=[pu], r=[wu[sl], xT])
                    sgb = sg[j % 2]
                    P.op("act", lambda e: e.activation(out=sgb.h[:], in_=pg.h[:], func=AF.Silu), w=[sgb], r=[pg])
                    P.op("dve", lambda e: e.tensor_tensor(out=hT.h[:, j, :], in0=sgb.h[:], in1=pu.h[:], op=ALU.mult),
                         w=[hT], r=[sgb, pu])
            for s in range(4):
                P.dma("sp", z[s].h[:], x_in.h[t0 + s * 128:t0 + (s + 1) * 128, :], w=z[s], r=x_in)
                P.op("act", lambda e: e.activation(out=z[s].h[:], in_=z[s].h[:], func=AF.Copy, scale=float(alpha)),
                     w=[z[s]], r=[z[s]])
            for dn in range(4):
                pb = [bank[(dn % 2) * 4 + s] for s in range(4)]
                for fg in range(4):
                    sl = nwd % 2
                    nwd += 1
                    for (a, n) in ((0, 4), (4, 4), (8, 3)):
                        r0 = (fg * 11 + a) * 128
                        P.dma("pool", wd[sl].h[:, a:a + n, :],
                              w_down.h[r0:r0 + n * 128, dn * 512:(dn + 1) * 512].rearrange("(k p) f -> p k f", p=128),
                              w=wd[sl], r=w_down)
                    for i in range(11):
                        fk = fg * 11 + i
                        for s in range(4):
                            P.op("pe", lambda e: e.matmul(pb[s].h[:], hT.h[:, fk, s * 128:(s + 1) * 128],
                                                          wd[sl].h[:, i, :], start=(fk == 0), stop=(fk == 43)),
                                 w=[pb[s]], r=[hT, wd[sl]])
                for s in range(4):
                    zc = z[s].h[:, dn * 512:(dn + 1) * 512]
                    P.op("dve", lambda e: e.scalar_tensor_tensor(out=zc, in0=pb[s].h[:], scalar=0.5, in1=zc,
                                                                 op0=ALU.mult, op1=ALU.add), w=[z[s]], r=[pb[s], z[s]])
            for s in range(4):
                layer_norm_tile(P, z[s], Gt, Bt, epsT, tmps[s % 2])
                P.dma("sp", x_out.h[t0 + s * 128:t0 + (s + 1) * 128, :], z[s].h[:], w=x_out, r=z[s], owner=z[s])
        P.end_stage()


NEGV = -30000.0


def gdn_consts_np():
    t = np.arange(128)
    same = (t[:, None] // 64) == (t[None, :] // 64)
    c = {}
    c["ident"] = np.eye(128, dtype=np.float32)
    c["ucs"] = (same & (t[:, None] <= t[None, :])).astype(np.float32)
    c["vsame"] = same.astype(np.float32)
    cind = np.zeros((128, 2, 128), np.float32)
    cind[:64, 0, :] = 1.0
    cind[64:, 1, :] = 1.0
    c["cind"] = cind.reshape(128, 256)
    c["negA"] = np.where(same & (t[None, :] >= t[:, None]), 0.0, NEGV).astype(np.float32)
    c["negL"] = np.where(same & (t[None, :] > t[:, None]), 0.0, NEGV).astype(np.float32)
    sel = np.zeros((32, 32, 128), np.float32)
    for h in range(32):
        sel[h, h, :] = 1.0
    c["sel"] = sel.reshape(32, 32 * 128)
    return c


def gdn_stage(P, x_in, og_out, w_c, conv_c, alog_c, dtb_c, normw, C, T, tag="g", rm=lambda t: t):
    Hh = H(P)
    SBK = 256
    NSB = T // SBK
    with contextlib.ExitStack() as st:
        sbuf = lambda n, s, d=F32: P.sb(st, tag + n, s, d)
        idt = sbuf("idt", [128, 128])
        idb = sbuf("idb", [128, 128], BF16)
        ucs = sbuf("ucs", [128, 128])
        vsame = sbuf("vsame", [128, 128])
        cind = sbuf("cind", [128, 256])
        negA = sbuf("negA", [128, 128])
        negL = sbuf("negL", [128, 128])
        sel = sbuf("sel", [32, 32 * 128])
        ones = sbuf("ones", [128, 1])
        cw = sbuf("cw", [128, 128])
        DTB = sbuf("DTB", [128, 16])
        NEGA = sbuf("NEGA", [128, 16])
        nw1 = sbuf("nw1", [128, 128])
        for (dst, src) in ((idt, C["ident"]), (ucs, C["ucs"]), (vsame, C["vsame"]), (cind, C["cind"]),
                           (negA, C["negA"]), (negL, C["negL"]), (sel, C["sel"]), (cw, conv_c)):
            P.dma("sp", dst.h[:], src.h[:, :], w=dst, r=src)
        P.dma("sp", DTB.h[:], dtb_c.h.partition_broadcast(128), w=DTB, r=dtb_c)
        P.dma("sp", NEGA.h[:], alog_c.h.partition_broadcast(128), w=NEGA, r=alog_c)
        P.dma("sp", nw1.h[:], normw.h.partition_broadcast(128), w=nw1, r=normw)
        Hh.cp("dve", idb.h[:], idt.h[:], [idb], [idt])
        Hh.memset("dve", ones.h[:], 1.0, [ones])
        Hh.act(NEGA.h[:], NEGA.h[:], AF.Exp, [NEGA], [NEGA])
        Hh.ts("dve", NEGA.h[:], NEGA.h[:], -1.0, ALU.mult, [NEGA], [NEGA])

        xs = sbuf("xs", [128, D])
        xT = sbuf("xT", [128, 16, SBK], BF16)
        wf = [sbuf("wf%d" % i, [128, 16, 256], BF16) for i in range(2)]
        wba = sbuf("wba", [128, 16, 32], BF16)
        wz = [sbuf("wz%d" % i, [128, 16, 256], BF16) for i in range(2)]
        qkvT = sbuf("qkvT", [128, 32, SBK], BF16)
        pre = [sbuf("pre%d" % i, [128, SBK + 3]) for i in range(2)]
        acc = [sbuf("acc%d" % i, [128, SBK]) for i in range(2)]
        sgl = [sbuf("sgl%d" % i, [128, SBK]) for i in range(2)]
        sq = [sbuf("sq%d" % i, [128, SBK]) for i in range(2)]
        carry = sbuf("carry", [128, 32, 3])
        baT = sbuf("baT", [32, SBK])
        zs = sbuf("zs", [128, 2, 2048], BF16)
        S = sbuf("S", [128, 16, 128])
        Sb = sbuf("Sb", [128, 16, 128], BF16)
        sm = {n: sbuf("sm_" + n, [128, 16]) for n in
              ("eb", "lb", "beta", "t1", "e1", "sp", "g", "gcs", "d", "kd", "a", "c1", "c2", "sa", "rq16", "lnrk16",
               "colb", "ssqo", "rms")}
        ba = sbuf("ba", [128, 32])
        lnr = sbuf("lnr", [128, 16])
        rqk = sbuf("rqk", [128, 16])
        R = sbuf("R", [128, 32])
        RT = sbuf("RT", [32, 128])
        EG = sbuf("EG", [128, 32])
        kba = sbuf("kba", [128, 16, 128])
        kdec = sbuf("kdec", [128, 16, 128], BF16)
        vb = sbuf("vb", [128, 16, 128])
        attnT = sbuf("attnT", [128, 16, 128], BF16)
        slot = []
        for i in range(2):
            slot.append({n: sbuf("s%d_%s" % (i, n), [128, 4, 128]) for n in
                         ("EL", "EA", "X0", "X1", "Y0", "Y1", "P0", "P1")})
        u = sbuf("u", [128, 16, 128])
        wT = sbuf("wT", [128, 16, 128], BF16)
        vnew = sbuf("vnew", [128, 16, 128], BF16)
        o = sbuf("o", [128, 16, 128])
        obf = sbuf("obf", [128, 16, 128], BF16)
        tmpS = [sbuf("tmpS%d" % i, [128, 4, 128]) for i in range(2)]
        epsr = sbuf("epsr", [128, 1])
        Hh.memset("dve", epsr.h[:], RMS_EPS, [epsr])
        Hh.memset("dve", S.h[:], 0.0, [S])
        Hh.memset("pool", Sb.h[:], 0.0, [Sb])
        Hh.memset("pool", carry.h[:], 0.0, [carry])

        banks = [P.ps(st, tag + "bank%d" % i, [128, 512]) for i in range(7)]
        smallbank = st.enter_context(P.nc.psum_tensor(tag + "smallbank", [128, 512], F32))
        ps_ssq = P.view(smallbank[:, 0:32], "ps_ssq")
        ps_ba = P.view(smallbank[:, 32:64], "ps_ba")
        ps_g = P.view(smallbank[:, 64:128], "ps_g")
        ps_rt = P.view(smallbank[:, 128:256], "ps_rt")
        bctr = [0]

        def nb():
            b = banks[bctr[0] % 7]
            bctr[0] += 1
            return b

        nwf = 0
        nwz = 0
        for sb in range(NSB):
            t0 = sb * SBK
            for s in range(2):
                P.dma("sp", xs.h[:], x_in.h[rm(t0 + s * 128):rm(t0 + s * 128) + 128, :], w=xs, r=x_in)
                for q in range(4):
                    bk = nb()
                    for i in range(4):
                        dk = q * 4 + i
                        Hh.tr(bk.h[:, i * 128:(i + 1) * 128], xs.h[:, dk * 128:(dk + 1) * 128], idt.h[:], [bk], [xs, idt])
                    Hh.cp("act" if q % 2 == 0 else "dve", xT.h[:, q * 4:(q + 1) * 4, s * 128:(s + 1) * 128],
                          bk.h[:].rearrange("p (a b) -> p a b", a=4), [xT], [bk])
            for fp in range(16):
                sl = nwf % 2
                nwf += 1
                for k0 in range(0, 16, 4):
                    P.dma("pool", wf[sl].h[:, k0:k0 + 4, :],
                          w_c.h[k0 * 128:(k0 + 4) * 128, fp * 256:(fp + 1) * 256].rearrange("(k p) f -> p k f", p=128),
                          w=wf[sl], r=w_c)
                for c in range(2):
                    fc = fp * 2 + c
                    bk = nb()
                    for dk in range(16):
                        Hh.mm(bk.h[:, 0:SBK], wf[sl].h[:, dk, c * 128:(c + 1) * 128], xT.h[:, dk, :], [bk], [wf[sl], xT],
                              start=(dk == 0), stop=(dk == 15))
                    pr = pre[fc % 2]
                    ac = acc[fc % 2]
                    Hh.cp("pool", pr.h[:, 0:3], carry.h[:, fc, :], [pr], [carry])
                    Hh.cp("act", pr.h[:, 3:SBK + 3], bk.h[:, 0:SBK], [pr], [bk])
                    Hh.cp("pool", carry.h[:, fc, :], pr.h[:, SBK:SBK + 3], [carry], [pr])
                    Hh.ts("dve", ac.h[:], pr.h[:, 0:SBK], cw.h[:, fc * 4:fc * 4 + 1], ALU.mult, [ac], [pr, cw])
                    for j in range(1, 4):
                        Hh.stt(ac.h[:], pr.h[:, j:j + SBK], cw.h[:, fc * 4 + j:fc * 4 + j + 1], ac.h[:], ALU.mult, ALU.add,
                               [ac], [pr, cw, ac])
                    if fc < 16:
                        sg_ = sgl[fc % 2]
                        sq_ = sq[fc % 2]
                        Hh.act(sg_.h[:], ac.h[:], AF.Silu, [sg_], [ac])
                        Hh.cp("pool", qkvT.h[:, fc, :], sg_.h[:], [qkvT], [sg_])
                        Hh.act(sq_.h[:], sg_.h[:], AF.Square, [sq_], [sg_])
                        for s in range(2):
                            Hh.mm(ps_ssq.h[:, s * 16 + fc:s * 16 + fc + 1], sq_.h[:, s * 128:(s + 1) * 128], ones.h[:, 0:1],
                                  [ps_ssq], [sq_, ones])
                    else:
                        Hh.act(qkvT.h[:, fc, :], ac.h[:], AF.Silu, [qkvT], [ac])
            for k0 in range(0, 16, 4):
                P.dma("pool", wba.h[:, k0:k0 + 4, :],
                      w_c.h[k0 * 128:(k0 + 4) * 128, 6144:6176].rearrange("(k p) f -> p k f", p=128), w=wba, r=w_c)
            bk = nb()
            for dk in range(16):
                Hh.mm(bk.h[0:32, 0:SBK], wba.h[:, dk, :], xT.h[:, dk, :], [bk], [wba, xT], start=(dk == 0), stop=(dk == 15))
            Hh.cp("act", baT.h[:], bk.h[0:32, 0:SBK], [baT], [bk])
            for zg in range(8):
                sl = nwz % 2
                nwz += 1
                for k0 in range(0, 16, 4):
                    P.dma("pool", wz[sl].h[:, k0:k0 + 4, :],
                          w_c.h[k0 * 128:(k0 + 4) * 128, 4096 + zg * 256:4096 + (zg + 1) * 256].rearrange(
                              "(k p) f -> p k f", p=128), w=wz[sl], r=w_c)
                for s in range(2):
                    bk = nb()
                    for dk in range(16):
                        Hh.mm(bk.h[:, 0:256], xT.h[:, dk, s * 128:(s + 1) * 128], wz[sl].h[:, dk, :], [bk], [wz[sl], xT],
                              start=(dk == 0), stop=(dk == 15))
                    Hh.act(zs.h[:, s, zg * 256:(zg + 1) * 256], bk.h[:, 0:256], AF.Silu, [zs], [bk])

            for s in range(2):
                tb = t0 + s * 128
                cs = slice(s * 128, (s + 1) * 128)
                Hh.tr(ps_ba.h[:, 0:32], baT.h[:, cs], idt.h[0:32, 0:32], [ps_ba], [baT, idt])
                Hh.cp("dve", ba.h[:], ps_ba.h[:, 0:32], [ba], [ps_ba])
                m = sm
                Hh.act(m["eb"].h[:], ba.h[:, 0:16], AF.Exp, [m["eb"]], [ba], scale=-1.0)
                Hh.act(m["lb"].h[:], m["eb"].h[:], AF.Ln, [m["lb"]], [m["eb"]], bias=1.0)
                Hh.act(m["beta"].h[:], m["lb"].h[:], AF.Exp, [m["beta"]], [m["lb"]], scale=-1.0)
                Hh.tt("dve", m["t1"].h[:], ba.h[:, 16:32], DTB.h[:], ALU.add, [m["t1"]], [ba, DTB])
                Hh.act(m["e1"].h[:], m["t1"].h[:], AF.Exp, [m["e1"]], [m["t1"]])
                Hh.act(m["sp"].h[:], m["e1"].h[:], AF.Ln, [m["sp"]], [m["e1"]], bias=1.0)
                Hh.tt("dve", m["g"].h[:], m["sp"].h[:], NEGA.h[:], ALU.mult, [m["g"]], [m["sp"], NEGA])
                Hh.mm(ps_g.h[:, 0:16], ucs.h[:], m["g"].h[:], [ps_g], [ucs, m["g"]])
                Hh.mm(ps_g.h[:, 16:32], vsame.h[:], m["g"].h[:], [ps_g], [vsame, m["g"]])
                Hh.mm(ps_g.h[:, 32:48], cind.h[:, 0:128], m["g"].h[:], [ps_g], [cind, m["g"]])
                Hh.mm(ps_g.h[:, 48:64], cind.h[:, 128:256], m["g"].h[:], [ps_g], [cind, m["g"]])
                Hh.cp("dve", m["gcs"].h[:], ps_g.h[:, 0:16], [m["gcs"]], [ps_g])
                Hh.tt("dve", m["d"].h[:], ps_g.h[:, 16:32], m["gcs"].h[:], ALU.subtract, [m["d"]], [ps_g, m["gcs"]])
                Hh.act(m["kd"].h[:], m["d"].h[:], AF.Exp, [m["kd"]], [m["d"]])
                Hh.act(m["a"].h[:], m["gcs"].h[:], AF.Exp, [m["a"]], [m["gcs"]])
                Hh.act(EG.h[:], ps_g.h[:, 32:64], AF.Exp, [EG], [ps_g])
                Hh.act(lnr.h[:], ps_ssq.h[:, s * 16:(s + 1) * 16], AF.Ln, [lnr], [ps_ssq, epsr], bias=epsr.h[:, 0:1])
                Hh.ts("dve", lnr.h[:], lnr.h[:], -0.5, ALU.mult, [lnr], [lnr])
                Hh.act(rqk.h[:, 0:8], lnr.h[:, 0:8], AF.Exp, [rqk], [lnr], bias=-0.5 * math.log(128.0))
                Hh.act(rqk.h[:, 8:16], lnr.h[:, 8:16], AF.Exp, [rqk], [lnr])
                v2 = lambda ap: ap.rearrange("p (a b) -> p a b", b=2)
                Hh.cp("dve", v2(m["rq16"].h[:]), bc(rqk.h[:, 0:8], [128, 8, 2]), [m["rq16"]], [rqk])
                Hh.cp("dve", v2(m["lnrk16"].h[:]), bc(lnr.h[:, 8:16], [128, 8, 2]), [m["lnrk16"]], [lnr])
                Hh.tt("dve", m["c1"].h[:], m["beta"].h[:], m["a"].h[:], ALU.mult, [m["c1"]], [m["beta"], m["a"]])
                Hh.tt("dve", v2(m["c1"].h[:]), v2(m["c1"].h[:]), bc(rqk.h[:, 8:16], [128, 8, 2]), ALU.mult, [m["c1"]],
                      [m["c1"], rqk])
                Hh.tt("dve", v2(m["c2"].h[:]), v2(m["kd"].h[:]), bc(rqk.h[:, 8:16], [128, 8, 2]), ALU.mult, [m["c2"]],
                      [m["kd"], rqk])
                Hh.tt("dve", m["sa"].h[:], m["rq16"].h[:], m["a"].h[:], ALU.mult, [m["sa"]], [m["rq16"], m["a"]])
                Hh.cp("dve", R.h[:, 0:16], m["gcs"].h[:], [R], [m["gcs"]])
                Hh.tt("dve", R.h[:, 16:32], m["gcs"].h[:], m["lb"].h[:], ALU.subtract, [R], [m["gcs"], m["lb"]])
                Hh.tt("dve", R.h[:, 16:32], R.h[:, 16:32], m["lnrk16"].h[:], ALU.add, [R], [R, m["lnrk16"]])
                Hh.tt("dve", m["colb"].h[:], m["lnrk16"].h[:], m["gcs"].h[:], ALU.subtract, [m["colb"]],
                      [m["lnrk16"], m["gcs"]])
                Hh.tr(ps_rt.h[0:32, :], R.h[:], idt.h[:], [ps_rt], [R, idt])
                Hh.cp("act", RT.h[:], ps_rt.h[0:32, :], [RT], [ps_rt])
                bk = nb()
                kv = bk.h[:].bitcast(BF16).rearrange("p (a b) -> p a b", a=8)
                for hk in range(8):
                    Hh.tr(kv[:, hk, :], qkvT.h[:, 8 + hk, cs], idb.h[:], [bk], [qkvT, idb])
                kvb = bc3 = kv.unsqueeze(2).to_broadcast([128, 8, 2, 128])
                v4 = lambda ap: ap.rearrange("p (a b) c -> p a b c", b=2)
                Hh.tt("dve", v4(kba.h[:]), kvb, v4(bc(m["c1"].h[:], [128, 16, 128])), ALU.mult, [kba], [bk, m["c1"]])
                Hh.tt("dve", v4(kdec.h[:]), kvb, v4(bc(m["c2"].h[:], [128, 16, 128])), ALU.mult, [kdec], [bk, m["c2"]])
                for half in range(2):
                    bk = nb()
                    vv = bk.h[:].bitcast(BF16).rearrange("p (a b) -> p a b", a=8)
                    for i in range(8):
                        h = half * 8 + i
                        Hh.tr(vv[:, i, :], qkvT.h[:, 16 + h, cs], idb.h[:], [bk], [qkvT, idb])
                    Hh.tt("dve", vb.h[:, half * 8:(half + 1) * 8, :], vv,
                          bc(m["beta"].h[:, half * 8:(half + 1) * 8], [128, 8, 128]), ALU.mult, [vb], [bk, m["beta"]])
                for gp in range(2):
                    grp = [gp * 2, gp * 2 + 1]
                    sd = {}
                    for gi, mgrp in enumerate(grp):
                        sl_ = slot[gi]
                        sd[mgrp] = sl_
                        bkq = nb()
                        for i in range(2):
                            hk = mgrp * 2 + i
                            Hh.mm(bkq.h[:, i * 128:(i + 1) * 128], qkvT.h[:, 8 + hk, cs], qkvT.h[:, 8 + hk, cs], [bkq], [qkvT])
                            Hh.mm(bkq.h[:, 256 + i * 128:256 + (i + 1) * 128], qkvT.h[:, 8 + hk, cs], qkvT.h[:, hk, cs],
                                  [bkq], [qkvT])
                        bl = nb()
                        ba_ = nb()
                        for i in range(4):
                            h = mgrp * 4 + i
                            Hh.mm(bl.h[:, i * 128:(i + 1) * 128], idt.h[:], negL.h[:], [bl], [idt, negL], start=True, stop=False)
                            Hh.mm(bl.h[:, i * 128:(i + 1) * 128], sel.h[:, (16 + h) * 128:(17 + h) * 128], RT.h[:], [bl],
                                  [sel, RT], start=False, stop=True)
                            Hh.mm(ba_.h[:, i * 128:(i + 1) * 128], idt.h[:], negA.h[:], [ba_], [idt, negA], start=True,
                                  stop=False)
                            Hh.mm(ba_.h[:, i * 128:(i + 1) * 128], sel.h[:, h * 128:(h + 1) * 128], RT.h[:], [ba_], [sel, RT],
                                  start=False, stop=True)
                        for i in range(4):
                            h = mgrp * 4 + i
                            Hh.act(sl_["EL"].h[:, i, :], bl.h[:, i * 128:(i + 1) * 128], AF.Exp, [sl_["EL"]], [bl, m["colb"]],
                                   bias=m["colb"].h[:, h:h + 1])
                            Hh.act(sl_["EA"].h[:, i, :], ba_.h[:, i * 128:(i + 1) * 128], AF.Exp, [sl_["EA"]], [ba_, m["colb"]],
                                   bias=m["colb"].h[:, h:h + 1])
                        kk = bkq.h[:, 0:256].rearrange("p (a c) -> p a c", a=2).unsqueeze(2).to_broadcast([128, 2, 2, 128])
                        kq = bkq.h[:, 256:512].rearrange("p (a c) -> p a c", a=2).unsqueeze(2).to_broadcast([128, 2, 2, 128])
                        Hh.tt("dve", v4(sl_["X0"].h[:]), kk, v4(sl_["EL"].h[:]), ALU.mult, [sl_["X0"]], [bkq, sl_["EL"]])
                        Hh.tt("dve", v4(attnT.h[:, mgrp * 4:(mgrp + 1) * 4, :]), kq, v4(sl_["EA"].h[:]), ALU.mult, [attnT],
                              [bkq, sl_["EA"]])
                        bt_ = nb()
                        for i in range(4):
                            Hh.tr(bt_.h[:, i * 128:(i + 1) * 128], sl_["X0"].h[:, i, :], idt.h[:], [bt_], [sl_["X0"], idt])
                        Hh.cp("act", sl_["Y0"].h[:], bt_.h[:].rearrange("p (a b) -> p a b", a=4), [sl_["Y0"]], [bt_])
                        Hh.tt("dve", sl_["P0"].h[:], idt.h[:].unsqueeze(1).to_broadcast([128, 4, 128]), sl_["X0"].h[:],
                              ALU.subtract, [sl_["P0"]], [idt, sl_["X0"]])
                    for lev in range(5):
                        a_, b_ = lev % 2, (lev + 1) % 2
                        bx, by, bp = {}, {}, {}
                        for mgrp in grp:
                            sl_ = sd[mgrp]
                            X, Y = sl_["X%d" % a_], sl_["Y%d" % a_]
                            by[mgrp] = nb()
                            for i in range(4):
                                Hh.mm(by[mgrp].h[:, i * 128:(i + 1) * 128], X.h[:, i, :], Y.h[:, i, :], [by[mgrp]], [X, Y])
                            if lev < 4:
                                bx[mgrp] = nb()
                                for i in range(4):
                                    Hh.mm(bx[mgrp].h[:, i * 128:(i + 1) * 128], Y.h[:, i, :], X.h[:, i, :], [bx[mgrp]], [X, Y])
                        for mgrp in grp:
                            sl_ = sd[mgrp]
                            Hh.cp("dve", sl_["Y%d" % b_].h[:], by[mgrp].h[:].rearrange("p (a b) -> p a b", a=4),
                                  [sl_["Y%d" % b_]], [by[mgrp]])
                            if lev < 4:
                                Hh.cp("act", sl_["X%d" % b_].h[:], bx[mgrp].h[:].rearrange("p (a b) -> p a b", a=4),
                                      [sl_["X%d" % b_]], [bx[mgrp]])
                        for mgrp in grp:
                            sl_ = sd[mgrp]
                            Yn, Pc = sl_["Y%d" % b_], sl_["P%d" % a_]
                            bp[mgrp] = nb()
                            for i in range(4):
                                Hh.mm(bp[mgrp].h[:, i * 128:(i + 1) * 128], Yn.h[:, i, :], Pc.h[:, i, :], [bp[mgrp]], [Yn, Pc])
                        for mgrp in grp:
                            sl_ = sd[mgrp]
                            Hh.tt("dve", sl_["P%d" % b_].h[:], sl_["P%d" % a_].h[:],
                                  bp[mgrp].h[:].rearrange("p (a b) -> p a b", a=4), ALU.add, [sl_["P%d" % b_]],
                                  [sl_["P%d" % a_], bp[mgrp]])
                    for mgrp in grp:
                        AT = sd[mgrp]["P1"]
                        bu = nb()
                        bw = nb()
                        for i in range(4):
                            h = mgrp * 4 + i
                            Hh.mm(bu.h[:, i * 128:(i + 1) * 128], AT.h[:, i, :], vb.h[:, h, :], [bu], [AT, vb])
                            Hh.mm(bw.h[:, i * 128:(i + 1) * 128], kba.h[:, h, :], AT.h[:, i, :], [bw], [AT, kba])
                        Hh.cp("act", u.h[:, mgrp * 4:(mgrp + 1) * 4, :], bu.h[:].rearrange("p (a b) -> p a b", a=4), [u], [bu])
                        Hh.cp("dve", wT.h[:, mgrp * 4:(mgrp + 1) * 4, :], bw.h[:].rearrange("p (a b) -> p a b", a=4), [wT], [bw])
                for c in range(2):
                    rs = slice(c * 64, (c + 1) * 64)
                    for mgrp in range(4):
                        hs = slice(mgrp * 4, (mgrp + 1) * 4)
                        bws = nb()
                        bo1 = nb()
                        for i in range(4):
                            h = mgrp * 4 + i
                            Hh.mm(bws.h[:, i * 128:(i + 1) * 128], wT.h[:, h, :], Sb.h[:, h, :], [bws], [wT, Sb])
                            Hh.mm(bo1.h[:, i * 128:(i + 1) * 128], qkvT.h[:, h // 2, cs], Sb.h[:, h, :], [bo1], [qkvT, Sb])
                        Hh.tt("dve", vnew.h[rs, hs, :], u.h[rs, hs, :], bws.h[rs, :].rearrange("p (a b) -> p a b", a=4),
                              ALU.subtract, [vnew], [u, bws])
                        Hh.tt("dve", o.h[rs, hs, :], bo1.h[rs, :].rearrange("p (a b) -> p a b", a=4),
                              bc(m["sa"].h[rs, hs], [64, 4, 128]), ALU.mult, [o], [bo1, m["sa"]])
                        bs = nb()
                        for i in range(4):
                            h = mgrp * 4 + i
                            Hh.mm(bs.h[:, i * 128:(i + 1) * 128], kdec.h[rs, h, :], vnew.h[rs, h, :], [bs], [kdec, vnew])
                        tS = tmpS[mgrp % 2]
                        Hh.tt("pool", tS.h[:], S.h[:, hs, :], bc(EG.h[:, c * 16 + mgrp * 4:c * 16 + mgrp * 4 + 4], [128, 4, 128]),
                              ALU.mult, [tS], [S, EG])
                        Hh.tt("dve", S.h[:, hs, :], tS.h[:], bs.h[:].rearrange("p (a b) -> p a b", a=4), ALU.add, [S], [tS, bs])
                        Hh.cp("act", Sb.h[:, hs, :], S.h[:, hs, :], [Sb], [S])
                for mgrp in range(4):
                    hs = slice(mgrp * 4, (mgrp + 1) * 4)
                    bo2 = nb()
                    for i in range(4):
                        h = mgrp * 4 + i
                        Hh.mm(bo2.h[:, i * 128:(i + 1) * 128], attnT.h[:, h, :], vnew.h[:, h, :], [bo2], [attnT, vnew])
                    Hh.tt("dve", tmpS[mgrp % 2].h[:], bo2.h[:].rearrange("p (a b) -> p a b", a=4),
                          bc(m["rq16"].h[:, hs], [128, 4, 128]), ALU.mult, [tmpS[mgrp % 2]], [bo2, m["rq16"]])
                    Hh.tt("pool", o.h[:, hs, :], o.h[:, hs, :], tmpS[mgrp % 2].h[:], ALU.add, [o], [o, tmpS[mgrp % 2]])
                Hh.tt("pool", u.h[:], o.h[:], o.h[:], ALU.mult, [u], [o])
                P.op("dve", lambda e: e.tensor_reduce(out=m["ssqo"].h[:], in_=u.h[:], op=ALU.add, axis=AX.X), w=[m["ssqo"]], r=[u])
                Hh.act(m["rms"].h[:], m["ssqo"].h[:], AF.Sqrt, [m["rms"]], [m["ssqo"], epsr], scale=1.0 / 128.0,
                       bias=epsr.h[:, 0:1])
                P.op("dve", lambda e: e.reciprocal(out=m["rms"].h[:], in_=m["rms"].h[:]), w=[m["rms"]], r=[m["rms"]])
                Hh.tt("dve", o.h[:], o.h[:], bc(m["rms"].h[:], [128, 16, 128]), ALU.mult, [o], [o, m["rms"]])
                Hh.tt("pool", o.h[:], o.h[:], nw1.h[:].unsqueeze(1).to_broadcast([128, 16, 128]), ALU.mult, [o], [o, nw1])
                Hh.tt("dve", obf.h[:], o.h[:], zs.h[:, s, :].rearrange("p (a b) -> p a b", a=16), ALU.mult, [obf], [o, zs])
                P.dma("sp", og_out.h[tb:tb + 128, :], obf.h[:].rearrange("p a b -> p (a b)"), w=og_out, r=obf, owner=obf)
        P.end_stage()


def gla_consts_np():
    t = np.arange(128)
    same = (t[:, None] // 64) == (t[None, :] // 64)
    c = {}
    c["ident"] = np.eye(128, dtype=np.float32)
    c["ucs"] = (same & (t[:, None] <= t[None, :])).astype(np.float32)
    c["vsame"] = same.astype(np.float32)
    c["mask01"] = (same & (t[None, :] >= t[:, None])).astype(np.float32)
    cind2 = np.zeros((128, 2), np.float32)
    cind2[:64, 0] = 1.0
    cind2[64:, 1] = 1.0
    c["cind2"] = cind2
    return c


def gla_stage(P, x_in, og_out, w_c, wg_aug, normw, C, T, tag="l", rm=lambda t: t):
    Hh = H(P)
    SBK = 256
    NSB = T // SBK
    with contextlib.ExitStack() as st:
        sbuf = lambda n, s, d=F32: P.sb(st, tag + n, s, d)
        idt = sbuf("idt", [128, 128])
        ucs = sbuf("ucs", [128, 128])
        vsame = sbuf("vsame", [128, 128])
        mask01 = sbuf("mask01", [128, 128])
        cind2 = sbuf("cind2", [128, 2])
        wga = sbuf("wga", [17, 512])
        nw1 = sbuf("nw1", [128, 512])
        for (dst, src) in ((idt, C["ident"]), (ucs, C["ucs"]), (vsame, C["vsame"]), (mask01, C["mask01"]),
                           (cind2, C["cind2"]), (wga, wg_aug)):
            P.dma("sp", dst.h[:], src.h[:, :], w=dst, r=src)
        P.dma("sp", nw1.h[:], normw.h.partition_broadcast(128), w=nw1, r=normw)
        xs = sbuf("xs", [128, D])
        xT = sbuf("xT", [128, 16, SBK], BF16)
        wf = [sbuf("wf%d" % i, [128, 16, 256], BF16) for i in range(2)]
        wgl = sbuf("wgl", [128, 16, 16], BF16)
        qkT = sbuf("qkT", [128, 8, SBK])
        ktok = sbuf("ktok", [128, 2, 512])
        vtok = sbuf("vtok", [128, 2, 1024], BF16)
        rs_ = sbuf("rs", [128, 2, 1024])
        glT = sbuf("glT", [32, SBK])
        S = sbuf("S", [128, 4, 512])
        Sb = sbuf("Sb", [128, 4, 512], BF16)
        ez = sbuf("ez", [128, 512])
        ftok = sbuf("ftok", [128, 512])
        bcs = sbuf("bcs", [128, 512])
        dd = sbuf("dd", [128, 512])
        kdec = sbuf("kdec", [128, 512], BF16)
        eP = sbuf("eP", [128, 4, 128])
        eN = sbuf("eN", [128, 4, 128])
        qt = sbuf("qt", [128, 4, 128], BF16)
        kt = sbuf("kt", [128, 4, 128], BF16)
        EB = sbuf("EB", [128, 8])
        attnT = sbuf("attnT", [128, 2, 128], BF16)
        o = sbuf("o", [128, 2, 512])
        obf = sbuf("obf", [128, 2, 512], BF16)
        sqo = sbuf("sqo", [128, 2, 512])
        ssq = sbuf("ssq", [128, 2])
        rms = sbuf("rms", [128, 2])
        epsr = sbuf("epsr", [128, 1])
        Hh.memset("dve", epsr.h[:], RMS_EPS, [epsr])
        Hh.memset("dve", S.h[:], 0.0, [S])
        Hh.memset("pool", Sb.h[:], 0.0, [Sb])
        Hh.memset("pool", glT.h[:], 1.0, [glT])
        banks = [P.ps(st, tag + "bank%d" % i, [128, 512]) for i in range(8)]
        bctr = [0]

        def nb():
            b = banks[bctr[0] % 8]
            bctr[0] += 1
            return b

        nwf = 0
        for sb in range(NSB):
            t0 = sb * SBK
            for s in range(2):
                P.dma("sp", xs.h[:], x_in.h[rm(t0 + s * 128):rm(t0 + s * 128) + 128, :], w=xs, r=x_in)
                for q in range(4):
                    bk = nb()
                    for i in range(4):
                        dk = q * 4 + i
                        Hh.tr(bk.h[:, i * 128:(i + 1) * 128], xs.h[:, dk * 128:(dk + 1) * 128], idt.h[:], [bk], [xs, idt])
                    Hh.cp("act" if q % 2 == 0 else "dve", xT.h[:, q * 4:(q + 1) * 4, s * 128:(s + 1) * 128],
                          bk.h[:].rearrange("p (a b) -> p a b", a=4), [xT], [bk])
            for g in range(12):
                sl = nwf % 2
                nwf += 1
                for k0 in range(0, 16, 4):
                    P.dma("pool", wf[sl].h[:, k0:k0 + 4, :],
                          w_c.h[k0 * 128:(k0 + 4) * 128, g * 256:(g + 1) * 256].rearrange("(k p) f -> p k f", p=128),
                          w=wf[sl], r=w_c)
                if g < 4:
                    for c in range(2):
                        fc = g * 2 + c
                        bk = nb()
                        for dk in range(16):
                            Hh.mm(bk.h[:, 0:SBK], wf[sl].h[:, dk, c * 128:(c + 1) * 128], xT.h[:, dk, :], [bk], [wf[sl], xT],
                                  start=(dk == 0), stop=(dk == 15))
                        Hh.cp("act", qkT.h[:, fc, :], bk.h[:, 0:SBK], [qkT], [bk])
                if g >= 2:
                    for s in range(2):
                        bk = nb()
                        for dk in range(16):
                            Hh.mm(bk.h[:, 0:256], xT.h[:, dk, s * 128:(s + 1) * 128], wf[sl].h[:, dk, :], [bk], [wf[sl], xT],
                                  start=(dk == 0), stop=(dk == 15))
                        if g < 4:
                            Hh.cp("dve", ktok.h[:, s, (g - 2) * 256:(g - 1) * 256], bk.h[:, 0:256], [ktok], [bk])
                        elif g < 8:
                            Hh.cp("dve", vtok.h[:, s, (g - 4) * 256:(g - 3) * 256], bk.h[:, 0:256], [vtok], [bk])
                        else:
                            Hh.act(rs_.h[:, s, (g - 8) * 256:(g - 7) * 256], bk.h[:, 0:256], AF.Silu, [rs_], [bk])
            for k0 in range(0, 16, 4):
                P.dma("pool", wgl.h[:, k0:k0 + 4, :],
                      w_c.h[k0 * 128:(k0 + 4) * 128, 3072:3088].rearrange("(k p) f -> p k f", p=128), w=wgl, r=w_c)
            bk = nb()
            for dk in range(16):
                Hh.mm(bk.h[0:16, 0:SBK], wgl.h[:, dk, :], xT.h[:, dk, :], [bk], [wgl, xT], start=(dk == 0), stop=(dk == 15))
            Hh.cp("act", glT.h[0:16, :], bk.h[0:16, 0:SBK], [glT], [bk])

            for s in range(2):
                tb = t0 + s * 128
                cs = slice(s * 128, (s + 1) * 128)
                bz = nb()
                Hh.mm(bz.h[:], glT.h[0:17, cs], wga.h[:], [bz], [glT, wga])
                Hh.act(ez.h[:], bz.h[:], AF.Exp, [ez], [bz], scale=-1.0)
                Hh.act(ez.h[:], ez.h[:], AF.Ln, [ez], [ez], bias=1.0)
                Hh.ts("dve", ftok.h[:], ez.h[:], -1.0 / 16.0, ALU.mult, [ftok], [ez])
                bcu = nb()
                bto = nb()
                Hh.mm(bcu.h[:], ucs.h[:], ftok.h[:], [bcu], [ucs, ftok])
                Hh.mm(bto.h[:], vsame.h[:], ftok.h[:], [bto], [vsame, ftok])
                Hh.cp("act", bcs.h[:], bcu.h[:], [bcs], [bcu])
                Hh.tt("dve", dd.h[:], bto.h[:], bcs.h[:], ALU.subtract, [dd], [bto, bcs])
                Hh.act(dd.h[:], dd.h[:], AF.Exp, [dd], [dd])
                Hh.tt("dve", kdec.h[:], ktok.h[:, s, :], dd.h[:], ALU.mult, [kdec], [ktok, dd])
                bT = nb()
                bl = nb()
                for kc in range(4):
                    Hh.mm(bT.h[:, kc * 128:(kc + 1) * 128], ftok.h[:, kc * 128:(kc + 1) * 128], ucs.h[:], [bT], [ftok, ucs])
                    Hh.mm(bl.h[:, kc * 2:(kc + 1) * 2], ftok.h[:, kc * 128:(kc + 1) * 128], cind2.h[:], [bl], [ftok, cind2])
                bT3 = bT.h[:].rearrange("p (a b) -> p a b", a=4)
                Hh.act(eP.h[:], bT3, AF.Exp, [eP], [bT], bias=-0.5 * math.log(256.0))
                Hh.act(eN.h[:], bT3, AF.Exp, [eN], [bT], scale=-1.0)
                Hh.act(EB.h[:], bl.h[:, 0:8], AF.Exp, [EB], [bl])
                Hh.tt("dve", qt.h[:], qkT.h[:, 0:4, cs], eP.h[:], ALU.mult, [qt], [qkT, eP])
                Hh.tt("dve", kt.h[:], qkT.h[:, 4:8, cs], eN.h[:], ALU.mult, [kt], [qkT, eN])
                ba_ = nb()
                for h in range(2):
                    for kc in range(2):
                        Hh.mm(ba_.h[:, h * 128:(h + 1) * 128], kt.h[:, h * 2 + kc, :], qt.h[:, h * 2 + kc, :], [ba_], [kt, qt],
                              start=(kc == 0), stop=(kc == 1))
                Hh.tt("dve", attnT.h[:], ba_.h[:, 0:256].rearrange("p (a b) -> p a b", a=2),
                      mask01.h[:].unsqueeze(1).to_broadcast([128, 2, 128]), ALU.mult, [attnT], [ba_, mask01])
                for c in range(2):
                    rs = slice(c * 64, (c + 1) * 64)
                    for h in range(2):
                        vh = vtok.h[rs, s, h * 512:(h + 1) * 512]
                        bo = nb()
                        Hh.mm(bo.h[:], qt.h[:, h * 2, :], Sb.h[:, h * 2, :], [bo], [qt, Sb], start=True, stop=False)
                        Hh.mm(bo.h[:], qt.h[:, h * 2 + 1, :], Sb.h[:, h * 2 + 1, :], [bo], [qt, Sb], start=False, stop=False)
                        Hh.mm(bo.h[:], attnT.h[rs, h, :], vh, [bo], [attnT, vtok], start=False, stop=True)
                        Hh.cp("act", o.h[rs, h, :], bo.h[rs, :], [o], [bo])
                        for kc in range(2):
                            i4 = h * 2 + kc
                            bs = nb()
                            Hh.mm(bs.h[:], kdec.h[rs, i4 * 128:(i4 + 1) * 128], vh, [bs], [kdec, vtok])
                            Hh.stt(S.h[:, i4, :], S.h[:, i4, :], EB.h[:, i4 * 2 + c:i4 * 2 + c + 1], bs.h[:], ALU.mult, ALU.add,
                                   [S], [S, EB, bs])
                            Hh.cp("act", Sb.h[:, i4, :], S.h[:, i4, :], [Sb], [S])
                Hh.tt("pool", sqo.h[:], o.h[:], o.h[:], ALU.mult, [sqo], [o])
                P.op("dve", lambda e: e.tensor_reduce(out=ssq.h[:], in_=sqo.h[:], op=ALU.add, axis=AX.X), w=[ssq], r=[sqo])
                Hh.act(rms.h[:], ssq.h[:], AF.Sqrt, [rms], [ssq, epsr], scale=1.0 / 512.0, bias=epsr.h[:, 0:1])
                P.op("dve", lambda e: e.reciprocal(out=rms.h[:], in_=rms.h[:]), w=[rms], r=[rms])
                Hh.tt("dve", o.h[:], o.h[:], bc(rms.h[:], [128, 2, 512]), ALU.mult, [o], [o, rms])
                Hh.tt("pool", o.h[:], o.h[:], nw1.h[:].unsqueeze(1).to_broadcast([128, 2, 512]), ALU.mult, [o], [o, nw1])
                Hh.tt("dve", obf.h[:], o.h[:], rs_.h[:, s, :].rearrange("p (a b) -> p a b", a=2), ALU.mult, [obf], [o, rs_])
                P.dma("sp", og_out.h[tb:tb + 128, :], obf.h[:].rearrange("p a b -> p (a b)"), w=og_out, r=obf, owner=obf)
        P.end_stage()


I32 = mybir.dt.int32
LC = 128


def s5_layout_np(a_re, a_im, log_dt, b_re, b_im, c_re, c_im, d_skip, hf):
    gs = slice(hf * 64, (hf + 1) * 64)

    def pp(a):
        return np.ascontiguousarray(a[gs].reshape(32, 2, 64).transpose(1, 2, 0).reshape(128, 32))
    out = {}
    out["are"] = pp(a_re)
    out["aim"] = pp(a_im)
    out["ldt"] = np.ascontiguousarray(np.broadcast_to(log_dt[gs].reshape(32, 2).T[:, None, :], (2, 64, 32)).reshape(128, 32))

    def bb(b):
        return np.ascontiguousarray(b[gs].reshape(32, 2, 64, 16).transpose(1, 2, 0, 3).reshape(128, 512))
    out["bre"] = bb(b_re)
    out["bim"] = bb(b_im)

    def cc(c):
        o = np.zeros((2, 64, 32, 2, 16), np.float32)
        cg = c[gs].reshape(32, 2, 16, 64)
        for g2 in range(2):
            o[g2, :, :, g2, :] = cg[:, g2].transpose(2, 0, 1)
        return o.reshape(128, 32 * 32)
    out["cre"] = cc(c_re)
    out["cim"] = cc(c_im)
    out["dsk"] = np.ascontiguousarray(d_skip[hf * 1024:(hf + 1) * 1024])
    ms = np.zeros(2, np.float32)
    ms[hf] = 1.0
    out["msel"] = ms
    return out


def s5_stage(P, x_in, y_out, L, ident_d, T, isel_d, tag="s", rm=lambda t: t):
    Hh = H(P)
    NCH = T // LC
    with contextlib.ExitStack() as st:
        sbuf = lambda n, s, d=F32: P.sb(st, tag + n, s, d)
        idt = sbuf("idt", [128, 128])
        P.dma("sp", idt.h[:], ident_d.h[:, :], w=idt, r=ident_d)
        sm = {n: sbuf("sm_" + n, [128, 32]) for n in
              ("are", "aim", "dt", "ar", "th", "kf", "hl", "sh", "ah", "ch", "sn", "cs", "abr", "abi", "nr", "den", "cre", "cim",
               "t1", "t2", "hpr", "hpi")}
        ki = sbuf("ki", [128, 32], I32)
        for n in ("are", "aim"):
            P.dma("sp", sm[n].h[:], L[n].h[:, :], w=sm[n], r=L[n])
        P.dma("sp", sm["dt"].h[:], L["ldt"].h[:, :], w=sm["dt"], r=L["ldt"])
        m = sm
        tt = lambda eng, o_, a, b, op: Hh.tt(eng, o_.h[:], a.h[:], b.h[:], op, [o_], [a, b])
        Hh.act(m["dt"].h[:], m["dt"].h[:], AF.Exp, [m["dt"]], [m["dt"]])
        tt("dve", m["ar"], m["are"], m["dt"], ALU.mult)
        Hh.act(m["ar"].h[:], m["ar"].h[:], AF.Exp, [m["ar"]], [m["ar"]])
        tt("dve", m["th"], m["aim"], m["dt"], ALU.mult)
        Hh.ts("dve", m["kf"].h[:], m["th"].h[:], 1.0 / (2.0 * math.pi), ALU.mult, [m["kf"]], [m["th"]])
        Hh.cp("dve", ki.h[:], m["kf"].h[:], [ki], [m["kf"]])
        Hh.cp("dve", m["kf"].h[:], ki.h[:], [m["kf"]], [ki])
        C1 = 6.28125
        C2 = 2.0 * math.pi - C1
        Hh.stt(m["th"].h[:], m["kf"].h[:], -C1, m["th"].h[:], ALU.mult, ALU.add, [m["th"]], [m["kf"], m["th"]])
        Hh.stt(m["th"].h[:], m["kf"].h[:], -C2, m["th"].h[:], ALU.mult, ALU.add, [m["th"]], [m["kf"], m["th"]])
        Hh.ts("dve", m["hl"].h[:], m["th"].h[:], 0.5, ALU.mult, [m["hl"]], [m["th"]])
        Hh.act(m["sh"].h[:], m["hl"].h[:], AF.Sin, [m["sh"]], [m["hl"]])
        Hh.act(m["ah"].h[:], m["hl"].h[:], AF.Abs, [m["ah"]], [m["hl"]])
        hpi2 = sbuf("hpi2", [128, 1])
        Hh.memset("dve", hpi2.h[:], math.pi / 2.0, [hpi2])
        Hh.act(m["ch"].h[:], m["ah"].h[:], AF.Sin, [m["ch"]], [m["ah"], hpi2], scale=-1.0, bias=hpi2.h[:, 0:1])
        tt("dve", m["sn"], m["sh"], m["ch"], ALU.mult)
        Hh.ts("dve", m["sn"].h[:], m["sn"].h[:], 2.0, ALU.mult, [m["sn"]], [m["sn"]])
        tt("dve", m["cs"], m["sh"], m["sh"], ALU.mult)
        Hh.ts("dve", m["cs"].h[:], m["cs"].h[:], -2.0, ALU.mult, [m["cs"]], [m["cs"]], s2=1.0, op1=ALU.add)
        tt("dve", m["abr"], m["ar"], m["cs"], ALU.mult)
        tt("dve", m["abi"], m["ar"], m["sn"], ALU.mult)
        Hh.ts("dve", m["nr"].h[:], m["abr"].h[:], -1.0, ALU.add, [m["nr"]], [m["abr"]])
        tt("dve", m["den"], m["are"], m["are"], ALU.mult)
        tt("dve", m["t1"], m["aim"], m["aim"], ALU.mult)
        tt("dve", m["den"], m["den"], m["t1"], ALU.add)
        P.op("dve", lambda e: e.reciprocal(out=m["den"].h[:], in_=m["den"].h[:]), w=[m["den"]], r=[m["den"]])
        tt("dve", m["t1"], m["nr"], m["are"], ALU.mult)
        tt("dve", m["t2"], m["abi"], m["aim"], ALU.mult)
        tt("dve", m["cre"], m["t1"], m["t2"], ALU.add)
        tt("dve", m["cre"], m["cre"], m["den"], ALU.mult)
        tt("dve", m["t1"], m["abi"], m["are"], ALU.mult)
        tt("dve", m["t2"], m["nr"], m["aim"], ALU.mult)
        tt("dve", m["cim"], m["t1"], m["t2"], ALU.subtract)
        tt("dve", m["cim"], m["cim"], m["den"], ALU.mult)

        K_re = sbuf("Kre", [128, 32, LC])
        K_im = sbuf("Kim", [128, 32, LC])
        H_re = sbuf("Hre", [128, 32, LC])
        H_im = sbuf("Him", [128, 32, LC])
        WbT_re = sbuf("WbTre", [128, 32, 128])
        WbT_im = sbuf("WbTim", [128, 32, 128])
        banks = [P.ps(st, tag + "bank%d" % i, [128, 512]) for i in range(8)]
        bctr = [0]

        def nb():
            b = banks[bctr[0] % 8]
            bctr[0] += 1
            return b
        Braw_re = P.view(K_re.h[:, 0:4, :].rearrange("p a b -> p (a b)"), "Braw_re")
        Braw_im = P.view(K_im.h[:, 0:4, :].rearrange("p a b -> p (a b)"), "Braw_im")
        bb_re = P.view(H_re.h[:, 0:4, :].rearrange("p a b -> p (a b)"), "bb_re")
        bb_im = P.view(H_im.h[:, 0:4, :].rearrange("p a b -> p (a b)"), "bb_im")
        tmpa = P.view(K_re.h[:, 4:8, :].rearrange("p a b -> p (a b)"), "tmpa")
        tmpb = P.view(K_im.h[:, 4:8, :].rearrange("p a b -> p (a b)"), "tmpb")
        P.dma("sp", Braw_re.h, L["bre"].h[:, :], w=Braw_re, r=L["bre"])
        P.dma("sp", Braw_im.h, L["bim"].h[:, :], w=Braw_im, r=L["bim"])
        v3 = lambda b: b.h.rearrange("p (a c) -> p a c", a=32)
        cre_b = bc(m["cre"].h[:], [128, 32, 16])
        cim_b = bc(m["cim"].h[:], [128, 32, 16])
        Hh.tt("dve", v3(tmpa), v3(Braw_re), cre_b, ALU.mult, [tmpa], [Braw_re, m["cre"]])
        Hh.tt("dve", v3(tmpb), v3(Braw_im), cim_b, ALU.mult, [tmpb], [Braw_im, m["cim"]])
        Hh.tt("dve", bb_re.h, tmpa.h, tmpb.h, ALU.subtract, [bb_re], [tmpa, tmpb])
        Hh.tt("dve", v3(tmpa), v3(Braw_im), cre_b, ALU.mult, [tmpa], [Braw_im, m["cre"]])
        Hh.tt("dve", v3(tmpb), v3(Braw_re), cim_b, ALU.mult, [tmpb], [Braw_re, m["cim"]])
        Hh.tt("dve", bb_im.h, tmpa.h, tmpb.h, ALU.add, [bb_im], [tmpa, tmpb])
        pad = [sbuf("pad%d" % i, [128, 128]) for i in range(2)]
        for i in range(2):
            Hh.memset("dve", pad[i].h[:], 0.0, [pad[i]])
        npad = 0
        for pi in range(32):
            c0 = (pi % 4) * 32
            for (src, dstW) in ((bb_re, WbT_re), (bb_im, WbT_im)):
                pd = pad[npad % 2]
                npad += 1
                Hh.memset("pool", pd.h[:], 0.0, [pd])
                Hh.cp("pool", pd.h[0:64, c0:c0 + 16], src.h[0:64, pi * 16:(pi + 1) * 16], [pd], [src])
                Hh.cp("pool", pd.h[64:128, c0 + 16:c0 + 32], src.h[64:128, pi * 16:(pi + 1) * 16], [pd], [src])
                bk = nb()
                Hh.tr(bk.h[:, 0:128], pd.h[:], idt.h[:], [bk], [pd, idt])
                Hh.cp("act", dstW.h[:, pi, :], bk.h[:, 0:128], [dstW], [bk])
        Cre = sbuf("Cre", [128, 32, 32])
        Cni = sbuf("Cni", [128, 32, 32])
        P.dma("sp", Cre.h[:].rearrange("p a b -> p (a b)"), L["cre"].h[:, :], w=Cre, r=L["cre"])
        P.dma("sp", Cni.h[:].rearrange("p a b -> p (a b)"), L["cim"].h[:, :], w=Cni, r=L["cim"])
        Hh.ts("dve", Cni.h[:], Cni.h[:], -1.0, ALU.mult, [Cni], [Cni])
        Dt = sbuf("Dt", [128, 1024])
        msel = sbuf("msel", [128, 2])
        P.dma("sp", msel.h[:], L["msel"].h.partition_broadcast(128), w=msel, r=L["msel"])
        iself = sbuf("iself", [128, 2, 128])
        for h in range(2):
            P.dma("sp", iself.h[:, h, :], isel_d.h[h], w=iself, r=isel_d)
        P.dma("sp", Dt.h[:], L["dsk"].h.partition_broadcast(128), w=Dt, r=L["dsk"])
        msk = sbuf("msk", [128, 32, LC], BF16)
        Hh.memset("dve", msk.h[:], 1.0, [msk])
        Hh.memset("dve", msk.h[:, :, 0:1], 0.0, [msk])
        Tp_re = sbuf("Tpre", [128, 32, LC])
        Tp_im = sbuf("Tpim", [128, 32, LC])
        Tn_re = sbuf("Tnre", [128, 32, LC])
        Tn_im = sbuf("Tnim", [128, 32, LC])
        P.barrier()
        Hh.cp("dve", Tp_re.h[:, :, 0], m["abr"].h[:], [Tp_re], [m["abr"]])
        Hh.cp("dve", Tp_im.h[:, :, 0], m["abi"].h[:], [Tp_im], [m["abi"]])
        n = 1
        while n < LC:
            sre = bc(Tp_re.h[:, :, n - 1], [128, 32, n])
            sim = bc(Tp_im.h[:, :, n - 1], [128, 32, n])
            A_re = Tp_re.h[:, :, 0:n]
            A_im = Tp_im.h[:, :, 0:n]
            t1 = K_re.h[:, :, 0:n]
            t2 = K_im.h[:, :, 0:n]
            t3 = H_re.h[:, :, 0:n]
            t4 = H_im.h[:, :, 0:n]
            Hh.tt("dve", t1, A_re, sre, ALU.mult, [K_re], [Tp_re])
            Hh.tt("dve", t2, A_im, sim, ALU.mult, [K_im], [Tp_im])
            Hh.tt("dve", t3, A_re, sim, ALU.mult, [H_re], [Tp_re, Tp_im])
            Hh.tt("dve", t4, A_im, sre, ALU.mult, [H_im], [Tp_re, Tp_im])
            Hh.tt("dve", Tp_re.h[:, :, n:2 * n], t1, t2, ALU.subtract, [Tp_re], [K_re, K_im])
            Hh.tt("dve", Tp_im.h[:, :, n:2 * n], t3, t4, ALU.add, [Tp_im], [H_re, H_im])
            n *= 2
        Hh.tt("dve", K_re.h[:], Tp_re.h[:], Tp_re.h[:], ALU.mult, [K_re], [Tp_re])
        Hh.tt("dve", K_im.h[:], Tp_im.h[:], Tp_im.h[:], ALU.mult, [K_im], [Tp_im])
        Hh.tt("dve", K_re.h[:], K_re.h[:], K_im.h[:], ALU.add, [K_re], [K_re, K_im])
        P.op("dve", lambda e: e.reciprocal(out=K_re.h[:], in_=K_re.h[:]), w=[K_re], r=[K_re])
        Hh.tt("dve", Tn_re.h[:], Tp_re.h[:], K_re.h[:], ALU.mult, [Tn_re], [Tp_re, K_re])
        Hh.tt("dve", Tn_im.h[:], Tp_im.h[:], K_re.h[:], ALU.mult, [Tn_im], [Tp_im, K_re])
        Hh.ts("dve", Tn_im.h[:], Tn_im.h[:], -1.0, ALU.mult, [Tn_im], [Tn_im])
        Hh.memset("dve", m["hpr"].h[:], 0.0, [m["hpr"]])
        Hh.memset("dve", m["hpi"].h[:], 0.0, [m["hpi"]])

        xu = sbuf("xu", [128, 2048])
        uT = sbuf("uT", [128, 8, 128])
        tq = [sbuf("tq%d" % i, [128, 4, LC]) for i in range(2)]
        tq = tq + tq
        du = sbuf("du", [128, 1024])
        ybf = sbuf("ybf", [128, 1024], BF16)
        for ch in range(NCH):
            tb = ch * LC
            P.dma("sp", xu.h[:], x_in.h[rm(tb):rm(tb) + 128, :], w=xu, r=x_in)
            for q in range(2):
                bk = nb()
                for i in range(4):
                    for h in range(2):
                        c0 = h * 1024 + (q * 4 + i) * 128
                        Hh.mm(bk.h[:, i * 128:(i + 1) * 128], xu.h[:, c0:c0 + 128], iself.h[:, h, :], [bk], [xu, iself],
                              start=(h == 0), stop=(h == 1))
                Hh.cp("act", uT.h[:, q * 4:(q + 1) * 4, :], bk.h[:].rearrange("p (a b) -> p a b", a=4), [uT], [bk])
            for q in range(8):
                bre = nb()
                bim = nb()
                for i in range(4):
                    pi = q * 4 + i
                    Hh.mm(bre.h[:, i * 128:(i + 1) * 128], WbT_re.h[:, pi, :], uT.h[:, q, :], [bre], [WbT_re, uT])
                    Hh.mm(bim.h[:, i * 128:(i + 1) * 128], WbT_im.h[:, pi, :], uT.h[:, q, :], [bim], [WbT_im, uT])
                qs = slice(q * 4, (q + 1) * 4)
                b3 = lambda b: b.h[:].rearrange("p (a b) -> p a b", a=4)
                Hh.tt("dve", tq[0].h[:], Tn_re.h[:, qs, :], b3(bre), ALU.mult, [tq[0]], [Tn_re, bre])
                Hh.tt("dve", tq[1].h[:], Tn_im.h[:, qs, :], b3(bim), ALU.mult, [tq[1]], [Tn_im, bim])
                Hh.tt("pool", K_re.h[:, qs, :], tq[0].h[:], tq[1].h[:], ALU.subtract, [K_re], [tq[0], tq[1]])
                Hh.tt("dve", tq[2].h[:], Tn_re.h[:, qs, :], b3(bim), ALU.mult, [tq[2]], [Tn_re, bim])
                Hh.tt("dve", tq[3].h[:], Tn_im.h[:, qs, :], b3(bre), ALU.mult, [tq[3]], [Tn_im, bre])
                Hh.tt("pool", K_im.h[:, qs, :], tq[2].h[:], tq[3].h[:], ALU.add, [K_im], [tq[2], tq[3]])
            Hh.tt("pool", K_re.h[:, :, 0], K_re.h[:, :, 0], m["hpr"].h[:], ALU.add, [K_re], [K_re, m["hpr"]])
            Hh.tt("pool", K_im.h[:, :, 0], K_im.h[:, :, 0], m["hpi"].h[:], ALU.add, [K_im], [K_im, m["hpi"]])
            f2 = lambda b: b.h[:].rearrange("p a b -> p (a b)")
            P.op("dve", lambda e: e.tensor_tensor_scan(out=f2(H_re), data0=f2(msk), data1=f2(K_re), initial=0.0,
                                                       op0=ALU.mult, op1=ALU.add), w=[H_re], r=[msk, K_re])
            P.op("dve", lambda e: e.tensor_tensor_scan(out=f2(H_im), data0=f2(msk), data1=f2(K_im), initial=0.0,
                                                       op0=ALU.mult, op1=ALU.add), w=[H_im], r=[msk, K_im])
            for q in range(8):
                qs = slice(q * 4, (q + 1) * 4)
                Hh.tt("dve", tq[0].h[:], Tp_re.h[:, qs, :], H_re.h[:, qs, :], ALU.mult, [tq[0]], [Tp_re, H_re])
                Hh.tt("dve", tq[1].h[:], Tp_im.h[:, qs, :], H_im.h[:, qs, :], ALU.mult, [tq[1]], [Tp_im, H_im])
                Hh.tt("pool", K_re.h[:, qs, :], tq[0].h[:], tq[1].h[:], ALU.subtract, [K_re], [tq[0], tq[1]])
                Hh.tt("dve", tq[2].h[:], Tp_re.h[:, qs, :], H_im.h[:, qs, :], ALU.mult, [tq[2]], [Tp_re, H_im])
                Hh.tt("dve", tq[3].h[:], Tp_im.h[:, qs, :], H_re.h[:, qs, :], ALU.mult, [tq[3]], [Tp_im, H_re])
                Hh.tt("pool", K_im.h[:, qs, :], tq[2].h[:], tq[3].h[:], ALU.add, [K_im], [tq[2], tq[3]])
            Hh.cp("pool", m["hpr"].h[:], K_re.h[:, :, LC - 1], [m["hpr"]], [K_re])
            Hh.cp("pool", m["hpi"].h[:], K_im.h[:, :, LC - 1], [m["hpi"]], [K_im])
            by = [nb(), nb()]
            for pi in range(32):
                o_ = by[pi // 16].h[:, (pi % 16) * 32:(pi % 16 + 1) * 32]
                Hh.mm(o_, K_re.h[:, pi, :], Cre.h[:, pi, :], [by[pi // 16]], [K_re, Cre], start=True, stop=False)
                Hh.mm(o_, K_im.h[:, pi, :], Cni.h[:, pi, :], [by[pi // 16]], [K_im, Cni], start=False, stop=True)
            Hh.ts("pool", du.h[:], xu.h[:, 0:1024], msel.h[:, 0:1], ALU.mult, [du], [xu, msel])
            Hh.stt(du.h[:], xu.h[:, 1024:2048], msel.h[:, 1:2], du.h[:], ALU.mult, ALU.add, [du], [xu, msel, du])
            Hh.tt("pool", du.h[:], du.h[:], Dt.h[:], ALU.mult, [du], [du, Dt])
            for i in range(2):
                Hh.tt("dve", du.h[:, i * 512:(i + 1) * 512], by[i].h[:], du.h[:, i * 512:(i + 1) * 512], ALU.add, [du], [by[i], du])
            Hh.act(ybf.h[:], du.h[:], AF.Gelu_apprx_tanh, [ybf], [du])
            P.dma("sp", y_out.h[tb:tb + 128, :], ybf.h[:], w=y_out, r=ybf, owner=ybf)
        P.end_stage()


FF = 5632
PAIRS = [[0, 1], [2, 3], [4, 5], [6, 7]]
DEEPNORM_ALPHA = (2.0 * 4) ** 0.25


def load_hT_gathered(P, Hh, src_g, SEQ, TC, Vh, iselb, t0, xs, hT, banks, nbk):
    PR = min(TC, (2 << 20) // (Vh * 2))
    NK = 2 * Vh // 128
    for s in range(4):
        off = t0 + s * 128
        for h in range(2):
            for r in range(2):
                row = h * 2 * TC + (off // PR) * 2 * PR + r * PR + off % PR
                P.dma("sp", xs[h].h[:, r * Vh:(r + 1) * Vh], src_g.h[row:row + 128, :], w=xs[h], r=src_g)
        for q in range(NK // 4):
            bk = banks[nbk[0] % 8]
            nbk[0] += 1
            for i in range(4):
                dk = q * 4 + i
                for h in range(2):
                    Hh.mm(bk.h[:, i * 128:(i + 1) * 128], xs[h].h[:, dk * 128:(dk + 1) * 128], iselb.h[:, h, :], [bk],
                          [xs[h], iselb], start=(h == 0), stop=(h == 1))
            Hh.cp("act" if q % 2 == 0 else "dve", hT.h[:, q * 4:(q + 1) * 4, s * 128:(s + 1) * 128],
                  bk.h[:].rearrange("p (a b) -> p a b", a=4), [hT], [bk])


def load_isel(P, Hh, st, isel_d, tag):
    iself = P.sb(st, tag + "iself", [128, 2, 128], F32)
    for h in range(2):
        P.dma("sp", iself.h[:, h, :], isel_d.h[h], w=iself, r=isel_d)
    iselb = P.sb(st, tag + "iselb", [128, 2, 128], BF16)
    Hh.cp("dve", iselb.h[:], iself.h[:], [iselb], [iself])
    return iself, iselb


def outproj_stage(P, og_g, SEQ, Vh, x_res, x_out, w_out, ln_g, ln_b, ident_d, TC, alpha, isel_d, tag="p"):
    Hh = H(P)
    NB = TC // 512
    V = 2 * Vh
    NK = V // 128
    NKG = 8
    with contextlib.ExitStack() as st:
        idt, epsT = load_consts(P, st, ident_d)
        iself, iselb = load_isel(P, Hh, st, isel_d, tag)
        Gt = P.sb(st, tag + "G", [128, D], F32)
        Bt = P.sb(st, tag + "B", [128, D], F32)
        P.dma("sp", Gt.h[:], ln_g.h.partition_broadcast(128), w=Gt, r=ln_g)
        P.dma("sp", Bt.h[:], ln_b.h.partition_broadcast(128), w=Bt, r=ln_b)
        xs = [P.sb(st, tag + "xs%d" % i, [128, V], BF16) for i in range(2)]
        hT = P.sb(st, tag + "hT", [128, NK, 512], BF16)
        wd = [P.sb(st, tag + "wd%d" % i, [128, NKG, 512], BF16) for i in range(2)]
        z = [P.sb(st, tag + "z%d" % i, [128, D], F32) for i in range(4)]
        tmps = [ln_tmp(P, st, tag + "t%d" % i) for i in range(2)]
        bank = [P.ps(st, tag + "bank%d" % i, [128, 512]) for i in range(8)]
        nwd = 0
        nbk = [0]
        for blk in range(NB):
            t0 = blk * 512
            load_hT_gathered(P, Hh, og_g, SEQ, TC, Vh, iselb, t0, xs, hT, bank, nbk)
            for s in range(4):
                P.dma("sp", z[s].h[:], x_res.h[t0 + s * 128:t0 + (s + 1) * 128, :], w=z[s], r=x_res)
                Hh.act(z[s].h[:], z[s].h[:], AF.Copy, [z[s]], [z[s]], scale=float(alpha))
            for dn in range(4):
                pb = [bank[(nbk[0] + s) % 8] for s in range(4)]
                nbk[0] += 4
                for fg in range(NK // NKG):
                    sl = nwd % 2
                    nwd += 1
                    for a in range(0, NKG, 4):
                        r0 = (fg * NKG + a) * 128
                        P.dma("pool", wd[sl].h[:, a:a + 4, :],
                              w_out.h[r0:r0 + 4 * 128, dn * 512:(dn + 1) * 512].rearrange("(k p) f -> p k f", p=128),
                              w=wd[sl], r=w_out)
                    for i in range(NKG):
                        fk = fg * NKG + i
                        for s in range(4):
                            Hh.mm(pb[s].h[:], hT.h[:, fk, s * 128:(s + 1) * 128], wd[sl].h[:, i, :], [pb[s]], [hT, wd[sl]],
                                  start=(fk == 0), stop=(fk == NK - 1))
                for s in range(4):
                    zc = z[s].h[:, dn * 512:(dn + 1) * 512]
                    Hh.tt("dve", zc, pb[s].h[:], zc, ALU.add, [z[s]], [pb[s], z[s]])
            for s in range(4):
                layer_norm_tile(P, z[s], Gt, Bt, epsT, tmps[s % 2])
                P.dma("sp", x_out.h[t0 + s * 128:t0 + (s + 1) * 128, :], z[s].h[:], w=x_out, r=z[s], owner=z[s])
        P.end_stage()


def s5out_stage(P, y_g, SEQ, x_res, x_out, w_glu, ln_g, ln_b, ident_d, TC, alpha, isel_d, tag="q"):
    Hh = H(P)
    NB = TC // 512
    with contextlib.ExitStack() as st:
        idt, epsT = load_consts(P, st, ident_d)
        iself, iselb = load_isel(P, Hh, st, isel_d, tag)
        Gt = P.sb(st, tag + "G", [128, D], F32)
        Bt = P.sb(st, tag + "B", [128, D], F32)
        P.dma("sp", Gt.h[:], ln_g.h.partition_broadcast(128), w=Gt, r=ln_g)
        P.dma("sp", Bt.h[:], ln_b.h.partition_broadcast(128), w=Bt, r=ln_b)
        xs = [P.sb(st, tag + "xs%d" % i, [128, D], BF16) for i in range(2)]
        yT = P.sb(st, tag + "yT", [128, 16, 512], BF16)
        wv = [P.sb(st, tag + "wv%d" % i, [128, 16, 512], BF16) for i in range(2)]
        wg = [P.sb(st, tag + "wg%d" % i, [128, 16, 512], BF16) for i in range(2)]
        sg = [P.sb(st, tag + "sg%d" % i, [128, 512], F32) for i in range(2)]
        z = [P.sb(st, tag + "z%d" % i, [128, D], F32) for i in range(4)]
        tmps = [ln_tmp(P, st, tag + "t%d" % i) for i in range(2)]
        bank = [P.ps(st, tag + "bank%d" % i, [128, 512]) for i in range(8)]
        nw = 0
        nbk = [0]
        for blk in range(NB):
            t0 = blk * 512
            load_hT_gathered(P, Hh, y_g, SEQ, TC, 1024, iselb, t0, xs, yT, bank, nbk)
            for s in range(4):
                P.dma("sp", z[s].h[:], x_res.h[t0 + s * 128:t0 + (s + 1) * 128, :], w=z[s], r=x_res)
                Hh.act(z[s].h[:], z[s].h[:], AF.Copy, [z[s]], [z[s]], scale=float(alpha))
            for dn in range(4):
                sl = nw % 2
                nw += 1
                for k0 in range(0, 16, 4):
                    P.dma("pool", wv[sl].h[:, k0:k0 + 4, :],
                          w_glu.h[k0 * 128:(k0 + 4) * 128, dn * 512:(dn + 1) * 512].rearrange("(k p) f -> p k f", p=128),
                          w=wv[sl], r=w_glu)
                    P.dma("pool", wg[sl].h[:, k0:k0 + 4, :],
                          w_glu.h[k0 * 128:(k0 + 4) * 128, D + dn * 512:D + (dn + 1) * 512].rearrange(
                              "(k p) f -> p k f", p=128), w=wg[sl], r=w_glu)
                for s in range(4):
                    pv = bank[nbk[0] % 8]
                    pg = bank[(nbk[0] + 1) % 8]
                    nbk[0] += 2
                    for dk in range(16):
                        Hh.mm(pg.h[:], yT.h[:, dk, s * 128:(s + 1) * 128], wg[sl].h[:, dk, :], [pg], [yT, wg[sl]],
                              start=(dk == 0), stop=(dk == 15))
                    for dk in range(16):
                        Hh.mm(pv.h[:], yT.h[:, dk, s * 128:(s + 1) * 128], wv[sl].h[:, dk, :], [pv], [yT, wv[sl]],
                              start=(dk == 0), stop=(dk == 15))
                    sgb = sg[s % 2]
                    Hh.act(sgb.h[:], pg.h[:], AF.Sigmoid, [sgb], [pg])
                    Hh.tt("dve", sgb.h[:], sgb.h[:], pv.h[:], ALU.mult, [sgb], [sgb, pv])
                    zc = z[s].h[:, dn * 512:(dn + 1) * 512]
                    Hh.tt("pool", zc, zc, sgb.h[:], ALU.add, [z[s]], [z[s], sgb])
            for s in range(4):
                layer_norm_tile(P, z[s], Gt, Bt, epsT, tmps[s % 2])
                P.dma("sp", x_out.h[t0 + s * 128:t0 + (s + 1) * 128, :], z[s].h[:], w=x_out, r=z[s], owner=z[s])
        P.end_stage()


def _consts_np():
    c = gdn_consts_np()
    c.update(gla_consts_np())
    return c


S5_KEYS = ("are", "aim", "ldt", "bre", "bim", "cre", "cim", "dsk", "msel")


def build_program(SEQ, layers=(0, 1, 2, 3)):
    TC = SEQ // 2
    nc = bass.Bass("TRN2", target_bir_lowering=False)
    P = Prog(nc)
    ext = lambda n, s, d=F32: P.dram(n, s, d, kind="ExternalInput")
    x = ext("x", [TC, D])
    y = P.dram("y", [TC, D], F32, kind="ExternalOutput")
    ln_g = ext("ln_g", [4, 3, D])
    ln_b = ext("ln_b", [4, 3, D])
    w_up = ext("ffn_w_up", [4, 2, D, 2 * FF])
    w_dn = ext("ffn_w_down", [4, 2, FF, D])
    gdn_wout = ext("gdn_w_out", [2, 4096, D])
    gla_wout = ext("gla_w_out", [1, 2048, D])
    s5_wglu = ext("s5_w_glu", [1, D, 2 * D])
    gdn_nw = ext("gdn_norm_w", [2, 128])
    gla_nw = ext("gla_norm_w", [1, 512])
    gdn_wc = ext("gdn_wc", [2, D, 6176])
    gdn_conv = ext("gdn_conv", [2, 128, 128])
    gdn_alog = ext("gdn_alog", [2, 16])
    gdn_dtb = ext("gdn_dtb", [2, 16])
    gla_wc = ext("gla_wc", [D, 3088])
    gla_wga = ext("gla_wga", [17, 512])
    L = {}
    l0 = s5_layout_np(*[np.zeros(s, np.float32) for s in ((128, 64), (128, 64), (128,), (128, 64, 16), (128, 64, 16),
                                                          (128, 16, 64), (128, 16, 64), (2048,))], 0)
    for k in S5_KEYS:
        L[k] = ext("s5L_" + k, list(l0[k].shape))
    C = {k: ext("c_" + k, list(v.shape)) for k, v in _consts_np().items()}
    A = P.dram("actA", [TC, D], F32)
    B = P.dram("actB", [TC, D], F32)
    Cb = P.dram("actC", [TC, D], F32)
    XF = P.dram("actXF", [SEQ, D], F32)
    OG2 = P.dram("og2", [SEQ, 2048], BF16)
    OGG2 = P.dram("ogg2", [2 * SEQ, 2048], BF16)
    OG1 = P.dram("og1", [SEQ, 1024], BF16)
    OGG1 = P.dram("ogg1", [2 * SEQ, 1024], BF16)
    isel = ext("isel", [2, 128, 128])
    V = lambda ap, n: Buf(ap, n)
    rm = lambda t: ((t % TC) // 256) * 512 + (t // TC) * 256 + (t % 256)
    alpha = DEEPNORM_ALPHA
    ident = C["ident"]
    xin = x
    for li, i in enumerate(layers):
        last = (li == len(layers) - 1)
        kind, j = i % 3, i // 3
        ffn_stage(P, xin, A, V(w_up.h[i, 0], "wu"), V(w_dn.h[i, 0], "wd"), V(ln_g.h[i, 0], "g"), V(ln_b.h[i, 0], "b"),
                  ident, TC, alpha, tag="f%da" % i)
        P.gather_pairs(A, XF, TC, 256)
        if kind == 0:
            gdn_stage(P, XF, OG2, V(gdn_wc.h[j], "gw"), V(gdn_conv.h[j], "gc"), V(gdn_alog.h[j], "ga"),
                      V(gdn_dtb.h[j], "gd"), V(gdn_nw.h[j], "gn"), C, SEQ, tag="g%d" % i, rm=rm)
            P.gather_pairs(OG2, OGG2, SEQ, min(TC, 512))
            outproj_stage(P, OGG2, SEQ, 2048, A, B, V(gdn_wout.h[j], "gwo"), V(ln_g.h[i, 1], "g"), V(ln_b.h[i, 1], "b"),
                          ident, TC, alpha, isel, tag="p%d" % i)
        elif kind == 1:
            gla_stage(P, XF, OG1, gla_wc, gla_wga, V(gla_nw.h[j], "ln"), C, SEQ, tag="l%d" % i, rm=rm)
            P.gather_pairs(OG1, OGG1, SEQ, min(TC, 1024))
            outproj_stage(P, OGG1, SEQ, 1024, A, B, V(gla_wout.h[j], "lwo"), V(ln_g.h[i, 1], "g"), V(ln_b.h[i, 1], "b"),
                          ident, TC, alpha, isel, tag="p%d" % i)
        else:
            s5_stage(P, XF, OG1, L, ident, SEQ, isel, tag="s%d" % i, rm=rm)
            P.gather_pairs(OG1, OGG1, SEQ, min(TC, 1024))
            s5out_stage(P, OGG1, SEQ, A, B, V(s5_wglu.h[j], "sw"), V(ln_g.h[i, 1], "g"), V(ln_b.h[i, 1], "b"),
                        ident, TC, alpha, isel, tag="q%d" % i)
        dst = y if last else Cb
        ffn_stage(P, B, dst, V(w_up.h[i, 1], "wu"), V(w_dn.h[i, 1], "wd"), V(ln_g.h[i, 2], "g"), V(ln_b.h[i, 2], "b"),
                  ident, TC, alpha, tag="f%db" % i)
        xin = Cb
    P.finish()
    return nc, P


def make_in_maps(inputs, SEQ):
    TC = SEQ // 2
    f = lambda a: np.ascontiguousarray(np.asarray(a, dtype=np.float32))
    rep = {k: f(inputs[k]) for k in ("ln_g", "ln_b", "ffn_w_up", "ffn_w_down", "gdn_w_out", "gla_w_out", "s5_w_glu",
                                     "gdn_norm_w", "gla_norm_w")}
    consts = {"c_" + k: v for k, v in _consts_np().items()}
    x = np.asarray(inputs["x"], dtype=np.float32)
    gdn_w_in = np.asarray(inputs["gdn_w_in"], np.float32)
    gdn_conv_w = np.asarray(inputs["gdn_conv_w"], np.float32)
    gla_w_in = np.asarray(inputs["gla_w_in"], np.float32)[0]
    gla_w_gate = np.asarray(inputs["gla_w_gate"], np.float32)[0]
    gla_gb = np.asarray(inputs["gla_gate_bias"], np.float32)[0]
    half = []
    for hf in range(2):
        d = {}
        cols = np.concatenate([np.arange(hf * 1024, hf * 1024 + 1024), 2048 + np.arange(hf * 1024, hf * 1024 + 1024),
                               4096 + np.arange(hf * 2048, hf * 2048 + 2048), 8192 + np.arange(hf * 2048, hf * 2048 + 2048),
                               12288 + np.arange(hf * 16, hf * 16 + 16), 12320 + np.arange(hf * 16, hf * 16 + 16)])
        d["gdn_wc"] = np.ascontiguousarray(gdn_w_in[:, :, cols])
        cc = cols[:4096]
        d["gdn_conv"] = np.ascontiguousarray(
            gdn_conv_w[:, :, cc].reshape(2, 4, 32, 128).transpose(0, 3, 2, 1).reshape(2, 128, 128))
        d["gdn_alog"] = f(np.asarray(inputs["gdn_a_log"])[:, hf * 16:(hf + 1) * 16])
        d["gdn_dtb"] = f(np.asarray(inputs["gdn_dt_bias"])[:, hf * 16:(hf + 1) * 16])
        lc = np.concatenate([np.arange(hf * 512, hf * 512 + 512), 1024 + np.arange(hf * 512, hf * 512 + 512),
                             2048 + np.arange(hf * 1024, hf * 1024 + 1024), 4096 + np.arange(hf * 1024, hf * 1024 + 1024),
                             6144 + np.arange(16)])
        d["gla_wc"] = np.ascontiguousarray(gla_w_in[:, lc])
        d["gla_wga"] = np.ascontiguousarray(
            np.concatenate([gla_w_gate[:, hf * 512:(hf + 1) * 512], gla_gb[None, hf * 512:(hf + 1) * 512]], 0))
        Lnp = s5_layout_np(*[np.asarray(inputs[k], np.float32)[0] for k in
                             ("s5_a_re", "s5_a_im", "s5_log_dt", "s5_b_re", "s5_b_im", "s5_c_re", "s5_c_im", "s5_d")], hf)
        for k in S5_KEYS:
            d["s5L_" + k] = Lnp[k]
        half.append(d)
    in_maps = []
    for c in range(8):
        b, hf = c // 2, c % 2
        m = dict(rep)
        m.update(consts)
        m.update(half[hf])
        m["x"] = np.ascontiguousarray(x[b, hf * TC:(hf + 1) * TC])
        isel = np.zeros((2, 128, 128), np.float32)
        isel[hf] = np.eye(128, dtype=np.float32)
        m["isel"] = isel
        in_maps.append(m)
    return in_maps


_CACHE = {}


def run_model(inputs, SEQ, trace=False):
    from concourse.bass_utils import run_bass_kernel_spmd
    if SEQ not in _CACHE:
        _CACHE[SEQ] = build_program(SEQ)[0]
    nc = _CACHE[SEQ]
    in_maps = make_in_maps(inputs, SEQ)
    res = run_bass_kernel_spmd(nc, in_maps, core_ids=list(range(8)))
    TC = SEQ // 2
    out = np.empty((4, SEQ, D), np.float32)
    for c in range(8):
        b, hf = c // 2, c % 2
        out[b, hf * TC:(hf + 1) * TC] = res.results[c]["y"]
    return out


def kernel(**inputs):
    return run_model(inputs, 8192)
```

```python
import contextlib
import numpy as np
import concourse.bass as bass
import concourse.mybir as mybir

F32 = mybir.dt.float32
BF16 = mybir.dt.bfloat16
AF = mybir.ActivationFunctionType
ALU = mybir.AluOpType
AX = mybir.AxisListType


class Buf:
    __slots__ = ("h", "name", "w", "r", "dsem")

    def __init__(self, h, name):
        self.h = h
        self.name = name
        self.w = {}
        self.r = {}
        self.dsem = None


class Prog:
    ENG = ("pe", "act", "dve", "pool", "sp")

    def __init__(self, nc, n_dma_sems=80, same_engine_sync=True):
        self.nc = nc
        self.es = contextlib.ExitStack()
        self.eng = {"pe": nc.tensor, "act": nc.scalar, "dve": nc.vector, "pool": nc.gpsimd, "sp": nc.sync}
        self.sem = {}
        self.cnt = {}
        for e in self.ENG:
            self.sem[e] = self.es.enter_context(nc.semaphore("prog_" + e))
            self.cnt[e] = 0
        self.free_dsems = []
        for i in range(n_dma_sems):
            k = "d%d" % i
            self.sem[k] = self.es.enter_context(nc.semaphore("dma_" + k))
            self.cnt[k] = 0
            self.free_dsems.append(k)
        self.seen = {e: {} for e in self.ENG}
        self.same_engine_sync = same_engine_sync
        self.n_wait = 0
        self.n_inst = 0
        self.stage_bufs = []
        self.sem["cc"] = self.es.enter_context(nc.semaphore("cc_sem"))
        self.cnt["cc"] = 0

    def uniq(self, base):
        self._u = getattr(self, '_u', 0) + 1
        return '%s_%d' % (base, self._u)

    def sb(self, stack, name, shape, dt):
        b = Buf(stack.enter_context(self.nc.sbuf_tensor(name, list(shape), dt)), name)
        self.stage_bufs.append(b)
        return b

    def view(self, h, name):
        b = Buf(h, name)
        self.stage_bufs.append(b)
        return b

    def ps(self, stack, name, shape, dt=F32):
        return Buf(stack.enter_context(self.nc.psum_tensor(name, list(shape), dt)), name)

    def dram(self, name, shape, dt, kind="Internal"):
        return Buf(self.nc.dram_tensor(name, list(shape), dt, kind=kind).ap(), name)

    def release(self, bufs):
        for b in bufs:
            if b.dsem is not None:
                self.free_dsems.append(b.dsem)
                b.dsem = None

    def _wait(self, e, key, idx):
        if key == e and (e == "pe" or not self.same_engine_sync):
            return
        if self.seen[e].get(key, 0) >= idx:
            return
        self.seen[e][key] = idx
        self.eng[e].wait_ge(self.sem[key], idx)
        self.n_wait += 1

    def _deps(self, e, w, r):
        for b in r:
            for k, i in b.w.items():
                self._wait(e, k, i)
        for b in w:
            for k, i in b.w.items():
                self._wait(e, k, i)
            for k, i in b.r.items():
                self._wait(e, k, i)

    def op(self, e, fn, w=(), r=()):
        self._deps(e, w, r)
        inst = fn(self.eng[e])
        self.cnt[e] += 1
        idx = self.cnt[e]
        inst.then_inc(self.sem[e], 1)
        for b in w:
            b.w[e] = idx
        for b in r:
            b.r[e] = idx
        self.n_inst += 1
        return inst

    def dma(self, q, out_ap, in_ap, w, r, owner=None):
        self._deps(q, [w], [r])
        if owner is None:
            owner = w
        if owner.dsem is None:
            owner.dsem = self.free_dsems.pop()
        k = owner.dsem
        inst = self.eng[q].dma_start(out=out_ap, in_=in_ap)
        self.cnt[k] += 16
        inst.then_inc(self.sem[k], 16)
        w.w[k] = self.cnt[k]
        r.r[k] = self.cnt[k]
        self.n_inst += 1
        return inst

    def barrier(self):
        keys = [k for k in self.cnt if self.cnt[k] > 0]
        for e in self.ENG:
            for k in keys:
                if self.seen[e].get(k, 0) >= self.cnt[k]:
                    continue
                self.seen[e][k] = self.cnt[k]
                self.eng[e].wait_ge(self.sem[k], self.cnt[k])
                self.n_wait += 1

    def end_stage(self):
        self.barrier()
        self.release(self.stage_bufs)
        self.stage_bufs = []

    def collective(self, kind, src, dst, rg, src_ap=None, dst_ap=None):
        self._deps("pool", [dst], [src])
        src_ap = src.h if src_ap is None else src_ap
        dst_ap = dst.h if dst_ap is None else dst_ap
        inst = self.nc.gpsimd.collective_compute(kind, ALU.bypass, ins=[src_ap.opt()], outs=[dst_ap.opt()],
                                                 replica_groups=rg)
        self.cnt["cc"] += 1
        inst.then_inc(self.sem["cc"], 1)
        dst.w["cc"] = self.cnt["cc"]
        src.r["cc"] = self.cnt["cc"]
        self.n_inst += 1

    def gather_pairs(self, src, dstP, rows, PR):
        for p in range(rows // PR):
            self.collective("AllGather", src, dstP, [[0, 1], [2, 3], [4, 5], [6, 7]],
                            src_ap=src.h[p * PR:(p + 1) * PR, :], dst_ap=dstP.h[p * 2 * PR:(p + 1) * 2 * PR, :])

    def finish(self):
        self.barrier()
        self.es.close()


def bc(ap, shape):
    return ap.unsqueeze(len(ap.shape)).to_broadcast(list(shape))


class H:
    def __init__(self, P):
        self.P = P

    def mm(self, out, lhsT, rhs, w, r, start=True, stop=True):
        return self.P.op("pe", lambda e: e.matmul(out, lhsT, rhs, start=start, stop=stop), w=w, r=r)

    def tr(self, out, in_, ident, w, r):
        return self.P.op("pe", lambda e: e.transpose(out=out, in_=in_, identity=ident), w=w, r=r)

    def act(self, out, in_, func, w, r, scale=1.0, bias=None):
        if bias is None:
            return self.P.op("act", lambda e: e.activation(out=out, in_=in_, func=func, scale=scale), w=w, r=r)
        return self.P.op("act", lambda e: e.activation(out=out, in_=in_, func=func, scale=scale, bias=bias), w=w, r=r)

    def tt(self, eng, out, in0, in1, op, w, r):
        return self.P.op(eng, lambda e: e.tensor_tensor(out=out, in0=in0, in1=in1, op=op), w=w, r=r)

    def ts(self, eng, out, in0, s1, op0, w, r, s2=None, op1=None):
        if op1 is None:
            return self.P.op(eng, lambda e: e.tensor_scalar(out=out, in0=in0, scalar1=s1, scalar2=None, op0=op0), w=w, r=r)
        return self.P.op(eng, lambda e: e.tensor_scalar(out=out, in0=in0, scalar1=s1, scalar2=s2, op0=op0, op1=op1), w=w, r=r)

    def stt(self, out, in0, scalar, in1, op0, op1, w, r):
        return self.P.op("dve", lambda e: e.scalar_tensor_tensor(out=out, in0=in0, scalar=scalar, in1=in1, op0=op0, op1=op1), w=w, r=r)

    def cp(self, eng, out, in_, w, r):
        if eng == "act":
            return self.P.op("act", lambda e: e.activation(out=out, in_=in_, func=AF.Copy), w=w, r=r)
        return self.P.op(eng, lambda e: e.tensor_copy(out=out, in_=in_), w=w, r=r)

    def memset(self, eng, out, val, w):
        return self.P.op(eng, lambda e: e.memset(out, val), w=w)

import math
D = 2048
RMS_EPS = 1e-6

FF = 5632
LN_EPS = 1e-5


def load_consts(P, st, ident_d):
    idt = P.sb(st, P.uniq("identsb"), [128, 128], F32)
    P.dma("sp", idt.h[:], ident_d.h[:, :], w=idt, r=ident_d)
    epsT = P.sb(st, P.uniq("epsT"), [128, 1], F32)
    P.op("dve", lambda e: e.memset(epsT.h[:], LN_EPS), w=[epsT])
    return idt, epsT


def layer_norm_tile(P, z, Gt, Bt, epsT, tmp):
    stats, mv, rstd, nmr = tmp["stats"], tmp["mv"], tmp["rstd"], tmp["nmr"]
    for c in range(4):
        P.op("dve", lambda e: e.bn_stats(out=stats.h[:, c * 6:(c + 1) * 6], in_=z.h[:, c * 512:(c + 1) * 512]),
             w=[stats], r=[z])
    P.op("dve", lambda e: e.bn_aggr(out=mv.h[:], in_=stats.h[:]), w=[mv], r=[stats])
    P.op("act", lambda e: e.activation(out=rstd.h[:], in_=mv.h[:, 1:2], func=AF.Sqrt, bias=epsT.h[:, 0:1], scale=1.0),
         w=[rstd], r=[mv, epsT])
    P.op("dve", lambda e: e.reciprocal(out=rstd.h[:], in_=rstd.h[:]), w=[rstd], r=[rstd])
    P.op("dve", lambda e: e.tensor_scalar(out=nmr.h[:], in0=mv.h[:, 0:1], scalar1=rstd.h[:, 0:1], scalar2=-1.0,
                                          op0=ALU.mult, op1=ALU.mult), w=[nmr], r=[mv, rstd])
    P.op("act", lambda e: e.activation(out=z.h[:], in_=z.h[:], func=AF.Identity, scale=rstd.h[:, 0:1],
                                       bias=nmr.h[:, 0:1]), w=[z], r=[z, rstd, nmr])
    P.op("dve", lambda e: e.tensor_tensor(out=z.h[:], in0=z.h[:], in1=Gt.h[:], op=ALU.mult), w=[z], r=[z, Gt])
    P.op("dve", lambda e: e.tensor_tensor(out=z.h[:], in0=z.h[:], in1=Bt.h[:], op=ALU.add), w=[z], r=[z, Bt])


def ln_tmp(P, st, tag):
    return {"stats": P.sb(st, tag + "stats", [128, 24], F32), "mv": P.sb(st, tag + "mv", [128, 2], F32),
            "rstd": P.sb(st, tag + "rstd", [128, 1], F32), "nmr": P.sb(st, tag + "nmr", [128, 1], F32)}


def ffn_stage(P, x_in, x_out, w_up, w_down, ln_g, ln_b, ident_d, T, alpha, tag="f"):
    NB = T // 512
    with contextlib.ExitStack() as st:
        idt, epsT = load_consts(P, st, ident_d)
        Gt = P.sb(st, tag + "G", [128, D], F32)
        Bt = P.sb(st, tag + "B", [128, D], F32)
        P.dma("sp", Gt.h[:], ln_g.h.partition_broadcast(128), w=Gt, r=ln_g)
        P.dma("sp", Bt.h[:], ln_b.h.partition_broadcast(128), w=Bt, r=ln_b)
        xs = [P.sb(st, tag + "xs%d" % i, [128, D], F32) for i in range(2)]
        xT = P.sb(st, tag + "xT", [128, 16, 512], BF16)
        hT = P.sb(st, tag + "hT", [128, 44, 512], BF16)
        wg = [P.sb(st, tag + "wg%d" % i, [128, 16, 256], BF16) for i in range(3)]
        wu = [P.sb(st, tag + "wu%d" % i, [128, 16, 256], BF16) for i in range(3)]
        wd = [P.sb(st, tag + "wd%d" % i, [128, 11, 512], BF16) for i in range(2)]
        sg = [P.sb(st, tag + "sg%d" % i, [128, 512], F32) for i in range(2)]
        z = [P.sb(st, tag + "z%d" % i, [128, D], F32) for i in range(4)]
        tmps = [ln_tmp(P, st, tag + "t%d" % i) for i in range(2)]
        bank = [P.ps(st, tag + "bank%d" % i, [128, 512]) for i in range(8)]
        allb = [idt, epsT, Gt, Bt, xT, hT] + xs + wg + wu + wd + sg + z
        for t in tmps:
            allb += list(t.values())

        nwl = 0
        nwd = 0
        for blk in range(NB):
            t0 = blk * 512
            for s in range(4):
                xb = xs[s % 2]
                P.dma("sp", xb.h[:], x_in.h[t0 + s * 128:t0 + (s + 1) * 128, :], w=xb, r=x_in)
                for q in range(4):
                    bk = bank[4 + (s * 4 + q) % 4]
                    for i in range(4):
                        dk = q * 4 + i
                        P.op("pe", lambda e: e.transpose(out=bk.h[:, i * 128:(i + 1) * 128],
                                                         in_=xb.h[:, dk * 128:(dk + 1) * 128], identity=idt.h[:]),
                             w=[bk], r=[xb, idt])
                    eng = "act" if q % 2 == 0 else "dve"
                    src = bk.h[:].rearrange("p (a b) -> p a b", a=4)
                    dst = xT.h[:, q * 4:(q + 1) * 4, s * 128:(s + 1) * 128]
                    if eng == "act":
                        P.op("act", lambda e: e.activation(out=dst, in_=src, func=AF.Copy), w=[xT], r=[bk])
                    else:
                        P.op("dve", lambda e: e.tensor_copy(out=dst, in_=src), w=[xT], r=[bk])
            for g in range(22):
                sl = nwl % 3
                nwl += 1
                for k0 in range(0, 16, 4):
                    P.dma("pool", wg[sl].h[:, k0:k0 + 4, :],
                          w_up.h[k0 * 128:(k0 + 4) * 128, g * 256:(g + 1) * 256].rearrange("(k p) f -> p k f", p=128),
                          w=wg[sl], r=w_up)
                    P.dma("pool", wu[sl].h[:, k0:k0 + 4, :],
                          w_up.h[k0 * 128:(k0 + 4) * 128, FF + g * 256:FF + (g + 1) * 256].rearrange(
                              "(k p) f -> p k f", p=128),
                          w=wu[sl], r=w_up)
                for c in range(2):
                    j = g * 2 + c
                    pg = bank[(j % 2) * 2]
                    pu = bank[(j % 2) * 2 + 1]
                    for dk in range(16):
                        P.op("pe", lambda e: e.matmul(pg.h[:], wg[sl].h[:, dk, c * 128:(c + 1) * 128], xT.h[:, dk, :],
                                                      start=(dk == 0), stop=(dk == 15)), w=[pg], r=[wg[sl], xT])
                    for dk in range(16):
                        P.op("pe", lambda e: e.matmul(pu.h[:], wu[sl].h[:, dk, c * 128:(c + 1) * 128], xT.h[:, dk, :],
                                                      start=(dk == 0), stop=(dk == 15)), w=[pu], r=[wu[sl], xT])
                    sgb = sg[j % 2]
                    P.op("act", lambda e: e.activation(out=sgb.h[:], in_=pg.h[:], func=AF.Silu), w=[sgb], r=[pg])
                    P.op("dve", lambda e: e.tensor_tensor(out=hT.h[:, j, :], in0=sgb.h[:], in1=pu.h[:], op=ALU.mult),
                         w=[hT], r=[sgb, pu])
            for s in range(4):
                P.dma("sp", z[s].h[:], x_in.h[t0 + s * 128:t0 + (s + 1) * 128, :], w=z[s], r=x_in)
                P.op("act", lambda e: e.activation(out=z[s].h[:], in_=z[s].h[:], func=AF.Copy, scale=float(alpha)),
                     w=[z[s]], r=[z[s]])
            for dn in range(4):
                pb = [bank[(dn % 2) * 4 + s] for s in range(4)]
                for fg in range(4):
                    sl = nwd % 2
                    nwd += 1
                    for (a, n) in ((0, 4), (4, 4), (8, 3)):
                        r0 = (fg * 11 + a) * 128
                        P.dma("pool", wd[sl].h[:, a:a + n, :],
                              w_down.h[r0:r0 + n * 128, dn * 512:(dn + 1) * 512].rearrange("(k p) f -> p k f", p=128),
                              w=wd[sl], r=w_down)
                    for i in range(11):
                        fk = fg * 11 + i
                        for s in range(4):
                            P.op("pe", lambda e: e.matmul(pb[s].h[:], hT.h[:, fk, s * 128:(s + 1) * 128],
                                                          wd[sl].h[:, i, :], start=(fk == 0), stop=(fk == 43)),
                                 w=[pb[s]], r=[hT, wd[sl]])
                for s in range(4):
                    zc = z[s].h[:, dn * 512:(dn + 1) * 512]
                    P.op("dve", lambda e: e.scalar_tensor_tensor(out=zc, in0=pb[s].h[:], scalar=0.5, in1=zc,
                                                                 op0=ALU.mult, op1=ALU.add), w=[z[s]], r=[pb[s], z[s]])
            for s in range(4):
                layer_norm_tile(P, z[s], Gt, Bt, epsT, tmps[s % 2])
                P.dma("sp", x_out.h[t0 + s * 128:t0 + (s + 1) * 128, :], z[s].h[:], w=x_out, r=z[s], owner=z[s])
        P.end_stage()


NEGV = -30000.0


def gdn_consts_np():
    t = np.arange(128)
    same = (t[:, None] // 64) == (t[None, :] // 64)
    c = {}
    c["ident"] = np.eye(128, dtype=np.float32)
    c["ucs"] = (same & (t[:, None] <= t[None, :])).astype(np.float32)
    c["vsame"] = same.astype(np.float32)
    cind = np.zeros((128, 2, 128), np.float32)
    cind[:64, 0, :] = 1.0
    cind[64:, 1, :] = 1.0
    c["cind"] = cind.reshape(128, 256)
    c["negA"] = np.where(same & (t[None, :] >= t[:, None]), 0.0, NEGV).astype(np.float32)
    c["negL"] = np.where(same & (t[None, :] > t[:, None]), 0.0, NEGV).astype(np.float32)
    sel = np.zeros((32, 32, 128), np.float32)
    for h in range(32):
        sel[h, h, :] = 1.0
    c["sel"] = sel.reshape(32, 32 * 128)
    return c


def gdn_proj_stage(P, x_in, QKV, Zd, SSQd, BAd, w_c, conv_c, C, T, tag="gp", rm=lambda t: t):
    Hh = H(P)
    SBK = 512
    NSB = T // SBK
    with contextlib.ExitStack() as st:
        sbuf = lambda n, s, d=F32: P.sb(st, tag + n, s, d)
        idt = sbuf("idt", [128, 128])
        cw = sbuf("cw", [128, 128])
        ones = sbuf("ones", [128, 1])
        P.dma("sp", idt.h[:], C["ident"].h[:, :], w=idt, r=C["ident"])
        P.dma("sp", cw.h[:], conv_c.h[:, :], w=cw, r=conv_c)
        Hh.memset("dve", ones.h[:], 1.0, [ones])
        xs = [sbuf("xs%d" % i, [128, D]) for i in range(2)]
        xT = sbuf("xT", [128, 16, SBK], BF16)
        wf = [sbuf("wf%d" % i, [128, 16, 256], BF16) for i in range(3)]
        wba = sbuf("wba", [128, 16, 32], BF16)
        pre = [sbuf("pre%d" % i, [128, SBK + 3]) for i in range(2)]
        acc = [sbuf("acc%d" % i, [128, SBK]) for i in range(2)]
        sgl = [sbuf("sgl%d" % i, [128, SBK]) for i in range(2)]
        sq = [sbuf("sq%d" % i, [128, SBK]) for i in range(2)]
        ob = [sbuf("ob%d" % i, [128, SBK], BF16) for i in range(2)]
        carry = sbuf("carry", [128, 32, 3])
        baT = sbuf("baT", [32, SBK])
        zs4 = sbuf("zs4", [128, 4, 2048], BF16)
        ssq_sb = sbuf("ssqsb", [128, 64])
        Hh.memset("pool", carry.h[:], 0.0, [carry])
        banks = [P.ps(st, tag + "bank%d" % i, [128, 512]) for i in range(7)]
        smallbank = st.enter_context(P.nc.psum_tensor(tag + "smallbank", [128, 512], F32))
        ps_ssq = P.view(smallbank[:, 0:64], "ps_ssq")
        bctr = [0]

        def nb():
            b = banks[bctr[0] % 7]
            bctr[0] += 1
            return b

        nwf = 0
        for sb in range(NSB):
            t0 = sb * SBK
            for s in range(4):
                xb = xs[s % 2]
                P.dma("sp", xb.h[:], x_in.h[rm(t0 + s * 128):rm(t0 + s * 128) + 128, :], w=xb, r=x_in)
                for q in range(4):
                    bk = nb()
                    for i in range(4):
                        dk = q * 4 + i
                        Hh.tr(bk.h[:, i * 128:(i + 1) * 128], xb.h[:, dk * 128:(dk + 1) * 128], idt.h[:], [bk], [xb, idt])
                    Hh.cp("act" if q % 2 == 0 else "dve", xT.h[:, q * 4:(q + 1) * 4, s * 128:(s + 1) * 128],
                          bk.h[:].rearrange("p (a b) -> p a b", a=4), [xT], [bk])
            for fp in range(16):
                sl = nwf % 3
                nwf += 1
                for k0 in range(0, 16, 4):
                    P.dma("pool", wf[sl].h[:, k0:k0 + 4, :],
                          w_c.h[k0 * 128:(k0 + 4) * 128, fp * 256:(fp + 1) * 256].rearrange("(k p) f -> p k f", p=128),
                          w=wf[sl], r=w_c)
                for c in range(2):
                    fc = fp * 2 + c
                    bk = nb()
                    for dk in range(16):
                        Hh.mm(bk.h[:], wf[sl].h[:, dk, c * 128:(c + 1) * 128], xT.h[:, dk, :], [bk], [wf[sl], xT],
                              start=(dk == 0), stop=(dk == 15))
                    pr = pre[fc % 2]
                    ac = acc[fc % 2]
                    obb = ob[fc % 2]
                    Hh.cp("pool", pr.h[:, 0:3], carry.h[:, fc, :], [pr], [carry])
                    Hh.cp("act", pr.h[:, 3:SBK + 3], bk.h[:], [pr], [bk])
                    Hh.cp("pool", carry.h[:, fc, :], pr.h[:, SBK:SBK + 3], [carry], [pr])
                    Hh.ts("dve", ac.h[:], pr.h[:, 0:SBK], cw.h[:, fc * 4:fc * 4 + 1], ALU.mult, [ac], [pr, cw])
                    for j in range(1, 4):
                        Hh.stt(ac.h[:], pr.h[:, j:j + SBK], cw.h[:, fc * 4 + j:fc * 4 + j + 1], ac.h[:], ALU.mult, ALU.add,
                               [ac], [pr, cw, ac])
                    if fc < 16:
                        sg_ = sgl[fc % 2]
                        sq_ = sq[fc % 2]
                        Hh.act(sg_.h[:], ac.h[:], AF.Silu, [sg_], [ac])
                        Hh.cp("pool", obb.h[:], sg_.h[:], [obb], [sg_])
                        Hh.act(sq_.h[:], sg_.h[:], AF.Square, [sq_], [sg_])
                        for s in range(4):
                            Hh.mm(ps_ssq.h[:, s * 16 + fc:s * 16 + fc + 1], sq_.h[:, s * 128:(s + 1) * 128], ones.h[:, 0:1],
                                  [ps_ssq], [sq_, ones])
                    else:
                        Hh.act(obb.h[:], ac.h[:], AF.Silu, [obb], [ac])
                    P.dma("sp", QKV.h[fc * 128:(fc + 1) * 128, t0:t0 + SBK], obb.h[:], w=QKV, r=obb, owner=obb)
            for k0 in range(0, 16, 4):
                P.dma("pool", wba.h[:, k0:k0 + 4, :],
                      w_c.h[k0 * 128:(k0 + 4) * 128, 6144:6176].rearrange("(k p) f -> p k f", p=128), w=wba, r=w_c)
            bk = nb()
            for dk in range(16):
                Hh.mm(bk.h[0:32, :], wba.h[:, dk, :], xT.h[:, dk, :], [bk], [wba, xT], start=(dk == 0), stop=(dk == 15))
            Hh.cp("act", baT.h[:], bk.h[0:32, :], [baT], [bk])
            P.dma("sp", BAd.h[:, t0:t0 + SBK], baT.h[:], w=BAd, r=baT, owner=baT)
            Hh.cp("dve", ssq_sb.h[:], ps_ssq.h[:], [ssq_sb], [ps_ssq])
            for s in range(4):
                P.dma("sp", SSQd.h[t0 + s * 128:t0 + (s + 1) * 128, :], ssq_sb.h[:, s * 16:(s + 1) * 16], w=SSQd, r=ssq_sb,
                      owner=ssq_sb)
            for zg in range(8):
                sl = nwf % 3
                nwf += 1
                for k0 in range(0, 16, 4):
                    P.dma("pool", wf[sl].h[:, k0:k0 + 4, :],
                          w_c.h[k0 * 128:(k0 + 4) * 128, 4096 + zg * 256:4096 + (zg + 1) * 256].rearrange(
                              "(k p) f -> p k f", p=128), w=wf[sl], r=w_c)
                for s in range(4):
                    bk = nb()
                    for dk in range(16):
                        Hh.mm(bk.h[:, 0:256], xT.h[:, dk, s * 128:(s + 1) * 128], wf[sl].h[:, dk, :], [bk], [wf[sl], xT],
                              start=(dk == 0), stop=(dk == 15))
                    Hh.act(zs4.h[:, s, zg * 256:(zg + 1) * 256], bk.h[:, 0:256], AF.Silu, [zs4], [bk])
            for s in range(4):
                P.dma("sp", Zd.h[t0 + s * 128:t0 + (s + 1) * 128, :], zs4.h[:, s, :], w=Zd, r=zs4, owner=zs4)
        P.end_stage()


def gdn_rec_stage(P, QKV, Zd, SSQd, BAd, og_out, alog_c, dtb_c, normw, C, T, tag="g"):
    Hh = H(P)
    SBK = 512
    NSB = T // SBK
    with contextlib.ExitStack() as st:
        sbuf = lambda n, s, d=F32: P.sb(st, tag + n, s, d)
        idt = sbuf("idt", [128, 128])
        idb = sbuf("idb", [128, 128], BF16)
        ucs = sbuf("ucs", [128, 128])
        vsame = sbuf("vsame", [128, 128])
        cind = sbuf("cind", [128, 256])
        negA = sbuf("negA", [128, 128])
        negL = sbuf("negL", [128, 128])
        sel = sbuf("sel", [32, 32 * 128])
        DTB = sbuf("DTB", [128, 16])
        NEGA = sbuf("NEGA", [128, 16])
        nw1 = sbuf("nw1", [128, 128])
        for (dst, src) in ((idt, C["ident"]), (ucs, C["ucs"]), (vsame, C["vsame"]), (cind, C["cind"]),
                           (negA, C["negA"]), (negL, C["negL"]), (sel, C["sel"])):
            P.dma("sp", dst.h[:], src.h[:, :], w=dst, r=src)
        P.dma("sp", DTB.h[:], dtb_c.h.partition_broadcast(128), w=DTB, r=dtb_c)
        P.dma("sp", NEGA.h[:], alog_c.h.partition_broadcast(128), w=NEGA, r=alog_c)
        P.dma("sp", nw1.h[:], normw.h.partition_broadcast(128), w=nw1, r=normw)
        Hh.cp("dve", idb.h[:], idt.h[:], [idb], [idt])
        Hh.act(NEGA.h[:], NEGA.h[:], AF.Exp, [NEGA], [NEGA])
        Hh.ts("dve", NEGA.h[:], NEGA.h[:], -1.0, ALU.mult, [NEGA], [NEGA])

        S = sbuf("S", [128, 16, 128])
        Sb = sbuf("Sb", [128, 16, 128], BF16)
        sm = {n: sbuf("sm_" + n, [128, 16]) for n in
              ("eb", "lb", "beta", "t1", "e1", "sp", "g", "gcs", "d", "kd", "a", "c1", "c2", "sa", "rq16", "lnrk16",
               "colb", "ssqo", "rms")}
        ba = sbuf("ba", [128, 32])
        lnr = sbuf("lnr", [128, 16])
        rqk = sbuf("rqk", [128, 16])
        R = sbuf("R", [128, 32])
        RT = sbuf("RT", [32, 128])
        EG = sbuf("EG", [128, 32])
        kba = sbuf("kba", [128, 16, 128])
        kdec = sbuf("kdec", [128, 16, 128], BF16)
        vb = sbuf("vb", [128, 16, 128])
        attnT = sbuf("attnT", [128, 16, 128], BF16)
        slot = []
        for i in range(2):
            slot.append({n: sbuf("s%d_%s" % (i, n), [128, 4, 128]) for n in
                         ("EL", "EA", "X0", "X1", "Y0", "Y1", "P0", "P1")})
        u = sbuf("u", [128, 16, 128])
        wT = sbuf("wT", [128, 16, 128], BF16)
        vnew = sbuf("vnew", [128, 16, 128], BF16)
        o = sbuf("o", [128, 16, 128])
        obf = sbuf("obf", [128, 16, 128], BF16)
        tmpS = [sbuf("tmpS%d" % i, [128, 4, 128]) for i in range(2)]
        epsr = sbuf("epsr", [128, 1])
        Hh.memset("dve", epsr.h[:], RMS_EPS, [epsr])
        Hh.memset("dve", S.h[:], 0.0, [S])
        Hh.memset("pool", Sb.h[:], 0.0, [Sb])

        banks = [P.ps(st, tag + "bank%d" % i, [128, 512]) for i in range(7)]
        smallbank = st.enter_context(P.nc.psum_tensor(tag + "smallbank", [128, 512], F32))
        ps_ba = P.view(smallbank[:, 32:64], "ps_ba")
        ps_g = P.view(smallbank[:, 64:128], "ps_g")
        ps_rt = P.view(smallbank[:, 128:256], "ps_rt")
        bctr = [0]

        def nb():
            b = banks[bctr[0] % 7]
            bctr[0] += 1
            return b


        qbuf = [sbuf("qkvT%d" % i, [128, 32, SBK], BF16) for i in range(2)]
        baT = sbuf("baT", [32, SBK])
        zsb = [sbuf("zsb%d" % i, [128, 2048], BF16) for i in range(2)]
        ssqb = [sbuf("ssqb%d" % i, [128, 16]) for i in range(2)]
        for sb in range(NSB):
            t0 = sb * SBK
            qkvT = qbuf[sb % 2]
            for c0 in range(0, 32, 4):
                P.dma("sp", qkvT.h[:, c0:c0 + 4, :],
                      QKV.h[c0 * 128:(c0 + 4) * 128, t0:t0 + SBK].rearrange("(c p) t -> p c t", p=128), w=qkvT, r=QKV)
            P.dma("sp", baT.h[:], BAd.h[:, t0:t0 + SBK], w=baT, r=BAd)
            for s in range(4):
                tb = t0 + s * 128
                zs_t = zsb[s % 2]
                ssq_t = ssqb[s % 2]
                P.dma("sp", zs_t.h[:], Zd.h[tb:tb + 128, :], w=zs_t, r=Zd)
                P.dma("sp", ssq_t.h[:], SSQd.h[tb:tb + 128, :], w=ssq_t, r=SSQd)
                cs = slice(s * 128, (s + 1) * 128)
                Hh.tr(ps_ba.h[:, 0:32], baT.h[:, cs], idt.h[0:32, 0:32], [ps_ba], [baT, idt])
                Hh.cp("dve", ba.h[:], ps_ba.h[:, 0:32], [ba], [ps_ba])
                m = sm
                Hh.act(m["eb"].h[:], ba.h[:, 0:16], AF.Exp, [m["eb"]], [ba], scale=-1.0)
                Hh.act(m["lb"].h[:], m["eb"].h[:], AF.Ln, [m["lb"]], [m["eb"]], bias=1.0)
                Hh.act(m["beta"].h[:], m["lb"].h[:], AF.Exp, [m["beta"]], [m["lb"]], scale=-1.0)
                Hh.tt("dve", m["t1"].h[:], ba.h[:, 16:32], DTB.h[:], ALU.add, [m["t1"]], [ba, DTB])
                Hh.act(m["e1"].h[:], m["t1"].h[:], AF.Exp, [m["e1"]], [m["t1"]])
                Hh.act(m["sp"].h[:], m["e1"].h[:], AF.Ln, [m["sp"]], [m["e1"]], bias=1.0)
                Hh.tt("dve", m["g"].h[:], m["sp"].h[:], NEGA.h[:], ALU.mult, [m["g"]], [m["sp"], NEGA])
                Hh.mm(ps_g.h[:, 0:16], ucs.h[:], m["g"].h[:], [ps_g], [ucs, m["g"]])
                Hh.mm(ps_g.h[:, 16:32], vsame.h[:], m["g"].h[:], [ps_g], [vsame, m["g"]])
                Hh.mm(ps_g.h[:, 32:48], cind.h[:, 0:128], m["g"].h[:], [ps_g], [cind, m["g"]])
                Hh.mm(ps_g.h[:, 48:64], cind.h[:, 128:256], m["g"].h[:], [ps_g], [cind, m["g"]])
                Hh.cp("dve", m["gcs"].h[:], ps_g.h[:, 0:16], [m["gcs"]], [ps_g])
                Hh.tt("dve", m["d"].h[:], ps_g.h[:, 16:32], m["gcs"].h[:], ALU.subtract, [m["d"]], [ps_g, m["gcs"]])
                Hh.act(m["kd"].h[:], m["d"].h[:], AF.Exp, [m["kd"]], [m["d"]])
                Hh.act(m["a"].h[:], m["gcs"].h[:], AF.Exp, [m["a"]], [m["gcs"]])
                Hh.act(EG.h[:], ps_g.h[:, 32:64], AF.Exp, [EG], [ps_g])
                Hh.act(lnr.h[:], ssq_t.h[:], AF.Ln, [lnr], [ssq_t, epsr], bias=epsr.h[:, 0:1])
                Hh.ts("dve", lnr.h[:], lnr.h[:], -0.5, ALU.mult, [lnr], [lnr])
                Hh.act(rqk.h[:, 0:8], lnr.h[:, 0:8], AF.Exp, [rqk], [lnr], bias=-0.5 * math.log(128.0))
                Hh.act(rqk.h[:, 8:16], lnr.h[:, 8:16], AF.Exp, [rqk], [lnr])
                v2 = lambda ap: ap.rearrange("p (a b) -> p a b", b=2)
                Hh.cp("dve", v2(m["rq16"].h[:]), bc(rqk.h[:, 0:8], [128, 8, 2]), [m["rq16"]], [rqk])
                Hh.cp("dve", v2(m["lnrk16"].h[:]), bc(lnr.h[:, 8:16], [128, 8, 2]), [m["lnrk16"]], [lnr])
                Hh.tt("dve", m["c1"].h[:], m["beta"].h[:], m["a"].h[:], ALU.mult, [m["c1"]], [m["beta"], m["a"]])
                Hh.tt("dve", v2(m["c1"].h[:]), v2(m["c1"].h[:]), bc(rqk.h[:, 8:16], [128, 8, 2]), ALU.mult, [m["c1"]],
                      [m["c1"], rqk])
                Hh.tt("dve", v2(m["c2"].h[:]), v2(m["kd"].h[:]), bc(rqk.h[:, 8:16], [128, 8, 2]), ALU.mult, [m["c2"]],
                      [m["kd"], rqk])
                Hh.tt("dve", m["sa"].h[:], m["rq16"].h[:], m["a"].h[:], ALU.mult, [m["sa"]], [m["rq16"], m["a"]])
                Hh.cp("dve", R.h[:, 0:16], m["gcs"].h[:], [R], [m["gcs"]])
                Hh.tt("dve", R.h[:, 16:32], m["gcs"].h[:], m["lb"].h[:], ALU.subtract, [R], [m["gcs"], m["lb"]])
                Hh.tt("dve", R.h[:, 16:32], R.h[:, 16:32], m["lnrk16"].h[:], ALU.add, [R], [R, m["lnrk16"]])
                Hh.tt("dve", m["colb"].h[:], m["lnrk16"].h[:], m["gcs"].h[:], ALU.subtract, [m["colb"]],
                      [m["lnrk16"], m["gcs"]])
                Hh.tr(ps_rt.h[0:32, :], R.h[:], idt.h[:], [ps_rt], [R, idt])
                Hh.cp("act", RT.h[:], ps_rt.h[0:32, :], [RT], [ps_rt])
                bk = nb()
                kv = bk.h[:].bitcast(BF16).rearrange("p (a b) -> p a b", a=8)
                for hk in range(8):
                    Hh.tr(kv[:, hk, :], qkvT.h[:, 8 + hk, cs], idb.h[:], [bk], [qkvT, idb])
                kvb = bc3 = kv.unsqueeze(2).to_broadcast([128, 8, 2, 128])
                v4 = lambda ap: ap.rearrange("p (a b) c -> p a b c", b=2)
                Hh.tt("dve", v4(kba.h[:]), kvb, v4(bc(m["c1"].h[:], [128, 16, 128])), ALU.mult, [kba], [bk, m["c1"]])
                Hh.tt("dve", v4(kdec.h[:]), kvb, v4(bc(m["c2"].h[:], [128, 16, 128])), ALU.mult, [kdec], [bk, m["c2"]])
                for half in range(2):
                    bk = nb()
                    vv = bk.h[:].bitcast(BF16).rearrange("p (a b) -> p a b", a=8)
                    for i in range(8):
                        h = half * 8 + i
                        Hh.tr(vv[:, i, :], qkvT.h[:, 16 + h, cs], idb.h[:], [bk], [qkvT, idb])
                    Hh.tt("dve", vb.h[:, half * 8:(half + 1) * 8, :], vv,
                          bc(m["beta"].h[:, half * 8:(half + 1) * 8], [128, 8, 128]), ALU.mult, [vb], [bk, m["beta"]])
                for gp in range(2):
                    grp = [gp * 2, gp * 2 + 1]
                    sd = {}
                    for gi, mgrp in enumerate(grp):
                        sl_ = slot[gi]
                        sd[mgrp] = sl_
                        bkq = nb()
                        for i in range(2):
                            hk = mgrp * 2 + i
                            Hh.mm(bkq.h[:, i * 128:(i + 1) * 128], qkvT.h[:, 8 + hk, cs], qkvT.h[:, 8 + hk, cs], [bkq], [qkvT])
                            Hh.mm(bkq.h[:, 256 + i * 128:256 + (i + 1) * 128], qkvT.h[:, 8 + hk, cs], qkvT.h[:, hk, cs],
                                  [bkq], [qkvT])
                        bl = nb()
                        ba_ = nb()
                        for i in range(4):
                            h = mgrp * 4 + i
                            Hh.mm(bl.h[:, i * 128:(i + 1) * 128], idt.h[:], negL.h[:], [bl], [idt, negL], start=True, stop=False)
                            Hh.mm(bl.h[:, i * 128:(i + 1) * 128], sel.h[:, (16 + h) * 128:(17 + h) * 128], RT.h[:], [bl],
                                  [sel, RT], start=False, stop=True)
                            Hh.mm(ba_.h[:, i * 128:(i + 1) * 128], idt.h[:], negA.h[:], [ba_], [idt, negA], start=True,
                                  stop=False)
                            Hh.mm(ba_.h[:, i * 128:(i + 1) * 128], sel.h[:, h * 128:(h + 1) * 128], RT.h[:], [ba_], [sel, RT],
                                  start=False, stop=True)
                        for i in range(4):
                            h = mgrp * 4 + i
                            Hh.act(sl_["EL"].h[:, i, :], bl.h[:, i * 128:(i + 1) * 128], AF.Exp, [sl_["EL"]], [bl, m["colb"]],
                                   bias=m["colb"].h[:, h:h + 1])
                            Hh.act(sl_["EA"].h[:, i, :], ba_.h[:, i * 128:(i + 1) * 128], AF.Exp, [sl_["EA"]], [ba_, m["colb"]],
                                   bias=m["colb"].h[:, h:h + 1])
                        kk = bkq.h[:, 0:256].rearrange("p (a c) -> p a c", a=2).unsqueeze(2).to_broadcast([128, 2, 2, 128])
                        kq = bkq.h[:, 256:512].rearrange("p (a c) -> p a c", a=2).unsqueeze(2).to_broadcast([128, 2, 2, 128])
                        Hh.tt("dve", v4(sl_["X0"].h[:]), kk, v4(sl_["EL"].h[:]), ALU.mult, [sl_["X0"]], [bkq, sl_["EL"]])
                        Hh.tt("dve", v4(attnT.h[:, mgrp * 4:(mgrp + 1) * 4, :]), kq, v4(sl_["EA"].h[:]), ALU.mult, [attnT],
                              [bkq, sl_["EA"]])
                        bt_ = nb()
                        for i in range(4):
                            Hh.tr(bt_.h[:, i * 128:(i + 1) * 128], sl_["X0"].h[:, i, :], idt.h[:], [bt_], [sl_["X0"], idt])
                        Hh.cp("act", sl_["Y0"].h[:], bt_.h[:].rearrange("p (a b) -> p a b", a=4), [sl_["Y0"]], [bt_])
                        Hh.tt("dve", sl_["P0"].h[:], idt.h[:].unsqueeze(1).to_broadcast([128, 4, 128]), sl_["X0"].h[:],
                              ALU.subtract, [sl_["P0"]], [idt, sl_["X0"]])
                    for lev in range(5):
                        a_, b_ = lev % 2, (lev + 1) % 2
                        bx, by, bp = {}, {}, {}
                        for mgrp in grp:
                            sl_ = sd[mgrp]
                            X, Y = sl_["X%d" % a_], sl_["Y%d" % a_]
                            by[mgrp] = nb()
                            for i in range(4):
                                Hh.mm(by[mgrp].h[:, i * 128:(i + 1) * 128], X.h[:, i, :], Y.h[:, i, :], [by[mgrp]], [X, Y])
                            if lev < 4:
                                bx[mgrp] = nb()
                                for i in range(4):
                                    Hh.mm(bx[mgrp].h[:, i * 128:(i + 1) * 128], Y.h[:, i, :], X.h[:, i, :], [bx[mgrp]], [X, Y])
                        for mgrp in grp:
                            sl_ = sd[mgrp]
                            Hh.cp("dve", sl_["Y%d" % b_].h[:], by[mgrp].h[:].rearrange("p (a b) -> p a b", a=4),
                                  [sl_["Y%d" % b_]], [by[mgrp]])
                            if lev < 4:
                                Hh.cp("act", sl_["X%d" % b_].h[:], bx[mgrp].h[:].rearrange("p (a b) -> p a b", a=4),
                                      [sl_["X%d" % b_]], [bx[mgrp]])
                        for mgrp in grp:
                            sl_ = sd[mgrp]
                            Yn, Pc = sl_["Y%d" % b_], sl_["P%d" % a_]
                            bp[mgrp] = nb()
                            for i in range(4):
                                Hh.mm(bp[mgrp].h[:, i * 128:(i + 1) * 128], Yn.h[:, i, :], Pc.h[:, i, :], [bp[mgrp]], [Yn, Pc])
                        for mgrp in grp:
                            sl_ = sd[mgrp]
                            Hh.tt("dve", sl_["P%d" % b_].h[:], sl_["P%d" % a_].h[:],
                                  bp[mgrp].h[:].rearrange("p (a b) -> p a b", a=4), ALU.add, [sl_["P%d" % b_]],
                                  [sl_["P%d" % a_], bp[mgrp]])
                    for mgrp in grp:
                        AT = sd[mgrp]["P1"]
                        bu = nb()
                        bw = nb()
                        for i in range(4):
                            h = mgrp * 4 + i
                            Hh.mm(bu.h[:, i * 128:(i + 1) * 128], AT.h[:, i, :], vb.h[:, h, :], [bu], [AT, vb])
                            Hh.mm(bw.h[:, i * 128:(i + 1) * 128], kba.h[:, h, :], AT.h[:, i, :], [bw], [AT, kba])
                        Hh.cp("act", u.h[:, mgrp * 4:(mgrp + 1) * 4, :], bu.h[:].rearrange("p (a b) -> p a b", a=4), [u], [bu])
                        Hh.cp("dve", wT.h[:, mgrp * 4:(mgrp + 1) * 4, :], bw.h[:].rearrange("p (a b) -> p a b", a=4), [wT], [bw])
                for c in range(2):
                    rs = slice(c * 64, (c + 1) * 64)
                    for mgrp in range(4):
                        hs = slice(mgrp * 4, (mgrp + 1) * 4)
                        bws = nb()
                        bo1 = nb()
                        for i in range(4):
                            h = mgrp * 4 + i
                            Hh.mm(bws.h[:, i * 128:(i + 1) * 128], wT.h[:, h, :], Sb.h[:, h, :], [bws], [wT, Sb])
                            Hh.mm(bo1.h[:, i * 128:(i + 1) * 128], qkvT.h[:, h // 2, cs], Sb.h[:, h, :], [bo1], [qkvT, Sb])
                        Hh.tt("dve", vnew.h[rs, hs, :], u.h[rs, hs, :], bws.h[rs, :].rearrange("p (a b) -> p a b", a=4),
                              ALU.subtract, [vnew], [u, bws])
                        Hh.tt("dve", o.h[rs, hs, :], bo1.h[rs, :].rearrange("p (a b) -> p a b", a=4),
                              bc(m["sa"].h[rs, hs], [64, 4, 128]), ALU.mult, [o], [bo1, m["sa"]])
                        bs = nb()
                        for i in range(4):
                            h = mgrp * 4 + i
                            Hh.mm(bs.h[:, i * 128:(i + 1) * 128], kdec.h[rs, h, :], vnew.h[rs, h, :], [bs], [kdec, vnew])
                        tS = tmpS[mgrp % 2]
                        Hh.tt("pool", tS.h[:], S.h[:, hs, :], bc(EG.h[:, c * 16 + mgrp * 4:c * 16 + mgrp * 4 + 4], [128, 4, 128]),
                              ALU.mult, [tS], [S, EG])
                        Hh.tt("dve", S.h[:, hs, :], tS.h[:], bs.h[:].rearrange("p (a b) -> p a b", a=4), ALU.add, [S], [tS, bs])
                        Hh.cp("act", Sb.h[:, hs, :], S.h[:, hs, :], [Sb], [S])
                for mgrp in range(4):
                    hs = slice(mgrp * 4, (mgrp + 1) * 4)
                    bo2 = nb()
                    for i in range(4):
                        h = mgrp * 4 + i
                        Hh.mm(bo2.h[:, i * 128:(i + 1) * 128], attnT.h[:, h, :], vnew.h[:, h, :], [bo2], [attnT, vnew])
                    Hh.tt("dve", tmpS[mgrp % 2].h[:], bo2.h[:].rearrange("p (a b) -> p a b", a=4),
                          bc(m["rq16"].h[:, hs], [128, 4, 128]), ALU.mult, [tmpS[mgrp % 2]], [bo2, m["rq16"]])
                    Hh.tt("pool", o.h[:, hs, :], o.h[:, hs, :], tmpS[mgrp % 2].h[:], ALU.add, [o], [o, tmpS[mgrp % 2]])
                Hh.tt("pool", u.h[:], o.h[:], o.h[:], ALU.mult, [u], [o])
                P.op("dve", lambda e: e.tensor_reduce(out=m["ssqo"].h[:], in_=u.h[:], op=ALU.add, axis=AX.X), w=[m["ssqo"]], r=[u])
                Hh.act(m["rms"].h[:], m["ssqo"].h[:], AF.Sqrt, [m["rms"]], [m["ssqo"], epsr], scale=1.0 / 128.0,
                       bias=epsr.h[:, 0:1])
                P.op("dve", lambda e: e.reciprocal(out=m["rms"].h[:], in_=m["rms"].h[:]), w=[m["rms"]], r=[m["rms"]])
                Hh.tt("dve", o.h[:], o.h[:], bc(m["rms"].h[:], [128, 16, 128]), ALU.mult, [o], [o, m["rms"]])
                Hh.tt("pool", o.h[:], o.h[:], nw1.h[:].unsqueeze(1).to_broadcast([128, 16, 128]), ALU.mult, [o], [o, nw1])
                Hh.tt("dve", obf.h[:], o.h[:], zs_t.h[:].rearrange("p (a b) -> p a b", a=16), ALU.mult, [obf], [o, zs_t])
                P.dma("sp", og_out.h[tb:tb + 128, :], obf.h[:].rearrange("p a b -> p (a b)"), w=og_out, r=obf, owner=obf)
        P.end_stage()


def gla_consts_np():
    t = np.arange(128)
    same = (t[:, None] // 64) == (t[None, :] // 64)
    c = {}
    c["ident"] = np.eye(128, dtype=np.float32)
    c["ucs"] = (same & (t[:, None] <= t[None, :])).astype(np.float32)
    c["vsame"] = same.astype(np.float32)
    c["mask01"] = (same & (t[None, :] >= t[:, None])).astype(np.float32)
    cind2 = np.zeros((128, 2), np.float32)
    cind2[:64, 0] = 1.0
    cind2[64:, 1] = 1.0
    c["cind2"] = cind2
    return c


def gla_stage(P, x_in, og_out, w_c, wg_aug, normw, C, T, tag="l", rm=lambda t: t):
    Hh = H(P)
    SBK = 256
    NSB = T // SBK
    with contextlib.ExitStack() as st:
        sbuf = lambda n, s, d=F32: P.sb(st, tag + n, s, d)
        idt = sbuf("idt", [128, 128])
        ucs = sbuf("ucs", [128, 128])
        vsame = sbuf("vsame", [128, 128])
        mask01 = sbuf("mask01", [128, 128])
        cind2 = sbuf("cind2", [128, 2])
        wga = sbuf("wga", [17, 512])
        nw1 = sbuf("nw1", [128, 512])
        for (dst, src) in ((idt, C["ident"]), (ucs, C["ucs"]), (vsame, C["vsame"]), (mask01, C["mask01"]),
                           (cind2, C["cind2"]), (wga, wg_aug)):
            P.dma("sp", dst.h[:], src.h[:, :], w=dst, r=src)
        P.dma("sp", nw1.h[:], normw.h.partition_broadcast(128), w=nw1, r=normw)
        xs = sbuf("xs", [128, D])
        xT = sbuf("xT", [128, 16, SBK], BF16)
        wf = [sbuf("wf%d" % i, [128, 16, 256], BF16) for i in range(2)]
        wgl = sbuf("wgl", [128, 16, 16], BF16)
        qkT = sbuf("qkT", [128, 8, SBK])
        ktok = sbuf("ktok", [128, 2, 512])
        vtok = sbuf("vtok", [128, 2, 1024], BF16)
        rs_ = sbuf("rs", [128, 2, 1024])
        glT = sbuf("glT", [32, SBK])
        S = sbuf("S", [128, 4, 512])
        Sb = sbuf("Sb", [128, 4, 512], BF16)
        ez = sbuf("ez", [128, 512])
        ftok = sbuf("ftok", [128, 512])
        bcs = sbuf("bcs", [128, 512])
        dd = sbuf("dd", [128, 512])
        kdec = sbuf("kdec", [128, 512], BF16)
        eP = sbuf("eP", [128, 4, 128])
        eN = sbuf("eN", [128, 4, 128])
        qt = sbuf("qt", [128, 4, 128], BF16)
        kt = sbuf("kt", [128, 4, 128], BF16)
        EB = sbuf("EB", [128, 8])
        attnT = sbuf("attnT", [128, 2, 128], BF16)
        o = sbuf("o", [128, 2, 512])
        obf = sbuf("obf", [128, 2, 512], BF16)
        sqo = sbuf("sqo", [128, 2, 512])
        ssq = sbuf("ssq", [128, 2])
        rms = sbuf("rms", [128, 2])
        epsr = sbuf("epsr", [128, 1])
        Hh.memset("dve", epsr.h[:], RMS_EPS, [epsr])
        Hh.memset("dve", S.h[:], 0.0, [S])
        Hh.memset("pool", Sb.h[:], 0.0, [Sb])
        Hh.memset("pool", glT.h[:], 1.0, [glT])
        banks = [P.ps(st, tag + "bank%d" % i, [128, 512]) for i in range(8)]
        bctr = [0]

        def nb():
            b = banks[bctr[0] % 8]
            bctr[0] += 1
            return b

        nwf = 0
        for sb in range(NSB):
            t0 = sb * SBK
            for s in range(2):
                P.dma("sp", xs.h[:], x_in.h[rm(t0 + s * 128):rm(t0 + s * 128) + 128, :], w=xs, r=x_in)
                for q in range(4):
                    bk = nb()
                    for i in range(4):
                        dk = q * 4 + i
                        Hh.tr(bk.h[:, i * 128:(i + 1) * 128], xs.h[:, dk * 128:(dk + 1) * 128], idt.h[:], [bk], [xs, idt])
                    Hh.cp("act" if q % 2 == 0 else "dve", xT.h[:, q * 4:(q + 1) * 4, s * 128:(s + 1) * 128],
                          bk.h[:].rearrange("p (a b) -> p a b", a=4), [xT], [bk])
            for g in range(12):
                sl = nwf % 2
                nwf += 1
                for k0 in range(0, 16, 4):
                    P.dma("pool", wf[sl].h[:, k0:k0 + 4, :],
                          w_c.h[k0 * 128:(k0 + 4) * 128, g * 256:(g + 1) * 256].rearrange("(k p) f -> p k f", p=128),
                          w=wf[sl], r=w_c)
                if g < 4:
                    for c in range(2):
                        fc = g * 2 + c
                        bk = nb()
                        for dk in range(16):
                            Hh.mm(bk.h[:, 0:SBK], wf[sl].h[:, dk, c * 128:(c + 1) * 128], xT.h[:, dk, :], [bk], [wf[sl], xT],
                                  start=(dk == 0), stop=(dk == 15))
                        Hh.cp("act", qkT.h[:, fc, :], bk.h[:, 0:SBK], [qkT], [bk])
                if g >= 2:
                    for s in range(2):
                        bk = nb()
                        for dk in range(16):
                            Hh.mm(bk.h[:, 0:256], xT.h[:, dk, s * 128:(s + 1) * 128], wf[sl].h[:, dk, :], [bk], [wf[sl], xT],
                                  start=(dk == 0), stop=(dk == 15))
                        if g < 4:
                            Hh.cp("dve", ktok.h[:, s, (g - 2) * 256:(g - 1) * 256], bk.h[:, 0:256], [ktok], [bk])
                        elif g < 8:
                            Hh.cp("dve", vtok.h[:, s, (g - 4) * 256:(g - 3) * 256], bk.h[:, 0:256], [vtok], [bk])
                        else:
                            Hh.act(rs_.h[:, s, (g - 8) * 256:(g - 7) * 256], bk.h[:, 0:256], AF.Silu, [rs_], [bk])
            for k0 in range(0, 16, 4):
                P.dma("pool", wgl.h[:, k0:k0 + 4, :],
                      w_c.h[k0 * 128:(k0 + 4) * 128, 3072:3088].rearrange("(k p) f -> p k f", p=128), w=wgl, r=w_c)
            bk = nb()
            for dk in range(16):
                Hh.mm(bk.h[0:16, 0:SBK], wgl.h[:, dk, :], xT.h[:, dk, :], [bk], [wgl, xT], start=(dk == 0), stop=(dk == 15))
            Hh.cp("act", glT.h[0:16, :], bk.h[0:16, 0:SBK], [glT], [bk])

            for s in range(2):
                tb = t0 + s * 128
                cs = slice(s * 128, (s + 1) * 128)
                bz = nb()
                Hh.mm(bz.h[:], glT.h[0:17, cs], wga.h[:], [bz], [glT, wga])
                Hh.act(ez.h[:], bz.h[:], AF.Exp, [ez], [bz], scale=-1.0)
                Hh.act(ez.h[:], ez.h[:], AF.Ln, [ez], [ez], bias=1.0)
                Hh.ts("dve", ftok.h[:], ez.h[:], -1.0 / 16.0, ALU.mult, [ftok], [ez])
                bcu = nb()
                bto = nb()
                Hh.mm(bcu.h[:], ucs.h[:], ftok.h[:], [bcu], [ucs, ftok])
                Hh.mm(bto.h[:], vsame.h[:], ftok.h[:], [bto], [vsame, ftok])
                Hh.cp("act", bcs.h[:], bcu.h[:], [bcs], [bcu])
                Hh.tt("dve", dd.h[:], bto.h[:], bcs.h[:], ALU.subtract, [dd], [bto, bcs])
                Hh.act(dd.h[:], dd.h[:], AF.Exp, [dd], [dd])
                Hh.tt("dve", kdec.h[:], ktok.h[:, s, :], dd.h[:], ALU.mult, [kdec], [ktok, dd])
                bT = nb()
                bl = nb()
                for kc in range(4):
                    Hh.mm(bT.h[:, kc * 128:(kc + 1) * 128], ftok.h[:, kc * 128:(kc + 1) * 128], ucs.h[:], [bT], [ftok, ucs])
                    Hh.mm(bl.h[:, kc * 2:(kc + 1) * 2], ftok.h[:, kc * 128:(kc + 1) * 128], cind2.h[:], [bl], [ftok, cind2])
                bT3 = bT.h[:].rearrange("p (a b) -> p a b", a=4)
                Hh.act(eP.h[:], bT3, AF.Exp, [eP], [bT], bias=-0.5 * math.log(256.0))
                Hh.act(eN.h[:], bT3, AF.Exp, [eN], [bT], scale=-1.0)
                Hh.act(EB.h[:], bl.h[:, 0:8], AF.Exp, [EB], [bl])
                Hh.tt("dve", qt.h[:], qkT.h[:, 0:4, cs], eP.h[:], ALU.mult, [qt], [qkT, eP])
                Hh.tt("dve", kt.h[:], qkT.h[:, 4:8, cs], eN.h[:], ALU.mult, [kt], [qkT, eN])
                ba_ = nb()
                for h in range(2):
                    for kc in range(2):
                        Hh.mm(ba_.h[:, h * 128:(h + 1) * 128], kt.h[:, h * 2 + kc, :], qt.h[:, h * 2 + kc, :], [ba_], [kt, qt],
                              start=(kc == 0), stop=(kc == 1))
                Hh.tt("dve", attnT.h[:], ba_.h[:, 0:256].rearrange("p (a b) -> p a b", a=2),
                      mask01.h[:].unsqueeze(1).to_broadcast([128, 2, 128]), ALU.mult, [attnT], [ba_, mask01])
                for c in range(2):
                    rs = slice(c * 64, (c + 1) * 64)
                    for h in range(2):
                        vh = vtok.h[rs, s, h * 512:(h + 1) * 512]
                        bo = nb()
                        Hh.mm(bo.h[:], qt.h[:, h * 2, :], Sb.h[:, h * 2, :], [bo], [qt, Sb], start=True, stop=False)
                        Hh.mm(bo.h[:], qt.h[:, h * 2 + 1, :], Sb.h[:, h * 2 + 1, :], [bo], [qt, Sb], start=False, stop=False)
                        Hh.mm(bo.h[:], attnT.h[rs, h, :], vh, [bo], [attnT, vtok], start=False, stop=True)
                        Hh.cp("act", o.h[rs, h, :], bo.h[rs, :], [o], [bo])
                        for kc in range(2):
                            i4 = h * 2 + kc
                            bs = nb()
                            Hh.mm(bs.h[:], kdec.h[rs, i4 * 128:(i4 + 1) * 128], vh, [bs], [kdec, vtok])
                            Hh.stt(S.h[:, i4, :], S.h[:, i4, :], EB.h[:, i4 * 2 + c:i4 * 2 + c + 1], bs.h[:], ALU.mult, ALU.add,
                                   [S], [S, EB, bs])
                            Hh.cp("act", Sb.h[:, i4, :], S.h[:, i4, :], [Sb], [S])
                Hh.tt("pool", sqo.h[:], o.h[:], o.h[:], ALU.mult, [sqo], [o])
                P.op("dve", lambda e: e.tensor_reduce(out=ssq.h[:], in_=sqo.h[:], op=ALU.add, axis=AX.X), w=[ssq], r=[sqo])
                Hh.act(rms.h[:], ssq.h[:], AF.Sqrt, [rms], [ssq, epsr], scale=1.0 / 512.0, bias=epsr.h[:, 0:1])
                P.op("dve", lambda e: e.reciprocal(out=rms.h[:], in_=rms.h[:]), w=[rms], r=[rms])
                Hh.tt("dve", o.h[:], o.h[:], bc(rms.h[:], [128, 2, 512]), ALU.mult, [o], [o, rms])
                Hh.tt("pool", o.h[:], o.h[:], nw1.h[:].unsqueeze(1).to_broadcast([128, 2, 512]), ALU.mult, [o], [o, nw1])
                Hh.tt("dve", obf.h[:], o.h[:], rs_.h[:, s, :].rearrange("p (a b) -> p a b", a=2), ALU.mult, [obf], [o, rs_])
                P.dma("sp", og_out.h[tb:tb + 128, :], obf.h[:].rearrange("p a b -> p (a b)"), w=og_out, r=obf, owner=obf)
        P.end_stage()


I32 = mybir.dt.int32
LC = 128


def s5_layout_np(a_re, a_im, log_dt, b_re, b_im, c_re, c_im, d_skip, hf):
    gs = slice(hf * 64, (hf + 1) * 64)

    def pp(a):
        return np.ascontiguousarray(a[gs].reshape(32, 2, 64).transpose(1, 2, 0).reshape(128, 32))
    out = {}
    out["are"] = pp(a_re)
    out["aim"] = pp(a_im)
    out["ldt"] = np.ascontiguousarray(np.broadcast_to(log_dt[gs].reshape(32, 2).T[:, None, :], (2, 64, 32)).reshape(128, 32))

    def bb(b):
        return np.ascontiguousarray(b[gs].reshape(32, 2, 64, 16).transpose(1, 2, 0, 3).reshape(128, 512))
    out["bre"] = bb(b_re)
    out["bim"] = bb(b_im)

    def cc(c):
        o = np.zeros((2, 64, 32, 2, 16), np.float32)
        cg = c[gs].reshape(32, 2, 16, 64)
        for g2 in range(2):
            o[g2, :, :, g2, :] = cg[:, g2].transpose(2, 0, 1)
        return o.reshape(128, 32 * 32)
    out["cre"] = cc(c_re)
    out["cim"] = cc(c_im)
    out["dsk"] = np.ascontiguousarray(d_skip[hf * 1024:(hf + 1) * 1024])
    ms = np.zeros(2, np.float32)
    ms[hf] = 1.0
    out["msel"] = ms
    return out


def s5_stage(P, x_in, y_out, L, ident_d, T, isel_d, tag="s", rm=lambda t: t):
    Hh = H(P)
    NCH = T // LC
    with contextlib.ExitStack() as st:
        sbuf = lambda n, s, d=F32: P.sb(st, tag + n, s, d)
        idt = sbuf("idt", [128, 128])
        P.dma("sp", idt.h[:], ident_d.h[:, :], w=idt, r=ident_d)
        sm = {n: sbuf("sm_" + n, [128, 32]) for n in
              ("are", "aim", "dt", "ar", "th", "kf", "hl", "sh", "ah", "ch", "sn", "cs", "abr", "abi", "nr", "den", "cre", "cim",
               "t1", "t2", "hpr", "hpi")}
        ki = sbuf("ki", [128, 32], I32)
        for n in ("are", "aim"):
            P.dma("sp", sm[n].h[:], L[n].h[:, :], w=sm[n], r=L[n])
        P.dma("sp", sm["dt"].h[:], L["ldt"].h[:, :], w=sm["dt"], r=L["ldt"])
        m = sm
        tt = lambda eng, o_, a, b, op: Hh.tt(eng, o_.h[:], a.h[:], b.h[:], op, [o_], [a, b])
        Hh.act(m["dt"].h[:], m["dt"].h[:], AF.Exp, [m["dt"]], [m["dt"]])
        tt("dve", m["ar"], m["are"], m["dt"], ALU.mult)
        Hh.act(m["ar"].h[:], m["ar"].h[:], AF.Exp, [m["ar"]], [m["ar"]])
        tt("dve", m["th"], m["aim"], m["dt"], ALU.mult)
        Hh.ts("dve", m["kf"].h[:], m["th"].h[:], 1.0 / (2.0 * math.pi), ALU.mult, [m["kf"]], [m["th"]])
        Hh.cp("dve", ki.h[:], m["kf"].h[:], [ki], [m["kf"]])
        Hh.cp("dve", m["kf"].h[:], ki.h[:], [m["kf"]], [ki])
        C1 = 6.28125
        C2 = 2.0 * math.pi - C1
        Hh.stt(m["th"].h[:], m["kf"].h[:], -C1, m["th"].h[:], ALU.mult, ALU.add, [m["th"]], [m["kf"], m["th"]])
        Hh.stt(m["th"].h[:], m["kf"].h[:], -C2, m["th"].h[:], ALU.mult, ALU.add, [m["th"]], [m["kf"], m["th"]])
        Hh.ts("dve", m["hl"].h[:], m["th"].h[:], 0.5, ALU.mult, [m["hl"]], [m["th"]])
        Hh.act(m["sh"].h[:], m["hl"].h[:], AF.Sin, [m["sh"]], [m["hl"]])
        Hh.act(m["ah"].h[:], m["hl"].h[:], AF.Abs, [m["ah"]], [m["hl"]])
        hpi2 = sbuf("hpi2", [128, 1])
        Hh.memset("dve", hpi2.h[:], math.pi / 2.0, [hpi2])
        Hh.act(m["ch"].h[:], m["ah"].h[:], AF.Sin, [m["ch"]], [m["ah"], hpi2], scale=-1.0, bias=hpi2.h[:, 0:1])
        tt("dve", m["sn"], m["sh"], m["ch"], ALU.mult)
        Hh.ts("dve", m["sn"].h[:], m["sn"].h[:], 2.0, ALU.mult, [m["sn"]], [m["sn"]])
        tt("dve", m["cs"], m["sh"], m["sh"], ALU.mult)
        Hh.ts("dve", m["cs"].h[:], m["cs"].h[:], -2.0, ALU.mult, [m["cs"]], [m["cs"]], s2=1.0, op1=ALU.add)
        tt("dve", m["abr"], m["ar"], m["cs"], ALU.mult)
        tt("dve", m["abi"], m["ar"], m["sn"], ALU.mult)
        Hh.ts("dve", m["nr"].h[:], m["abr"].h[:], -1.0, ALU.add, [m["nr"]], [m["abr"]])
        tt("dve", m["den"], m["are"], m["are"], ALU.mult)
        tt("dve", m["t1"], m["aim"], m["aim"], ALU.mult)
        tt("dve", m["den"], m["den"], m["t1"], ALU.add)
        P.op("dve", lambda e: e.reciprocal(out=m["den"].h[:], in_=m["den"].h[:]), w=[m["den"]], r=[m["den"]])
        tt("dve", m["t1"], m["nr"], m["are"], ALU.mult)
        tt("dve", m["t2"], m["abi"], m["aim"], ALU.mult)
        tt("dve", m["cre"], m["t1"], m["t2"], ALU.add)
        tt("dve", m["cre"], m["cre"], m["den"], ALU.mult)
        tt("dve", m["t1"], m["abi"], m["are"], ALU.mult)
        tt("dve", m["t2"], m["nr"], m["aim"], ALU.mult)
        tt("dve", m["cim"], m["t1"], m["t2"], ALU.subtract)
        tt("dve", m["cim"], m["cim"], m["den"], ALU.mult)

        K_re = sbuf("Kre", [128, 32, LC])
        K_im = sbuf("Kim", [128, 32, LC])
        H_re = sbuf("Hre", [128, 32, LC])
        H_im = sbuf("Him", [128, 32, LC])
        WbT_re = sbuf("WbTre", [128, 32, 128])
        WbT_im = sbuf("WbTim", [128, 32, 128])
        banks = [P.ps(st, tag + "bank%d" % i, [128, 512]) for i in range(8)]
        bctr = [0]

        def nb():
            b = banks[bctr[0] % 8]
            bctr[0] += 1
            return b
        Braw_re = P.view(K_re.h[:, 0:4, :].rearrange("p a b -> p (a b)"), "Braw_re")
        Braw_im = P.view(K_im.h[:, 0:4, :].rearrange("p a b -> p (a b)"), "Braw_im")
        bb_re = P.view(H_re.h[:, 0:4, :].rearrange("p a b -> p (a b)"), "bb_re")
        bb_im = P.view(H_im.h[:, 0:4, :].rearrange("p a b -> p (a b)"), "bb_im")
        tmpa = P.view(K_re.h[:, 4:8, :].rearrange("p a b -> p (a b)"), "tmpa")
        tmpb = P.view(K_im.h[:, 4:8, :].rearrange("p a b -> p (a b)"), "tmpb")
        P.dma("sp", Braw_re.h, L["bre"].h[:, :], w=Braw_re, r=L["bre"])
        P.dma("sp", Braw_im.h, L["bim"].h[:, :], w=Braw_im, r=L["bim"])
        v3 = lambda b: b.h.rearrange("p (a c) -> p a c", a=32)
        cre_b = bc(m["cre"].h[:], [128, 32, 16])
        cim_b = bc(m["cim"].h[:], [128, 32, 16])
        Hh.tt("dve", v3(tmpa), v3(Braw_re), cre_b, ALU.mult, [tmpa], [Braw_re, m["cre"]])
        Hh.tt("dve", v3(tmpb), v3(Braw_im), cim_b, ALU.mult, [tmpb], [Braw_im, m["cim"]])
        Hh.tt("dve", bb_re.h, tmpa.h, tmpb.h, ALU.subtract, [bb_re], [tmpa, tmpb])
        Hh.tt("dve", v3(tmpa), v3(Braw_im), cre_b, ALU.mult, [tmpa], [Braw_im, m["cre"]])
        Hh.tt("dve", v3(tmpb), v3(Braw_re), cim_b, ALU.mult, [tmpb], [Braw_re, m["cim"]])
        Hh.tt("dve", bb_im.h, tmpa.h, tmpb.h, ALU.add, [bb_im], [tmpa, tmpb])
        pad = [sbuf("pad%d" % i, [128, 128]) for i in range(2)]
        for i in range(2):
            Hh.memset("dve", pad[i].h[:], 0.0, [pad[i]])
        npad = 0
        for pi in range(32):
            c0 = (pi % 4) * 32
            for (src, dstW) in ((bb_re, WbT_re), (bb_im, WbT_im)):
                pd = pad[npad % 2]
                npad += 1
                Hh.memset("pool", pd.h[:], 0.0, [pd])
                Hh.cp("pool", pd.h[0:64, c0:c0 + 16], src.h[0:64, pi * 16:(pi + 1) * 16], [pd], [src])
                Hh.cp("pool", pd.h[64:128, c0 + 16:c0 + 32], src.h[64:128, pi * 16:(pi + 1) * 16], [pd], [src])
                bk = nb()
                Hh.tr(bk.h[:, 0:128], pd.h[:], idt.h[:], [bk], [pd, idt])
                Hh.cp("act", dstW.h[:, pi, :], bk.h[:, 0:128], [dstW], [bk])
        Cre = sbuf("Cre", [128, 32, 32])
        Cni = sbuf("Cni", [128, 32, 32])
        P.dma("sp", Cre.h[:].rearrange("p a b -> p (a b)"), L["cre"].h[:, :], w=Cre, r=L["cre"])
        P.dma("sp", Cni.h[:].rearrange("p a b -> p (a b)"), L["cim"].h[:, :], w=Cni, r=L["cim"])
        Hh.ts("dve", Cni.h[:], Cni.h[:], -1.0, ALU.mult, [Cni], [Cni])
        Dt = sbuf("Dt", [128, 1024])
        msel = sbuf("msel", [128, 2])
        P.dma("sp", msel.h[:], L["msel"].h.partition_broadcast(128), w=msel, r=L["msel"])
        iself = sbuf("iself", [128, 2, 128])
        for h in range(2):
            P.dma("sp", iself.h[:, h, :], isel_d.h[h], w=iself, r=isel_d)
        P.dma("sp", Dt.h[:], L["dsk"].h.partition_broadcast(128), w=Dt, r=L["dsk"])
        msk = sbuf("msk", [128, 32, LC], BF16)
        Hh.memset("dve", msk.h[:], 1.0, [msk])
        Hh.memset("dve", msk.h[:, :, 0:1], 0.0, [msk])
        Tp_re = sbuf("Tpre", [128, 32, LC])
        Tp_im = sbuf("Tpim", [128, 32, LC])
        Tn_re = sbuf("Tnre", [128, 32, LC])
        Tn_im = sbuf("Tnim", [128, 32, LC])
        P.barrier()
        Hh.cp("dve", Tp_re.h[:, :, 0], m["abr"].h[:], [Tp_re], [m["abr"]])
        Hh.cp("dve", Tp_im.h[:, :, 0], m["abi"].h[:], [Tp_im], [m["abi"]])
        n = 1
        while n < LC:
            sre = bc(Tp_re.h[:, :, n - 1], [128, 32, n])
            sim = bc(Tp_im.h[:, :, n - 1], [128, 32, n])
            A_re = Tp_re.h[:, :, 0:n]
            A_im = Tp_im.h[:, :, 0:n]
            t1 = K_re.h[:, :, 0:n]
            t2 = K_im.h[:, :, 0:n]
            t3 = H_re.h[:, :, 0:n]
            t4 = H_im.h[:, :, 0:n]
            Hh.tt("dve", t1, A_re, sre, ALU.mult, [K_re], [Tp_re])
            Hh.tt("dve", t2, A_im, sim, ALU.mult, [K_im], [Tp_im])
            Hh.tt("dve", t3, A_re, sim, ALU.mult, [H_re], [Tp_re, Tp_im])
            Hh.tt("dve", t4, A_im, sre, ALU.mult, [H_im], [Tp_re, Tp_im])
            Hh.tt("dve", Tp_re.h[:, :, n:2 * n], t1, t2, ALU.subtract, [Tp_re], [K_re, K_im])
            Hh.tt("dve", Tp_im.h[:, :, n:2 * n], t3, t4, ALU.add, [Tp_im], [H_re, H_im])
            n *= 2
        Hh.tt("dve", K_re.h[:], Tp_re.h[:], Tp_re.h[:], ALU.mult, [K_re], [Tp_re])
        Hh.tt("dve", K_im.h[:], Tp_im.h[:], Tp_im.h[:], ALU.mult, [K_im], [Tp_im])
        Hh.tt("dve", K_re.h[:], K_re.h[:], K_im.h[:], ALU.add, [K_re], [K_re, K_im])
        P.op("dve", lambda e: e.reciprocal(out=K_re.h[:], in_=K_re.h[:]), w=[K_re], r=[K_re])
        Hh.tt("dve", Tn_re.h[:], Tp_re.h[:], K_re.h[:], ALU.mult, [Tn_re], [Tp_re, K_re])
        Hh.tt("dve", Tn_im.h[:], Tp_im.h[:], K_re.h[:], ALU.mult, [Tn_im], [Tp_im, K_re])
        Hh.ts("dve", Tn_im.h[:], Tn_im.h[:], -1.0, ALU.mult, [Tn_im], [Tn_im])
        Hh.memset("dve", m["hpr"].h[:], 0.0, [m["hpr"]])
        Hh.memset("dve", m["hpi"].h[:], 0.0, [m["hpi"]])

        xu = sbuf("xu", [128, 2048])
        uT = sbuf("uT", [128, 8, 128])
        tq = [sbuf("tq%d" % i, [128, 4, LC]) for i in range(2)]
        tq = tq + tq
        du = sbuf("du", [128, 1024])
        ybf = sbuf("ybf", [128, 1024], BF16)
        for ch in range(NCH):
            tb = ch * LC
            P.dma("sp", xu.h[:], x_in.h[rm(tb):rm(tb) + 128, :], w=xu, r=x_in)
            for q in range(2):
                bk = nb()
                for i in range(4):
                    for h in range(2):
                        c0 = h * 1024 + (q * 4 + i) * 128
                        Hh.mm(bk.h[:, i * 128:(i + 1) * 128], xu.h[:, c0:c0 + 128], iself.h[:, h, :], [bk], [xu, iself],
                              start=(h == 0), stop=(h == 1))
                Hh.cp("act", uT.h[:, q * 4:(q + 1) * 4, :], bk.h[:].rearrange("p (a b) -> p a b", a=4), [uT], [bk])
            for q in range(8):
                bre = nb()
                bim = nb()
                for i in range(4):
                    pi = q * 4 + i
                    Hh.mm(bre.h[:, i * 128:(i + 1) * 128], WbT_re.h[:, pi, :], uT.h[:, q, :], [bre], [WbT_re, uT])
                    Hh.mm(bim.h[:, i * 128:(i + 1) * 128], WbT_im.h[:, pi, :], uT.h[:, q, :], [bim], [WbT_im, uT])
                qs = slice(q * 4, (q + 1) * 4)
                b3 = lambda b: b.h[:].rearrange("p (a b) -> p a b", a=4)
                Hh.tt("dve", tq[0].h[:], Tn_re.h[:, qs, :], b3(bre), ALU.mult, [tq[0]], [Tn_re, bre])
                Hh.tt("dve", tq[1].h[:], Tn_im.h[:, qs, :], b3(bim), ALU.mult, [tq[1]], [Tn_im, bim])
                Hh.tt("pool", K_re.h[:, qs, :], tq[0].h[:], tq[1].h[:], ALU.subtract, [K_re], [tq[0], tq[1]])
                Hh.tt("dve", tq[2].h[:], Tn_re.h[:, qs, :], b3(bim), ALU.mult, [tq[2]], [Tn_re, bim])
                Hh.tt("dve", tq[3].h[:], Tn_im.h[:, qs, :], b3(bre), ALU.mult, [tq[3]], [Tn_im, bre])
                Hh.tt("pool", K_im.h[:, qs, :], tq[2].h[:], tq[3].h[:], ALU.add, [K_im], [tq[2], tq[3]])
            Hh.tt("pool", K_re.h[:, :, 0], K_re.h[:, :, 0], m["hpr"].h[:], ALU.add, [K_re], [K_re, m["hpr"]])
            Hh.tt("pool", K_im.h[:, :, 0], K_im.h[:, :, 0], m["hpi"].h[:], ALU.add, [K_im], [K_im, m["hpi"]])
            f2 = lambda b: b.h[:].rearrange("p a b -> p (a b)")
            P.op("dve", lambda e: e.tensor_tensor_scan(out=f2(H_re), data0=f2(msk), data1=f2(K_re), initial=0.0,
                                                       op0=ALU.mult, op1=ALU.add), w=[H_re], r=[msk, K_re])
            P.op("dve", lambda e: e.tensor_tensor_scan(out=f2(H_im), data0=f2(msk), data1=f2(K_im), initial=0.0,
                                                       op0=ALU.mult, op1=ALU.add), w=[H_im], r=[msk, K_im])
            for q in range(8):
                qs = slice(q * 4, (q + 1) * 4)
                Hh.tt("dve", tq[0].h[:], Tp_re.h[:, qs, :], H_re.h[:, qs, :], ALU.mult, [tq[0]], [Tp_re, H_re])
                Hh.tt("dve", tq[1].h[:], Tp_im.h[:, qs, :], H_im.h[:, qs, :], ALU.mult, [tq[1]], [Tp_im, H_im])
                Hh.tt("pool", K_re.h[:, qs, :], tq[0].h[:], tq[1].h[:], ALU.subtract, [K_re], [tq[0], tq[1]])
                Hh.tt("dve", tq[2].h[:], Tp_re.h[:, qs, :], H_im.h[:, qs, :], ALU.mult, [tq[2]], [Tp_re, H_im])
                Hh.tt("dve", tq[3].h[:], Tp_im.h[:, qs, :], H_re.h[:, qs, :], ALU.mult, [tq[3]], [Tp_im, H_re])
                Hh.tt("pool", K_im.h[:, qs, :], tq[2].h[:], tq[3].h[:], ALU.add, [K_im], [tq[2], tq[3]])
            Hh.cp("pool", m["hpr"].h[:], K_re.h[:, :, LC - 1], [m["hpr"]], [K_re])
            Hh.cp("pool", m["hpi"].h[:], K_im.h[:, :, LC - 1], [m["hpi"]], [K_im])
            by = [nb(), nb()]
            for pi in range(32):
                o_ = by[pi // 16].h[:, (pi % 16) * 32:(pi % 16 + 1) * 32]
                Hh.mm(o_, K_re.h[:, pi, :], Cre.h[:, pi, :], [by[pi // 16]], [K_re, Cre], start=True, stop=False)
                Hh.mm(o_, K_im.h[:, pi, :], Cni.h[:, pi, :], [by[pi // 16]], [K_im, Cni], start=False, stop=True)
            Hh.ts("pool", du.h[:], xu.h[:, 0:1024], msel.h[:, 0:1], ALU.mult, [du], [xu, msel])
            Hh.stt(du.h[:], xu.h[:, 1024:2048], msel.h[:, 1:2], du.h[:], ALU.mult, ALU.add, [du], [xu, msel, du])
            Hh.tt("pool", du.h[:], du.h[:], Dt.h[:], ALU.mult, [du], [du, Dt])
            for i in range(2):
                Hh.tt("dve", du.h[:, i * 512:(i + 1) * 512], by[i].h[:], du.h[:, i * 512:(i + 1) * 512], ALU.add, [du], [by[i], du])
            Hh.act(ybf.h[:], du.h[:], AF.Gelu_apprx_tanh, [ybf], [du])
            P.dma("sp", y_out.h[tb:tb + 128, :], ybf.h[:], w=y_out, r=ybf, owner=ybf)
        P.end_stage()


FF = 5632
PAIRS = [[0, 1], [2, 3], [4, 5], [6, 7]]
DEEPNORM_ALPHA = (2.0 * 4) ** 0.25


def load_hT_gathered(P, Hh, src_g, SEQ, TC, Vh, iselb, t0, xs, hT, banks, nbk):
    PR = min(TC, (2 << 20) // (Vh * 2))
    NK = 2 * Vh // 128
    for s in range(4):
        off = t0 + s * 128
        for h in range(2):
            for r in range(2):
                row = h * 2 * TC + (off // PR) * 2 * PR + r * PR + off % PR
                P.dma("sp", xs[h].h[:, r * Vh:(r + 1) * Vh], src_g.h[row:row + 128, :], w=xs[h], r=src_g)
        for q in range(NK // 4):
            bk = banks[nbk[0] % 8]
            nbk[0] += 1
            for i in range(4):
                dk = q * 4 + i
                for h in range(2):
                    Hh.mm(bk.h[:, i * 128:(i + 1) * 128], xs[h].h[:, dk * 128:(dk + 1) * 128], iselb.h[:, h, :], [bk],
                          [xs[h], iselb], start=(h == 0), stop=(h == 1))
            Hh.cp("act" if q % 2 == 0 else "dve", hT.h[:, q * 4:(q + 1) * 4, s * 128:(s + 1) * 128],
                  bk.h[:].rearrange("p (a b) -> p a b", a=4), [hT], [bk])


def load_isel(P, Hh, st, isel_d, tag):
    iself = P.sb(st, tag + "iself", [128, 2, 128], F32)
    for h in range(2):
        P.dma("sp", iself.h[:, h, :], isel_d.h[h], w=iself, r=isel_d)
    iselb = P.sb(st, tag + "iselb", [128, 2, 128], BF16)
    Hh.cp("dve", iselb.h[:], iself.h[:], [iselb], [iself])
    return iself, iselb


def outproj_stage(P, og_g, SEQ, Vh, x_res, x_out, w_out, ln_g, ln_b, ident_d, TC, alpha, isel_d, tag="p"):
    Hh = H(P)
    NB = TC // 512
    V = 2 * Vh
    NK = V // 128
    NKG = 8
    with contextlib.ExitStack() as st:
        idt, epsT = load_consts(P, st, ident_d)
        iself, iselb = load_isel(P, Hh, st, isel_d, tag)
        Gt = P.sb(st, tag + "G", [128, D], F32)
        Bt = P.sb(st, tag + "B", [128, D], F32)
        P.dma("sp", Gt.h[:], ln_g.h.partition_broadcast(128), w=Gt, r=ln_g)
        P.dma("sp", Bt.h[:], ln_b.h.partition_broadcast(128), w=Bt, r=ln_b)
        xs = [P.sb(st, tag + "xs%d" % i, [128, V], BF16) for i in range(2)]
        hT = P.sb(st, tag + "hT", [128, NK, 512], BF16)
        wd = [P.sb(st, tag + "wd%d" % i, [128, NKG, 512], BF16) for i in range(2)]
        z = [P.sb(st, tag + "z%d" % i, [128, D], F32) for i in range(4)]
        tmps = [ln_tmp(P, st, tag + "t%d" % i) for i in range(2)]
        bank = [P.ps(st, tag + "bank%d" % i, [128, 512]) for i in range(8)]
        nwd = 0
        nbk = [0]
        for blk in range(NB):
            t0 = blk * 512
            load_hT_gathered(P, Hh, og_g, SEQ, TC, Vh, iselb, t0, xs, hT, bank, nbk)
            for s in range(4):
                P.dma("sp", z[s].h[:], x_res.h[t0 + s * 128:t0 + (s + 1) * 128, :], w=z[s], r=x_res)
                Hh.act(z[s].h[:], z[s].h[:], AF.Copy, [z[s]], [z[s]], scale=float(alpha))
            for dn in range(4):
                pb = [bank[(nbk[0] + s) % 8] for s in range(4)]
                nbk[0] += 4
                for fg in range(NK // NKG):
                    sl = nwd % 2
                    nwd += 1
                    for a in range(0, NKG, 4):
                        r0 = (fg * NKG + a) * 128
                        P.dma("pool", wd[sl].h[:, a:a + 4, :],
                              w_out.h[r0:r0 + 4 * 128, dn * 512:(dn + 1) * 512].rearrange("(k p) f -> p k f", p=128),
                              w=wd[sl], r=w_out)
                    for i in range(NKG):
                        fk = fg * NKG + i
                        for s in range(4):
                            Hh.mm(pb[s].h[:], hT.h[:, fk, s * 128:(s + 1) * 128], wd[sl].h[:, i, :], [pb[s]], [hT, wd[sl]],
                                  start=(fk == 0), stop=(fk == NK - 1))
                for s in range(4):
                    zc = z[s].h[:, dn * 512:(dn + 1) * 512]
                    Hh.tt("dve", zc, pb[s].h[:], zc, ALU.add, [z[s]], [pb[s], z[s]])
            for s in range(4):
                layer_norm_tile(P, z[s], Gt, Bt, epsT, tmps[s % 2])
                P.dma("sp", x_out.h[t0 + s * 128:t0 + (s + 1) * 128, :], z[s].h[:], w=x_out, r=z[s], owner=z[s])
        P.end_stage()


def s5out_stage(P, y_g, SEQ, x_res, x_out, w_glu, ln_g, ln_b, ident_d, TC, alpha, isel_d, tag="q"):
    Hh = H(P)
    NB = TC // 512
    with contextlib.ExitStack() as st:
        idt, epsT = load_consts(P, st, ident_d)
        iself, iselb = load_isel(P, Hh, st, isel_d, tag)
        Gt = P.sb(st, tag + "G", [128, D], F32)
        Bt = P.sb(st, tag + "B", [128, D], F32)
        P.dma("sp", Gt.h[:], ln_g.h.partition_broadcast(128), w=Gt, r=ln_g)
        P.dma("sp", Bt.h[:], ln_b.h.partition_broadcast(128), w=Bt, r=ln_b)
        xs = [P.sb(st, tag + "xs%d" % i, [128, D], BF16) for i in range(2)]
        yT = P.sb(st, tag + "yT", [128, 16, 512], BF16)
        wv = [P.sb(st, tag + "wv%d" % i, [128, 16, 512], BF16) for i in range(2)]
        wg = [P.sb(st, tag + "wg%d" % i, [128, 16, 512], BF16) for i in range(2)]
        sg = [P.sb(st, tag + "sg%d" % i, [128, 512], F32) for i in range(2)]
        z = [P.sb(st, tag + "z%d" % i, [128, D], F32) for i in range(4)]
        tmps = [ln_tmp(P, st, tag + "t%d" % i) for i in range(2)]
        bank = [P.ps(st, tag + "bank%d" % i, [128, 512]) for i in range(8)]
        nw = 0
        nbk = [0]
        for blk in range(NB):
            t0 = blk * 512
            load_hT_gathered(P, Hh, y_g, SEQ, TC, 1024, iselb, t0, xs, yT, bank, nbk)
            for s in range(4):
                P.dma("sp", z[s].h[:], x_res.h[t0 + s * 128:t0 + (s + 1) * 128, :], w=z[s], r=x_res)
                Hh.act(z[s].h[:], z[s].h[:], AF.Copy, [z[s]], [z[s]], scale=float(alpha))
            for dn in range(4):
                sl = nw % 2
                nw += 1
                for k0 in range(0, 16, 4):
                    P.dma("pool", wv[sl].h[:, k0:k0 + 4, :],
                          w_glu.h[k0 * 128:(k0 + 4) * 128, dn * 512:(dn + 1) * 512].rearrange("(k p) f -> p k f", p=128),
                          w=wv[sl], r=w_glu)
                    P.dma("pool", wg[sl].h[:, k0:k0 + 4, :],
                          w_glu.h[k0 * 128:(k0 + 4) * 128, D + dn * 512:D + (dn + 1) * 512].rearrange(
                              "(k p) f -> p k f", p=128), w=wg[sl], r=w_glu)
                for s in range(4):
                    pv = bank[nbk[0] % 8]
                    pg = bank[(nbk[0] + 1) % 8]
                    nbk[0] += 2
                    for dk in range(16):
                        Hh.mm(pg.h[:], yT.h[:, dk, s * 128:(s + 1) * 128], wg[sl].h[:, dk, :], [pg], [yT, wg[sl]],
                              start=(dk == 0), stop=(dk == 15))
                    for dk in range(16):
                        Hh.mm(pv.h[:], yT.h[:, dk, s * 128:(s + 1) * 128], wv[sl].h[:, dk, :], [pv], [yT, wv[sl]],
                              start=(dk == 0), stop=(dk == 15))
                    sgb = sg[s % 2]
                    Hh.act(sgb.h[:], pg.h[:], AF.Sigmoid, [sgb], [pg])
                    Hh.tt("dve", sgb.h[:], sgb.h[:], pv.h[:], ALU.mult, [sgb], [sgb, pv])
                    zc = z[s].h[:, dn * 512:(dn + 1) * 512]
                    Hh.tt("pool", zc, zc, sgb.h[:], ALU.add, [z[s]], [z[s], sgb])
            for s in range(4):
                layer_norm_tile(P, z[s], Gt, Bt, epsT, tmps[s % 2])
                P.dma("sp", x_out.h[t0 + s * 128:t0 + (s + 1) * 128, :], z[s].h[:], w=x_out, r=z[s], owner=z[s])
        P.end_stage()


def _consts_np():
    c = gdn_consts_np()
    c.update(gla_consts_np())
    return c


S5_KEYS = ("are", "aim", "ldt", "bre", "bim", "cre", "cim", "dsk", "msel")


def build_program(SEQ, layers=(0, 1, 2, 3)):
    TC = SEQ // 2
    nc = bass.Bass("TRN2", target_bir_lowering=False)
    P = Prog(nc)
    ext = lambda n, s, d=F32: P.dram(n, s, d, kind="ExternalInput")
    x = ext("x", [TC, D])
    y = P.dram("y", [TC, D], F32, kind="ExternalOutput")
    ln_g = ext("ln_g", [4, 3, D])
    ln_b = ext("ln_b", [4, 3, D])
    w_up = ext("ffn_w_up", [4, 2, D, 2 * FF])
    w_dn = ext("ffn_w_down", [4, 2, FF, D])
    gdn_wout = ext("gdn_w_out", [2, 4096, D])
    gla_wout = ext("gla_w_out", [1, 2048, D])
    s5_wglu = ext("s5_w_glu", [1, D, 2 * D])
    gdn_nw = ext("gdn_norm_w", [2, 128])
    gla_nw = ext("gla_norm_w", [1, 512])
    gdn_wc = ext("gdn_wc", [2, D, 6176])
    gdn_conv = ext("gdn_conv", [2, 128, 128])
    gdn_alog = ext("gdn_alog", [2, 16])
    gdn_dtb = ext("gdn_dtb", [2, 16])
    gla_wc = ext("gla_wc", [D, 3088])
    gla_wga = ext("gla_wga", [17, 512])
    L = {}
    l0 = s5_layout_np(*[np.zeros(s, np.float32) for s in ((128, 64), (128, 64), (128,), (128, 64, 16), (128, 64, 16),
                                                          (128, 16, 64), (128, 16, 64), (2048,))], 0)
    for k in S5_KEYS:
        L[k] = ext("s5L_" + k, list(l0[k].shape))
    C = {k: ext("c_" + k, list(v.shape)) for k, v in _consts_np().items()}
    A = P.dram("actA", [TC, D], F32)
    B = P.dram("actB", [TC, D], F32)
    Cb = P.dram("actC", [TC, D], F32)
    XF = P.dram("actXF", [SEQ, D], F32)
    OG2 = P.dram("og2", [SEQ, 2048], BF16)
    QKV = P.dram("gdn_qkv", [4096, SEQ], BF16)
    Zd = P.dram("gdn_z", [SEQ, 2048], BF16)
    SSQd = P.dram("gdn_ssq", [SEQ, 16], F32)
    BAd = P.dram("gdn_ba", [32, SEQ], F32)
    OGG2 = P.dram("ogg2", [2 * SEQ, 2048], BF16)
    OG1 = P.dram("og1", [SEQ, 1024], BF16)
    OGG1 = P.dram("ogg1", [2 * SEQ, 1024], BF16)
    isel = ext("isel", [2, 128, 128])
    V = lambda ap, n: Buf(ap, n)
    rm = lambda t: ((t % TC) // 256) * 512 + (t // TC) * 256 + (t % 256)
    alpha = DEEPNORM_ALPHA
    ident = C["ident"]
    xin = x
    for li, i in enumerate(layers):
        last = (li == len(layers) - 1)
        kind, j = i % 3, i // 3
        ffn_stage(P, xin, A, V(w_up.h[i, 0], "wu"), V(w_dn.h[i, 0], "wd"), V(ln_g.h[i, 0], "g"), V(ln_b.h[i, 0], "b"),
                  ident, TC, alpha, tag="f%da" % i)
        P.gather_pairs(A, XF, TC, 256)
        if kind == 0:
            gdn_proj_stage(P, XF, QKV, Zd, SSQd, BAd, V(gdn_wc.h[j], "gw"), V(gdn_conv.h[j], "gc"), C, SEQ,
                           tag="gp%d" % i, rm=rm)
            gdn_rec_stage(P, QKV, Zd, SSQd, BAd, OG2, V(gdn_alog.h[j], "ga"), V(gdn_dtb.h[j], "gd"),
                          V(gdn_nw.h[j], "gn"), C, SEQ, tag="g%d" % i)
            P.gather_pairs(OG2, OGG2, SEQ, min(TC, 512))
            outproj_stage(P, OGG2, SEQ, 2048, A, B, V(gdn_wout.h[j], "gwo"), V(ln_g.h[i, 1], "g"), V(ln_b.h[i, 1], "b"),
                          ident, TC, alpha, isel, tag="p%d" % i)
        elif kind == 1:
            gla_stage(P, XF, OG1, gla_wc, gla_wga, V(gla_nw.h[j], "ln"), C, SEQ, tag="l%d" % i, rm=rm)
            P.gather_pairs(OG1, OGG1, SEQ, min(TC, 1024))
            outproj_stage(P, OGG1, SEQ, 1024, A, B, V(gla_wout.h[j], "lwo"), V(ln_g.h[i, 1], "g"), V(ln_b.h[i, 1], "b"),
                          ident, TC, alpha, isel, tag="p%d" % i)
        else:
            s5_stage(P, XF, OG1, L, ident, SEQ, isel, tag="s%d" % i, rm=rm)
            P.gather_pairs(OG1, OGG1, SEQ, min(TC, 1024))
            s5out_stage(P, OGG1, SEQ, A, B, V(s5_wglu.h[j], "sw"), V(ln_g.h[i, 1], "g"), V(ln_b.h[i, 1], "b"),
                        ident, TC, alpha, isel, tag="q%d" % i)
        dst = y if last else Cb
        ffn_stage(P, B, dst, V(w_up.h[i, 1], "wu"), V(w_dn.h[i, 1], "wd"), V(ln_g.h[i, 2], "g"), V(ln_b.h[i, 2], "b"),
                  ident, TC, alpha, tag="f%db" % i)
        xin = Cb
    P.finish()
    return nc, P


def make_in_maps(inputs, SEQ):
    TC = SEQ // 2
    f = lambda a: np.ascontiguousarray(np.asarray(a, dtype=np.float32))
    rep = {k: f(inputs[k]) for k in ("ln_g", "ln_b", "ffn_w_up", "ffn_w_down", "gdn_w_out", "gla_w_out", "s5_w_glu",
                                     "gdn_norm_w", "gla_norm_w")}
    consts = {"c_" + k: v for k, v in _consts_np().items()}
    x = np.asarray(inputs["x"], dtype=np.float32)
    gdn_w_in = np.asarray(inputs["gdn_w_in"], np.float32)
    gdn_conv_w = np.asarray(inputs["gdn_conv_w"], np.float32)
    gla_w_in = np.asarray(inputs["gla_w_in"], np.float32)[0]
    gla_w_gate = np.asarray(inputs["gla_w_gate"], np.float32)[0]
    gla_gb = np.asarray(inputs["gla_gate_bias"], np.float32)[0]
    half = []
    for hf in range(2):
        d = {}
        cols = np.concatenate([np.arange(hf * 1024, hf * 1024 + 1024), 2048 + np.arange(hf * 1024, hf * 1024 + 1024),
                               4096 + np.arange(hf * 2048, hf * 2048 + 2048), 8192 + np.arange(hf * 2048, hf * 2048 + 2048),
                               12288 + np.arange(hf * 16, hf * 16 + 16), 12320 + np.arange(hf * 16, hf * 16 + 16)])
        d["gdn_wc"] = np.ascontiguousarray(gdn_w_in[:, :, cols])
        cc = cols[:4096]
        d["gdn_conv"] = np.ascontiguousarray(
            gdn_conv_w[:, :, cc].reshape(2, 4, 32, 128).transpose(0, 3, 2, 1).reshape(2, 128, 128))
        d["gdn_alog"] = f(np.asarray(inputs["gdn_a_log"])[:, hf * 16:(hf + 1) * 16])
        d["gdn_dtb"] = f(np.asarray(inputs["gdn_dt_bias"])[:, hf * 16:(hf + 1) * 16])
        lc = np.concatenate([np.arange(hf * 512, hf * 512 + 512), 1024 + np.arange(hf * 512, hf * 512 + 512),
                             2048 + np.arange(hf * 1024, hf * 1024 + 1024), 4096 + np.arange(hf * 1024, hf * 1024 + 1024),
                             6144 + np.arange(16)])
        d["gla_wc"] = np.ascontiguousarray(gla_w_in[:, lc])
        d["gla_wga"] = np.ascontiguousarray(
            np.concatenate([gla_w_gate[:, hf * 512:(hf + 1) * 512], gla_gb[None, hf * 512:(hf + 1) * 512]], 0))
        Lnp = s5_layout_np(*[np.asarray(inputs[k], np.float32)[0] for k in
                             ("s5_a_re", "s5_a_im", "s5_log_dt", "s5_b_re", "s5_b_im", "s5_c_re", "s5_c_im", "s5_d")], hf)
        for k in S5_KEYS:
            d["s5L_" + k] = Lnp[k]
        half.append(d)
    in_maps = []
    for c in range(8):
        b, hf = c // 2, c % 2
        m = dict(rep)
        m.update(consts)
        m.update(half[hf])
        m["x"] = np.ascontiguousarray(x[b, hf * TC:(hf + 1) * TC])
        isel = np.zeros((2, 128, 128), np.float32)
        isel[hf] = np.eye(128, dtype=np.float32)
        m["isel"] = isel
        in_maps.append(m)
    return in_maps


_CACHE = {}


def run_model(inputs, SEQ, trace=False):
    from concourse.bass_utils import run_bass_kernel_spmd
    if SEQ not in _CACHE:
        _CACHE[SEQ] = build_program(SEQ)[0]
    nc = _CACHE[SEQ]
    in_maps = make_in_maps(inputs, SEQ)
    res = run_bass_kernel_spmd(nc, in_maps, core_ids=list(range(8)))
    TC = SEQ // 2
    out = np.empty((4, SEQ, D), np.float32)
    for c in range(8):
        b, hf = c // 2, c % 2
        out[b, hf * TC:(hf + 1) * TC] = res.results[c]["y"]
    return out


def kernel(**inputs):
    return run_model(inputs, 8192)
```

```python
import contextlib
import numpy as np
import concourse.bass as bass
import concourse.mybir as mybir

F32 = mybir.dt.float32
BF16 = mybir.dt.bfloat16
AF = mybir.ActivationFunctionType
ALU = mybir.AluOpType
AX = mybir.AxisListType


class Buf:
    __slots__ = ("h", "name", "w", "r", "dsem")

    def __init__(self, h, name):
        self.h = h
        self.name = name
        self.w = {}
        self.r = {}
        self.dsem = None


class Prog:
    ENG = ("pe", "act", "dve", "pool", "sp")

    def __init__(self, nc, n_dma_sems=80, same_engine_sync=True):
        self.nc = nc
        self.es = contextlib.ExitStack()
        self.eng = {"pe": nc.tensor, "act": nc.scalar, "dve": nc.vector, "pool": nc.gpsimd, "sp": nc.sync}
        self.sem = {}
        self.cnt = {}
        for e in self.ENG:
            self.sem[e] = self.es.enter_context(nc.semaphore("prog_" + e))
            self.cnt[e] = 0
        self.free_dsems = []
        for i in range(n_dma_sems):
            k = "d%d" % i
            self.sem[k] = self.es.enter_context(nc.semaphore("dma_" + k))
            self.cnt[k] = 0
            self.free_dsems.append(k)
        self.seen = {e: {} for e in self.ENG}
        self.same_engine_sync = same_engine_sync
        self.n_wait = 0
        self.n_inst = 0
        self.stage_bufs = []
        self.sem["cc"] = self.es.enter_context(nc.semaphore("cc_sem"))
        self.cnt["cc"] = 0

    def uniq(self, base):
        self._u = getattr(self, '_u', 0) + 1
        return '%s_%d' % (base, self._u)

    def sb(self, stack, name, shape, dt):
        b = Buf(stack.enter_context(self.nc.sbuf_tensor(name, list(shape), dt)), name)
        self.stage_bufs.append(b)
        return b

    def view(self, h, name):
        b = Buf(h, name)
        self.stage_bufs.append(b)
        return b

    def ps(self, stack, name, shape, dt=F32):
        return Buf(stack.enter_context(self.nc.psum_tensor(name, list(shape), dt)), name)

    def dram(self, name, shape, dt, kind="Internal"):
        return Buf(self.nc.dram_tensor(name, list(shape), dt, kind=kind).ap(), name)

    def release(self, bufs):
        for b in bufs:
            if b.dsem is not None:
                self.free_dsems.append(b.dsem)
                b.dsem = None

    def _wait(self, e, key, idx):
        if key == e and (e == "pe" or not self.same_engine_sync):
            return
        if self.seen[e].get(key, 0) >= idx:
            return
        self.seen[e][key] = idx
        self.eng[e].wait_ge(self.sem[key], idx)
        self.n_wait += 1

    def _deps(self, e, w, r):
        for b in r:
            for k, i in b.w.items():
                self._wait(e, k, i)
        for b in w:
            for k, i in b.w.items():
                self._wait(e, k, i)
            for k, i in b.r.items():
                self._wait(e, k, i)

    def op(self, e, fn, w=(), r=()):
        self._deps(e, w, r)
        inst = fn(self.eng[e])
        self.cnt[e] += 1
        idx = self.cnt[e]
        inst.then_inc(self.sem[e], 1)
        for b in w:
            b.w[e] = idx
        for b in r:
            b.r[e] = idx
        self.n_inst += 1
        return inst

    def dma(self, q, out_ap, in_ap, w, r, owner=None):
        self._deps(q, [w], [r])
        if owner is None:
            owner = w
        if owner.dsem is None:
            owner.dsem = self.free_dsems.pop()
        k = owner.dsem
        inst = self.eng[q].dma_start(out=out_ap, in_=in_ap)
        self.cnt[k] += 16
        inst.then_inc(self.sem[k], 16)
        w.w[k] = self.cnt[k]
        r.r[k] = self.cnt[k]
        self.n_inst += 1
        return inst

    def barrier(self):
        keys = [k for k in self.cnt if self.cnt[k] > 0]
        for e in self.ENG:
            for k in keys:
                if self.seen[e].get(k, 0) >= self.cnt[k]:
                    continue
                self.seen[e][k] = self.cnt[k]
                self.eng[e].wait_ge(self.sem[k], self.cnt[k])
                self.n_wait += 1

    def end_stage(self):
        self.barrier()
        self.release(self.stage_bufs)
        self.stage_bufs = []

    def collective(self, kind, src, dst, rg, src_ap=None, dst_ap=None):
        self._deps("pool", [dst], [src])
        src_ap = src.h if src_ap is None else src_ap
        dst_ap = dst.h if dst_ap is None else dst_ap
        inst = self.nc.gpsimd.collective_compute(kind, ALU.bypass, ins=[src_ap.opt()], outs=[dst_ap.opt()],
                                                 replica_groups=rg)
        self.cnt["cc"] += 1
        inst.then_inc(self.sem["cc"], 1)
        dst.w["cc"] = self.cnt["cc"]
        src.r["cc"] = self.cnt["cc"]
        self.n_inst += 1

    def gather_pairs(self, src, dstP, rows, PR):
        for p in range(rows // PR):
            self.collective("AllGather", src, dstP, [[0, 1], [2, 3], [4, 5], [6, 7]],
                            src_ap=src.h[p * PR:(p + 1) * PR, :], dst_ap=dstP.h[p * 2 * PR:(p + 1) * 2 * PR, :])

    def finish(self):
        self.barrier()
        self.es.close()


def bc(ap, shape):
    return ap.unsqueeze(len(ap.shape)).to_broadcast(list(shape))


class H:
    def __init__(self, P):
        self.P = P

    def mm(self, out, lhsT, rhs, w, r, start=True, stop=True):
        return self.P.op("pe", lambda e: e.matmul(out, lhsT, rhs, start=start, stop=stop), w=w, r=r)

    def tr(self, out, in_, ident, w, r):
        return self.P.op("pe", lambda e: e.transpose(out=out, in_=in_, identity=ident), w=w, r=r)

    def act(self, out, in_, func, w, r, scale=1.0, bias=None):
        if bias is None:
            return self.P.op("act", lambda e: e.activation(out=out, in_=in_, func=func, scale=scale), w=w, r=r)
        return self.P.op("act", lambda e: e.activation(out=out, in_=in_, func=func, scale=scale, bias=bias), w=w, r=r)

    def tt(self, eng, out, in0, in1, op, w, r):
        return self.P.op(eng, lambda e: e.tensor_tensor(out=out, in0=in0, in1=in1, op=op), w=w, r=r)

    def ts(self, eng, out, in0, s1, op0, w, r, s2=None, op1=None):
        if op1 is None:
            return self.P.op(eng, lambda e: e.tensor_scalar(out=out, in0=in0, scalar1=s1, scalar2=None, op0=op0), w=w, r=r)
        return self.P.op(eng, lambda e: e.tensor_scalar(out=out, in0=in0, scalar1=s1, scalar2=s2, op0=op0, op1=op1), w=w, r=r)

    def stt(self, out, in0, scalar, in1, op0, op1, w, r):
        return self.P.op("dve", lambda e: e.scalar_tensor_tensor(out=out, in0=in0, scalar=scalar, in1=in1, op0=op0, op1=op1), w=w, r=r)

    def cp(self, eng, out, in_, w, r):
        if eng == "act":
            return self.P.op("act", lambda e: e.activation(out=out, in_=in_, func=AF.Copy), w=w, r=r)
        return self.P.op(eng, lambda e: e.tensor_copy(out=out, in_=in_), w=w, r=r)

    def memset(self, eng, out, val, w):
        return self.P.op(eng, lambda e: e.memset(out, val), w=w)

import math
D = 2048
RMS_EPS = 1e-6

FF = 5632
LN_EPS = 1e-5


def load_consts(P, st, ident_d):
    idt = P.sb(st, P.uniq("identsb"), [128, 128], F32)
    P.dma("sp", idt.h[:], ident_d.h[:, :], w=idt, r=ident_d)
    epsT = P.sb(st, P.uniq("epsT"), [128, 1], F32)
    P.op("dve", lambda e: e.memset(epsT.h[:], LN_EPS), w=[epsT])
    return idt, epsT


def layer_norm_tile(P, z, Gt, Bt, epsT, tmp):
    stats, mv, rstd, nmr = tmp["stats"], tmp["mv"], tmp["rstd"], tmp["nmr"]
    for c in range(4):
        P.op("dve", lambda e: e.bn_stats(out=stats.h[:, c * 6:(c + 1) * 6], in_=z.h[:, c * 512:(c + 1) * 512]),
             w=[stats], r=[z])
    P.op("dve", lambda e: e.bn_aggr(out=mv.h[:], in_=stats.h[:]), w=[mv], r=[stats])
    P.op("act", lambda e: e.activation(out=rstd.h[:], in_=mv.h[:, 1:2], func=AF.Sqrt, bias=epsT.h[:, 0:1], scale=1.0),
         w=[rstd], r=[mv, epsT])
    P.op("dve", lambda e: e.reciprocal(out=rstd.h[:], in_=rstd.h[:]), w=[rstd], r=[rstd])
    P.op("dve", lambda e: e.tensor_scalar(out=nmr.h[:], in0=mv.h[:, 0:1], scalar1=rstd.h[:, 0:1], scalar2=-1.0,
                                          op0=ALU.mult, op1=ALU.mult), w=[nmr], r=[mv, rstd])
    P.op("act", lambda e: e.activation(out=z.h[:], in_=z.h[:], func=AF.Identity, scale=rstd.h[:, 0:1],
                                       bias=nmr.h[:, 0:1]), w=[z], r=[z, rstd, nmr])
    P.op("dve", lambda e: e.tensor_tensor(out=z.h[:], in0=z.h[:], in1=Gt.h[:], op=ALU.mult), w=[z], r=[z, Gt])
    P.op("dve", lambda e: e.tensor_tensor(out=z.h[:], in0=z.h[:], in1=Bt.h[:], op=ALU.add), w=[z], r=[z, Bt])


def ln_tmp(P, st, tag):
    return {"stats": P.sb(st, tag + "stats", [128, 24], F32), "mv": P.sb(st, tag + "mv", [128, 2], F32),
            "rstd": P.sb(st, tag + "rstd", [128, 1], F32), "nmr": P.sb(st, tag + "nmr", [128, 1], F32)}


def ffn_stage(P, x_in, x_out, w_up, w_down, ln_g, ln_b, ident_d, T, alpha, tag="f"):
    NB = T // 512
    with contextlib.ExitStack() as st:
        idt, epsT = load_consts(P, st, ident_d)
        Gt = P.sb(st, tag + "G", [128, D], F32)
        Bt = P.sb(st, tag + "B", [128, D], F32)
        P.dma("sp", Gt.h[:], ln_g.h.partition_broadcast(128), w=Gt, r=ln_g)
        P.dma("sp", Bt.h[:], ln_b.h.partition_broadcast(128), w=Bt, r=ln_b)
        xs = [P.sb(st, tag + "xs%d" % i, [128, D], F32) for i in range(2)]
        xT = P.sb(st, tag + "xT", [128, 16, 512], BF16)
        hT = P.sb(st, tag + "hT", [128, 44, 512], BF16)
        wg = [P.sb(st, tag + "wg%d" % i, [128, 16, 256], BF16) for i in range(3)]
        wu = [P.sb(st, tag + "wu%d" % i, [128, 16, 256], BF16) for i in range(3)]
        wd = [P.sb(st, tag + "wd%d" % i, [128, 11, 512], BF16) for i in range(2)]
        sg = [P.sb(st, tag + "sg%d" % i, [128, 512], F32) for i in range(2)]
        z = [P.sb(st, tag + "z%d" % i, [128, D], F32) for i in range(4)]
        tmps = [ln_tmp(P, st, tag + "t%d" % i) for i in range(2)]
        bank = [P.ps(st, tag + "bank%d" % i, [128, 512]) for i in range(8)]
        allb = [idt, epsT, Gt, Bt, xT, hT] + xs + wg + wu + wd + sg + z
        for t in tmps:
            allb += list(t.values())

        nwl = 0
        nwd = 0
        for blk in range(NB):
            t0 = blk * 512
            for s in range(4):
                xb = xs[s % 2]
                P.dma("sp", xb.h[:], x_in.h[t0 + s * 128:t0 + (s + 1) * 128, :], w=xb, r=x_in)
                for q in range(4):
                    bk = bank[4 + (s * 4 + q) % 4]
                    for i in range(4):
                        dk = q * 4 + i
                        P.op("pe", lambda e: e.transpose(out=bk.h[:, i * 128:(i + 1) * 128],
                                                         in_=xb.h[:, dk * 128:(dk + 1) * 128], identity=idt.h[:]),
                             w=[bk], r=[xb, idt])
                    eng = "act" if q % 2 == 0 else "dve"
                    src = bk.h[:].rearrange("p (a b) -> p a b", a=4)
                    dst = xT.h[:, q * 4:(q + 1) * 4, s * 128:(s + 1) * 128]
                    if eng == "act":
                        P.op("act", lambda e: e.activation(out=dst, in_=src, func=AF.Copy), w=[xT], r=[bk])
                    else:
                        P.op("dve", lambda e: e.tensor_copy(out=dst, in_=src), w=[xT], r=[bk])
            for g in range(22):
                sl = nwl % 3
                nwl += 1
                for k0 in range(0, 16, 4):
                    P.dma("pool", wg[sl].h[:, k0:k0 + 4, :],
                          w_up.h[k0 * 128:(k0 + 4) * 128, g * 256:(g + 1) * 256].rearrange("(k p) f -> p k f", p=128),
                          w=wg[sl], r=w_up)
                    P.dma("pool", wu[sl].h[:, k0:k0 + 4, :],
                          w_up.h[k0 * 128:(k0 + 4) * 128, FF + g * 256:FF + (g + 1) * 256].rearrange(
                              "(k p) f -> p k f", p=128),
                          w=wu[sl], r=w_up)
                for c in range(2):
                    j = g * 2 + c
                    pg = bank[(j % 2) * 2]
                    pu = bank[(j % 2) * 2 + 1]
                    for dk in range(16):
                        P.op("pe", lambda e: e.matmul(pg.h[:], wg[sl].h[:, dk, c * 128:(c + 1) * 128], xT.h[:, dk, :],
                                                      start=(dk == 0), stop=(dk == 15)), w=[pg], r=[wg[sl], xT])
                    for dk in range(16):
                        P.op("pe", lambda e: e.matmul(pu.h[:], wu[sl].h[:, dk, c * 128:(c + 1) * 128], xT.h[:, dk, :],
                                                      start=(dk == 0), stop=(dk == 15)), w=[pu], r=[wu[sl], xT])
                    sgb = sg[j % 2]
                    P.op("act", lambda e: e.activation(out=sgb.h[:], in_=pg.h[:], func=AF.Silu), w=[sgb], r=[pg])
                    P.op("dve", lambda e: e.tensor_tensor(out=hT.h[:, j, :], in0=sgb.h[:], in1=pu.h[:], op=ALU.mult),
                         w=[hT], r=[sgb, pu])
            for s in range(4):
                P.dma("sp", z[s].h[:], x_in.h[t0 + s * 128:t0 + (s + 1) * 128, :], w=z[s], r=x_in)
                P.op("act", lambda e: e.activation(out=z[s].h[:], in_=z[s].h[:], func=AF.Copy, scale=float(alpha)),
                     w=[z[s]], r=[z[s]])
            for dn in range(4):
                pb = [bank[(dn % 2) * 4 + s] for s in range(4)]
                for fg in range(4):
                    sl = nwd % 2
                    nwd += 1
                    for (a, n) in ((0, 4), (4, 4), (8, 3)):
                        r0 = (fg * 11 + a) * 128
                        P.dma("pool", wd[sl].h[:, a:a + n, :],
                              w_down.h[r0:r0 + n * 128, dn * 512:(dn + 1) * 512].rearrange("(k p) f -> p k f", p=128),
                              w=wd[sl], r=w_down)
                    for i in range(11):
                        fk = fg * 11 + i
                        for s in range(4):
                            P.op("pe", lambda e: e.matmul(pb[s].h[:], hT.h[:, fk, s * 128:(s + 1) * 128],
                                                          wd[sl].h[:, i, :], start=(fk == 0), stop=(fk == 43)),
                                 w=[pb[s]], r=[hT, wd[sl]])
                for s in range(4):
                    zc = z[s].h[:, dn * 512:(dn + 1) * 512]
                    P.op("dve", lambda e: e.scalar_tensor_tensor(out=zc, in0=pb[s].h[:], scalar=0.5, in1=zc,
                                                                 op0=ALU.mult, op1=ALU.add), w=[z[s]], r=[pb[s], z[s]])
            for s in range(4):
                layer_norm_tile(P, z[s], Gt, Bt, epsT, tmps[s % 2])
                P.dma("sp", x_out.h[t0 + s * 128:t0 + (s + 1) * 128, :], z[s].h[:], w=x_out, r=z[s], owner=z[s])
        P.end_stage()


NEGV = -30000.0


def gdn_consts_np():
    t = np.arange(128)
    same = (t[:, None] // 64) == (t[None, :] // 64)
    c = {}
    c["ident"] = np.eye(128, dtype=np.float32)
    c["ucs"] = (same & (t[:, None] <= t[None, :])).astype(np.float32)
    c["vsame"] = same.astype(np.float32)
    cind = np.zeros((128, 2, 128), np.float32)
    cind[:64, 0, :] = 1.0
    cind[64:, 1, :] = 1.0
    c["cind"] = cind.reshape(128, 256)
    c["negA"] = np.where(same & (t[None, :] >= t[:, None]), 0.0, NEGV).astype(np.float32)
    c["negL"] = np.where(same & (t[None, :] > t[:, None]), 0.0, NEGV).astype(np.float32)
    sel = np.zeros((32, 32, 128), np.float32)
    for h in range(32):
        sel[h, h, :] = 1.0
    c["sel"] = sel.reshape(32, 32 * 128)
    return c


def gdn_proj_stage(P, x_in, QKV, Zd, SSQd, BAd, w_c, conv_c, C, T, tag="gp", rm=lambda t: t):
    Hh = H(P)
    SBK = 512
    NSB = T // SBK
    with contextlib.ExitStack() as st:
        sbuf = lambda n, s, d=F32: P.sb(st, tag + n, s, d)
        idt = sbuf("idt", [128, 128])
        cw = sbuf("cw", [128, 128])
        ones = sbuf("ones", [128, 1])
        P.dma("sp", idt.h[:], C["ident"].h[:, :], w=idt, r=C["ident"])
        P.dma("sp", cw.h[:], conv_c.h[:, :], w=cw, r=conv_c)
        Hh.memset("dve", ones.h[:], 1.0, [ones])
        xs = [sbuf("xs%d" % i, [128, D]) for i in range(2)]
        xT = sbuf("xT", [128, 16, SBK], BF16)
        wf = [sbuf("wf%d" % i, [128, 16, 256], BF16) for i in range(3)]
        wba = sbuf("wba", [128, 16, 32], BF16)
        pre = [sbuf("pre%d" % i, [128, SBK + 3]) for i in range(2)]
        acc = [sbuf("acc%d" % i, [128, SBK]) for i in range(2)]
        sgl = [sbuf("sgl%d" % i, [128, SBK]) for i in range(2)]
        sq = [sbuf("sq%d" % i, [128, SBK]) for i in range(2)]
        ob = [sbuf("ob%d" % i, [128, SBK], BF16) for i in range(2)]
        carry = sbuf("carry", [128, 32, 3])
        baT = sbuf("baT", [32, SBK])
        zs4 = sbuf("zs4", [128, 4, 2048], BF16)
        ssq_sb = sbuf("ssqsb", [128, 64])
        Hh.memset("pool", carry.h[:], 0.0, [carry])
        banks = [P.ps(st, tag + "bank%d" % i, [128, 512]) for i in range(7)]
        smallbank = st.enter_context(P.nc.psum_tensor(tag + "smallbank", [128, 512], F32))
        ps_ssq = P.view(smallbank[:, 0:64], "ps_ssq")
        bctr = [0]

        def nb():
            b = banks[bctr[0] % 7]
            bctr[0] += 1
            return b

        for sb in range(NSB):
            t0 = sb * SBK
            for s in range(4):
                xb = xs[s % 2]
                P.dma("sp", xb.h[:], x_in.h[rm(t0 + s * 128):rm(t0 + s * 128) + 128, :], w=xb, r=x_in)
                for q in range(4):
                    bk = nb()
                    for i in range(4):
                        dk = q * 4 + i
                        Hh.tr(bk.h[:, i * 128:(i + 1) * 128], xb.h[:, dk * 128:(dk + 1) * 128], idt.h[:], [bk], [xb, idt])
                    Hh.cp("act" if q % 2 == 0 else "dve", xT.h[:, q * 4:(q + 1) * 4, s * 128:(s + 1) * 128],
                          bk.h[:].rearrange("p (a b) -> p a b", a=4), [xT], [bk])
            def load_w(idx, slot):
                col = idx * 256 if idx < 16 else 4096 + (idx - 16) * 256
                for k0 in range(0, 16, 4):
                    P.dma("pool", wf[slot].h[:, k0:k0 + 4, :],
                          w_c.h[k0 * 128:(k0 + 4) * 128, col:col + 256].rearrange("(k p) f -> p k f", p=128),
                          w=wf[slot], r=w_c)
            pend = []

            def flush_ssq():
                while pend:
                    sq__, fc__ = pend.pop(0)
                    for s_ in range(4):
                        Hh.mm(ps_ssq.h[:, s_ * 16 + fc__:s_ * 16 + fc__ + 1], sq__.h[:, s_ * 128:(s_ + 1) * 128],
                              ones.h[:, 0:1], [ps_ssq], [sq__, ones])
            load_w(0, 0)
            load_w(1, 1)
            for fp in range(16):
                sl = fp % 3
                load_w(fp + 2, (fp + 2) % 3)
                for c in range(2):
                    fc = fp * 2 + c
                    bk = nb()
                    for dk in range(16):
                        Hh.mm(bk.h[:], wf[sl].h[:, dk, c * 128:(c + 1) * 128], xT.h[:, dk, :], [bk], [wf[sl], xT],
                              start=(dk == 0), stop=(dk == 15))
                    flush_ssq()
                    pr = pre[fc % 2]
                    ac = acc[fc % 2]
                    obb = ob[fc % 2]
                    Hh.cp("dve", pr.h[:, 0:3], carry.h[:, fc, :], [pr], [carry])
                    Hh.cp("act", pr.h[:, 3:SBK + 3], bk.h[:], [pr], [bk])
                    Hh.cp("dve", carry.h[:, fc, :], pr.h[:, SBK:SBK + 3], [carry], [pr])
                    Hh.ts("dve", ac.h[:], pr.h[:, 0:SBK], cw.h[:, fc * 4:fc * 4 + 1], ALU.mult, [ac], [pr, cw])
                    for j in range(1, 4):
                        Hh.stt(ac.h[:], pr.h[:, j:j + SBK], cw.h[:, fc * 4 + j:fc * 4 + j + 1], ac.h[:], ALU.mult, ALU.add,
                               [ac], [pr, cw, ac])
                    if fc < 16:
                        sg_ = sgl[fc % 2]
                        sq_ = sq[fc % 2]
                        Hh.act(sg_.h[:], ac.h[:], AF.Silu, [sg_], [ac])
                        Hh.cp("dve", obb.h[:], sg_.h[:], [obb], [sg_])
                        Hh.act(sq_.h[:], sg_.h[:], AF.Square, [sq_], [sg_])
                        pend.append((sq_, fc))
                    else:
                        Hh.act(obb.h[:], ac.h[:], AF.Silu, [obb], [ac])
                    P.dma("sp", QKV.h[fc * 128:(fc + 1) * 128, t0:t0 + SBK], obb.h[:], w=QKV, r=obb, owner=obb)
            for k0 in range(0, 16, 4):
                P.dma("pool", wba.h[:, k0:k0 + 4, :],
                      w_c.h[k0 * 128:(k0 + 4) * 128, 6144:6176].rearrange("(k p) f -> p k f", p=128), w=wba, r=w_c)
            bk = nb()
            for dk in range(16):
                Hh.mm(bk.h[0:32, :], wba.h[:, dk, :], xT.h[:, dk, :], [bk], [wba, xT], start=(dk == 0), stop=(dk == 15))
            Hh.cp("act", baT.h[:], bk.h[0:32, :], [baT], [bk])
            P.dma("sp", BAd.h[:, t0:t0 + SBK], baT.h[:], w=BAd, r=baT, owner=baT)
            flush_ssq()
            Hh.cp("dve", ssq_sb.h[:], ps_ssq.h[:], [ssq_sb], [ps_ssq])
            for s in range(4):
                P.dma("sp", SSQd.h[t0 + s * 128:t0 + (s + 1) * 128, :], ssq_sb.h[:, s * 16:(s + 1) * 16], w=SSQd, r=ssq_sb,
                      owner=ssq_sb)
            for zg in range(8):
                sl = (16 + zg) % 3
                if zg + 2 < 8:
                    load_w(16 + zg + 2, (16 + zg + 2) % 3)
                for s in range(4):
                    bk = nb()
                    for dk in range(16):
                        Hh.mm(bk.h[:, 0:256], xT.h[:, dk, s * 128:(s + 1) * 128], wf[sl].h[:, dk, :], [bk], [wf[sl], xT],
                              start=(dk == 0), stop=(dk == 15))
                    Hh.act(zs4.h[:, s, zg * 256:(zg + 1) * 256], bk.h[:, 0:256], AF.Silu, [zs4], [bk])
            for s in range(4):
                P.dma("sp", Zd.h[t0 + s * 128:t0 + (s + 1) * 128, :], zs4.h[:, s, :], w=Zd, r=zs4, owner=zs4)
        P.end_stage()


def gdn_rec_stage(P, QKV, Zd, SSQd, BAd, og_out, alog_c, dtb_c, normw, C, T, tag="g"):
    Hh = H(P)
    SBK = 512
    NSB = T // SBK
    with contextlib.ExitStack() as st:
        sbuf = lambda n, s, d=F32: P.sb(st, tag + n, s, d)
        idt = sbuf("idt", [128, 128])
        idb = sbuf("idb", [128, 128], BF16)
        ucs = sbuf("ucs", [128, 128])
        vsame = sbuf("vsame", [128, 128])
        cind = sbuf("cind", [128, 256])
        negA = sbuf("negA", [128, 128])
        negL = sbuf("negL", [128, 128])
        sel = sbuf("sel", [32, 32 * 128])
        DTB = sbuf("DTB", [128, 16])
        NEGA = sbuf("NEGA", [128, 16])
        nw1 = sbuf("nw1", [128, 128])
        for (dst, src) in ((idt, C["ident"]), (ucs, C["ucs"]), (vsame, C["vsame"]), (cind, C["cind"]),
                           (negA, C["negA"]), (negL, C["negL"]), (sel, C["sel"])):
            P.dma("sp", dst.h[:], src.h[:, :], w=dst, r=src)
        P.dma("sp", DTB.h[:], dtb_c.h.partition_broadcast(128), w=DTB, r=dtb_c)
        P.dma("sp", NEGA.h[:], alog_c.h.partition_broadcast(128), w=NEGA, r=alog_c)
        P.dma("sp", nw1.h[:], normw.h.partition_broadcast(128), w=nw1, r=normw)
        Hh.cp("dve", idb.h[:], idt.h[:], [idb], [idt])
        Hh.act(NEGA.h[:], NEGA.h[:], AF.Exp, [NEGA], [NEGA])
        Hh.ts("dve", NEGA.h[:], NEGA.h[:], -1.0, ALU.mult, [NEGA], [NEGA])

        S = sbuf("S", [128, 16, 128])
        Sb = sbuf("Sb", [128, 16, 128], BF16)
        sm = {n: sbuf("sm_" + n, [128, 16]) for n in
              ("eb", "lb", "beta", "t1", "e1", "sp", "g", "gcs", "d", "kd", "a", "c1", "c2", "sa", "rq16", "lnrk16",
               "colb", "ssqo", "rms")}
        ba = sbuf("ba", [128, 32])
        lnr = sbuf("lnr", [128, 16])
        rqk = sbuf("rqk", [128, 16])
        R = sbuf("R", [128, 32])
        RT = sbuf("RT", [32, 128])
        EG = sbuf("EG", [128, 32])
        kba = sbuf("kba", [128, 16, 128])
        kdec = sbuf("kdec", [128, 16, 128], BF16)
        vb = sbuf("vb", [128, 16, 128])
        attnT = sbuf("attnT", [128, 16, 128], BF16)
        slot = []
        for i in range(2):
            slot.append({n: sbuf("s%d_%s" % (i, n), [128, 4, 128]) for n in
                         ("EL", "EA", "X0", "X1", "Y0", "Y1", "P0", "P1")})
        u = sbuf("u", [128, 16, 128])
        wT = sbuf("wT", [128, 16, 128], BF16)
        vnew = sbuf("vnew", [128, 16, 128], BF16)
        o = sbuf("o", [128, 16, 128])
        obf = sbuf("obf", [128, 16, 128], BF16)
        tmpS = [sbuf("tmpS%d" % i, [128, 4, 128]) for i in range(2)]
        epsr = sbuf("epsr", [128, 1])
        Hh.memset("dve", epsr.h[:], RMS_EPS, [epsr])
        Hh.memset("dve", S.h[:], 0.0, [S])
        Hh.memset("pool", Sb.h[:], 0.0, [Sb])

        banks = [P.ps(st, tag + "bank%d" % i, [128, 512]) for i in range(7)]
        smallbank = st.enter_context(P.nc.psum_tensor(tag + "smallbank", [128, 512], F32))
        ps_ba = P.view(smallbank[:, 32:64], "ps_ba")
        ps_g = P.view(smallbank[:, 64:128], "ps_g")
        ps_rt = P.view(smallbank[:, 128:256], "ps_rt")
        bctr = [0]

        def nb():
            b = banks[bctr[0] % 7]
            bctr[0] += 1
            return b


        qbuf = [sbuf("qkvT%d" % i, [128, 32, SBK], BF16) for i in range(2)]
        baT = sbuf("baT", [32, SBK])
        zsb = [sbuf("zsb%d" % i, [128, 2048], BF16) for i in range(2)]
        ssqb = [sbuf("ssqb%d" % i, [128, 16]) for i in range(2)]
        for sb in range(NSB):
            t0 = sb * SBK
            qkvT = qbuf[sb % 2]
            for c0 in range(0, 32, 4):
                P.dma("sp", qkvT.h[:, c0:c0 + 4, :],
                      QKV.h[c0 * 128:(c0 + 4) * 128, t0:t0 + SBK].rearrange("(c p) t -> p c t", p=128), w=qkvT, r=QKV)
            P.dma("sp", baT.h[:], BAd.h[:, t0:t0 + SBK], w=baT, r=BAd)
            for s in range(4):
                tb = t0 + s * 128
                zs_t = zsb[s % 2]
                ssq_t = ssqb[s % 2]
                P.dma("sp", zs_t.h[:], Zd.h[tb:tb + 128, :], w=zs_t, r=Zd)
                P.dma("sp", ssq_t.h[:], SSQd.h[tb:tb + 128, :], w=ssq_t, r=SSQd)
                cs = slice(s * 128, (s + 1) * 128)
                Hh.tr(ps_ba.h[:, 0:32], baT.h[:, cs], idt.h[0:32, 0:32], [ps_ba], [baT, idt])
                Hh.cp("dve", ba.h[:], ps_ba.h[:, 0:32], [ba], [ps_ba])
                m = sm
                Hh.act(m["eb"].h[:], ba.h[:, 0:16], AF.Exp, [m["eb"]], [ba], scale=-1.0)
                Hh.act(m["lb"].h[:], m["eb"].h[:], AF.Ln, [m["lb"]], [m["eb"]], bias=1.0)
                Hh.act(m["beta"].h[:], m["lb"].h[:], AF.Exp, [m["beta"]], [m["lb"]], scale=-1.0)
                Hh.tt("dve", m["t1"].h[:], ba.h[:, 16:32], DTB.h[:], ALU.add, [m["t1"]], [ba, DTB])
                Hh.act(m["e1"].h[:], m["t1"].h[:], AF.Exp, [m["e1"]], [m["t1"]])
                Hh.act(m["sp"].h[:], m["e1"].h[:], AF.Ln, [m["sp"]], [m["e1"]], bias=1.0)
                Hh.tt("dve", m["g"].h[:], m["sp"].h[:], NEGA.h[:], ALU.mult, [m["g"]], [m["sp"], NEGA])
                Hh.mm(ps_g.h[:, 0:16], ucs.h[:], m["g"].h[:], [ps_g], [ucs, m["g"]])
                Hh.mm(ps_g.h[:, 16:32], vsame.h[:], m["g"].h[:], [ps_g], [vsame, m["g"]])
                Hh.mm(ps_g.h[:, 32:48], cind.h[:, 0:128], m["g"].h[:], [ps_g], [cind, m["g"]])
                Hh.mm(ps_g.h[:, 48:64], cind.h[:, 128:256], m["g"].h[:], [ps_g], [cind, m["g"]])
                Hh.cp("dve", m["gcs"].h[:], ps_g.h[:, 0:16], [m["gcs"]], [ps_g])
                Hh.tt("dve", m["d"].h[:], ps_g.h[:, 16:32], m["gcs"].h[:], ALU.subtract, [m["d"]], [ps_g, m["gcs"]])
                Hh.act(m["kd"].h[:], m["d"].h[:], AF.Exp, [m["kd"]], [m["d"]])
                Hh.act(m["a"].h[:], m["gcs"].h[:], AF.Exp, [m["a"]], [m["gcs"]])
                Hh.act(EG.h[:], ps_g.h[:, 32:64], AF.Exp, [EG], [ps_g])
                Hh.act(lnr.h[:], ssq_t.h[:], AF.Ln, [lnr], [ssq_t, epsr], bias=epsr.h[:, 0:1])
                Hh.ts("dve", lnr.h[:], lnr.h[:], -0.5, ALU.mult, [lnr], [lnr])
                Hh.act(rqk.h[:, 0:8], lnr.h[:, 0:8], AF.Exp, [rqk], [lnr], bias=-0.5 * math.log(128.0))
                Hh.act(rqk.h[:, 8:16], lnr.h[:, 8:16], AF.Exp, [rqk], [lnr])
                v2 = lambda ap: ap.rearrange("p (a b) -> p a b", b=2)
                Hh.cp("dve", v2(m["rq16"].h[:]), bc(rqk.h[:, 0:8], [128, 8, 2]), [m["rq16"]], [rqk])
                Hh.cp("dve", v2(m["lnrk16"].h[:]), bc(lnr.h[:, 8:16], [128, 8, 2]), [m["lnrk16"]], [lnr])
                Hh.tt("dve", m["c1"].h[:], m["beta"].h[:], m["a"].h[:], ALU.mult, [m["c1"]], [m["beta"], m["a"]])
                Hh.tt("dve", v2(m["c1"].h[:]), v2(m["c1"].h[:]), bc(rqk.h[:, 8:16], [128, 8, 2]), ALU.mult, [m["c1"]],
                      [m["c1"], rqk])
                Hh.tt("dve", v2(m["c2"].h[:]), v2(m["kd"].h[:]), bc(rqk.h[:, 8:16], [128, 8, 2]), ALU.mult, [m["c2"]],
                      [m["kd"], rqk])
                Hh.tt("dve", m["sa"].h[:], m["rq16"].h[:], m["a"].h[:], ALU.mult, [m["sa"]], [m["rq16"], m["a"]])
                Hh.cp("dve", R.h[:, 0:16], m["gcs"].h[:], [R], [m["gcs"]])
                Hh.tt("dve", R.h[:, 16:32], m["gcs"].h[:], m["lb"].h[:], ALU.subtract, [R], [m["gcs"], m["lb"]])
                Hh.tt("dve", R.h[:, 16:32], R.h[:, 16:32], m["lnrk16"].h[:], ALU.add, [R], [R, m["lnrk16"]])
                Hh.tt("dve", m["colb"].h[:], m["lnrk16"].h[:], m["gcs"].h[:], ALU.subtract, [m["colb"]],
                      [m["lnrk16"], m["gcs"]])
                Hh.tr(ps_rt.h[0:32, :], R.h[:], idt.h[:], [ps_rt], [R, idt])
                Hh.cp("act", RT.h[:], ps_rt.h[0:32, :], [RT], [ps_rt])
                bk = nb()
                kv = bk.h[:].bitcast(BF16).rearrange("p (a b) -> p a b", a=8)
                for hk in range(8):
                    Hh.tr(kv[:, hk, :], qkvT.h[:, 8 + hk, cs], idb.h[:], [bk], [qkvT, idb])
                kvb = bc3 = kv.unsqueeze(2).to_broadcast([128, 8, 2, 128])
                v4 = lambda ap: ap.rearrange("p (a b) c -> p a b c", b=2)
                Hh.tt("dve", v4(kba.h[:]), kvb, v4(bc(m["c1"].h[:], [128, 16, 128])), ALU.mult, [kba], [bk, m["c1"]])
                Hh.tt("dve", v4(kdec.h[:]), kvb, v4(bc(m["c2"].h[:], [128, 16, 128])), ALU.mult, [kdec], [bk, m["c2"]])
                for half in range(2):
                    bk = nb()
                    vv = bk.h[:].bitcast(BF16).rearrange("p (a b) -> p a b", a=8)
                    for i in range(8):
                        h = half * 8 + i
                        Hh.tr(vv[:, i, :], qkvT.h[:, 16 + h, cs], idb.h[:], [bk], [qkvT, idb])
                    Hh.tt("dve", vb.h[:, half * 8:(half + 1) * 8, :], vv,
                          bc(m["beta"].h[:, half * 8:(half + 1) * 8], [128, 8, 128]), ALU.mult, [vb], [bk, m["beta"]])
                for gp in range(2):
                    grp = [gp * 2, gp * 2 + 1]
                    sd = {}
                    for gi, mgrp in enumerate(grp):
                        sl_ = slot[gi]
                        sd[mgrp] = sl_
                        bkq = nb()
                        for i in range(2):
                            hk = mgrp * 2 + i
                            Hh.mm(bkq.h[:, i * 128:(i + 1) * 128], qkvT.h[:, 8 + hk, cs], qkvT.h[:, 8 + hk, cs], [bkq], [qkvT])
                            Hh.mm(bkq.h[:, 256 + i * 128:256 + (i + 1) * 128], qkvT.h[:, 8 + hk, cs], qkvT.h[:, hk, cs],
                                  [bkq], [qkvT])
                        bl = nb()
                        ba_ = nb()
                        for i in range(4):
                            h = mgrp * 4 + i
                            Hh.mm(bl.h[:, i * 128:(i + 1) * 128], idt.h[:], negL.h[:], [bl], [idt, negL], start=True, stop=False)
                            Hh.mm(bl.h[:, i * 128:(i + 1) * 128], sel.h[:, (16 + h) * 128:(17 + h) * 128], RT.h[:], [bl],
                                  [sel, RT], start=False, stop=True)
                            Hh.mm(ba_.h[:, i * 128:(i + 1) * 128], idt.h[:], negA.h[:], [ba_], [idt, negA], start=True,
                                  stop=False)
                            Hh.mm(ba_.h[:, i * 128:(i + 1) * 128], sel.h[:, h * 128:(h + 1) * 128], RT.h[:], [ba_], [sel, RT],
                                  start=False, stop=True)
                        for i in range(4):
                            h = mgrp * 4 + i
                            Hh.act(sl_["EL"].h[:, i, :], bl.h[:, i * 128:(i + 1) * 128], AF.Exp, [sl_["EL"]], [bl, m["colb"]],
                                   bias=m["colb"].h[:, h:h + 1])
                            Hh.act(sl_["EA"].h[:, i, :], ba_.h[:, i * 128:(i + 1) * 128], AF.Exp, [sl_["EA"]], [ba_, m["colb"]],
                                   bias=m["colb"].h[:, h:h + 1])
                        kk = bkq.h[:, 0:256].rearrange("p (a c) -> p a c", a=2).unsqueeze(2).to_broadcast([128, 2, 2, 128])
                        kq = bkq.h[:, 256:512].rearrange("p (a c) -> p a c", a=2).unsqueeze(2).to_broadcast([128, 2, 2, 128])
                        Hh.tt("dve", v4(sl_["X0"].h[:]), kk, v4(sl_["EL"].h[:]), ALU.mult, [sl_["X0"]], [bkq, sl_["EL"]])
                        Hh.tt("dve", v4(attnT.h[:, mgrp * 4:(mgrp + 1) * 4, :]), kq, v4(sl_["EA"].h[:]), ALU.mult, [attnT],
                              [bkq, sl_["EA"]])
                        bt_ = nb()
                        for i in range(4):
                            Hh.tr(bt_.h[:, i * 128:(i + 1) * 128], sl_["X0"].h[:, i, :], idt.h[:], [bt_], [sl_["X0"], idt])
                        Hh.cp("act", sl_["Y0"].h[:], bt_.h[:].rearrange("p (a b) -> p a b", a=4), [sl_["Y0"]], [bt_])
                        Hh.tt("dve", sl_["P0"].h[:], idt.h[:].unsqueeze(1).to_broadcast([128, 4, 128]), sl_["X0"].h[:],
                              ALU.subtract, [sl_["P0"]], [idt, sl_["X0"]])
                    for lev in range(5):
                        a_, b_ = lev % 2, (lev + 1) % 2
                        bx, by, bp = {}, {}, {}
                        for mgrp in grp:
                            sl_ = sd[mgrp]
                            X, Y = sl_["X%d" % a_], sl_["Y%d" % a_]
                            by[mgrp] = nb()
                            for i in range(4):
                                Hh.mm(by[mgrp].h[:, i * 128:(i + 1) * 128], X.h[:, i, :], Y.h[:, i, :], [by[mgrp]], [X, Y])
                            if lev < 4:
                                bx[mgrp] = nb()
                                for i in range(4):
                                    Hh.mm(bx[mgrp].h[:, i * 128:(i + 1) * 128], Y.h[:, i, :], X.h[:, i, :], [bx[mgrp]], [X, Y])
                        for mgrp in grp:
                            sl_ = sd[mgrp]
                            Hh.cp("dve", sl_["Y%d" % b_].h[:], by[mgrp].h[:].rearrange("p (a b) -> p a b", a=4),
                                  [sl_["Y%d" % b_]], [by[mgrp]])
                            if lev < 4:
                                Hh.cp("act", sl_["X%d" % b_].h[:], bx[mgrp].h[:].rearrange("p (a b) -> p a b", a=4),
                                      [sl_["X%d" % b_]], [bx[mgrp]])
                        for mgrp in grp:
                            sl_ = sd[mgrp]
                            Yn, Pc = sl_["Y%d" % b_], sl_["P%d" % a_]
                            bp[mgrp] = nb()
                            for i in range(4):
                                Hh.mm(bp[mgrp].h[:, i * 128:(i + 1) * 128], Yn.h[:, i, :], Pc.h[:, i, :], [bp[mgrp]], [Yn, Pc])
                        for mgrp in grp:
                            sl_ = sd[mgrp]
                            Hh.tt("dve", sl_["P%d" % b_].h[:], sl_["P%d" % a_].h[:],
                                  bp[mgrp].h[:].rearrange("p (a b) -> p a b", a=4), ALU.add, [sl_["P%d" % b_]],
                                  [sl_["P%d" % a_], bp[mgrp]])
                    for mgrp in grp:
                        AT = sd[mgrp]["P1"]
                        bu = nb()
                        bw = nb()
                        for i in range(4):
                            h = mgrp * 4 + i
                            Hh.mm(bu.h[:, i * 128:(i + 1) * 128], AT.h[:, i, :], vb.h[:, h, :], [bu], [AT, vb])
                            Hh.mm(bw.h[:, i * 128:(i + 1) * 128], kba.h[:, h, :], AT.h[:, i, :], [bw], [AT, kba])
                        Hh.cp("act", u.h[:, mgrp * 4:(mgrp + 1) * 4, :], bu.h[:].rearrange("p (a b) -> p a b", a=4), [u], [bu])
                        Hh.cp("dve", wT.h[:, mgrp * 4:(mgrp + 1) * 4, :], bw.h[:].rearrange("p (a b) -> p a b", a=4), [wT], [bw])
                for c in range(2):
                    rs = slice(c * 64, (c + 1) * 64)
                    for mgrp in range(4):
                        hs = slice(mgrp * 4, (mgrp + 1) * 4)
                        bws = nb()
                        bo1 = nb()
                        for i in range(4):
                            h = mgrp * 4 + i
                            Hh.mm(bws.h[:, i * 128:(i + 1) * 128], wT.h[:, h, :], Sb.h[:, h, :], [bws], [wT, Sb])
                            Hh.mm(bo1.h[:, i * 128:(i + 1) * 128], qkvT.h[:, h // 2, cs], Sb.h[:, h, :], [bo1], [qkvT, Sb])
                        Hh.tt("dve", vnew.h[rs, hs, :], u.h[rs, hs, :], bws.h[rs, :].rearrange("p (a b) -> p a b", a=4),
                              ALU.subtract, [vnew], [u, bws])
                        Hh.tt("dve", o.h[rs, hs, :], bo1.h[rs, :].rearrange("p (a b) -> p a b", a=4),
                              bc(m["sa"].h[rs, hs], [64, 4, 128]), ALU.mult, [o], [bo1, m["sa"]])
                        bs = nb()
                        for i in range(4):
                            h = mgrp * 4 + i
                            Hh.mm(bs.h[:, i * 128:(i + 1) * 128], kdec.h[rs, h, :], vnew.h[rs, h, :], [bs], [kdec, vnew])
                        tS = tmpS[mgrp % 2]
                        Hh.tt("pool", tS.h[:], S.h[:, hs, :], bc(EG.h[:, c * 16 + mgrp * 4:c * 16 + mgrp * 4 + 4], [128, 4, 128]),
                              ALU.mult, [tS], [S, EG])
                        Hh.tt("dve", S.h[:, hs, :], tS.h[:], bs.h[:].rearrange("p (a b) -> p a b", a=4), ALU.add, [S], [tS, bs])
                        Hh.cp("act", Sb.h[:, hs, :], S.h[:, hs, :], [Sb], [S])
                for mgrp in range(4):
                    hs = slice(mgrp * 4, (mgrp + 1) * 4)
                    bo2 = nb()
                    for i in range(4):
                        h = mgrp * 4 + i
                        Hh.mm(bo2.h[:, i * 128:(i + 1) * 128], attnT.h[:, h, :], vnew.h[:, h, :], [bo2], [attnT, vnew])
                    Hh.tt("dve", tmpS[mgrp % 2].h[:], bo2.h[:].rearrange("p (a b) -> p a b", a=4),
                          bc(m["rq16"].h[:, hs], [128, 4, 128]), ALU.mult, [tmpS[mgrp % 2]], [bo2, m["rq16"]])
                    Hh.tt("pool", o.h[:, hs, :], o.h[:, hs, :], tmpS[mgrp % 2].h[:], ALU.add, [o], [o, tmpS[mgrp % 2]])
                Hh.tt("pool", u.h[:], o.h[:], o.h[:], ALU.mult, [u], [o])
                P.op("dve", lambda e: e.tensor_reduce(out=m["ssqo"].h[:], in_=u.h[:], op=ALU.add, axis=AX.X), w=[m["ssqo"]], r=[u])
                Hh.act(m["rms"].h[:], m["ssqo"].h[:], AF.Sqrt, [m["rms"]], [m["ssqo"], epsr], scale=1.0 / 128.0,
                       bias=epsr.h[:, 0:1])
                P.op("dve", lambda e: e.reciprocal(out=m["rms"].h[:], in_=m["rms"].h[:]), w=[m["rms"]], r=[m["rms"]])
                Hh.tt("dve", o.h[:], o.h[:], bc(m["rms"].h[:], [128, 16, 128]), ALU.mult, [o], [o, m["rms"]])
                Hh.tt("pool", o.h[:], o.h[:], nw1.h[:].unsqueeze(1).to_broadcast([128, 16, 128]), ALU.mult, [o], [o, nw1])
                Hh.tt("dve", obf.h[:], o.h[:], zs_t.h[:].rearrange("p (a b) -> p a b", a=16), ALU.mult, [obf], [o, zs_t])
                P.dma("sp", og_out.h[tb:tb + 128, :], obf.h[:].rearrange("p a b -> p (a b)"), w=og_out, r=obf, owner=obf)
        P.end_stage()


def gla_consts_np():
    t = np.arange(128)
    same = (t[:, None] // 64) == (t[None, :] // 64)
    c = {}
    c["ident"] = np.eye(128, dtype=np.float32)
    c["ucs"] = (same & (t[:, None] <= t[None, :])).astype(np.float32)
    c["vsame"] = same.astype(np.float32)
    c["mask01"] = (same & (t[None, :] >= t[:, None])).astype(np.float32)
    cind2 = np.zeros((128, 2), np.float32)
    cind2[:64, 0] = 1.0
    cind2[64:, 1] = 1.0
    c["cind2"] = cind2
    return c


def gla_stage(P, x_in, og_out, w_c, wg_aug, normw, C, T, tag="l", rm=lambda t: t):
    Hh = H(P)
    SBK = 256
    NSB = T // SBK
    with contextlib.ExitStack() as st:
        sbuf = lambda n, s, d=F32: P.sb(st, tag + n, s, d)
        idt = sbuf("idt", [128, 128])
        ucs = sbuf("ucs", [128, 128])
        vsame = sbuf("vsame", [128, 128])
        mask01 = sbuf("mask01", [128, 128])
        cind2 = sbuf("cind2", [128, 2])
        wga = sbuf("wga", [17, 512])
        nw1 = sbuf("nw1", [128, 512])
        for (dst, src) in ((idt, C["ident"]), (ucs, C["ucs"]), (vsame, C["vsame"]), (mask01, C["mask01"]),
                           (cind2, C["cind2"]), (wga, wg_aug)):
            P.dma("sp", dst.h[:], src.h[:, :], w=dst, r=src)
        P.dma("sp", nw1.h[:], normw.h.partition_broadcast(128), w=nw1, r=normw)
        xs = sbuf("xs", [128, D])
        xT = sbuf("xT", [128, 16, SBK], BF16)
        wf = [sbuf("wf%d" % i, [128, 16, 256], BF16) for i in range(2)]
        wgl = sbuf("wgl", [128, 16, 16], BF16)
        qkT = sbuf("qkT", [128, 8, SBK])
        ktok = sbuf("ktok", [128, 2, 512])
        vtok = sbuf("vtok", [128, 2, 1024], BF16)
        rs_ = sbuf("rs", [128, 2, 1024])
        glT = sbuf("glT", [32, SBK])
        S = sbuf("S", [128, 4, 512])
        Sb = sbuf("Sb", [128, 4, 512], BF16)
        ez = sbuf("ez", [128, 512])
        ftok = sbuf("ftok", [128, 512])
        bcs = sbuf("bcs", [128, 512])
        dd = sbuf("dd", [128, 512])
        kdec = sbuf("kdec", [128, 512], BF16)
        eP = sbuf("eP", [128, 4, 128])
        eN = sbuf("eN", [128, 4, 128])
        qt = sbuf("qt", [128, 4, 128], BF16)
        kt = sbuf("kt", [128, 4, 128], BF16)
        EB = sbuf("EB", [128, 8])
        attnT = sbuf("attnT", [128, 2, 128], BF16)
        o = sbuf("o", [128, 2, 512])
        obf = sbuf("obf", [128, 2, 512], BF16)
        sqo = sbuf("sqo", [128, 2, 512])
        ssq = sbuf("ssq", [128, 2])
        rms = sbuf("rms", [128, 2])
        epsr = sbuf("epsr", [128, 1])
        Hh.memset("dve", epsr.h[:], RMS_EPS, [epsr])
        Hh.memset("dve", S.h[:], 0.0, [S])
        Hh.memset("pool", Sb.h[:], 0.0, [Sb])
        Hh.memset("pool", glT.h[:], 1.0, [glT])
        banks = [P.ps(st, tag + "bank%d" % i, [128, 512]) for i in range(8)]
        bctr = [0]

        def nb():
            b = banks[bctr[0] % 8]
            bctr[0] += 1
            return b

        nwf = 0
        for sb in range(NSB):
            t0 = sb * SBK
            for s in range(2):
                P.dma("sp", xs.h[:], x_in.h[rm(t0 + s * 128):rm(t0 + s * 128) + 128, :], w=xs, r=x_in)
                for q in range(4):
                    bk = nb()
                    for i in range(4):
                        dk = q * 4 + i
                        Hh.tr(bk.h[:, i * 128:(i + 1) * 128], xs.h[:, dk * 128:(dk + 1) * 128], idt.h[:], [bk], [xs, idt])
                    Hh.cp("act" if q % 2 == 0 else "dve", xT.h[:, q * 4:(q + 1) * 4, s * 128:(s + 1) * 128],
                          bk.h[:].rearrange("p (a b) -> p a b", a=4), [xT], [bk])
            for g in range(12):
                sl = nwf % 2
                nwf += 1
                for k0 in range(0, 16, 4):
                    P.dma("pool", wf[sl].h[:, k0:k0 + 4, :],
                          w_c.h[k0 * 128:(k0 + 4) * 128, g * 256:(g + 1) * 256].rearrange("(k p) f -> p k f", p=128),
                          w=wf[sl], r=w_c)
                if g < 4:
                    for c in range(2):
                        fc = g * 2 + c
                        bk = nb()
                        for dk in range(16):
                            Hh.mm(bk.h[:, 0:SBK], wf[sl].h[:, dk, c * 128:(c + 1) * 128], xT.h[:, dk, :], [bk], [wf[sl], xT],
                                  start=(dk == 0), stop=(dk == 15))
                        Hh.cp("act", qkT.h[:, fc, :], bk.h[:, 0:SBK], [qkT], [bk])
                if g >= 2:
                    for s in range(2):
                        bk = nb()
                        for dk in range(16):
                            Hh.mm(bk.h[:, 0:256], xT.h[:, dk, s * 128:(s + 1) * 128], wf[sl].h[:, dk, :], [bk], [wf[sl], xT],
                                  start=(dk == 0), stop=(dk == 15))
                        if g < 4:
                            Hh.cp("dve", ktok.h[:, s, (g - 2) * 256:(g - 1) * 256], bk.h[:, 0:256], [ktok], [bk])
                        elif g < 8:
                            Hh.cp("dve", vtok.h[:, s, (g - 4) * 256:(g - 3) * 256], bk.h[:, 0:256], [vtok], [bk])
                        else:
                            Hh.act(rs_.h[:, s, (g - 8) * 256:(g - 7) * 256], bk.h[:, 0:256], AF.Silu, [rs_], [bk])
            for k0 in range(0, 16, 4):
                P.dma("pool", wgl.h[:, k0:k0 + 4, :],
                      w_c.h[k0 * 128:(k0 + 4) * 128, 3072:3088].rearrange("(k p) f -> p k f", p=128), w=wgl, r=w_c)
            bk = nb()
            for dk in range(16):
                Hh.mm(bk.h[0:16, 0:SBK], wgl.h[:, dk, :], xT.h[:, dk, :], [bk], [wgl, xT], start=(dk == 0), stop=(dk == 15))
            Hh.cp("act", glT.h[0:16, :], bk.h[0:16, 0:SBK], [glT], [bk])

            for s in range(2):
                tb = t0 + s * 128
                cs = slice(s * 128, (s + 1) * 128)
                bz = nb()
                Hh.mm(bz.h[:], glT.h[0:17, cs], wga.h[:], [bz], [glT, wga])
                Hh.act(ez.h[:], bz.h[:], AF.Exp, [ez], [bz], scale=-1.0)
                Hh.act(ez.h[:], ez.h[:], AF.Ln, [ez], [ez], bias=1.0)
                Hh.ts("dve", ftok.h[:], ez.h[:], -1.0 / 16.0, ALU.mult, [ftok], [ez])
                bcu = nb()
                bto = nb()
                Hh.mm(bcu.h[:], ucs.h[:], ftok.h[:], [bcu], [ucs, ftok])
                Hh.mm(bto.h[:], vsame.h[:], ftok.h[:], [bto], [vsame, ftok])
                Hh.cp("act", bcs.h[:], bcu.h[:], [bcs], [bcu])
                Hh.tt("dve", dd.h[:], bto.h[:], bcs.h[:], ALU.subtract, [dd], [bto, bcs])
                Hh.act(dd.h[:], dd.h[:], AF.Exp, [dd], [dd])
                Hh.tt("dve", kdec.h[:], ktok.h[:, s, :], dd.h[:], ALU.mult, [kdec], [ktok, dd])
                bT = nb()
                bl = nb()
                for kc in range(4):
                    Hh.mm(bT.h[:, kc * 128:(kc + 1) * 128], ftok.h[:, kc * 128:(kc + 1) * 128], ucs.h[:], [bT], [ftok, ucs])
                    Hh.mm(bl.h[:, kc * 2:(kc + 1) * 2], ftok.h[:, kc * 128:(kc + 1) * 128], cind2.h[:], [bl], [ftok, cind2])
                bT3 = bT.h[:].rearrange("p (a b) -> p a b", a=4)
                Hh.act(eP.h[:], bT3, AF.Exp, [eP], [bT], bias=-0.5 * math.log(256.0))
                Hh.act(eN.h[:], bT3, AF.Exp, [eN], [bT], scale=-1.0)
                Hh.act(EB.h[:], bl.h[:, 0:8], AF.Exp, [EB], [bl])
                Hh.tt("dve", qt.h[:], qkT.h[:, 0:4, cs], eP.h[:], ALU.mult, [qt], [qkT, eP])
                Hh.tt("dve", kt.h[:], qkT.h[:, 4:8, cs], eN.h[:], ALU.mult, [kt], [qkT, eN])
                ba_ = nb()
                for h in range(2):
                    for kc in range(2):
                        Hh.mm(ba_.h[:, h * 128:(h + 1) * 128], kt.h[:, h * 2 + kc, :], qt.h[:, h * 2 + kc, :], [ba_], [kt, qt],
                              start=(kc == 0), stop=(kc == 1))
                Hh.tt("dve", attnT.h[:], ba_.h[:, 0:256].rearrange("p (a b) -> p a b", a=2),
                      mask01.h[:].unsqueeze(1).to_broadcast([128, 2, 128]), ALU.mult, [attnT], [ba_, mask01])
                for c in range(2):
                    rs = slice(c * 64, (c + 1) * 64)
                    for h in range(2):
                        vh = vtok.h[rs, s, h * 512:(h + 1) * 512]
                        bo = nb()
                        Hh.mm(bo.h[:], qt.h[:, h * 2, :], Sb.h[:, h * 2, :], [bo], [qt, Sb], start=True, stop=False)
                        Hh.mm(bo.h[:], qt.h[:, h * 2 + 1, :], Sb.h[:, h * 2 + 1, :], [bo], [qt, Sb], start=False, stop=False)
                        Hh.mm(bo.h[:], attnT.h[rs, h, :], vh, [bo], [attnT, vtok], start=False, stop=True)
                        Hh.cp("act", o.h[rs, h, :], bo.h[rs, :], [o], [bo])
                        for kc in range(2):
                            i4 = h * 2 + kc
                            bs = nb()
                            Hh.mm(bs.h[:], kdec.h[rs, i4 * 128:(i4 + 1) * 128], vh, [bs], [kdec, vtok])
                            Hh.stt(S.h[:, i4, :], S.h[:, i4, :], EB.h[:, i4 * 2 + c:i4 * 2 + c + 1], bs.h[:], ALU.mult, ALU.add,
                                   [S], [S, EB, bs])
                            Hh.cp("act", Sb.h[:, i4, :], S.h[:, i4, :], [Sb], [S])
                Hh.tt("pool", sqo.h[:], o.h[:], o.h[:], ALU.mult, [sqo], [o])
                P.op("dve", lambda e: e.tensor_reduce(out=ssq.h[:], in_=sqo.h[:], op=ALU.add, axis=AX.X), w=[ssq], r=[sqo])
                Hh.act(rms.h[:], ssq.h[:], AF.Sqrt, [rms], [ssq, epsr], scale=1.0 / 512.0, bias=epsr.h[:, 0:1])
                P.op("dve", lambda e: e.reciprocal(out=rms.h[:], in_=rms.h[:]), w=[rms], r=[rms])
                Hh.tt("dve", o.h[:], o.h[:], bc(rms.h[:], [128, 2, 512]), ALU.mult, [o], [o, rms])
                Hh.tt("pool", o.h[:], o.h[:], nw1.h[:].unsqueeze(1).to_broadcast([128, 2, 512]), ALU.mult, [o], [o, nw1])
                Hh.tt("dve", obf.h[:], o.h[:], rs_.h[:, s, :].rearrange("p (a b) -> p a b", a=2), ALU.mult, [obf], [o, rs_])
                P.dma("sp", og_out.h[tb:tb + 128, :], obf.h[:].rearrange("p a b -> p (a b)"), w=og_out, r=obf, owner=obf)
        P.end_stage()


I32 = mybir.dt.int32
LC = 128


def s5_layout_np(a_re, a_im, log_dt, b_re, b_im, c_re, c_im, d_skip, hf):
    gs = slice(hf * 64, (hf + 1) * 64)

    def pp(a):
        return np.ascontiguousarray(a[gs].reshape(32, 2, 64).transpose(1, 2, 0).reshape(128, 32))
    out = {}
    out["are"] = pp(a_re)
    out["aim"] = pp(a_im)
    out["ldt"] = np.ascontiguousarray(np.broadcast_to(log_dt[gs].reshape(32, 2).T[:, None, :], (2, 64, 32)).reshape(128, 32))

    def bb(b):
        return np.ascontiguousarray(b[gs].reshape(32, 2, 64, 16).transpose(1, 2, 0, 3).reshape(128, 512))
    out["bre"] = bb(b_re)
    out["bim"] = bb(b_im)

    def cc(c):
        o = np.zeros((2, 64, 32, 2, 16), np.float32)
        cg = c[gs].reshape(32, 2, 16, 64)
        for g2 in range(2):
            o[g2, :, :, g2, :] = cg[:, g2].transpose(2, 0, 1)
        return o.reshape(128, 32 * 32)
    out["cre"] = cc(c_re)
    out["cim"] = cc(c_im)
    out["dsk"] = np.ascontiguousarray(d_skip[hf * 1024:(hf + 1) * 1024])
    ms = np.zeros(2, np.float32)
    ms[hf] = 1.0
    out["msel"] = ms
    return out


def s5_stage(P, x_in, y_out, L, ident_d, T, isel_d, tag="s", rm=lambda t: t):
    Hh = H(P)
    NCH = T // LC
    with contextlib.ExitStack() as st:
        sbuf = lambda n, s, d=F32: P.sb(st, tag + n, s, d)
        idt = sbuf("idt", [128, 128])
        P.dma("sp", idt.h[:], ident_d.h[:, :], w=idt, r=ident_d)
        sm = {n: sbuf("sm_" + n, [128, 32]) for n in
              ("are", "aim", "dt", "ar", "th", "kf", "hl", "sh", "ah", "ch", "sn", "cs", "abr", "abi", "nr", "den", "cre", "cim",
               "t1", "t2", "hpr", "hpi")}
        ki = sbuf("ki", [128, 32], I32)
        for n in ("are", "aim"):
            P.dma("sp", sm[n].h[:], L[n].h[:, :], w=sm[n], r=L[n])
        P.dma("sp", sm["dt"].h[:], L["ldt"].h[:, :], w=sm["dt"], r=L["ldt"])
        m = sm
        tt = lambda eng, o_, a, b, op: Hh.tt(eng, o_.h[:], a.h[:], b.h[:], op, [o_], [a, b])
        Hh.act(m["dt"].h[:], m["dt"].h[:], AF.Exp, [m["dt"]], [m["dt"]])
        tt("dve", m["ar"], m["are"], m["dt"], ALU.mult)
        Hh.act(m["ar"].h[:], m["ar"].h[:], AF.Exp, [m["ar"]], [m["ar"]])
        tt("dve", m["th"], m["aim"], m["dt"], ALU.mult)
        Hh.ts("dve", m["kf"].h[:], m["th"].h[:], 1.0 / (2.0 * math.pi), ALU.mult, [m["kf"]], [m["th"]])
        Hh.cp("dve", ki.h[:], m["kf"].h[:], [ki], [m["kf"]])
        Hh.cp("dve", m["kf"].h[:], ki.h[:], [m["kf"]], [ki])
        C1 = 6.28125
        C2 = 2.0 * math.pi - C1
        Hh.stt(m["th"].h[:], m["kf"].h[:], -C1, m["th"].h[:], ALU.mult, ALU.add, [m["th"]], [m["kf"], m["th"]])
        Hh.stt(m["th"].h[:], m["kf"].h[:], -C2, m["th"].h[:], ALU.mult, ALU.add, [m["th"]], [m["kf"], m["th"]])
        Hh.ts("dve", m["hl"].h[:], m["th"].h[:], 0.5, ALU.mult, [m["hl"]], [m["th"]])
        Hh.act(m["sh"].h[:], m["hl"].h[:], AF.Sin, [m["sh"]], [m["hl"]])
        Hh.act(m["ah"].h[:], m["hl"].h[:], AF.Abs, [m["ah"]], [m["hl"]])
        hpi2 = sbuf("hpi2", [128, 1])
        Hh.memset("dve", hpi2.h[:], math.pi / 2.0, [hpi2])
        Hh.act(m["ch"].h[:], m["ah"].h[:], AF.Sin, [m["ch"]], [m["ah"], hpi2], scale=-1.0, bias=hpi2.h[:, 0:1])
        tt("dve", m["sn"], m["sh"], m["ch"], ALU.mult)
        Hh.ts("dve", m["sn"].h[:], m["sn"].h[:], 2.0, ALU.mult, [m["sn"]], [m["sn"]])
        tt("dve", m["cs"], m["sh"], m["sh"], ALU.mult)
        Hh.ts("dve", m["cs"].h[:], m["cs"].h[:], -2.0, ALU.mult, [m["cs"]], [m["cs"]], s2=1.0, op1=ALU.add)
        tt("dve", m["abr"], m["ar"], m["cs"], ALU.mult)
        tt("dve", m["abi"], m["ar"], m["sn"], ALU.mult)
        Hh.ts("dve", m["nr"].h[:], m["abr"].h[:], -1.0, ALU.add, [m["nr"]], [m["abr"]])
        tt("dve", m["den"], m["are"], m["are"], ALU.mult)
        tt("dve", m["t1"], m["aim"], m["aim"], ALU.mult)
        tt("dve", m["den"], m["den"], m["t1"], ALU.add)
        P.op("dve", lambda e: e.reciprocal(out=m["den"].h[:], in_=m["den"].h[:]), w=[m["den"]], r=[m["den"]])
        tt("dve", m["t1"], m["nr"], m["are"], ALU.mult)
        tt("dve", m["t2"], m["abi"], m["aim"], ALU.mult)
        tt("dve", m["cre"], m["t1"], m["t2"], ALU.add)
        tt("dve", m["cre"], m["cre"], m["den"], ALU.mult)
        tt("dve", m["t1"], m["abi"], m["are"], ALU.mult)
        tt("dve", m["t2"], m["nr"], m["aim"], ALU.mult)
        tt("dve", m["cim"], m["t1"], m["t2"], ALU.subtract)
        tt("dve", m["cim"], m["cim"], m["den"], ALU.mult)

        K_re = sbuf("Kre", [128, 32, LC])
        K_im = sbuf("Kim", [128, 32, LC])
        H_re = sbuf("Hre", [128, 32, LC])
        H_im = sbuf("Him", [128, 32, LC])
        WbT_re = sbuf("WbTre", [128, 32, 128])
        WbT_im = sbuf("WbTim", [128, 32, 128])
        banks = [P.ps(st, tag + "bank%d" % i, [128, 512]) for i in range(8)]
        bctr = [0]

        def nb():
            b = banks[bctr[0] % 8]
            bctr[0] += 1
            return b
        Braw_re = P.view(K_re.h[:, 0:4, :].rearrange("p a b -> p (a b)"), "Braw_re")
        Braw_im = P.view(K_im.h[:, 0:4, :].rearrange("p a b -> p (a b)"), "Braw_im")
        bb_re = P.view(H_re.h[:, 0:4, :].rearrange("p a b -> p (a b)"), "bb_re")
        bb_im = P.view(H_im.h[:, 0:4, :].rearrange("p a b -> p (a b)"), "bb_im")
        tmpa = P.view(K_re.h[:, 4:8, :].rearrange("p a b -> p (a b)"), "tmpa")
        tmpb = P.view(K_im.h[:, 4:8, :].rearrange("p a b -> p (a b)"), "tmpb")
        P.dma("sp", Braw_re.h, L["bre"].h[:, :], w=Braw_re, r=L["bre"])
        P.dma("sp", Braw_im.h, L["bim"].h[:, :], w=Braw_im, r=L["bim"])
        v3 = lambda b: b.h.rearrange("p (a c) -> p a c", a=32)
        cre_b = bc(m["cre"].h[:], [128, 32, 16])
        cim_b = bc(m["cim"].h[:], [128, 32, 16])
        Hh.tt("dve", v3(tmpa), v3(Braw_re), cre_b, ALU.mult, [tmpa], [Braw_re, m["cre"]])
        Hh.tt("dve", v3(tmpb), v3(Braw_im), cim_b, ALU.mult, [tmpb], [Braw_im, m["cim"]])
        Hh.tt("dve", bb_re.h, tmpa.h, tmpb.h, ALU.subtract, [bb_re], [tmpa, tmpb])
        Hh.tt("dve", v3(tmpa), v3(Braw_im), cre_b, ALU.mult, [tmpa], [Braw_im, m["cre"]])
        Hh.tt("dve", v3(tmpb), v3(Braw_re), cim_b, ALU.mult, [tmpb], [Braw_re, m["cim"]])
        Hh.tt("dve", bb_im.h, tmpa.h, tmpb.h, ALU.add, [bb_im], [tmpa, tmpb])
        pad = [sbuf("pad%d" % i, [128, 128]) for i in range(2)]
        for i in range(2):
            Hh.memset("dve", pad[i].h[:], 0.0, [pad[i]])
        npad = 0
        for pi in range(32):
            c0 = (pi % 4) * 32
            for (src, dstW) in ((bb_re, WbT_re), (bb_im, WbT_im)):
                pd = pad[npad % 2]
                npad += 1
                Hh.memset("pool", pd.h[:], 0.0, [pd])
                Hh.cp("pool", pd.h[0:64, c0:c0 + 16], src.h[0:64, pi * 16:(pi + 1) * 16], [pd], [src])
                Hh.cp("pool", pd.h[64:128, c0 + 16:c0 + 32], src.h[64:128, pi * 16:(pi + 1) * 16], [pd], [src])
                bk = nb()
                Hh.tr(bk.h[:, 0:128], pd.h[:], idt.h[:], [bk], [pd, idt])
                Hh.cp("act", dstW.h[:, pi, :], bk.h[:, 0:128], [dstW], [bk])
        Cre = sbuf("Cre", [128, 32, 32])
        Cni = sbuf("Cni", [128, 32, 32])
        P.dma("sp", Cre.h[:].rearrange("p a b -> p (a b)"), L["cre"].h[:, :], w=Cre, r=L["cre"])
        P.dma("sp", Cni.h[:].rearrange("p a b -> p (a b)"), L["cim"].h[:, :], w=Cni, r=L["cim"])
        Hh.ts("dve", Cni.h[:], Cni.h[:], -1.0, ALU.mult, [Cni], [Cni])
        Dt = sbuf("Dt", [128, 1024])
        msel = sbuf("msel", [128, 2])
        P.dma("sp", msel.h[:], L["msel"].h.partition_broadcast(128), w=msel, r=L["msel"])
        iself = sbuf("iself", [128, 2, 128])
        for h in range(2):
            P.dma("sp", iself.h[:, h, :], isel_d.h[h], w=iself, r=isel_d)
        P.dma("sp", Dt.h[:], L["dsk"].h.partition_broadcast(128), w=Dt, r=L["dsk"])
        msk = sbuf("msk", [128, 32, LC], BF16)
        Hh.memset("dve", msk.h[:], 1.0, [msk])
        Hh.memset("dve", msk.h[:, :, 0:1], 0.0, [msk])
        Tp_re = sbuf("Tpre", [128, 32, LC])
        Tp_im = sbuf("Tpim", [128, 32, LC])
        Tn_re = sbuf("Tnre", [128, 32, LC])
        Tn_im = sbuf("Tnim", [128, 32, LC])
        P.barrier()
        Hh.cp("dve", Tp_re.h[:, :, 0], m["abr"].h[:], [Tp_re], [m["abr"]])
        Hh.cp("dve", Tp_im.h[:, :, 0], m["abi"].h[:], [Tp_im], [m["abi"]])
        n = 1
        while n < LC:
            sre = bc(Tp_re.h[:, :, n - 1], [128, 32, n])
            sim = bc(Tp_im.h[:, :, n - 1], [128, 32, n])
            A_re = Tp_re.h[:, :, 0:n]
            A_im = Tp_im.h[:, :, 0:n]
            t1 = K_re.h[:, :, 0:n]
            t2 = K_im.h[:, :, 0:n]
            t3 = H_re.h[:, :, 0:n]
            t4 = H_im.h[:, :, 0:n]
            Hh.tt("dve", t1, A_re, sre, ALU.mult, [K_re], [Tp_re])
            Hh.tt("dve", t2, A_im, sim, ALU.mult, [K_im], [Tp_im])
            Hh.tt("dve", t3, A_re, sim, ALU.mult, [H_re], [Tp_re, Tp_im])
            Hh.tt("dve", t4, A_im, sre, ALU.mult, [H_im], [Tp_re, Tp_im])
            Hh.tt("dve", Tp_re.h[:, :, n:2 * n], t1, t2, ALU.subtract, [Tp_re], [K_re, K_im])
            Hh.tt("dve", Tp_im.h[:, :, n:2 * n], t3, t4, ALU.add, [Tp_im], [H_re, H_im])
            n *= 2
        Hh.tt("dve", K_re.h[:], Tp_re.h[:], Tp_re.h[:], ALU.mult, [K_re], [Tp_re])
        Hh.tt("dve", K_im.h[:], Tp_im.h[:], Tp_im.h[:], ALU.mult, [K_im], [Tp_im])
        Hh.tt("dve", K_re.h[:], K_re.h[:], K_im.h[:], ALU.add, [K_re], [K_re, K_im])
        P.op("dve", lambda e: e.reciprocal(out=K_re.h[:], in_=K_re.h[:]), w=[K_re], r=[K_re])
        Hh.tt("dve", Tn_re.h[:], Tp_re.h[:], K_re.h[:], ALU.mult, [Tn_re], [Tp_re, K_re])
        Hh.tt("dve", Tn_im.h[:], Tp_im.h[:], K_re.h[:], ALU.mult, [Tn_im], [Tp_im, K_re])
        Hh.ts("dve", Tn_im.h[:], Tn_im.h[:], -1.0, ALU.mult, [Tn_im], [Tn_im])
        Hh.memset("dve", m["hpr"].h[:], 0.0, [m["hpr"]])
        Hh.memset("dve", m["hpi"].h[:], 0.0, [m["hpi"]])

        xu = sbuf("xu", [128, 2048])
        uT = sbuf("uT", [128, 8, 128])
        tq = [sbuf("tq%d" % i, [128, 4, LC]) for i in range(2)]
        tq = tq + tq
        du = sbuf("du", [128, 1024])
        ybf = sbuf("ybf", [128, 1024], BF16)
        for ch in range(NCH):
            tb = ch * LC
            P.dma("sp", xu.h[:], x_in.h[rm(tb):rm(tb) + 128, :], w=xu, r=x_in)
            for q in range(2):
                bk = nb()
                for i in range(4):
                    for h in range(2):
                        c0 = h * 1024 + (q * 4 + i) * 128
                        Hh.mm(bk.h[:, i * 128:(i + 1) * 128], xu.h[:, c0:c0 + 128], iself.h[:, h, :], [bk], [xu, iself],
                              start=(h == 0), stop=(h == 1))
                Hh.cp("act", uT.h[:, q * 4:(q + 1) * 4, :], bk.h[:].rearrange("p (a b) -> p a b", a=4), [uT], [bk])
            for q in range(8):
                bre = nb()
                bim = nb()
                for i in range(4):
                    pi = q * 4 + i
                    Hh.mm(bre.h[:, i * 128:(i + 1) * 128], WbT_re.h[:, pi, :], uT.h[:, q, :], [bre], [WbT_re, uT])
                    Hh.mm(bim.h[:, i * 128:(i + 1) * 128], WbT_im.h[:, pi, :], uT.h[:, q, :], [bim], [WbT_im, uT])
                qs = slice(q * 4, (q + 1) * 4)
                b3 = lambda b: b.h[:].rearrange("p (a b) -> p a b", a=4)
                Hh.tt("dve", tq[0].h[:], Tn_re.h[:, qs, :], b3(bre), ALU.mult, [tq[0]], [Tn_re, bre])
                Hh.tt("dve", tq[1].h[:], Tn_im.h[:, qs, :], b3(bim), ALU.mult, [tq[1]], [Tn_im, bim])
                Hh.tt("pool", K_re.h[:, qs, :], tq[0].h[:], tq[1].h[:], ALU.subtract, [K_re], [tq[0], tq[1]])
                Hh.tt("dve", tq[2].h[:], Tn_re.h[:, qs, :], b3(bim), ALU.mult, [tq[2]], [Tn_re, bim])
                Hh.tt("dve", tq[3].h[:], Tn_im.h[:, qs, :], b3(bre), ALU.mult, [tq[3]], [Tn_im, bre])
                Hh.tt("pool", K_im.h[:, qs, :], tq[2].h[:], tq[3].h[:], ALU.add, [K_im], [tq[2], tq[3]])
            Hh.tt("pool", K_re.h[:, :, 0], K_re.h[:, :, 0], m["hpr"].h[:], ALU.add, [K_re], [K_re, m["hpr"]])
            Hh.tt("pool", K_im.h[:, :, 0], K_im.h[:, :, 0], m["hpi"].h[:], ALU.add, [K_im], [K_im, m["hpi"]])
            f2 = lambda b: b.h[:].rearrange("p a b -> p (a b)")
            P.op("dve", lambda e: e.tensor_tensor_scan(out=f2(H_re), data0=f2(msk), data1=f2(K_re), initial=0.0,
                                                       op0=ALU.mult, op1=ALU.add), w=[H_re], r=[msk, K_re])
            P.op("dve", lambda e: e.tensor_tensor_scan(out=f2(H_im), data0=f2(msk), data1=f2(K_im), initial=0.0,
                                                       op0=ALU.mult, op1=ALU.add), w=[H_im], r=[msk, K_im])
            for q in range(8):
                qs = slice(q * 4, (q + 1) * 4)
                Hh.tt("dve", tq[0].h[:], Tp_re.h[:, qs, :], H_re.h[:, qs, :], ALU.mult, [tq[0]], [Tp_re, H_re])
                Hh.tt("dve", tq[1].h[:], Tp_im.h[:, qs, :], H_im.h[:, qs, :], ALU.mult, [tq[1]], [Tp_im, H_im])
                Hh.tt("pool", K_re.h[:, qs, :], tq[0].h[:], tq[1].h[:], ALU.subtract, [K_re], [tq[0], tq[1]])
                Hh.tt("dve", tq[2].h[:], Tp_re.h[:, qs, :], H_im.h[:, qs, :], ALU.mult, [tq[2]], [Tp_re, H_im])
                Hh.tt("dve", tq[3].h[:], Tp_im.h[:, qs, :], H_re.h[:, qs, :], ALU.mult, [tq[3]], [Tp_im, H_re])
                Hh.tt("pool", K_im.h[:, qs, :], tq[2].h[:], tq[3].h[:], ALU.add, [K_im], [tq[2], tq[3]])
            Hh.cp("pool", m["hpr"].h[:], K_re.h[:, :, LC - 1], [m["hpr"]], [K_re])
            Hh.cp("pool", m["hpi"].h[:], K_im.h[:, :, LC - 1], [m["hpi"]], [K_im])
            by = [nb(), nb()]
            for pi in range(32):
                o_ = by[pi // 16].h[:, (pi % 16) * 32:(pi % 16 + 1) * 32]
                Hh.mm(o_, K_re.h[:, pi, :], Cre.h[:, pi, :], [by[pi // 16]], [K_re, Cre], start=True, stop=False)
                Hh.mm(o_, K_im.h[:, pi, :], Cni.h[:, pi, :], [by[pi // 16]], [K_im, Cni], start=False, stop=True)
            Hh.ts("pool", du.h[:], xu.h[:, 0:1024], msel.h[:, 0:1], ALU.mult, [du], [xu, msel])
            Hh.stt(du.h[:], xu.h[:, 1024:2048], msel.h[:, 1:2], du.h[:], ALU.mult, ALU.add, [du], [xu, msel, du])
            Hh.tt("pool", du.h[:], du.h[:], Dt.h[:], ALU.mult, [du], [du, Dt])
            for i in range(2):
                Hh.tt("dve", du.h[:, i * 512:(i + 1) * 512], by[i].h[:], du.h[:, i * 512:(i + 1) * 512], ALU.add, [du], [by[i], du])
            Hh.act(ybf.h[:], du.h[:], AF.Gelu_apprx_tanh, [ybf], [du])
            P.dma("sp", y_out.h[tb:tb + 128, :], ybf.h[:], w=y_out, r=ybf, owner=ybf)
        P.end_stage()


FF = 5632
PAIRS = [[0, 1], [2, 3], [4, 5], [6, 7]]
DEEPNORM_ALPHA = (2.0 * 4) ** 0.25


def load_hT_gathered(P, Hh, src_g, SEQ, TC, Vh, iselb, t0, xs, hT, banks, nbk):
    PR = min(TC, (2 << 20) // (Vh * 2))
    NK = 2 * Vh // 128
    for s in range(4):
        off = t0 + s * 128
        for h in range(2):
            for r in range(2):
                row = h * 2 * TC + (off // PR) * 2 * PR + r * PR + off % PR
                P.dma("sp", xs[h].h[:, r * Vh:(r + 1) * Vh], src_g.h[row:row + 128, :], w=xs[h], r=src_g)
        for q in range(NK // 4):
            bk = banks[nbk[0] % 8]
            nbk[0] += 1
            for i in range(4):
                dk = q * 4 + i
                for h in range(2):
                    Hh.mm(bk.h[:, i * 128:(i + 1) * 128], xs[h].h[:, dk * 128:(dk + 1) * 128], iselb.h[:, h, :], [bk],
                          [xs[h], iselb], start=(h == 0), stop=(h == 1))
            Hh.cp("act" if q % 2 == 0 else "dve", hT.h[:, q * 4:(q + 1) * 4, s * 128:(s + 1) * 128],
                  bk.h[:].rearrange("p (a b) -> p a b", a=4), [hT], [bk])


def load_isel(P, Hh, st, isel_d, tag):
    iself = P.sb(st, tag + "iself", [128, 2, 128], F32)
    for h in range(2):
        P.dma("sp", iself.h[:, h, :], isel_d.h[h], w=iself, r=isel_d)
    iselb = P.sb(st, tag + "iselb", [128, 2, 128], BF16)
    Hh.cp("dve", iselb.h[:], iself.h[:], [iselb], [iself])
    return iself, iselb


def outproj_stage(P, og_g, SEQ, Vh, x_res, x_out, w_out, ln_g, ln_b, ident_d, TC, alpha, isel_d, tag="p"):
    Hh = H(P)
    NB = TC // 512
    V = 2 * Vh
    NK = V // 128
    NKG = 8
    with contextlib.ExitStack() as st:
        idt, epsT = load_consts(P, st, ident_d)
        iself, iselb = load_isel(P, Hh, st, isel_d, tag)
        Gt = P.sb(st, tag + "G", [128, D], F32)
        Bt = P.sb(st, tag + "B", [128, D], F32)
        P.dma("sp", Gt.h[:], ln_g.h.partition_broadcast(128), w=Gt, r=ln_g)
        P.dma("sp", Bt.h[:], ln_b.h.partition_broadcast(128), w=Bt, r=ln_b)
        xs = [P.sb(st, tag + "xs%d" % i, [128, V], BF16) for i in range(2)]
        hT = P.sb(st, tag + "hT", [128, NK, 512], BF16)
        wd = [P.sb(st, tag + "wd%d" % i, [128, NKG, 512], BF16) for i in range(2)]
        z = [P.sb(st, tag + "z%d" % i, [128, D], F32) for i in range(4)]
        tmps = [ln_tmp(P, st, tag + "t%d" % i) for i in range(2)]
        bank = [P.ps(st, tag + "bank%d" % i, [128, 512]) for i in range(8)]
        nwd = 0
        nbk = [0]
        for blk in range(NB):
            t0 = blk * 512
            load_hT_gathered(P, Hh, og_g, SEQ, TC, Vh, iselb, t0, xs, hT, bank, nbk)
            for s in range(4):
                P.dma("sp", z[s].h[:], x_res.h[t0 + s * 128:t0 + (s + 1) * 128, :], w=z[s], r=x_res)
                Hh.act(z[s].h[:], z[s].h[:], AF.Copy, [z[s]], [z[s]], scale=float(alpha))
            for dn in range(4):
                pb = [bank[(nbk[0] + s) % 8] for s in range(4)]
                nbk[0] += 4
                for fg in range(NK // NKG):
                    sl = nwd % 2
                    nwd += 1
                    for a in range(0, NKG, 4):
                        r0 = (fg * NKG + a) * 128
                        P.dma("pool", wd[sl].h[:, a:a + 4, :],
                              w_out.h[r0:r0 + 4 * 128, dn * 512:(dn + 1) * 512].rearrange("(k p) f -> p k f", p=128),
                              w=wd[sl], r=w_out)
                    for i in range(NKG):
                        fk = fg * NKG + i
                        for s in range(4):
                            Hh.mm(pb[s].h[:], hT.h[:, fk, s * 128:(s + 1) * 128], wd[sl].h[:, i, :], [pb[s]], [hT, wd[sl]],
                                  start=(fk == 0), stop=(fk == NK - 1))
                for s in range(4):
                    zc = z[s].h[:, dn * 512:(dn + 1) * 512]
                    Hh.tt("dve", zc, pb[s].h[:], zc, ALU.add, [z[s]], [pb[s], z[s]])
            for s in range(4):
                layer_norm_tile(P, z[s], Gt, Bt, epsT, tmps[s % 2])
                P.dma("sp", x_out.h[t0 + s * 128:t0 + (s + 1) * 128, :], z[s].h[:], w=x_out, r=z[s], owner=z[s])
        P.end_stage()


def s5out_stage(P, y_g, SEQ, x_res, x_out, w_glu, ln_g, ln_b, ident_d, TC, alpha, isel_d, tag="q"):
    Hh = H(P)
    NB = TC // 512
    with contextlib.ExitStack() as st:
        idt, epsT = load_consts(P, st, ident_d)
        iself, iselb = load_isel(P, Hh, st, isel_d, tag)
        Gt = P.sb(st, tag + "G", [128, D], F32)
        Bt = P.sb(st, tag + "B", [128, D], F32)
        P.dma("sp", Gt.h[:], ln_g.h.partition_broadcast(128), w=Gt, r=ln_g)
        P.dma("sp", Bt.h[:], ln_b.h.partition_broadcast(128), w=Bt, r=ln_b)
        xs = [P.sb(st, tag + "xs%d" % i, [128, D], BF16) for i in range(2)]
        yT = P.sb(st, tag + "yT", [128, 16, 512], BF16)
        wv = [P.sb(st, tag + "wv%d" % i, [128, 16, 512], BF16) for i in range(2)]
        wg = [P.sb(st, tag + "wg%d" % i, [128, 16, 512], BF16) for i in range(2)]
        sg = [P.sb(st, tag + "sg%d" % i, [128, 512], F32) for i in range(2)]
        z = [P.sb(st, tag + "z%d" % i, [128, D], F32) for i in range(4)]
        tmps = [ln_tmp(P, st, tag + "t%d" % i) for i in range(2)]
        bank = [P.ps(st, tag + "bank%d" % i, [128, 512]) for i in range(8)]
        nw = 0
        nbk = [0]
        for blk in range(NB):
            t0 = blk * 512
            load_hT_gathered(P, Hh, y_g, SEQ, TC, 1024, iselb, t0, xs, yT, bank, nbk)
            for s in range(4):
                P.dma("sp", z[s].h[:], x_res.h[t0 + s * 128:t0 + (s + 1) * 128, :], w=z[s], r=x_res)
                Hh.act(z[s].h[:], z[s].h[:], AF.Copy, [z[s]], [z[s]], scale=float(alpha))
            for dn in range(4):
                sl = nw % 2
                nw += 1
                for k0 in range(0, 16, 4):
                    P.dma("pool", wv[sl].h[:, k0:k0 + 4, :],
                          w_glu.h[k0 * 128:(k0 + 4) * 128, dn * 512:(dn + 1) * 512].rearrange("(k p) f -> p k f", p=128),
                          w=wv[sl], r=w_glu)
                    P.dma("pool", wg[sl].h[:, k0:k0 + 4, :],
                          w_glu.h[k0 * 128:(k0 + 4) * 128, D + dn * 512:D + (dn + 1) * 512].rearrange(
                              "(k p) f -> p k f", p=128), w=wg[sl], r=w_glu)
                for s in range(4):
                    pv = bank[nbk[0] % 8]
                    pg = bank[(nbk[0] + 1) % 8]
                    nbk[0] += 2
                    for dk in range(16):
                        Hh.mm(pg.h[:], yT.h[:, dk, s * 128:(s + 1) * 128], wg[sl].h[:, dk, :], [pg], [yT, wg[sl]],
                              start=(dk == 0), stop=(dk == 15))
                    for dk in range(16):
                        Hh.mm(pv.h[:], yT.h[:, dk, s * 128:(s + 1) * 128], wv[sl].h[:, dk, :], [pv], [yT, wv[sl]],
                              start=(dk == 0), stop=(dk == 15))
                    sgb = sg[s % 2]
                    Hh.act(sgb.h[:], pg.h[:], AF.Sigmoid, [sgb], [pg])
                    Hh.tt("dve", sgb.h[:], sgb.h[:], pv.h[:], ALU.mult, [sgb], [sgb, pv])
                    zc = z[s].h[:, dn * 512:(dn + 1) * 512]
                    Hh.tt("dve", zc, zc, sgb.h[:], ALU.add, [z[s]], [z[s], sgb])
            for s in range(4):
                layer_norm_tile(P, z[s], Gt, Bt, epsT, tmps[s % 2])
                P.dma("sp", x_out.h[t0 + s * 128:t0 + (s + 1) * 128, :], z[s].h[:], w=x_out, r=z[s], owner=z[s])
        P.end_stage()


def _consts_np():
    c = gdn_consts_np()
    c.update(gla_consts_np())
    return c


S5_KEYS = ("are", "aim", "ldt", "bre", "bim", "cre", "cim", "dsk", "msel")


def build_program(SEQ, layers=(0, 1, 2, 3)):
    TC = SEQ // 2
    nc = bass.Bass("TRN2", target_bir_lowering=False)
    P = Prog(nc)
    ext = lambda n, s, d=F32: P.dram(n, s, d, kind="ExternalInput")
    x = ext("x", [TC, D])
    y = P.dram("y", [TC, D], F32, kind="ExternalOutput")
    ln_g = ext("ln_g", [4, 3, D])
    ln_b = ext("ln_b", [4, 3, D])
    w_up = ext("ffn_w_up", [4, 2, D, 2 * FF])
    w_dn = ext("ffn_w_down", [4, 2, FF, D])
    gdn_wout = ext("gdn_w_out", [2, 4096, D])
    gla_wout = ext("gla_w_out", [1, 2048, D])
    s5_wglu = ext("s5_w_glu", [1, D, 2 * D])
    gdn_nw = ext("gdn_norm_w", [2, 128])
    gla_nw = ext("gla_norm_w", [1, 512])
    gdn_wc = ext("gdn_wc", [2, D, 6176])
    gdn_conv = ext("gdn_conv", [2, 128, 128])
    gdn_alog = ext("gdn_alog", [2, 16])
    gdn_dtb = ext("gdn_dtb", [2, 16])
    gla_wc = ext("gla_wc", [D, 3088])
    gla_wga = ext("gla_wga", [17, 512])
    L = {}
    l0 = s5_layout_np(*[np.zeros(s, np.float32) for s in ((128, 64), (128, 64), (128,), (128, 64, 16), (128, 64, 16),
                                                          (128, 16, 64), (128, 16, 64), (2048,))], 0)
    for k in S5_KEYS:
        L[k] = ext("s5L_" + k, list(l0[k].shape))
    C = {k: ext("c_" + k, list(v.shape)) for k, v in _consts_np().items()}
    A = P.dram("actA", [TC, D], F32)
    B = P.dram("actB", [TC, D], F32)
    Cb = P.dram("actC", [TC, D], F32)
    XF = P.dram("actXF", [SEQ, D], F32)
    OG2 = P.dram("og2", [SEQ, 2048], BF16)
    QKV = P.dram("gdn_qkv", [4096, SEQ], BF16)
    Zd = P.dram("gdn_z", [SEQ, 2048], BF16)
    SSQd = P.dram("gdn_ssq", [SEQ, 16], F32)
    BAd = P.dram("gdn_ba", [32, SEQ], F32)
    OGG2 = P.dram("ogg2", [2 * SEQ, 2048], BF16)
    OG1 = P.dram("og1", [SEQ, 1024], BF16)
    OGG1 = P.dram("ogg1", [2 * SEQ, 1024], BF16)
    isel = ext("isel", [2, 128, 128])
    V = lambda ap, n: Buf(ap, n)
    rm = lambda t: ((t % TC) // 256) * 512 + (t // TC) * 256 + (t % 256)
    alpha = DEEPNORM_ALPHA
    ident = C["ident"]
    xin = x
    for li, i in enumerate(layers):
        last = (li == len(layers) - 1)
        kind, j = i % 3, i // 3
        ffn_stage(P, xin, A, V(w_up.h[i, 0], "wu"), V(w_dn.h[i, 0], "wd"), V(ln_g.h[i, 0], "g"), V(ln_b.h[i, 0], "b"),
                  ident, TC, alpha, tag="f%da" % i)
        P.gather_pairs(A, XF, TC, 256)
        if kind == 0:
            gdn_proj_stage(P, XF, QKV, Zd, SSQd, BAd, V(gdn_wc.h[j], "gw"), V(gdn_conv.h[j], "gc"), C, SEQ,
                           tag="gp%d" % i, rm=rm)
            gdn_rec_stage(P, QKV, Zd, SSQd, BAd, OG2, V(gdn_alog.h[j], "ga"), V(gdn_dtb.h[j], "gd"),
                          V(gdn_nw.h[j], "gn"), C, SEQ, tag="g%d" % i)
            P.gather_pairs(OG2, OGG2, SEQ, min(TC, 512))
            outproj_stage(P, OGG2, SEQ, 2048, A, B, V(gdn_wout.h[j], "gwo"), V(ln_g.h[i, 1], "g"), V(ln_b.h[i, 1], "b"),
                          ident, TC, alpha, isel, tag="p%d" % i)
        elif kind == 1:
            gla_stage(P, XF, OG1, gla_wc, gla_wga, V(gla_nw.h[j], "ln"), C, SEQ, tag="l%d" % i, rm=rm)
            P.gather_pairs(OG1, OGG1, SEQ, min(TC, 1024))
            outproj_stage(P, OGG1, SEQ, 1024, A, B, V(gla_wout.h[j], "lwo"), V(ln_g.h[i, 1], "g"), V(ln_b.h[i, 1], "b"),
                          ident, TC, alpha, isel, tag="p%d" % i)
        else:
            s5_stage(P, XF, OG1, L, ident, SEQ, isel, tag="s%d" % i, rm=rm)
            P.gather_pairs(OG1, OGG1, SEQ, min(TC, 1024))
            s5out_stage(P, OGG1, SEQ, A, B, V(s5_wglu.h[j], "sw"), V(ln_g.h[i, 1], "g"), V(ln_b.h[i, 1], "b"),
                        ident, TC, alpha, isel, tag="q%d" % i)
        dst = y if last else Cb
        ffn_stage(P, B, dst, V(w_up.h[i, 1], "wu"), V(w_dn.h[i, 1], "wd"), V(ln_g.h[i, 2], "g"), V(ln_b.h[i, 2], "b"),
                  ident, TC, alpha, tag="f%db" % i)
        xin = Cb
    P.finish()
    return nc, P


def make_in_maps(inputs, SEQ):
    TC = SEQ // 2
    f = lambda a: np.ascontiguousarray(np.asarray(a, dtype=np.float32))
    rep = {k: f(inputs[k]) for k in ("ln_g", "ln_b", "ffn_w_up", "ffn_w_down", "gdn_w_out", "gla_w_out", "s5_w_glu",
                                     "gdn_norm_w", "gla_norm_w")}
    consts = {"c_" + k: v for k, v in _consts_np().items()}
    x = np.asarray(inputs["x"], dtype=np.float32)
    gdn_w_in = np.asarray(inputs["gdn_w_in"], np.float32)
    gdn_conv_w = np.asarray(inputs["gdn_conv_w"], np.float32)
    gla_w_in = np.asarray(inputs["gla_w_in"], np.float32)[0]
    gla_w_gate = np.asarray(inputs["gla_w_gate"], np.float32)[0]
    gla_gb = np.asarray(inputs["gla_gate_bias"], np.float32)[0]
    half = []
    for hf in range(2):
        d = {}
        cols = np.concatenate([np.arange(hf * 1024, hf * 1024 + 1024), 2048 + np.arange(hf * 1024, hf * 1024 + 1024),
                               4096 + np.arange(hf * 2048, hf * 2048 + 2048), 8192 + np.arange(hf * 2048, hf * 2048 + 2048),
                               12288 + np.arange(hf * 16, hf * 16 + 16), 12320 + np.arange(hf * 16, hf * 16 + 16)])
        d["gdn_wc"] = np.ascontiguousarray(gdn_w_in[:, :, cols])
        cc = cols[:4096]
        d["gdn_conv"] = np.ascontiguousarray(
            gdn_conv_w[:, :, cc].reshape(2, 4, 32, 128).transpose(0, 3, 2, 1).reshape(2, 128, 128))
        d["gdn_alog"] = f(np.asarray(inputs["gdn_a_log"])[:, hf * 16:(hf + 1) * 16])
        d["gdn_dtb"] = f(np.asarray(inputs["gdn_dt_bias"])[:, hf * 16:(hf + 1) * 16])
        lc = np.concatenate([np.arange(hf * 512, hf * 512 + 512), 1024 + np.arange(hf * 512, hf * 512 + 512),
                             2048 + np.arange(hf * 1024, hf * 1024 + 1024), 4096 + np.arange(hf * 1024, hf * 1024 + 1024),
                             6144 + np.arange(16)])
        d["gla_wc"] = np.ascontiguousarray(gla_w_in[:, lc])
        d["gla_wga"] = np.ascontiguousarray(
            np.concatenate([gla_w_gate[:, hf * 512:(hf + 1) * 512], gla_gb[None, hf * 512:(hf + 1) * 512]], 0))
        Lnp = s5_layout_np(*[np.asarray(inputs[k], np.float32)[0] for k in
                             ("s5_a_re", "s5_a_im", "s5_log_dt", "s5_b_re", "s5_b_im", "s5_c_re", "s5_c_im", "s5_d")], hf)
        for k in S5_KEYS:
            d["s5L_" + k] = Lnp[k]
        half.append(d)
    in_maps = []
    for c in range(8):
        b, hf = c // 2, c % 2
        m = dict(rep)
        m.update(consts)
        m.update(half[hf])
        m["x"] = np.ascontiguousarray(x[b, hf * TC:(hf + 1) * TC])
        isel = np.zeros((2, 128, 128), np.float32)
        isel[hf] = np.eye(128, dtype=np.float32)
        m["isel"] = isel
        in_maps.append(m)
    return in_maps


_CACHE = {}


def run_model(inputs, SEQ, trace=False):
    from concourse.bass_utils import run_bass_kernel_spmd
    if SEQ not in _CACHE:
        _CACHE[SEQ] = build_program(SEQ)[0]
    nc = _CACHE[SEQ]
    in_maps = make_in_maps(inputs, SEQ)
    res = run_bass_kernel_spmd(nc, in_maps, core_ids=list(range(8)))
    TC = SEQ // 2
    out = np.empty((4, SEQ, D), np.float32)
    for c in range(8):
        b, hf = c // 2, c % 2
        out[b, hf * TC:(hf + 1) * TC] = res.results[c]["y"]
    return out


def kernel(**inputs):
    return run_model(inputs, 8192)
```

```python
import contextlib
import numpy as np
import concourse.bass as bass
import concourse.mybir as mybir

F32 = mybir.dt.float32
BF16 = mybir.dt.bfloat16
AF = mybir.ActivationFunctionType
ALU = mybir.AluOpType
AX = mybir.AxisListType


class Buf:
    __slots__ = ("h", "name", "w", "r", "dsem")

    def __init__(self, h, name):
        self.h = h
        self.name = name
        self.w = {}
        self.r = {}
        self.dsem = None


class Prog:
    ENG = ("pe", "act", "dve", "pool", "sp")

    def __init__(self, nc, n_dma_sems=80, same_engine_sync=True):
        self.nc = nc
        self.es = contextlib.ExitStack()
        self.eng = {"pe": nc.tensor, "act": nc.scalar, "dve": nc.vector, "pool": nc.gpsimd, "sp": nc.sync}
        self.sem = {}
        self.cnt = {}
        for e in self.ENG:
            self.sem[e] = self.es.enter_context(nc.semaphore("prog_" + e))
            self.cnt[e] = 0
        self.free_dsems = []
        for i in range(n_dma_sems):
            k = "d%d" % i
            self.sem[k] = self.es.enter_context(nc.semaphore("dma_" + k))
            self.cnt[k] = 0
            self.free_dsems.append(k)
        self.seen = {e: {} for e in self.ENG}
        self.same_engine_sync = same_engine_sync
        self.n_wait = 0
        self.n_inst = 0
        self.stage_bufs = []
        self.sem["cc"] = self.es.enter_context(nc.semaphore("cc_sem"))
        self.cnt["cc"] = 0

    def uniq(self, base):
        self._u = getattr(self, '_u', 0) + 1
        return '%s_%d' % (base, self._u)

    def sb(self, stack, name, shape, dt):
        b = Buf(stack.enter_context(self.nc.sbuf_tensor(name, list(shape), dt)), name)
        self.stage_bufs.append(b)
        return b

    def view(self, h, name):
        b = Buf(h, name)
        self.stage_bufs.append(b)
        return b

    def ps(self, stack, name, shape, dt=F32):
        return Buf(stack.enter_context(self.nc.psum_tensor(name, list(shape), dt)), name)

    def dram(self, name, shape, dt, kind="Internal"):
        return Buf(self.nc.dram_tensor(name, list(shape), dt, kind=kind).ap(), name)

    def release(self, bufs):
        for b in bufs:
            if b.dsem is not None:
                self.free_dsems.append(b.dsem)
                b.dsem = None

    def _wait(self, e, key, idx):
        if key == e and (e == "pe" or not self.same_engine_sync):
            return
        if self.seen[e].get(key, 0) >= idx:
            return
        self.seen[e][key] = idx
        self.eng[e].wait_ge(self.sem[key], idx)
        self.n_wait += 1

    def _deps(self, e, w, r):
        for b in r:
            for k, i in b.w.items():
                self._wait(e, k, i)
        for b in w:
            for k, i in b.w.items():
                self._wait(e, k, i)
            for k, i in b.r.items():
                self._wait(e, k, i)

    def op(self, e, fn, w=(), r=()):
        self._deps(e, w, r)
        inst = fn(self.eng[e])
        self.cnt[e] += 1
        idx = self.cnt[e]
        inst.then_inc(self.sem[e], 1)
        for b in w:
            b.w[e] = idx
        for b in r:
            b.r[e] = idx
        self.n_inst += 1
        return inst

    def dma(self, q, out_ap, in_ap, w, r, owner=None):
        self._deps(q, [w], [r])
        if owner is None:
            owner = w
        if owner.dsem is None:
            owner.dsem = self.free_dsems.pop()
        k = owner.dsem
        inst = self.eng[q].dma_start(out=out_ap, in_=in_ap)
        self.cnt[k] += 16
        inst.then_inc(self.sem[k], 16)
        w.w[k] = self.cnt[k]
        r.r[k] = self.cnt[k]
        self.n_inst += 1
        return inst

    def barrier(self):
        keys = [k for k in self.cnt if self.cnt[k] > 0]
        for e in self.ENG:
            for k in keys:
                if self.seen[e].get(k, 0) >= self.cnt[k]:
                    continue
                self.seen[e][k] = self.cnt[k]
                self.eng[e].wait_ge(self.sem[k], self.cnt[k])
                self.n_wait += 1

    def end_stage(self):
        self.barrier()
        self.release(self.stage_bufs)
        self.stage_bufs = []

    def collective(self, kind, src, dst, rg, src_ap=None, dst_ap=None):
        self._deps("pool", [dst], [src])
        src_ap = src.h if src_ap is None else src_ap
        dst_ap = dst.h if dst_ap is None else dst_ap
        inst = self.nc.gpsimd.collective_compute(kind, ALU.bypass, ins=[src_ap.opt()], outs=[dst_ap.opt()],
                                                 replica_groups=rg)
        self.cnt["cc"] += 1
        inst.then_inc(self.sem["cc"], 1)
        dst.w["cc"] = self.cnt["cc"]
        src.r["cc"] = self.cnt["cc"]
        self.n_inst += 1

    def gather_pairs(self, src, dstP, rows, PR):
        for p in range(rows // PR):
            self.collective("AllGather", src, dstP, [[0, 1], [2, 3], [4, 5], [6, 7]],
                            src_ap=src.h[p * PR:(p + 1) * PR, :], dst_ap=dstP.h[p * 2 * PR:(p + 1) * 2 * PR, :])

    def finish(self):
        self.barrier()
        self.es.close()


def bc(ap, shape):
    return ap.unsqueeze(len(ap.shape)).to_broadcast(list(shape))


class H:
    def __init__(self, P):
        self.P = P

    def mm(self, out, lhsT, rhs, w, r, start=True, stop=True):
        return self.P.op("pe", lambda e: e.matmul(out, lhsT, rhs, start=start, stop=stop), w=w, r=r)

    def tr(self, out, in_, ident, w, r):
        return self.P.op("pe", lambda e: e.transpose(out=out, in_=in_, identity=ident), w=w, r=r)

    def act(self, out, in_, func, w, r, scale=1.0, bias=None):
        if bias is None:
            return self.P.op("act", lambda e: e.activation(out=out, in_=in_, func=func, scale=scale), w=w, r=r)
        return self.P.op("act", lambda e: e.activation(out=out, in_=in_, func=func, scale=scale, bias=bias), w=w, r=r)

    def tt(self, eng, out, in0, in1, op, w, r):
        return self.P.op(eng, lambda e: e.tensor_tensor(out=out, in0=in0, in1=in1, op=op), w=w, r=r)

    def ts(self, eng, out, in0, s1, op0, w, r, s2=None, op1=None):
        if op1 is None:
            return self.P.op(eng, lambda e: e.tensor_scalar(out=out, in0=in0, scalar1=s1, scalar2=None, op0=op0), w=w, r=r)
        return self.P.op(eng, lambda e: e.tensor_scalar(out=out, in0=in0, scalar1=s1, scalar2=s2, op0=op0, op1=op1), w=w, r=r)

    def stt(self, out, in0, scalar, in1, op0, op1, w, r):
        return self.P.op("dve", lambda e: e.scalar_tensor_tensor(out=out, in0=in0, scalar=scalar, in1=in1, op0=op0, op1=op1), w=w, r=r)

    def cp(self, eng, out, in_, w, r):
        if eng == "act":
            return self.P.op("act", lambda e: e.activation(out=out, in_=in_, func=AF.Copy), w=w, r=r)
        return self.P.op(eng, lambda e: e.tensor_copy(out=out, in_=in_), w=w, r=r)

    def memset(self, eng, out, val, w):
        return self.P.op(eng, lambda e: e.memset(out, val), w=w)

import math
D = 2048
RMS_EPS = 1e-6

FF = 5632
LN_EPS = 1e-5


def load_consts(P, st, ident_d):
    idt = P.sb(st, P.uniq("identsb"), [128, 128], F32)
    P.dma("sp", idt.h[:], ident_d.h[:, :], w=idt, r=ident_d)
    epsT = P.sb(st, P.uniq("epsT"), [128, 1], F32)
    P.op("dve", lambda e: e.memset(epsT.h[:], LN_EPS), w=[epsT])
    return idt, epsT


def layer_norm_tile(P, z, Gt, Bt, epsT, tmp):
    stats, mv, rstd, nmr = tmp["stats"], tmp["mv"], tmp["rstd"], tmp["nmr"]
    for c in range(4):
        P.op("dve", lambda e: e.bn_stats(out=stats.h[:, c * 6:(c + 1) * 6], in_=z.h[:, c * 512:(c + 1) * 512]),
             w=[stats], r=[z])
    P.op("dve", lambda e: e.bn_aggr(out=mv.h[:], in_=stats.h[:]), w=[mv], r=[stats])
    P.op("act", lambda e: e.activation(out=rstd.h[:], in_=mv.h[:, 1:2], func=AF.Sqrt, bias=epsT.h[:, 0:1], scale=1.0),
         w=[rstd], r=[mv, epsT])
    P.op("dve", lambda e: e.reciprocal(out=rstd.h[:], in_=rstd.h[:]), w=[rstd], r=[rstd])
    P.op("dve", lambda e: e.tensor_scalar(out=nmr.h[:], in0=mv.h[:, 0:1], scalar1=rstd.h[:, 0:1], scalar2=-1.0,
                                          op0=ALU.mult, op1=ALU.mult), w=[nmr], r=[mv, rstd])
    P.op("act", lambda e: e.activation(out=z.h[:], in_=z.h[:], func=AF.Identity, scale=rstd.h[:, 0:1],
                                       bias=nmr.h[:, 0:1]), w=[z], r=[z, rstd, nmr])
    P.op("dve", lambda e: e.tensor_tensor(out=z.h[:], in0=z.h[:], in1=Gt.h[:], op=ALU.mult), w=[z], r=[z, Gt])
    P.op("dve", lambda e: e.tensor_tensor(out=z.h[:], in0=z.h[:], in1=Bt.h[:], op=ALU.add), w=[z], r=[z, Bt])


def ln_tmp(P, st, tag):
    return {"stats": P.sb(st, tag + "stats", [128, 24], F32), "mv": P.sb(st, tag + "mv", [128, 2], F32),
            "rstd": P.sb(st, tag + "rstd", [128, 1], F32), "nmr": P.sb(st, tag + "nmr", [128, 1], F32)}


def ffn_stage(P, x_in, x_out, w_up, w_down, ln_g, ln_b, ident_d, T, alpha, tag="f"):
    NB = T // 512
    with contextlib.ExitStack() as st:
        idt, epsT = load_consts(P, st, ident_d)
        Gt = P.sb(st, tag + "G", [128, D], F32)
        Bt = P.sb(st, tag + "B", [128, D], F32)
        P.dma("sp", Gt.h[:], ln_g.h.partition_broadcast(128), w=Gt, r=ln_g)
        P.dma("sp", Bt.h[:], ln_b.h.partition_broadcast(128), w=Bt, r=ln_b)
        xs = [P.sb(st, tag + "xs%d" % i, [128, D], F32) for i in range(2)]
        xT = P.sb(st, tag + "xT", [128, 16, 512], BF16)
        hT = P.sb(st, tag + "hT", [128, 44, 512], BF16)
        wg = [P.sb(st, tag + "wg%d" % i, [128, 16, 256], BF16) for i in range(3)]
        wu = [P.sb(st, tag + "wu%d" % i, [128, 16, 256], BF16) for i in range(3)]
        wd = [P.sb(st, tag + "wd%d" % i, [128, 11, 512], BF16) for i in range(2)]
        sg = [P.sb(st, tag + "sg%d" % i, [128, 512], F32) for i in range(2)]
        z = [P.sb(st, tag + "z%d" % i, [128, D], F32) for i in range(4)]
        tmps = [ln_tmp(P, st, tag + "t%d" % i) for i in range(2)]
        bank = [P.ps(st, tag + "bank%d" % i, [128, 512]) for i in range(8)]
        allb = [idt, epsT, Gt, Bt, xT, hT] + xs + wg + wu + wd + sg + z
        for t in tmps:
            allb += list(t.values())

        nwl = 0
        nwd = 0
        for blk in range(NB):
            t0 = blk * 512
            for s in range(4):
                xb = xs[s % 2]
                P.dma("sp", xb.h[:], x_in.h[t0 + s * 128:t0 + (s + 1) * 128, :], w=xb, r=x_in)
                for q in range(4):
                    bk = bank[4 + (s * 4 + q) % 4]
                    for i in range(4):
                        dk = q * 4 + i
                        P.op("pe", lambda e: e.transpose(out=bk.h[:, i * 128:(i + 1) * 128],
                                                         in_=xb.h[:, dk * 128:(dk + 1) * 128], identity=idt.h[:]),
                             w=[bk], r=[xb, idt])
                    eng = "act" if q % 2 == 0 else "dve"
                    src = bk.h[:].rearrange("p (a b) -> p a b", a=4)
                    dst = xT.h[:, q * 4:(q + 1) * 4, s * 128:(s + 1) * 128]
                    if eng == "act":
                        P.op("act", lambda e: e.activation(out=dst, in_=src, func=AF.Copy), w=[xT], r=[bk])
                    else:
                        P.op("dve", lambda e: e.tensor_copy(out=dst, in_=src), w=[xT], r=[bk])
            for g in range(22):
                sl = nwl % 3
                nwl += 1
                for k0 in range(0, 16, 4):
                    P.dma("pool", wg[sl].h[:, k0:k0 + 4, :],
                          w_up.h[k0 * 128:(k0 + 4) * 128, g * 256:(g + 1) * 256].rearrange("(k p) f -> p k f", p=128),
                          w=wg[sl], r=w_up)
                    P.dma("pool", wu[sl].h[:, k0:k0 + 4, :],
                          w_up.h[k0 * 128:(k0 + 4) * 128, FF + g * 256:FF + (g + 1) * 256].rearrange(
                              "(k p) f -> p k f", p=128),
                          w=wu[sl], r=w_up)
                for c in range(2):
                    j = g * 2 + c
                    pg = bank[(j % 2) * 2]
                    pu = bank[(j % 2) * 2 + 1]
                    for dk in range(16):
                        P.op("pe", lambda e: e.matmul(pg.h[:], wg[sl].h[:, dk, c * 128:(c + 1) * 128], xT.h[:, dk, :],
                                                      start=(dk == 0), stop=(dk == 15)), w=[pg], r=[wg[sl], xT])
                    for dk in range(16):
                        P.op("pe", lambda e: e.matmul(pu.h[:], wu[sl].h[:, dk, c * 128:(c + 1) * 128], xT.h[:, dk, :],
                                                      start=(dk == 0), stop=(dk == 15)), w=[pu], r=[wu[sl], xT])
                    sgb = sg[j % 2]
                    P.op("act", lambda e: e.activation(out=sgb.h[:], in_=pg.h[:], func=AF.Silu), w=[sgb], r=[pg])
                    P.op("dve", lambda e: e.tensor_tensor(out=hT.h[:, j, :], in0=sgb.h[:], in1=pu.h[:], op=ALU.mult),
                         w=[hT], r=[sgb, pu])
            for s in range(4):
                P.dma("sp", z[s].h[:], x_in.h[t0 + s * 128:t0 + (s + 1) * 128, :], w=z[s], r=x_in)
                P.op("act", lambda e: e.activation(out=z[s].h[:], in_=z[s].h[:], func=AF.Copy, scale=float(alpha)),
                     w=[z[s]], r=[z[s]])
            for dn in range(4):
                pb = [bank[(dn % 2) * 4 + s] for s in range(4)]
                for fg in range(4):
                    sl = nwd % 2
                    nwd += 1
                    for (a, n) in ((0, 4), (4, 4), (8, 3)):
                        r0 = (fg * 11 + a) * 128
                        P.dma("pool", wd[sl].h[:, a:a + n, :],
                              w_down.h[r0:r0 + n * 128, dn * 512:(dn + 1) * 512].rearrange("(k p) f -> p k f", p=128),
                              w=wd[sl], r=w_down)
                    for i in range(11):
                        fk = fg * 11 + i
                        for s in range(4):
                            P.op("pe", lambda e: e.matmul(pb[s].h[:], hT.h[:, fk, s * 128:(s + 1) * 128],
                                                          wd[sl].h[:, i, :], start=(fk == 0), stop=(fk == 43)),
                                 w=[pb[s]], r=[hT, wd[sl]])
                for s in range(4):
                    zc = z[s].h[:, dn * 512:(dn + 1) * 512]
                    P.op("dve", lambda e: e.scalar_tensor_tensor(out=zc, in0=pb[s].h[:], scalar=0.5, in1=zc,
                                                                 op0=ALU.mult, op1=ALU.add), w=[z[s]], r=[pb[s], z[s]])
            for s in range(4):
                layer_norm_tile(P, z[s], Gt, Bt, epsT, tmps[s % 2])
                P.dma("sp", x_out.h[t0 + s * 128:t0 + (s + 1) * 128, :], z[s].h[:], w=x_out, r=z[s], owner=z[s])
        P.end_stage()


NEGV = -30000.0


def gdn_consts_np():
    t = np.arange(128)
    same = (t[:, None] // 64) == (t[None, :] // 64)
    c = {}
    c["ident"] = np.eye(128, dtype=np.float32)
    c["ucs"] = (same & (t[:, None] <= t[None, :])).astype(np.float32)
    c["vsame"] = same.astype(np.float32)
    cind = np.zeros((128, 2, 128), np.float32)
    cind[:64, 0, :] = 1.0
    cind[64:, 1, :] = 1.0
    c["cind"] = cind.reshape(128, 256)
    c["negA"] = np.where(same & (t[None, :] >= t[:, None]), 0.0, NEGV).astype(np.float32)
    c["negL"] = np.where(same & (t[None, :] > t[:, None]), 0.0, NEGV).astype(np.float32)
    sel = np.zeros((32, 32, 128), np.float32)
    for h in range(32):
        sel[h, h, :] = 1.0
    c["sel"] = sel.reshape(32, 32 * 128)
    return c


def gdn_proj_stage(P, x_in, QKV, Zd, SSQd, BAd, w_c, conv_c, C, T, tag="gp", rm=lambda t: t):
    Hh = H(P)
    SBK = 512
    NSB = T // SBK
    with contextlib.ExitStack() as st:
        sbuf = lambda n, s, d=F32: P.sb(st, tag + n, s, d)
        idt = sbuf("idt", [128, 128])
        cw = sbuf("cw", [128, 128])
        ones = sbuf("ones", [128, 1])
        P.dma("sp", idt.h[:], C["ident"].h[:, :], w=idt, r=C["ident"])
        P.dma("sp", cw.h[:], conv_c.h[:, :], w=cw, r=conv_c)
        Hh.memset("dve", ones.h[:], 1.0, [ones])
        xs = [sbuf("xs%d" % i, [128, D]) for i in range(2)]
        xT = sbuf("xT", [128, 16, SBK], BF16)
        wf = [sbuf("wf%d" % i, [128, 16, 256], BF16) for i in range(3)]
        wba = sbuf("wba", [128, 16, 32], BF16)
        pre = [sbuf("pre%d" % i, [128, SBK + 3]) for i in range(2)]
        acc = [sbuf("acc%d" % i, [128, SBK]) for i in range(2)]
        sgl = [sbuf("sgl%d" % i, [128, SBK]) for i in range(2)]
        sq = [sbuf("sq%d" % i, [128, SBK]) for i in range(2)]
        ob = [sbuf("ob%d" % i, [128, SBK], BF16) for i in range(2)]
        carry = sbuf("carry", [128, 32, 3])
        baT = sbuf("baT", [32, SBK])
        zs4 = sbuf("zs4", [128, 4, 2048], BF16)
        ssq_sb = sbuf("ssqsb", [128, 64])
        Hh.memset("pool", carry.h[:], 0.0, [carry])
        banks = [P.ps(st, tag + "bank%d" % i, [128, 512]) for i in range(7)]
        smallbank = st.enter_context(P.nc.psum_tensor(tag + "smallbank", [128, 512], F32))
        ps_ssq = P.view(smallbank[:, 0:64], "ps_ssq")
        bctr = [0]

        def nb():
            b = banks[bctr[0] % 7]
            bctr[0] += 1
            return b

        for sb in range(NSB):
            t0 = sb * SBK
            for s in range(4):
                xb = xs[s % 2]
                P.dma("sp", xb.h[:], x_in.h[rm(t0 + s * 128):rm(t0 + s * 128) + 128, :], w=xb, r=x_in)
                for q in range(4):
                    bk = nb()
                    for i in range(4):
                        dk = q * 4 + i
                        Hh.tr(bk.h[:, i * 128:(i + 1) * 128], xb.h[:, dk * 128:(dk + 1) * 128], idt.h[:], [bk], [xb, idt])
                    Hh.cp("act" if q % 2 == 0 else "dve", xT.h[:, q * 4:(q + 1) * 4, s * 128:(s + 1) * 128],
                          bk.h[:].rearrange("p (a b) -> p a b", a=4), [xT], [bk])
            def load_w(idx, slot):
                col = idx * 256 if idx < 16 else 4096 + (idx - 16) * 256
                for k0 in range(0, 16, 4):
                    P.dma("pool", wf[slot].h[:, k0:k0 + 4, :],
                          w_c.h[k0 * 128:(k0 + 4) * 128, col:col + 256].rearrange("(k p) f -> p k f", p=128),
                          w=wf[slot], r=w_c)
            pend = []

            def flush_ssq():
                while pend:
                    sq__, fc__ = pend.pop(0)
                    for s_ in range(4):
                        Hh.mm(ps_ssq.h[:, s_ * 16 + fc__:s_ * 16 + fc__ + 1], sq__.h[:, s_ * 128:(s_ + 1) * 128],
                              ones.h[:, 0:1], [ps_ssq], [sq__, ones])
            load_w(0, 0)
            load_w(1, 1)
            for fp in range(16):
                sl = fp % 3
                load_w(fp + 2, (fp + 2) % 3)
                for c in range(2):
                    fc = fp * 2 + c
                    bk = nb()
                    for dk in range(16):
                        Hh.mm(bk.h[:], wf[sl].h[:, dk, c * 128:(c + 1) * 128], xT.h[:, dk, :], [bk], [wf[sl], xT],
                              start=(dk == 0), stop=(dk == 15))
                    flush_ssq()
                    pr = pre[fc % 2]
                    ac = acc[fc % 2]
                    obb = ob[fc % 2]
                    Hh.cp("dve", pr.h[:, 0:3], carry.h[:, fc, :], [pr], [carry])
                    Hh.cp("act", pr.h[:, 3:SBK + 3], bk.h[:], [pr], [bk])
                    Hh.cp("dve", carry.h[:, fc, :], pr.h[:, SBK:SBK + 3], [carry], [pr])
                    Hh.ts("dve", ac.h[:], pr.h[:, 0:SBK], cw.h[:, fc * 4:fc * 4 + 1], ALU.mult, [ac], [pr, cw])
                    for j in range(1, 4):
                        Hh.stt(ac.h[:], pr.h[:, j:j + SBK], cw.h[:, fc * 4 + j:fc * 4 + j + 1], ac.h[:], ALU.mult, ALU.add,
                               [ac], [pr, cw, ac])
                    if fc < 16:
                        sg_ = sgl[fc % 2]
                        sq_ = sq[fc % 2]
                        Hh.act(sg_.h[:], ac.h[:], AF.Silu, [sg_], [ac])
                        Hh.cp("dve", obb.h[:], sg_.h[:], [obb], [sg_])
                        Hh.act(sq_.h[:], sg_.h[:], AF.Square, [sq_], [sg_])
                        pend.append((sq_, fc))
                    else:
                        Hh.act(obb.h[:], ac.h[:], AF.Silu, [obb], [ac])
                    P.dma("sp", QKV.h[fc * 128:(fc + 1) * 128, t0:t0 + SBK], obb.h[:], w=QKV, r=obb, owner=obb)
            for k0 in range(0, 16, 4):
                P.dma("pool", wba.h[:, k0:k0 + 4, :],
                      w_c.h[k0 * 128:(k0 + 4) * 128, 6144:6176].rearrange("(k p) f -> p k f", p=128), w=wba, r=w_c)
            bk = nb()
            for dk in range(16):
                Hh.mm(bk.h[0:32, :], wba.h[:, dk, :], xT.h[:, dk, :], [bk], [wba, xT], start=(dk == 0), stop=(dk == 15))
            Hh.cp("act", baT.h[:], bk.h[0:32, :], [baT], [bk])
            P.dma("sp", BAd.h[:, t0:t0 + SBK], baT.h[:], w=BAd, r=baT, owner=baT)
            flush_ssq()
            Hh.cp("dve", ssq_sb.h[:], ps_ssq.h[:], [ssq_sb], [ps_ssq])
            for s in range(4):
                P.dma("sp", SSQd.h[t0 + s * 128:t0 + (s + 1) * 128, :], ssq_sb.h[:, s * 16:(s + 1) * 16], w=SSQd, r=ssq_sb,
                      owner=ssq_sb)
            for zg in range(8):
                sl = (16 + zg) % 3
                if zg + 2 < 8:
                    load_w(16 + zg + 2, (16 + zg + 2) % 3)
                for s in range(4):
                    bk = nb()
                    for dk in range(16):
                        Hh.mm(bk.h[:, 0:256], xT.h[:, dk, s * 128:(s + 1) * 128], wf[sl].h[:, dk, :], [bk], [wf[sl], xT],
                              start=(dk == 0), stop=(dk == 15))
                    Hh.act(zs4.h[:, s, zg * 256:(zg + 1) * 256], bk.h[:, 0:256], AF.Silu, [zs4], [bk])
            for s in range(4):
                P.dma("sp", Zd.h[t0 + s * 128:t0 + (s + 1) * 128, :], zs4.h[:, s, :], w=Zd, r=zs4, owner=zs4)
        P.end_stage()


def gdn_rec_stage(P, QKV, Zd, SSQd, BAd, og_out, alog_c, dtb_c, normw, C, T, tag="g"):
    Hh = H(P)
    SBK = 512
    NSB = T // SBK
    with contextlib.ExitStack() as st:
        sbuf = lambda n, s, d=F32: P.sb(st, tag + n, s, d)
        idt = sbuf("idt", [128, 128])
        idb = sbuf("idb", [128, 128], BF16)
        ucs = sbuf("ucs", [128, 128])
        vsame = sbuf("vsame", [128, 128])
        cind = sbuf("cind", [128, 256])
        negA = sbuf("negA", [128, 128])
        negL = sbuf("negL", [128, 128])
        sel = sbuf("sel", [32, 32 * 128])
        DTB = sbuf("DTB", [128, 16])
        NEGA = sbuf("NEGA", [128, 16])
        nw1 = sbuf("nw1", [128, 128])
        for (dst, src) in ((idt, C["ident"]), (ucs, C["ucs"]), (vsame, C["vsame"]), (cind, C["cind"]),
                           (negA, C["negA"]), (negL, C["negL"]), (sel, C["sel"])):
            P.dma("sp", dst.h[:], src.h[:, :], w=dst, r=src)
        P.dma("sp", DTB.h[:], dtb_c.h.partition_broadcast(128), w=DTB, r=dtb_c)
        P.dma("sp", NEGA.h[:], alog_c.h.partition_broadcast(128), w=NEGA, r=alog_c)
        P.dma("sp", nw1.h[:], normw.h.partition_broadcast(128), w=nw1, r=normw)
        Hh.cp("dve", idb.h[:], idt.h[:], [idb], [idt])
        Hh.act(NEGA.h[:], NEGA.h[:], AF.Exp, [NEGA], [NEGA])
        Hh.ts("dve", NEGA.h[:], NEGA.h[:], -1.0, ALU.mult, [NEGA], [NEGA])

        S = sbuf("S", [128, 16, 128])
        Sb = sbuf("Sb", [128, 16, 128], BF16)
        sm = {n: sbuf("sm_" + n, [128, 16]) for n in
              ("eb", "lb", "beta", "t1", "e1", "sp", "g", "gcs", "d", "kd", "a", "c1", "c2", "sa", "rq16", "lnrk16",
               "colb", "ssqo", "rms")}
        ba = sbuf("ba", [128, 32])
        lnr = sbuf("lnr", [128, 16])
        rqk = sbuf("rqk", [128, 16])
        R = sbuf("R", [128, 32])
        RT = sbuf("RT", [32, 128])
        EG = sbuf("EG", [128, 32])
        kba = sbuf("kba", [128, 16, 128])
        kdec = sbuf("kdec", [128, 16, 128], BF16)
        vb = sbuf("vb", [128, 16, 128])
        attnT = sbuf("attnT", [128, 16, 128], BF16)
        slot = []
        for i in range(2):
            slot.append({n: sbuf("s%d_%s" % (i, n), [128, 4, 128]) for n in
                         ("EL", "EA", "X0", "X1", "Y0", "Y1", "P0", "P1")})
        u = sbuf("u", [128, 16, 128])
        wT = sbuf("wT", [128, 16, 128], BF16)
        vnew = sbuf("vnew", [128, 16, 128], BF16)
        o = sbuf("o", [128, 16, 128])
        obf = sbuf("obf", [128, 16, 128], BF16)
        tmpS = [sbuf("tmpS%d" % i, [128, 4, 128]) for i in range(2)]
        epsr = sbuf("epsr", [128, 1])
        Hh.memset("dve", epsr.h[:], RMS_EPS, [epsr])
        Hh.memset("dve", S.h[:], 0.0, [S])
        Hh.memset("pool", Sb.h[:], 0.0, [Sb])

        banks = [P.ps(st, tag + "bank%d" % i, [128, 512]) for i in range(7)]
        smallbank = st.enter_context(P.nc.psum_tensor(tag + "smallbank", [128, 512], F32))
        ps_ba = P.view(smallbank[:, 32:64], "ps_ba")
        ps_g = P.view(smallbank[:, 64:128], "ps_g")
        ps_rt = P.view(smallbank[:, 128:256], "ps_rt")
        bctr = [0]

        def nb():
            b = banks[bctr[0] % 7]
            bctr[0] += 1
            return b


        qbuf = [sbuf("qkvT%d" % i, [128, 32, SBK], BF16) for i in range(2)]
        baT = sbuf("baT", [32, SBK])
        zsb = [sbuf("zsb%d" % i, [128, 2048], BF16) for i in range(2)]
        ssqb = [sbuf("ssqb%d" % i, [128, 16]) for i in range(2)]
        for sb in range(NSB):
            t0 = sb * SBK
            qkvT = qbuf[sb % 2]
            for c0 in range(0, 32, 4):
                P.dma("sp", qkvT.h[:, c0:c0 + 4, :],
                      QKV.h[c0 * 128:(c0 + 4) * 128, t0:t0 + SBK].rearrange("(c p) t -> p c t", p=128), w=qkvT, r=QKV)
            P.dma("sp", baT.h[:], BAd.h[:, t0:t0 + SBK], w=baT, r=BAd)
            for s in range(4):
                tb = t0 + s * 128
                zs_t = zsb[s % 2]
                ssq_t = ssqb[s % 2]
                P.dma("sp", zs_t.h[:], Zd.h[tb:tb + 128, :], w=zs_t, r=Zd)
                P.dma("sp", ssq_t.h[:], SSQd.h[tb:tb + 128, :], w=ssq_t, r=SSQd)
                cs = slice(s * 128, (s + 1) * 128)
                Hh.tr(ps_ba.h[:, 0:32], baT.h[:, cs], idt.h[0:32, 0:32], [ps_ba], [baT, idt])
                Hh.cp("dve", ba.h[:], ps_ba.h[:, 0:32], [ba], [ps_ba])
                m = sm
                Hh.act(m["eb"].h[:], ba.h[:, 0:16], AF.Exp, [m["eb"]], [ba], scale=-1.0)
                Hh.act(m["lb"].h[:], m["eb"].h[:], AF.Ln, [m["lb"]], [m["eb"]], bias=1.0)
                Hh.act(m["beta"].h[:], m["lb"].h[:], AF.Exp, [m["beta"]], [m["lb"]], scale=-1.0)
                Hh.tt("dve", m["t1"].h[:], ba.h[:, 16:32], DTB.h[:], ALU.add, [m["t1"]], [ba, DTB])
                Hh.act(m["e1"].h[:], m["t1"].h[:], AF.Exp, [m["e1"]], [m["t1"]])
                Hh.act(m["sp"].h[:], m["e1"].h[:], AF.Ln, [m["sp"]], [m["e1"]], bias=1.0)
                Hh.tt("dve", m["g"].h[:], m["sp"].h[:], NEGA.h[:], ALU.mult, [m["g"]], [m["sp"], NEGA])
                Hh.mm(ps_g.h[:, 0:16], ucs.h[:], m["g"].h[:], [ps_g], [ucs, m["g"]])
                Hh.mm(ps_g.h[:, 16:32], vsame.h[:], m["g"].h[:], [ps_g], [vsame, m["g"]])
                Hh.mm(ps_g.h[:, 32:48], cind.h[:, 0:128], m["g"].h[:], [ps_g], [cind, m["g"]])
                Hh.mm(ps_g.h[:, 48:64], cind.h[:, 128:256], m["g"].h[:], [ps_g], [cind, m["g"]])
                Hh.cp("dve", m["gcs"].h[:], ps_g.h[:, 0:16], [m["gcs"]], [ps_g])
                Hh.tt("dve", m["d"].h[:], ps_g.h[:, 16:32], m["gcs"].h[:], ALU.subtract, [m["d"]], [ps_g, m["gcs"]])
                Hh.act(m["kd"].h[:], m["d"].h[:], AF.Exp, [m["kd"]], [m["d"]])
                Hh.act(m["a"].h[:], m["gcs"].h[:], AF.Exp, [m["a"]], [m["gcs"]])
                Hh.act(EG.h[:], ps_g.h[:, 32:64], AF.Exp, [EG], [ps_g])
                Hh.act(lnr.h[:], ssq_t.h[:], AF.Ln, [lnr], [ssq_t, epsr], bias=epsr.h[:, 0:1])
                Hh.ts("dve", lnr.h[:], lnr.h[:], -0.5, ALU.mult, [lnr], [lnr])
                Hh.act(rqk.h[:, 0:8], lnr.h[:, 0:8], AF.Exp, [rqk], [lnr], bias=-0.5 * math.log(128.0))
                Hh.act(rqk.h[:, 8:16], lnr.h[:, 8:16], AF.Exp, [rqk], [lnr])
                v2 = lambda ap: ap.rearrange("p (a b) -> p a b", b=2)
                Hh.cp("dve", v2(m["rq16"].h[:]), bc(rqk.h[:, 0:8], [128, 8, 2]), [m["rq16"]], [rqk])
                Hh.cp("dve", v2(m["lnrk16"].h[:]), bc(lnr.h[:, 8:16], [128, 8, 2]), [m["lnrk16"]], [lnr])
                Hh.tt("dve", m["c1"].h[:], m["beta"].h[:], m["a"].h[:], ALU.mult, [m["c1"]], [m["beta"], m["a"]])
                Hh.tt("dve", v2(m["c1"].h[:]), v2(m["c1"].h[:]), bc(rqk.h[:, 8:16], [128, 8, 2]), ALU.mult, [m["c1"]],
                      [m["c1"], rqk])
                Hh.tt("dve", v2(m["c2"].h[:]), v2(m["kd"].h[:]), bc(rqk.h[:, 8:16], [128, 8, 2]), ALU.mult, [m["c2"]],
                      [m["kd"], rqk])
                Hh.tt("dve", m["sa"].h[:], m["rq16"].h[:], m["a"].h[:], ALU.mult, [m["sa"]], [m["rq16"], m["a"]])
                Hh.cp("dve", R.h[:, 0:16], m["gcs"].h[:], [R], [m["gcs"]])
                Hh.tt("dve", R.h[:, 16:32], m["gcs"].h[:], m["lb"].h[:], ALU.subtract, [R], [m["gcs"], m["lb"]])
                Hh.tt("dve", R.h[:, 16:32], R.h[:, 16:32], m["lnrk16"].h[:], ALU.add, [R], [R, m["lnrk16"]])
                Hh.tt("dve", m["colb"].h[:], m["lnrk16"].h[:], m["gcs"].h[:], ALU.subtract, [m["colb"]],
                      [m["lnrk16"], m["gcs"]])
                Hh.tr(ps_rt.h[0:32, :], R.h[:], idt.h[:], [ps_rt], [R, idt])
                Hh.cp("act", RT.h[:], ps_rt.h[0:32, :], [RT], [ps_rt])
                bk = nb()
                kv = bk.h[:].bitcast(BF16).rearrange("p (a b) -> p a b", a=8)
                for hk in range(8):
                    Hh.tr(kv[:, hk, :], qkvT.h[:, 8 + hk, cs], idb.h[:], [bk], [qkvT, idb])
                kvb = bc3 = kv.unsqueeze(2).to_broadcast([128, 8, 2, 128])
                v4 = lambda ap: ap.rearrange("p (a b) c -> p a b c", b=2)
                Hh.tt("dve", v4(kba.h[:]), kvb, v4(bc(m["c1"].h[:], [128, 16, 128])), ALU.mult, [kba], [bk, m["c1"]])
                Hh.tt("dve", v4(kdec.h[:]), kvb, v4(bc(m["c2"].h[:], [128, 16, 128])), ALU.mult, [kdec], [bk, m["c2"]])
                for half in range(2):
                    bk = nb()
                    vv = bk.h[:].bitcast(BF16).rearrange("p (a b) -> p a b", a=8)
                    for i in range(8):
                        h = half * 8 + i
                        Hh.tr(vv[:, i, :], qkvT.h[:, 16 + h, cs], idb.h[:], [bk], [qkvT, idb])
                    Hh.tt("dve", vb.h[:, half * 8:(half + 1) * 8, :], vv,
                          bc(m["beta"].h[:, half * 8:(half + 1) * 8], [128, 8, 128]), ALU.mult, [vb], [bk, m["beta"]])
                for gp in range(2):
                    grp = [gp * 2, gp * 2 + 1]
                    sd = {}
                    for gi, mgrp in enumerate(grp):
                        sl_ = slot[gi]
                        sd[mgrp] = sl_
                        bkq = nb()
                        for i in range(2):
                            hk = mgrp * 2 + i
                            Hh.mm(bkq.h[:, i * 128:(i + 1) * 128], qkvT.h[:, 8 + hk, cs], qkvT.h[:, 8 + hk, cs], [bkq], [qkvT])
                            Hh.mm(bkq.h[:, 256 + i * 128:256 + (i + 1) * 128], qkvT.h[:, 8 + hk, cs], qkvT.h[:, hk, cs],
                                  [bkq], [qkvT])
                        bl = nb()
                        ba_ = nb()
                        for i in range(4):
                            h = mgrp * 4 + i
                            Hh.mm(bl.h[:, i * 128:(i + 1) * 128], idt.h[:], negL.h[:], [bl], [idt, negL], start=True, stop=False)
                            Hh.mm(bl.h[:, i * 128:(i + 1) * 128], sel.h[:, (16 + h) * 128:(17 + h) * 128], RT.h[:], [bl],
                                  [sel, RT], start=False, stop=True)
                            Hh.mm(ba_.h[:, i * 128:(i + 1) * 128], idt.h[:], negA.h[:], [ba_], [idt, negA], start=True,
                                  stop=False)
                            Hh.mm(ba_.h[:, i * 128:(i + 1) * 128], sel.h[:, h * 128:(h + 1) * 128], RT.h[:], [ba_], [sel, RT],
                                  start=False, stop=True)
                        for i in range(4):
                            h = mgrp * 4 + i
                            Hh.act(sl_["EL"].h[:, i, :], bl.h[:, i * 128:(i + 1) * 128], AF.Exp, [sl_["EL"]], [bl, m["colb"]],
                                   bias=m["colb"].h[:, h:h + 1])
                            Hh.act(sl_["EA"].h[:, i, :], ba_.h[:, i * 128:(i + 1) * 128], AF.Exp, [sl_["EA"]], [ba_, m["colb"]],
                                   bias=m["colb"].h[:, h:h + 1])
                        kk = bkq.h[:, 0:256].rearrange("p (a c) -> p a c", a=2).unsqueeze(2).to_broadcast([128, 2, 2, 128])
                        kq = bkq.h[:, 256:512].rearrange("p (a c) -> p a c", a=2).unsqueeze(2).to_broadcast([128, 2, 2, 128])
                        Hh.tt("dve", v4(sl_["X0"].h[:]), kk, v4(sl_["EL"].h[:]), ALU.mult, [sl_["X0"]], [bkq, sl_["EL"]])
                        Hh.tt("dve", v4(attnT.h[:, mgrp * 4:(mgrp + 1) * 4, :]), kq, v4(sl_["EA"].h[:]), ALU.mult, [attnT],
                              [bkq, sl_["EA"]])
                        bt_ = nb()
                        for i in range(4):
                            Hh.tr(bt_.h[:, i * 128:(i + 1) * 128], sl_["X0"].h[:, i, :], idt.h[:], [bt_], [sl_["X0"], idt])
                        Hh.cp("act", sl_["Y0"].h[:], bt_.h[:].rearrange("p (a b) -> p a b", a=4), [sl_["Y0"]], [bt_])
                        Hh.tt("dve", sl_["P0"].h[:], idt.h[:].unsqueeze(1).to_broadcast([128, 4, 128]), sl_["X0"].h[:],
                              ALU.subtract, [sl_["P0"]], [idt, sl_["X0"]])
                    for lev in range(5):
                        a_, b_ = lev % 2, (lev + 1) % 2
                        bx, by, bp = {}, {}, {}
                        for mgrp in grp:
                            sl_ = sd[mgrp]
                            X, Y = sl_["X%d" % a_], sl_["Y%d" % a_]
                            by[mgrp] = nb()
                            for i in range(4):
                                Hh.mm(by[mgrp].h[:, i * 128:(i + 1) * 128], X.h[:, i, :], Y.h[:, i, :], [by[mgrp]], [X, Y])
                            if lev < 4:
                                bx[mgrp] = nb()
                                for i in range(4):
                                    Hh.mm(bx[mgrp].h[:, i * 128:(i + 1) * 128], Y.h[:, i, :], X.h[:, i, :], [bx[mgrp]], [X, Y])
                        for mgrp in grp:
                            sl_ = sd[mgrp]
                            Hh.cp("dve", sl_["Y%d" % b_].h[:], by[mgrp].h[:].rearrange("p (a b) -> p a b", a=4),
                                  [sl_["Y%d" % b_]], [by[mgrp]])
                            if lev < 4:
                                Hh.cp("act", sl_["X%d" % b_].h[:], bx[mgrp].h[:].rearrange("p (a b) -> p a b", a=4),
                                      [sl_["X%d" % b_]], [bx[mgrp]])
                        for mgrp in grp:
                            sl_ = sd[mgrp]
                            Yn, Pc = sl_["Y%d" % b_], sl_["P%d" % a_]
                            bp[mgrp] = nb()
                            for i in range(4):
                                Hh.mm(bp[mgrp].h[:, i * 128:(i + 1) * 128], Yn.h[:, i, :], Pc.h[:, i, :], [bp[mgrp]], [Yn, Pc])
                        for mgrp in grp:
                            sl_ = sd[mgrp]
                            Hh.tt("dve", sl_["P%d" % b_].h[:], sl_["P%d" % a_].h[:],
                                  bp[mgrp].h[:].rearrange("p (a b) -> p a b", a=4), ALU.add, [sl_["P%d" % b_]],
                                  [sl_["P%d" % a_], bp[mgrp]])
                    for mgrp in grp:
                        AT = sd[mgrp]["P1"]
                        bu = nb()
                        bw = nb()
                        for i in range(4):
                            h = mgrp * 4 + i
                            Hh.mm(bu.h[:, i * 128:(i + 1) * 128], AT.h[:, i, :], vb.h[:, h, :], [bu], [AT, vb])
                            Hh.mm(bw.h[:, i * 128:(i + 1) * 128], kba.h[:, h, :], AT.h[:, i, :], [bw], [AT, kba])
                        Hh.cp("act", u.h[:, mgrp * 4:(mgrp + 1) * 4, :], bu.h[:].rearrange("p (a b) -> p a b", a=4), [u], [bu])
                        Hh.cp("dve", wT.h[:, mgrp * 4:(mgrp + 1) * 4, :], bw.h[:].rearrange("p (a b) -> p a b", a=4), [wT], [bw])
                for c in range(2):
                    rs = slice(c * 64, (c + 1) * 64)
                    for mgrp in range(4):
                        hs = slice(mgrp * 4, (mgrp + 1) * 4)
                        bws = nb()
                        bo1 = nb()
                        for i in range(4):
                            h = mgrp * 4 + i
                            Hh.mm(bws.h[:, i * 128:(i + 1) * 128], wT.h[:, h, :], Sb.h[:, h, :], [bws], [wT, Sb])
                            Hh.mm(bo1.h[:, i * 128:(i + 1) * 128], qkvT.h[:, h // 2, cs], Sb.h[:, h, :], [bo1], [qkvT, Sb])
                        Hh.tt("dve", vnew.h[rs, hs, :], u.h[rs, hs, :], bws.h[rs, :].rearrange("p (a b) -> p a b", a=4),
                              ALU.subtract, [vnew], [u, bws])
                        Hh.tt("dve", o.h[rs, hs, :], bo1.h[rs, :].rearrange("p (a b) -> p a b", a=4),
                              bc(m["sa"].h[rs, hs], [64, 4, 128]), ALU.mult, [o], [bo1, m["sa"]])
                        bs = nb()
                        for i in range(4):
                            h = mgrp * 4 + i
                            Hh.mm(bs.h[:, i * 128:(i + 1) * 128], kdec.h[rs, h, :], vnew.h[rs, h, :], [bs], [kdec, vnew])
                        tS = tmpS[mgrp % 2]
                        Hh.tt("pool", tS.h[:], S.h[:, hs, :], bc(EG.h[:, c * 16 + mgrp * 4:c * 16 + mgrp * 4 + 4], [128, 4, 128]),
                              ALU.mult, [tS], [S, EG])
                        Hh.tt("dve", S.h[:, hs, :], tS.h[:], bs.h[:].rearrange("p (a b) -> p a b", a=4), ALU.add, [S], [tS, bs])
                        Hh.cp("act", Sb.h[:, hs, :], S.h[:, hs, :], [Sb], [S])
                for mgrp in range(4):
                    hs = slice(mgrp * 4, (mgrp + 1) * 4)
                    bo2 = nb()
                    for i in range(4):
                        h = mgrp * 4 + i
                        Hh.mm(bo2.h[:, i * 128:(i + 1) * 128], attnT.h[:, h, :], vnew.h[:, h, :], [bo2], [attnT, vnew])
                    Hh.tt("dve", tmpS[mgrp % 2].h[:], bo2.h[:].rearrange("p (a b) -> p a b", a=4),
                          bc(m["rq16"].h[:, hs], [128, 4, 128]), ALU.mult, [tmpS[mgrp % 2]], [bo2, m["rq16"]])
                    Hh.tt("pool", o.h[:, hs, :], o.h[:, hs, :], tmpS[mgrp % 2].h[:], ALU.add, [o], [o, tmpS[mgrp % 2]])
                Hh.tt("pool", u.h[:], o.h[:], o.h[:], ALU.mult, [u], [o])
                P.op("dve", lambda e: e.tensor_reduce(out=m["ssqo"].h[:], in_=u.h[:], op=ALU.add, axis=AX.X), w=[m["ssqo"]], r=[u])
                Hh.act(m["rms"].h[:], m["ssqo"].h[:], AF.Sqrt, [m["rms"]], [m["ssqo"], epsr], scale=1.0 / 128.0,
                       bias=epsr.h[:, 0:1])
                P.op("dve", lambda e: e.reciprocal(out=m["rms"].h[:], in_=m["rms"].h[:]), w=[m["rms"]], r=[m["rms"]])
                Hh.tt("dve", o.h[:], o.h[:], bc(m["rms"].h[:], [128, 16, 128]), ALU.mult, [o], [o, m["rms"]])
                Hh.tt("pool", o.h[:], o.h[:], nw1.h[:].unsqueeze(1).to_broadcast([128, 16, 128]), ALU.mult, [o], [o, nw1])
                Hh.tt("dve", obf.h[:], o.h[:], zs_t.h[:].rearrange("p (a b) -> p a b", a=16), ALU.mult, [obf], [o, zs_t])
                P.dma("sp", og_out.h[tb:tb + 128, :], obf.h[:].rearrange("p a b -> p (a b)"), w=og_out, r=obf, owner=obf)
        P.end_stage()


def gla_consts_np():
    t = np.arange(128)
    same = (t[:, None] // 64) == (t[None, :] // 64)
    c = {}
    c["ident"] = np.eye(128, dtype=np.float32)
    c["ucs"] = (same & (t[:, None] <= t[None, :])).astype(np.float32)
    c["vsame"] = same.astype(np.float32)
    c["mask01"] = (same & (t[None, :] >= t[:, None])).astype(np.float32)
    cind2 = np.zeros((128, 2), np.float32)
    cind2[:64, 0] = 1.0
    cind2[64:, 1] = 1.0
    c["cind2"] = cind2
    return c


def gla_stage(P, x_in, og_out, w_c, wg_aug, normw, C, T, tag="l", rm=lambda t: t):
    Hh = H(P)
    SBK = 512
    NSB = T // SBK
    with contextlib.ExitStack() as st:
        sbuf = lambda n, s, d=F32: P.sb(st, tag + n, s, d)
        idt = sbuf("idt", [128, 128])
        ucs = sbuf("ucs", [128, 128])
        vsame = sbuf("vsame", [128, 128])
        mask01 = sbuf("mask01", [128, 128])
        cind2 = sbuf("cind2", [128, 2])
        wga = sbuf("wga", [17, 512])
        nw1 = sbuf("nw1", [128, 512])
        for (dst, src) in ((idt, C["ident"]), (ucs, C["ucs"]), (vsame, C["vsame"]), (mask01, C["mask01"]),
                           (cind2, C["cind2"]), (wga, wg_aug)):
            P.dma("sp", dst.h[:], src.h[:, :], w=dst, r=src)
        P.dma("sp", nw1.h[:], normw.h.partition_broadcast(128), w=nw1, r=normw)
        xs = sbuf("xs", [128, D])
        xT = sbuf("xT", [128, 16, SBK], BF16)
        wf = [sbuf("wf%d" % i, [128, 16, 256], BF16) for i in range(2)]
        wgl = sbuf("wgl", [128, 16, 16], BF16)
        qkT = sbuf("qkT", [128, 8, SBK])
        ktok = sbuf("ktok", [128, 4, 512])
        vtok = sbuf("vtok", [128, 4, 1024], BF16)
        rs_ = sbuf("rs", [128, 4, 1024])
        glT = sbuf("glT", [32, SBK])
        S = sbuf("S", [128, 4, 512])
        Sb = sbuf("Sb", [128, 4, 512], BF16)
        ez = sbuf("ez", [128, 512])
        ftok = sbuf("ftok", [128, 512])
        bcs = sbuf("bcs", [128, 512])
        dd = sbuf("dd", [128, 512])
        kdec = sbuf("kdec", [128, 512], BF16)
        eP = sbuf("eP", [128, 4, 128])
        eN = sbuf("eN", [128, 4, 128])
        qt = sbuf("qt", [128, 4, 128], BF16)
        kt = sbuf("kt", [128, 4, 128], BF16)
        EB = sbuf("EB", [128, 8])
        attnT = sbuf("attnT", [128, 2, 128], BF16)
        o = sbuf("o", [128, 2, 512])
        obf = sbuf("obf", [128, 2, 512], BF16)
        sqo = sbuf("sqo", [128, 2, 512])
        ssq = sbuf("ssq", [128, 2])
        rms = sbuf("rms", [128, 2])
        epsr = sbuf("epsr", [128, 1])
        Hh.memset("dve", epsr.h[:], RMS_EPS, [epsr])
        Hh.memset("dve", S.h[:], 0.0, [S])
        Hh.memset("pool", Sb.h[:], 0.0, [Sb])
        Hh.memset("pool", glT.h[:], 1.0, [glT])
        banks = [P.ps(st, tag + "bank%d" % i, [128, 512]) for i in range(8)]
        bctr = [0]

        def nb():
            b = banks[bctr[0] % 8]
            bctr[0] += 1
            return b

        nwf = 0
        for sb in range(NSB):
            t0 = sb * SBK
            for s in range(4):
                P.dma("sp", xs.h[:], x_in.h[rm(t0 + s * 128):rm(t0 + s * 128) + 128, :], w=xs, r=x_in)
                for q in range(4):
                    bk = nb()
                    for i in range(4):
                        dk = q * 4 + i
                        Hh.tr(bk.h[:, i * 128:(i + 1) * 128], xs.h[:, dk * 128:(dk + 1) * 128], idt.h[:], [bk], [xs, idt])
                    Hh.cp("act" if q % 2 == 0 else "dve", xT.h[:, q * 4:(q + 1) * 4, s * 128:(s + 1) * 128],
                          bk.h[:].rearrange("p (a b) -> p a b", a=4), [xT], [bk])
            for g in range(12):
                sl = nwf % 2
                nwf += 1
                for k0 in range(0, 16, 4):
                    P.dma("pool", wf[sl].h[:, k0:k0 + 4, :],
                          w_c.h[k0 * 128:(k0 + 4) * 128, g * 256:(g + 1) * 256].rearrange("(k p) f -> p k f", p=128),
                          w=wf[sl], r=w_c)
                if g < 4:
                    for c in range(2):
                        fc = g * 2 + c
                        bk = nb()
                        for dk in range(16):
                            Hh.mm(bk.h[:, 0:SBK], wf[sl].h[:, dk, c * 128:(c + 1) * 128], xT.h[:, dk, :], [bk], [wf[sl], xT],
                                  start=(dk == 0), stop=(dk == 15))
                        Hh.cp("act", qkT.h[:, fc, :], bk.h[:, 0:SBK], [qkT], [bk])
                if g >= 2:
                    for s in range(4):
                        bk = nb()
                        for dk in range(16):
                            Hh.mm(bk.h[:, 0:256], xT.h[:, dk, s * 128:(s + 1) * 128], wf[sl].h[:, dk, :], [bk], [wf[sl], xT],
                                  start=(dk == 0), stop=(dk == 15))
                        if g < 4:
                            Hh.cp("dve", ktok.h[:, s, (g - 2) * 256:(g - 1) * 256], bk.h[:, 0:256], [ktok], [bk])
                        elif g < 8:
                            Hh.cp("dve", vtok.h[:, s, (g - 4) * 256:(g - 3) * 256], bk.h[:, 0:256], [vtok], [bk])
                        else:
                            Hh.act(rs_.h[:, s, (g - 8) * 256:(g - 7) * 256], bk.h[:, 0:256], AF.Silu, [rs_], [bk])
            for k0 in range(0, 16, 4):
                P.dma("pool", wgl.h[:, k0:k0 + 4, :],
                      w_c.h[k0 * 128:(k0 + 4) * 128, 3072:3088].rearrange("(k p) f -> p k f", p=128), w=wgl, r=w_c)
            bk = nb()
            for dk in range(16):
                Hh.mm(bk.h[0:16, 0:SBK], wgl.h[:, dk, :], xT.h[:, dk, :], [bk], [wgl, xT], start=(dk == 0), stop=(dk == 15))
            Hh.cp("act", glT.h[0:16, :], bk.h[0:16, 0:SBK], [glT], [bk])

            for s in range(4):
                tb = t0 + s * 128
                cs = slice(s * 128, (s + 1) * 128)
                bz = nb()
                Hh.mm(bz.h[:], glT.h[0:17, cs], wga.h[:], [bz], [glT, wga])
                Hh.act(ez.h[:], bz.h[:], AF.Exp, [ez], [bz], scale=-1.0)
                Hh.act(ez.h[:], ez.h[:], AF.Ln, [ez], [ez], bias=1.0)
                Hh.ts("dve", ftok.h[:], ez.h[:], -1.0 / 16.0, ALU.mult, [ftok], [ez])
                bcu = nb()
                bto = nb()
                Hh.mm(bcu.h[:], ucs.h[:], ftok.h[:], [bcu], [ucs, ftok])
                Hh.mm(bto.h[:], vsame.h[:], ftok.h[:], [bto], [vsame, ftok])
                Hh.cp("act", bcs.h[:], bcu.h[:], [bcs], [bcu])
                Hh.tt("dve", dd.h[:], bto.h[:], bcs.h[:], ALU.subtract, [dd], [bto, bcs])
                Hh.act(dd.h[:], dd.h[:], AF.Exp, [dd], [dd])
                Hh.tt("dve", kdec.h[:], ktok.h[:, s, :], dd.h[:], ALU.mult, [kdec], [ktok, dd])
                bT = nb()
                bl = nb()
                for kc in range(4):
                    Hh.mm(bT.h[:, kc * 128:(kc + 1) * 128], ftok.h[:, kc * 128:(kc + 1) * 128], ucs.h[:], [bT], [ftok, ucs])
                    Hh.mm(bl.h[:, kc * 2:(kc + 1) * 2], ftok.h[:, kc * 128:(kc + 1) * 128], cind2.h[:], [bl], [ftok, cind2])
                bT3 = bT.h[:].rearrange("p (a b) -> p a b", a=4)
                Hh.act(eP.h[:], bT3, AF.Exp, [eP], [bT], bias=-0.5 * math.log(256.0))
                Hh.act(eN.h[:], bT3, AF.Exp, [eN], [bT], scale=-1.0)
                Hh.act(EB.h[:], bl.h[:, 0:8], AF.Exp, [EB], [bl])
                Hh.tt("dve", qt.h[:], qkT.h[:, 0:4, cs], eP.h[:], ALU.mult, [qt], [qkT, eP])
                Hh.tt("dve", kt.h[:], qkT.h[:, 4:8, cs], eN.h[:], ALU.mult, [kt], [qkT, eN])
                ba_ = nb()
                for h in range(2):
                    for kc in range(2):
                        Hh.mm(ba_.h[:, h * 128:(h + 1) * 128], kt.h[:, h * 2 + kc, :], qt.h[:, h * 2 + kc, :], [ba_], [kt, qt],
                              start=(kc == 0), stop=(kc == 1))
                Hh.tt("dve", attnT.h[:], ba_.h[:, 0:256].rearrange("p (a b) -> p a b", a=2),
                      mask01.h[:].unsqueeze(1).to_broadcast([128, 2, 128]), ALU.mult, [attnT], [ba_, mask01])
                for c in range(2):
                    rs = slice(c * 64, (c + 1) * 64)
                    for h in range(2):
                        vh = vtok.h[rs, s, h * 512:(h + 1) * 512]
                        bo = nb()
                        Hh.mm(bo.h[:], qt.h[:, h * 2, :], Sb.h[:, h * 2, :], [bo], [qt, Sb], start=True, stop=False)
                        Hh.mm(bo.h[:], qt.h[:, h * 2 + 1, :], Sb.h[:, h * 2 + 1, :], [bo], [qt, Sb], start=False, stop=False)
                        Hh.mm(bo.h[:], attnT.h[rs, h, :], vh, [bo], [attnT, vtok], start=False, stop=True)
                        Hh.cp("act", o.h[rs, h, :], bo.h[rs, :], [o], [bo])
                        for kc in range(2):
                            i4 = h * 2 + kc
                            bs = nb()
                            Hh.mm(bs.h[:], kdec.h[rs, i4 * 128:(i4 + 1) * 128], vh, [bs], [kdec, vtok])
                            Hh.stt(S.h[:, i4, :], S.h[:, i4, :], EB.h[:, i4 * 2 + c:i4 * 2 + c + 1], bs.h[:], ALU.mult, ALU.add,
                                   [S], [S, EB, bs])
                            Hh.cp("act", Sb.h[:, i4, :], S.h[:, i4, :], [Sb], [S])
                Hh.tt("pool", sqo.h[:], o.h[:], o.h[:], ALU.mult, [sqo], [o])
                P.op("dve", lambda e: e.tensor_reduce(out=ssq.h[:], in_=sqo.h[:], op=ALU.add, axis=AX.X), w=[ssq], r=[sqo])
                Hh.act(rms.h[:], ssq.h[:], AF.Sqrt, [rms], [ssq, epsr], scale=1.0 / 512.0, bias=epsr.h[:, 0:1])
                P.op("dve", lambda e: e.reciprocal(out=rms.h[:], in_=rms.h[:]), w=[rms], r=[rms])
                Hh.tt("dve", o.h[:], o.h[:], bc(rms.h[:], [128, 2, 512]), ALU.mult, [o], [o, rms])
                Hh.tt("pool", o.h[:], o.h[:], nw1.h[:].unsqueeze(1).to_broadcast([128, 2, 512]), ALU.mult, [o], [o, nw1])
                Hh.tt("dve", obf.h[:], o.h[:], rs_.h[:, s, :].rearrange("p (a b) -> p a b", a=2), ALU.mult, [obf], [o, rs_])
                P.dma("sp", og_out.h[tb:tb + 128, :], obf.h[:].rearrange("p a b -> p (a b)"), w=og_out, r=obf, owner=obf)
        P.end_stage()


I32 = mybir.dt.int32
LC = 128


def s5_layout_np(a_re, a_im, log_dt, b_re, b_im, c_re, c_im, d_skip, hf):
    gs = slice(hf * 64, (hf + 1) * 64)

    def pp(a):
        return np.ascontiguousarray(a[gs].reshape(32, 2, 64).transpose(1, 2, 0).reshape(128, 32))
    out = {}
    out["are"] = pp(a_re)
    out["aim"] = pp(a_im)
    out["ldt"] = np.ascontiguousarray(np.broadcast_to(log_dt[gs].reshape(32, 2).T[:, None, :], (2, 64, 32)).reshape(128, 32))

    def bb(b):
        return np.ascontiguousarray(b[gs].reshape(32, 2, 64, 16).transpose(1, 2, 0, 3).reshape(128, 512))
    out["bre"] = bb(b_re)
    out["bim"] = bb(b_im)

    def cc(c):
        o = np.zeros((2, 64, 32, 2, 16), np.float32)
        cg = c[gs].reshape(32, 2, 16, 64)
        for g2 in range(2):
            o[g2, :, :, g2, :] = cg[:, g2].transpose(2, 0, 1)
        return o.reshape(128, 32 * 32)
    out["cre"] = cc(c_re)
    out["cim"] = cc(c_im)
    out["dsk"] = np.ascontiguousarray(d_skip[hf * 1024:(hf + 1) * 1024])
    ms = np.zeros(2, np.float32)
    ms[hf] = 1.0
    out["msel"] = ms
    return out


def s5_stage(P, x_in, y_out, L, ident_d, T, isel_d, tag="s", rm=lambda t: t):
    Hh = H(P)
    NCH = T // LC
    with contextlib.ExitStack() as st:
        sbuf = lambda n, s, d=F32: P.sb(st, tag + n, s, d)
        idt = sbuf("idt", [128, 128])
        P.dma("sp", idt.h[:], ident_d.h[:, :], w=idt, r=ident_d)
        sm = {n: sbuf("sm_" + n, [128, 32]) for n in
              ("are", "aim", "dt", "ar", "th", "kf", "hl", "sh", "ah", "ch", "sn", "cs", "abr", "abi", "nr", "den", "cre", "cim",
               "t1", "t2", "hpr", "hpi")}
        ki = sbuf("ki", [128, 32], I32)
        for n in ("are", "aim"):
            P.dma("sp", sm[n].h[:], L[n].h[:, :], w=sm[n], r=L[n])
        P.dma("sp", sm["dt"].h[:], L["ldt"].h[:, :], w=sm["dt"], r=L["ldt"])
        m = sm
        tt = lambda eng, o_, a, b, op: Hh.tt(eng, o_.h[:], a.h[:], b.h[:], op, [o_], [a, b])
        Hh.act(m["dt"].h[:], m["dt"].h[:], AF.Exp, [m["dt"]], [m["dt"]])
        tt("dve", m["ar"], m["are"], m["dt"], ALU.mult)
        Hh.act(m["ar"].h[:], m["ar"].h[:], AF.Exp, [m["ar"]], [m["ar"]])
        tt("dve", m["th"], m["aim"], m["dt"], ALU.mult)
        Hh.ts("dve", m["kf"].h[:], m["th"].h[:], 1.0 / (2.0 * math.pi), ALU.mult, [m["kf"]], [m["th"]])
        Hh.cp("dve", ki.h[:], m["kf"].h[:], [ki], [m["kf"]])
        Hh.cp("dve", m["kf"].h[:], ki.h[:], [m["kf"]], [ki])
        C1 = 6.28125
        C2 = 2.0 * math.pi - C1
        Hh.stt(m["th"].h[:], m["kf"].h[:], -C1, m["th"].h[:], ALU.mult, ALU.add, [m["th"]], [m["kf"], m["th"]])
        Hh.stt(m["th"].h[:], m["kf"].h[:], -C2, m["th"].h[:], ALU.mult, ALU.add, [m["th"]], [m["kf"], m["th"]])
        Hh.ts("dve", m["hl"].h[:], m["th"].h[:], 0.5, ALU.mult, [m["hl"]], [m["th"]])
        Hh.act(m["sh"].h[:], m["hl"].h[:], AF.Sin, [m["sh"]], [m["hl"]])
        Hh.act(m["ah"].h[:], m["hl"].h[:], AF.Abs, [m["ah"]], [m["hl"]])
        hpi2 = sbuf("hpi2", [128, 1])
        Hh.memset("dve", hpi2.h[:], math.pi / 2.0, [hpi2])
        Hh.act(m["ch"].h[:], m["ah"].h[:], AF.Sin, [m["ch"]], [m["ah"], hpi2], scale=-1.0, bias=hpi2.h[:, 0:1])
        tt("dve", m["sn"], m["sh"], m["ch"], ALU.mult)
        Hh.ts("dve", m["sn"].h[:], m["sn"].h[:], 2.0, ALU.mult, [m["sn"]], [m["sn"]])
        tt("dve", m["cs"], m["sh"], m["sh"], ALU.mult)
        Hh.ts("dve", m["cs"].h[:], m["cs"].h[:], -2.0, ALU.mult, [m["cs"]], [m["cs"]], s2=1.0, op1=ALU.add)
        tt("dve", m["abr"], m["ar"], m["cs"], ALU.mult)
        tt("dve", m["abi"], m["ar"], m["sn"], ALU.mult)
        Hh.ts("dve", m["nr"].h[:], m["abr"].h[:], -1.0, ALU.add, [m["nr"]], [m["abr"]])
        tt("dve", m["den"], m["are"], m["are"], ALU.mult)
        tt("dve", m["t1"], m["aim"], m["aim"], ALU.mult)
        tt("dve", m["den"], m["den"], m["t1"], ALU.add)
        P.op("dve", lambda e: e.reciprocal(out=m["den"].h[:], in_=m["den"].h[:]), w=[m["den"]], r=[m["den"]])
        tt("dve", m["t1"], m["nr"], m["are"], ALU.mult)
        tt("dve", m["t2"], m["abi"], m["aim"], ALU.mult)
        tt("dve", m["cre"], m["t1"], m["t2"], ALU.add)
        tt("dve", m["cre"], m["cre"], m["den"], ALU.mult)
        tt("dve", m["t1"], m["abi"], m["are"], ALU.mult)
        tt("dve", m["t2"], m["nr"], m["aim"], ALU.mult)
        tt("dve", m["cim"], m["t1"], m["t2"], ALU.subtract)
        tt("dve", m["cim"], m["cim"], m["den"], ALU.mult)

        K_re = sbuf("Kre", [128, 32, LC])
        K_im = sbuf("Kim", [128, 32, LC])
        H_re = sbuf("Hre", [128, 32, LC])
        H_im = sbuf("Him", [128, 32, LC])
        WbT_re = sbuf("WbTre", [128, 32, 128])
        WbT_im = sbuf("WbTim", [128, 32, 128])
        banks = [P.ps(st, tag + "bank%d" % i, [128, 512]) for i in range(8)]
        bctr = [0]

        def nb():
            b = banks[bctr[0] % 8]
            bctr[0] += 1
            return b
        Braw_re = P.view(K_re.h[:, 0:4, :].rearrange("p a b -> p (a b)"), "Braw_re")
        Braw_im = P.view(K_im.h[:, 0:4, :].rearrange("p a b -> p (a b)"), "Braw_im")
        bb_re = P.view(H_re.h[:, 0:4, :].rearrange("p a b -> p (a b)"), "bb_re")
        bb_im = P.view(H_im.h[:, 0:4, :].rearrange("p a b -> p (a b)"), "bb_im")
        tmpa = P.view(K_re.h[:, 4:8, :].rearrange("p a b -> p (a b)"), "tmpa")
        tmpb = P.view(K_im.h[:, 4:8, :].rearrange("p a b -> p (a b)"), "tmpb")
        P.dma("sp", Braw_re.h, L["bre"].h[:, :], w=Braw_re, r=L["bre"])
        P.dma("sp", Braw_im.h, L["bim"].h[:, :], w=Braw_im, r=L["bim"])
        v3 = lambda b: b.h.rearrange("p (a c) -> p a c", a=32)
        cre_b = bc(m["cre"].h[:], [128, 32, 16])
        cim_b = bc(m["cim"].h[:], [128, 32, 16])
        Hh.tt("dve", v3(tmpa), v3(Braw_re), cre_b, ALU.mult, [tmpa], [Braw_re, m["cre"]])
        Hh.tt("dve", v3(tmpb), v3(Braw_im), cim_b, ALU.mult, [tmpb], [Braw_im, m["cim"]])
        Hh.tt("dve", bb_re.h, tmpa.h, tmpb.h, ALU.subtract, [bb_re], [tmpa, tmpb])
        Hh.tt("dve", v3(tmpa), v3(Braw_im), cre_b, ALU.mult, [tmpa], [Braw_im, m["cre"]])
        Hh.tt("dve", v3(tmpb), v3(Braw_re), cim_b, ALU.mult, [tmpb], [Braw_re, m["cim"]])
        Hh.tt("dve", bb_im.h, tmpa.h, tmpb.h, ALU.add, [bb_im], [tmpa, tmpb])
        pad = [sbuf("pad%d" % i, [128, 128]) for i in range(2)]
        for i in range(2):
            Hh.memset("dve", pad[i].h[:], 0.0, [pad[i]])
        npad = 0
        for pi in range(32):
            c0 = (pi % 4) * 32
            for (src, dstW) in ((bb_re, WbT_re), (bb_im, WbT_im)):
                pd = pad[npad % 2]
                npad += 1
                Hh.memset("pool", pd.h[:], 0.0, [pd])
                Hh.cp("pool", pd.h[0:64, c0:c0 + 16], src.h[0:64, pi * 16:(pi + 1) * 16], [pd], [src])
                Hh.cp("pool", pd.h[64:128, c0 + 16:c0 + 32], src.h[64:128, pi * 16:(pi + 1) * 16], [pd], [src])
                bk = nb()
                Hh.tr(bk.h[:, 0:128], pd.h[:], idt.h[:], [bk], [pd, idt])
                Hh.cp("act", dstW.h[:, pi, :], bk.h[:, 0:128], [dstW], [bk])
        Cre = sbuf("Cre", [128, 32, 32])
        Cni = sbuf("Cni", [128, 32, 32])
        P.dma("sp", Cre.h[:].rearrange("p a b -> p (a b)"), L["cre"].h[:, :], w=Cre, r=L["cre"])
        P.dma("sp", Cni.h[:].rearrange("p a b -> p (a b)"), L["cim"].h[:, :], w=Cni, r=L["cim"])
        Hh.ts("dve", Cni.h[:], Cni.h[:], -1.0, ALU.mult, [Cni], [Cni])
        Dt = sbuf("Dt", [128, 1024])
        msel = sbuf("msel", [128, 2])
        P.dma("sp", msel.h[:], L["msel"].h.partition_broadcast(128), w=msel, r=L["msel"])
        iself = sbuf("iself", [128, 2, 128])
        for h in range(2):
            P.dma("sp", iself.h[:, h, :], isel_d.h[h], w=iself, r=isel_d)
        P.dma("sp", Dt.h[:], L["dsk"].h.partition_broadcast(128), w=Dt, r=L["dsk"])
        msk = sbuf("msk", [128, 32, LC], BF16)
        Hh.memset("dve", msk.h[:], 1.0, [msk])
        Hh.memset("dve", msk.h[:, :, 0:1], 0.0, [msk])
        Tp_re = sbuf("Tpre", [128, 32, LC])
        Tp_im = sbuf("Tpim", [128, 32, LC])
        Tn_re = sbuf("Tnre", [128, 32, LC])
        Tn_im = sbuf("Tnim", [128, 32, LC])
        P.barrier()
        Hh.cp("dve", Tp_re.h[:, :, 0], m["abr"].h[:], [Tp_re], [m["abr"]])
        Hh.cp("dve", Tp_im.h[:, :, 0], m["abi"].h[:], [Tp_im], [m["abi"]])
        n = 1
        while n < LC:
            sre = bc(Tp_re.h[:, :, n - 1], [128, 32, n])
            sim = bc(Tp_im.h[:, :, n - 1], [128, 32, n])
            A_re = Tp_re.h[:, :, 0:n]
            A_im = Tp_im.h[:, :, 0:n]
            t1 = K_re.h[:, :, 0:n]
            t2 = K_im.h[:, :, 0:n]
            t3 = H_re.h[:, :, 0:n]
            t4 = H_im.h[:, :, 0:n]
            Hh.tt("dve", t1, A_re, sre, ALU.mult, [K_re], [Tp_re])
            Hh.tt("dve", t2, A_im, sim, ALU.mult, [K_im], [Tp_im])
            Hh.tt("dve", t3, A_re, sim, ALU.mult, [H_re], [Tp_re, Tp_im])
            Hh.tt("dve", t4, A_im, sre, ALU.mult, [H_im], [Tp_re, Tp_im])
            Hh.tt("dve", Tp_re.h[:, :, n:2 * n], t1, t2, ALU.subtract, [Tp_re], [K_re, K_im])
            Hh.tt("dve", Tp_im.h[:, :, n:2 * n], t3, t4, ALU.add, [Tp_im], [H_re, H_im])
            n *= 2
        Hh.tt("dve", K_re.h[:], Tp_re.h[:], Tp_re.h[:], ALU.mult, [K_re], [Tp_re])
        Hh.tt("dve", K_im.h[:], Tp_im.h[:], Tp_im.h[:], ALU.mult, [K_im], [Tp_im])
        Hh.tt("dve", K_re.h[:], K_re.h[:], K_im.h[:], ALU.add, [K_re], [K_re, K_im])
        P.op("dve", lambda e: e.reciprocal(out=K_re.h[:], in_=K_re.h[:]), w=[K_re], r=[K_re])
        Hh.tt("dve", Tn_re.h[:], Tp_re.h[:], K_re.h[:], ALU.mult, [Tn_re], [Tp_re, K_re])
        Hh.tt("dve", Tn_im.h[:], Tp_im.h[:], K_re.h[:], ALU.mult, [Tn_im], [Tp_im, K_re])
        Hh.ts("dve", Tn_im.h[:], Tn_im.h[:], -1.0, ALU.mult, [Tn_im], [Tn_im])
        Hh.memset("dve", m["hpr"].h[:], 0.0, [m["hpr"]])
        Hh.memset("dve", m["hpi"].h[:], 0.0, [m["hpi"]])

        xu = sbuf("xu", [128, 2048])
        uT = sbuf("uT", [128, 8, 128])
        tq = [sbuf("tq%d" % i, [128, 4, LC]) for i in range(2)]
        tq = tq + tq
        du = sbuf("du", [128, 1024])
        ybf = sbuf("ybf", [128, 1024], BF16)
        for ch in range(NCH):
            tb = ch * LC
            P.dma("sp", xu.h[:], x_in.h[rm(tb):rm(tb) + 128, :], w=xu, r=x_in)
            for q in range(2):
                bk = nb()
                for i in range(4):
                    for h in range(2):
                        c0 = h * 1024 + (q * 4 + i) * 128
                        Hh.mm(bk.h[:, i * 128:(i + 1) * 128], xu.h[:, c0:c0 + 128], iself.h[:, h, :], [bk], [xu, iself],
                              start=(h == 0), stop=(h == 1))
                Hh.cp("act", uT.h[:, q * 4:(q + 1) * 4, :], bk.h[:].rearrange("p (a b) -> p a b", a=4), [uT], [bk])
            for q in range(8):
                bre = nb()
                bim = nb()
                for i in range(4):
                    pi = q * 4 + i
                    Hh.mm(bre.h[:, i * 128:(i + 1) * 128], WbT_re.h[:, pi, :], uT.h[:, q, :], [bre], [WbT_re, uT])
                    Hh.mm(bim.h[:, i * 128:(i + 1) * 128], WbT_im.h[:, pi, :], uT.h[:, q, :], [bim], [WbT_im, uT])
                qs = slice(q * 4, (q + 1) * 4)
                b3 = lambda b: b.h[:].rearrange("p (a b) -> p a b", a=4)
                Hh.tt("dve", tq[0].h[:], Tn_re.h[:, qs, :], b3(bre), ALU.mult, [tq[0]], [Tn_re, bre])
                Hh.tt("dve", tq[1].h[:], Tn_im.h[:, qs, :], b3(bim), ALU.mult, [tq[1]], [Tn_im, bim])
                Hh.tt("pool", K_re.h[:, qs, :], tq[0].h[:], tq[1].h[:], ALU.subtract, [K_re], [tq[0], tq[1]])
                Hh.tt("dve", tq[2].h[:], Tn_re.h[:, qs, :], b3(bim), ALU.mult, [tq[2]], [Tn_re, bim])
                Hh.tt("dve", tq[3].h[:], Tn_im.h[:, qs, :], b3(bre), ALU.mult, [tq[3]], [Tn_im, bre])
                Hh.tt("pool", K_im.h[:, qs, :], tq[2].h[:], tq[3].h[:], ALU.add, [K_im], [tq[2], tq[3]])
            Hh.tt("pool", K_re.h[:, :, 0], K_re.h[:, :, 0], m["hpr"].h[:], ALU.add, [K_re], [K_re, m["hpr"]])
            Hh.tt("pool", K_im.h[:, :, 0], K_im.h[:, :, 0], m["hpi"].h[:], ALU.add, [K_im], [K_im, m["hpi"]])
            f2 = lambda b: b.h[:].rearrange("p a b -> p (a b)")
            P.op("dve", lambda e: e.tensor_tensor_scan(out=f2(H_re), data0=f2(msk), data1=f2(K_re), initial=0.0,
                                                       op0=ALU.mult, op1=ALU.add), w=[H_re], r=[msk, K_re])
            P.op("dve", lambda e: e.tensor_tensor_scan(out=f2(H_im), data0=f2(msk), data1=f2(K_im), initial=0.0,
                                                       op0=ALU.mult, op1=ALU.add), w=[H_im], r=[msk, K_im])
            for q in range(8):
                qs = slice(q * 4, (q + 1) * 4)
                Hh.tt("dve", tq[0].h[:], Tp_re.h[:, qs, :], H_re.h[:, qs, :], ALU.mult, [tq[0]], [Tp_re, H_re])
                Hh.tt("dve", tq[1].h[:], Tp_im.h[:, qs, :], H_im.h[:, qs, :], ALU.mult, [tq[1]], [Tp_im, H_im])
                Hh.tt("pool", K_re.h[:, qs, :], tq[0].h[:], tq[1].h[:], ALU.subtract, [K_re], [tq[0], tq[1]])
                Hh.tt("dve", tq[2].h[:], Tp_re.h[:, qs, :], H_im.h[:, qs, :], ALU.mult, [tq[2]], [Tp_re, H_im])
                Hh.tt("dve", tq[3].h[:], Tp_im.h[:, qs, :], H_re.h[:, qs, :], ALU.mult, [tq[3]], [Tp_im, H_re])
                Hh.tt("pool", K_im.h[:, qs, :], tq[2].h[:], tq[3].h[:], ALU.add, [K_im], [tq[2], tq[3]])
            Hh.cp("pool", m["hpr"].h[:], K_re.h[:, :, LC - 1], [m["hpr"]], [K_re])
            Hh.cp("pool", m["hpi"].h[:], K_im.h[:, :, LC - 1], [m["hpi"]], [K_im])
            by = [nb(), nb()]
            for pi in range(32):
                o_ = by[pi // 16].h[:, (pi % 16) * 32:(pi % 16 + 1) * 32]
                Hh.mm(o_, K_re.h[:, pi, :], Cre.h[:, pi, :], [by[pi // 16]], [K_re, Cre], start=True, stop=False)
                Hh.mm(o_, K_im.h[:, pi, :], Cni.h[:, pi, :], [by[pi // 16]], [K_im, Cni], start=False, stop=True)
            Hh.ts("pool", du.h[:], xu.h[:, 0:1024], msel.h[:, 0:1], ALU.mult, [du], [xu, msel])
            Hh.stt(du.h[:], xu.h[:, 1024:2048], msel.h[:, 1:2], du.h[:], ALU.mult, ALU.add, [du], [xu, msel, du])
            Hh.tt("pool", du.h[:], du.h[:], Dt.h[:], ALU.mult, [du], [du, Dt])
            for i in range(2):
                Hh.tt("dve", du.h[:, i * 512:(i + 1) * 512], by[i].h[:], du.h[:, i * 512:(i + 1) * 512], ALU.add, [du], [by[i], du])
            Hh.act(ybf.h[:], du.h[:], AF.Gelu_apprx_tanh, [ybf], [du])
            P.dma("sp", y_out.h[tb:tb + 128, :], ybf.h[:], w=y_out, r=ybf, owner=ybf)
        P.end_stage()


FF = 5632
PAIRS = [[0, 1], [2, 3], [4, 5], [6, 7]]
DEEPNORM_ALPHA = (2.0 * 4) ** 0.25


def load_hT_gathered(P, Hh, src_g, SEQ, TC, Vh, iselb, t0, xs, hT, banks, nbk):
    PR = min(TC, (2 << 20) // (Vh * 2))
    NK = 2 * Vh // 128
    for s in range(4):
        off = t0 + s * 128
        for h in range(2):
            for r in range(2):
                row = h * 2 * TC + (off // PR) * 2 * PR + r * PR + off % PR
                P.dma("sp", xs[h].h[:, r * Vh:(r + 1) * Vh], src_g.h[row:row + 128, :], w=xs[h], r=src_g)
        for q in range(NK // 4):
            bk = banks[nbk[0] % 8]
            nbk[0] += 1
            for i in range(4):
                dk = q * 4 + i
                for h in range(2):
                    Hh.mm(bk.h[:, i * 128:(i + 1) * 128], xs[h].h[:, dk * 128:(dk + 1) * 128], iselb.h[:, h, :], [bk],
                          [xs[h], iselb], start=(h == 0), stop=(h == 1))
            Hh.cp("act" if q % 2 == 0 else "dve", hT.h[:, q * 4:(q + 1) * 4, s * 128:(s + 1) * 128],
                  bk.h[:].rearrange("p (a b) -> p a b", a=4), [hT], [bk])


def load_isel(P, Hh, st, isel_d, tag):
    iself = P.sb(st, tag + "iself", [128, 2, 128], F32)
    for h in range(2):
        P.dma("sp", iself.h[:, h, :], isel_d.h[h], w=iself, r=isel_d)
    iselb = P.sb(st, tag + "iselb", [128, 2, 128], BF16)
    Hh.cp("dve", iselb.h[:], iself.h[:], [iselb], [iself])
    return iself, iselb


def outproj_stage(P, og_g, SEQ, Vh, x_res, x_out, w_out, ln_g, ln_b, ident_d, TC, alpha, isel_d, tag="p"):
    Hh = H(P)
    NB = TC // 512
    V = 2 * Vh
    NK = V // 128
    NKG = 8
    with contextlib.ExitStack() as st:
        idt, epsT = load_consts(P, st, ident_d)
        iself, iselb = load_isel(P, Hh, st, isel_d, tag)
        Gt = P.sb(st, tag + "G", [128, D], F32)
        Bt = P.sb(st, tag + "B", [128, D], F32)
        P.dma("sp", Gt.h[:], ln_g.h.partition_broadcast(128), w=Gt, r=ln_g)
        P.dma("sp", Bt.h[:], ln_b.h.partition_broadcast(128), w=Bt, r=ln_b)
        xs = [P.sb(st, tag + "xs%d" % i, [128, V], BF16) for i in range(2)]
        hT = P.sb(st, tag + "hT", [128, NK, 512], BF16)
        wd = [P.sb(st, tag + "wd%d" % i, [128, NKG, 512], BF16) for i in range(2)]
        z = [P.sb(st, tag + "z%d" % i, [128, D], F32) for i in range(4)]
        tmps = [ln_tmp(P, st, tag + "t%d" % i) for i in range(2)]
        bank = [P.ps(st, tag + "bank%d" % i, [128, 512]) for i in range(8)]
        nwd = 0
        nbk = [0]
        for blk in range(NB):
            t0 = blk * 512
            load_hT_gathered(P, Hh, og_g, SEQ, TC, Vh, iselb, t0, xs, hT, bank, nbk)
            for s in range(4):
                P.dma("sp", z[s].h[:], x_res.h[t0 + s * 128:t0 + (s + 1) * 128, :], w=z[s], r=x_res)
                Hh.act(z[s].h[:], z[s].h[:], AF.Copy, [z[s]], [z[s]], scale=float(alpha))
            for dn in range(4):
                pb = [bank[(nbk[0] + s) % 8] for s in range(4)]
                nbk[0] += 4
                for fg in range(NK // NKG):
                    sl = nwd % 2
                    nwd += 1
                    for a in range(0, NKG, 4):
                        r0 = (fg * NKG + a) * 128
                        P.dma("pool", wd[sl].h[:, a:a + 4, :],
                              w_out.h[r0:r0 + 4 * 128, dn * 512:(dn + 1) * 512].rearrange("(k p) f -> p k f", p=128),
                              w=wd[sl], r=w_out)
                    for i in range(NKG):
                        fk = fg * NKG + i
                        for s in range(4):
                            Hh.mm(pb[s].h[:], hT.h[:, fk, s * 128:(s + 1) * 128], wd[sl].h[:, i, :], [pb[s]], [hT, wd[sl]],
                                  start=(fk == 0), stop=(fk == NK - 1))
                for s in range(4):
                    zc = z[s].h[:, dn * 512:(dn + 1) * 512]
                    Hh.tt("dve", zc, pb[s].h[:], zc, ALU.add, [z[s]], [pb[s], z[s]])
            for s in range(4):
                layer_norm_tile(P, z[s], Gt, Bt, epsT, tmps[s % 2])
                P.dma("sp", x_out.h[t0 + s * 128:t0 + (s + 1) * 128, :], z[s].h[:], w=x_out, r=z[s], owner=z[s])
        P.end_stage()


def s5out_stage(P, y_g, SEQ, x_res, x_out, w_glu, ln_g, ln_b, ident_d, TC, alpha, isel_d, tag="q"):
    Hh = H(P)
    NB = TC // 512
    with contextlib.ExitStack() as st:
        idt, epsT = load_consts(P, st, ident_d)
        iself, iselb = load_isel(P, Hh, st, isel_d, tag)
        Gt = P.sb(st, tag + "G", [128, D], F32)
        Bt = P.sb(st, tag + "B", [128, D], F32)
        P.dma("sp", Gt.h[:], ln_g.h.partition_broadcast(128), w=Gt, r=ln_g)
        P.dma("sp", Bt.h[:], ln_b.h.partition_broadcast(128), w=Bt, r=ln_b)
        xs = [P.sb(st, tag + "xs%d" % i, [128, D], BF16) for i in range(2)]
        yT = P.sb(st, tag + "yT", [128, 16, 512], BF16)
        wv = [P.sb(st, tag + "wv%d" % i, [128, 16, 512], BF16) for i in range(2)]
        wg = [P.sb(st, tag + "wg%d" % i, [128, 16, 512], BF16) for i in range(2)]
        sg = [P.sb(st, tag + "sg%d" % i, [128, 512], F32) for i in range(2)]
        z = [P.sb(st, tag + "z%d" % i, [128, D], F32) for i in range(4)]
        tmps = [ln_tmp(P, st, tag + "t%d" % i) for i in range(2)]
        bank = [P.ps(st, tag + "bank%d" % i, [128, 512]) for i in range(8)]
        nw = 0
        nbk = [0]
        for blk in range(NB):
            t0 = blk * 512
            load_hT_gathered(P, Hh, y_g, SEQ, TC, 1024, iselb, t0, xs, yT, bank, nbk)
            for s in range(4):
                P.dma("sp", z[s].h[:], x_res.h[t0 + s * 128:t0 + (s + 1) * 128, :], w=z[s], r=x_res)
                Hh.act(z[s].h[:], z[s].h[:], AF.Copy, [z[s]], [z[s]], scale=float(alpha))
            for dn in range(4):
                sl = nw % 2
                nw += 1
                for k0 in range(0, 16, 4):
                    P.dma("pool", wv[sl].h[:, k0:k0 + 4, :],
                          w_glu.h[k0 * 128:(k0 + 4) * 128, dn * 512:(dn + 1) * 512].rearrange("(k p) f -> p k f", p=128),
                          w=wv[sl], r=w_glu)
                    P.dma("pool", wg[sl].h[:, k0:k0 + 4, :],
                          w_glu.h[k0 * 128:(k0 + 4) * 128, D + dn * 512:D + (dn + 1) * 512].rearrange(
                              "(k p) f -> p k f", p=128), w=wg[sl], r=w_glu)
                for s in range(4):
                    pv = bank[nbk[0] % 8]
                    pg = bank[(nbk[0] + 1) % 8]
                    nbk[0] += 2
                    for dk in range(16):
                        Hh.mm(pg.h[:], yT.h[:, dk, s * 128:(s + 1) * 128], wg[sl].h[:, dk, :], [pg], [yT, wg[sl]],
                              start=(dk == 0), stop=(dk == 15))
                    for dk in range(16):
                        Hh.mm(pv.h[:], yT.h[:, dk, s * 128:(s + 1) * 128], wv[sl].h[:, dk, :], [pv], [yT, wv[sl]],
                              start=(dk == 0), stop=(dk == 15))
                    sgb = sg[s % 2]
                    Hh.act(sgb.h[:], pg.h[:], AF.Sigmoid, [sgb], [pg])
                    Hh.tt("dve", sgb.h[:], sgb.h[:], pv.h[:], ALU.mult, [sgb], [sgb, pv])
                    zc = z[s].h[:, dn * 512:(dn + 1) * 512]
                    Hh.tt("dve", zc, zc, sgb.h[:], ALU.add, [z[s]], [z[s], sgb])
            for s in range(4):
                layer_norm_tile(P, z[s], Gt, Bt, epsT, tmps[s % 2])
                P.dma("sp", x_out.h[t0 + s * 128:t0 + (s + 1) * 128, :], z[s].h[:], w=x_out, r=z[s], owner=z[s])
        P.end_stage()


def _consts_np():
    c = gdn_consts_np()
    c.update(gla_consts_np())
    return c


S5_KEYS = ("are", "aim", "ldt", "bre", "bim", "cre", "cim", "dsk", "msel")


def build_program(SEQ, layers=(0, 1, 2, 3)):
    TC = SEQ // 2
    nc = bass.Bass("TRN2", target_bir_lowering=False)
    P = Prog(nc)
    ext = lambda n, s, d=F32: P.dram(n, s, d, kind="ExternalInput")
    x = ext("x", [TC, D])
    y = P.dram("y", [TC, D], F32, kind="ExternalOutput")
    ln_g = ext("ln_g", [4, 3, D])
    ln_b = ext("ln_b", [4, 3, D])
    w_up = ext("ffn_w_up", [4, 2, D, 2 * FF])
    w_dn = ext("ffn_w_down", [4, 2, FF, D])
    gdn_wout = ext("gdn_w_out", [2, 4096, D])
    gla_wout = ext("gla_w_out", [1, 2048, D])
    s5_wglu = ext("s5_w_glu", [1, D, 2 * D])
    gdn_nw = ext("gdn_norm_w", [2, 128])
    gla_nw = ext("gla_norm_w", [1, 512])
    gdn_wc = ext("gdn_wc", [2, D, 6176])
    gdn_conv = ext("gdn_conv", [2, 128, 128])
    gdn_alog = ext("gdn_alog", [2, 16])
    gdn_dtb = ext("gdn_dtb", [2, 16])
    gla_wc = ext("gla_wc", [D, 3088])
    gla_wga = ext("gla_wga", [17, 512])
    L = {}
    l0 = s5_layout_np(*[np.zeros(s, np.float32) for s in ((128, 64), (128, 64), (128,), (128, 64, 16), (128, 64, 16),
                                                          (128, 16, 64), (128, 16, 64), (2048,))], 0)
    for k in S5_KEYS:
        L[k] = ext("s5L_" + k, list(l0[k].shape))
    C = {k: ext("c_" + k, list(v.shape)) for k, v in _consts_np().items()}
    A = P.dram("actA", [TC, D], F32)
    B = P.dram("actB", [TC, D], F32)
    Cb = P.dram("actC", [TC, D], F32)
    XF = P.dram("actXF", [SEQ, D], F32)
    OG2 = P.dram("og2", [SEQ, 2048], BF16)
    QKV = P.dram("gdn_qkv", [4096, SEQ], BF16)
    Zd = P.dram("gdn_z", [SEQ, 2048], BF16)
    SSQd = P.dram("gdn_ssq", [SEQ, 16], F32)
    BAd = P.dram("gdn_ba", [32, SEQ], F32)
    OGG2 = P.dram("ogg2", [2 * SEQ, 2048], BF16)
    OG1 = P.dram("og1", [SEQ, 1024], BF16)
    OGG1 = P.dram("ogg1", [2 * SEQ, 1024], BF16)
    isel = ext("isel", [2, 128, 128])
    V = lambda ap, n: Buf(ap, n)
    rm = lambda t: ((t % TC) // 256) * 512 + (t // TC) * 256 + (t % 256)
    alpha = DEEPNORM_ALPHA
    ident = C["ident"]
    xin = x
    for li, i in enumerate(layers):
        last = (li == len(layers) - 1)
        kind, j = i % 3, i // 3
        ffn_stage(P, xin, A, V(w_up.h[i, 0], "wu"), V(w_dn.h[i, 0], "wd"), V(ln_g.h[i, 0], "g"), V(ln_b.h[i, 0], "b"),
                  ident, TC, alpha, tag="f%da" % i)
        P.gather_pairs(A, XF, TC, 256)
        if kind == 0:
            gdn_proj_stage(P, XF, QKV, Zd, SSQd, BAd, V(gdn_wc.h[j], "gw"), V(gdn_conv.h[j], "gc"), C, SEQ,
                           tag="gp%d" % i, rm=rm)
            gdn_rec_stage(P, QKV, Zd, SSQd, BAd, OG2, V(gdn_alog.h[j], "ga"), V(gdn_dtb.h[j], "gd"),
                          V(gdn_nw.h[j], "gn"), C, SEQ, tag="g%d" % i)
            P.gather_pairs(OG2, OGG2, SEQ, min(TC, 512))
            outproj_stage(P, OGG2, SEQ, 2048, A, B, V(gdn_wout.h[j], "gwo"), V(ln_g.h[i, 1], "g"), V(ln_b.h[i, 1], "b"),
                          ident, TC, alpha, isel, tag="p%d" % i)
        elif kind == 1:
            gla_stage(P, XF, OG1, gla_wc, gla_wga, V(gla_nw.h[j], "ln"), C, SEQ, tag="l%d" % i, rm=rm)
            P.gather_pairs(OG1, OGG1, SEQ, min(TC, 1024))
            outproj_stage(P, OGG1, SEQ, 1024, A, B, V(gla_wout.h[j], "lwo"), V(ln_g.h[i, 1], "g"), V(ln_b.h[i, 1], "b"),
                          ident, TC, alpha, isel, tag="p%d" % i)
        else:
            s5_stage(P, XF, OG1, L, ident, SEQ, isel, tag="s%d" % i, rm=rm)
            P.gather_pairs(OG1, OGG1, SEQ, min(TC, 1024))
            s5out_stage(P, OGG1, SEQ, A, B, V(s5_wglu.h[j], "sw"), V(ln_g.h[i, 1], "g"), V(ln_b.h[i, 1], "b"),
                        ident, TC, alpha, isel, tag="q%d" % i)
        dst = y if last else Cb
        ffn_stage(P, B, dst, V(w_up.h[i, 1], "wu"), V(w_dn.h[i, 1], "wd"), V(ln_g.h[i, 2], "g"), V(ln_b.h[i, 2], "b"),
                  ident, TC, alpha, tag="f%db" % i)
        xin = Cb
    P.finish()
    return nc, P


def make_in_maps(inputs, SEQ):
    TC = SEQ // 2
    f = lambda a: np.ascontiguousarray(np.asarray(a, dtype=np.float32))
    rep = {k: f(inputs[k]) for k in ("ln_g", "ln_b", "ffn_w_up", "ffn_w_down", "gdn_w_out", "gla_w_out", "s5_w_glu",
                                     "gdn_norm_w", "gla_norm_w")}
    consts = {"c_" + k: v for k, v in _consts_np().items()}
    x = np.asarray(inputs["x"], dtype=np.float32)
    gdn_w_in = np.asarray(inputs["gdn_w_in"], np.float32)
    gdn_conv_w = np.asarray(inputs["gdn_conv_w"], np.float32)
    gla_w_in = np.asarray(inputs["gla_w_in"], np.float32)[0]
    gla_w_gate = np.asarray(inputs["gla_w_gate"], np.float32)[0]
    gla_gb = np.asarray(inputs["gla_gate_bias"], np.float32)[0]
    half = []
    for hf in range(2):
        d = {}
        cols = np.concatenate([np.arange(hf * 1024, hf * 1024 + 1024), 2048 + np.arange(hf * 1024, hf * 1024 + 1024),
                               4096 + np.arange(hf * 2048, hf * 2048 + 2048), 8192 + np.arange(hf * 2048, hf * 2048 + 2048),
                               12288 + np.arange(hf * 16, hf * 16 + 16), 12320 + np.arange(hf * 16, hf * 16 + 16)])
        d["gdn_wc"] = np.ascontiguousarray(gdn_w_in[:, :, cols])
        cc = cols[:4096]
        d["gdn_conv"] = np.ascontiguousarray(
            gdn_conv_w[:, :, cc].reshape(2, 4, 32, 128).transpose(0, 3, 2, 1).reshape(2, 128, 128))
        d["gdn_alog"] = f(np.asarray(inputs["gdn_a_log"])[:, hf * 16:(hf + 1) * 16])
        d["gdn_dtb"] = f(np.asarray(inputs["gdn_dt_bias"])[:, hf * 16:(hf + 1) * 16])
        lc = np.concatenate([np.arange(hf * 512, hf * 512 + 512), 1024 + np.arange(hf * 512, hf * 512 + 512),
                             2048 + np.arange(hf * 1024, hf * 1024 + 1024), 4096 + np.arange(hf * 1024, hf * 1024 + 1024),
                             6144 + np.arange(16)])
        d["gla_wc"] = np.ascontiguousarray(gla_w_in[:, lc])
        d["gla_wga"] = np.ascontiguousarray(
            np.concatenate([gla_w_gate[:, hf * 512:(hf + 1) * 512], gla_gb[None, hf * 512:(hf + 1) * 512]], 0))
        Lnp = s5_layout_np(*[np.asarray(inputs[k], np.float32)[0] for k in
                             ("s5_a_re", "s5_a_im", "s5_log_dt", "s5_b_re", "s5_b_im", "s5_c_re", "s5_c_im", "s5_d")], hf)
        for k in S5_KEYS:
            d["s5L_" + k] = Lnp[k]
        half.append(d)
    in_maps = []
    for c in range(8):
        b, hf = c // 2, c % 2
        m = dict(rep)
        m.update(consts)
        m.update(half[hf])
        m["x"] = np.ascontiguousarray(x[b, hf * TC:(hf + 1) * TC])
        isel = np.zeros((2, 128, 128), np.float32)
        isel[hf] = np.eye(128, dtype=np.float32)
        m["isel"] = isel
        in_maps.append(m)
    return in_maps


_CACHE = {}


def run_model(inputs, SEQ, trace=False):
    from concourse.bass_utils import run_bass_kernel_spmd
    if SEQ not in _CACHE:
        _CACHE[SEQ] = build_program(SEQ)[0]
    nc = _CACHE[SEQ]
    in_maps = make_in_maps(inputs, SEQ)
    res = run_bass_kernel_spmd(nc, in_maps, core_ids=list(range(8)))
    TC = SEQ // 2
    out = np.empty((4, SEQ, D), np.float32)
    for c in range(8):
        b, hf = c // 2, c % 2
        out[b, hf * TC:(hf + 1) * TC] = res.results[c]["y"]
    return out


def kernel(**inputs):
    return run_model(inputs, 8192)
```
